# Optimizing a Trainium2 kernel written in Bass

```python
import jax, jax.numpy as jnp
from jax import lax
import numpy as np

D_MODEL = 1024
BATCH = 32
SEQ = 256
DEPTH = 1
DEC_BATCH = 8
DEC_SEQ = 4096
PAST_LEN = 512

GRID_W = 64
CHUNK = 128
ROWS_PER_CHUNK = CHUNK // GRID_W
D_MIX = D_MODEL
D_A = D_MIX // 2
D_B = D_MIX - D_A
HEAD_A = 64
H_A = D_A // HEAD_A
H_B = 8
HEAD_B = D_B // H_B
LORA_W = 64
LORA_A = 64
D_IN = 3 * D_A + D_A + 2 * LORA_W + 2 * LORA_A + 3 * D_B
NORM_EPS = 1e-6
GN_EPS = 6.4e-4
DECAY_OFFSET = 0.5
L2_EPS = 1e-12

kernel_name = 'bidir_rwkv7_chunk_gmlp_hybrid_step'


def _rmsnorm(x, g):
    x32 = x.astype(jnp.float32)
    y = x32 * lax.rsqrt(jnp.mean(x32 * x32, axis=-1, keepdims=True) + NORM_EPS)
    return (y * g.astype(jnp.float32)).astype(x.dtype)


def _layernorm(x, g, b):
    x32 = x.astype(jnp.float32)
    mu = jnp.mean(x32, axis=-1, keepdims=True)
    var = jnp.mean(jnp.square(x32 - mu), axis=-1, keepdims=True)
    y = (x32 - mu) * lax.rsqrt(var + NORM_EPS)
    return (y * g.astype(jnp.float32) + b.astype(jnp.float32)).astype(x.dtype)


def _modulation(cvec, w_mod, b_mod):
    m = jax.nn.silu(cvec) @ w_mod + b_mod
    return jnp.split(m, 3, axis=-1)


def _split_proj(p):
    sizes = (3 * D_A, D_A, LORA_W, LORA_W, LORA_A, LORA_A, D_B, D_B, D_B)
    idx, acc = [], 0
    for s in sizes[:-1]:
        acc += s
        idx.append(acc)
    return jnp.split(p, idx, axis=-1)


def _token_shift(x, mu):
    x_prev = jnp.pad(x[:, :-1], ((0, 0), (1, 0), (0, 0)))
    x_next = jnp.pad(x[:, 1:], ((0, 0), (0, 1), (0, 0)))
    return x + mu[0] * (x_prev - x) + mu[1] * (x_next - x)


def _delta_scan(r, k, v, decay, kk, kka, s0, reverse):
    def step(s, inp):
        r_t, k_t, v_t, w_t, kk_t, b_t = inp
        sa = jnp.einsum('bhvk,bhk->bhv', s, -kk_t)
        s = (s * w_t[:, :, None, :] + sa[..., None] * b_t[:, :, None, :]
             + v_t[..., None] * k_t[:, :, None, :])
        return s, jnp.einsum('bhvk,bhk->bhv', s, r_t)
    xs = tuple(jnp.swapaxes(t, 0, 1) for t in (r, k, v, decay, kk, kka))
    s_final, ys = lax.scan(step, s0, xs, reverse=reverse)
    return jnp.swapaxes(ys, 0, 1), s_final


def _rwkv7_branch(rkv, wd, ad, s0, p):
    B, L, _ = rkv.shape
    f32 = jnp.float32
    heads = lambda t: t.reshape(B, L, H_A, HEAD_A).astype(f32)
    hv = lambda t: t.reshape(H_A, HEAD_A).astype(f32)
    r, k, v = (heads(t) for t in jnp.split(rkv, 3, axis=-1))
    y = jnp.zeros_like(r)
    bonus = jnp.zeros_like(r)
    finals = []
    for d in range(2):
        w_log = -jax.nn.softplus(-(p['w0'][d] + jnp.tanh(wd[d]) @ p['w_up'][d])) - DECAY_OFFSET
        decay = jnp.exp(-jnp.exp(heads(w_log)))
        a = heads(jax.nn.sigmoid(p['a0'][d] + ad[d] @ p['a_up'][d]))
        kk = k * hv(p['k_k'][d])
        kk = kk * lax.rsqrt(jnp.sum(kk * kk, axis=-1, keepdims=True) + L2_EPS)
        k_d = k * (1.0 + (a - 1.0) * hv(p['k_a'][d]))
        y_d, s_d = _delta_scan(r, k_d, v, decay, kk, kk * a, s0[d], reverse=(d == 1))
        y = y + y_d
        bonus = bonus + jnp.sum(r * k_d * p['r_k'][d].astype(f32), axis=-1, keepdims=True) * v
        finals.append(s_d)
    mu = jnp.mean(y, axis=-1, keepdims=True)
    var = jnp.mean(jnp.square(y - mu), axis=-1, keepdims=True)
    y = (y - mu) * lax.rsqrt(var + GN_EPS) * hv(p['gn_w']) + hv(p['gn_b']) + bonus
    return y.reshape(B, L, D_A).astype(rkv.dtype), finals[0], finals[1]


def _chunk_gmlp(u, vb, p):
    B, L, _ = u.shape
    rows = L // GRID_W
    n_chunks = rows // ROWS_PER_CHUNK
    vn = _layernorm(vb, p['sgu_ln_g'], p['sgu_ln_b']).reshape(B, n_chunks, CHUNK, H_B, HEAD_B)
    s = jnp.einsum('gpq,bnqgc->bnpgc', p['w_s'], vn) + p['b_s'].T[None, None, :, :, None]
    return u * s.reshape(B, L, D_B)


def _mixer_layer(x, mod, s0, p):
    shift, scale, gate = mod
    h = _rmsnorm(x, p['ln_pre']) * (1.0 + scale) + shift
    proj = h @ p['w_in']
    rkv, z_a, wd_f, wd_b, ad_f, ad_b, u, vb, z_b = _split_proj(proj)
    rkv = _token_shift(rkv, p['ts_mu'])
    y_a, s_f, s_b = _rwkv7_branch(rkv, (wd_f, wd_b), (ad_f, ad_b), s0, p)
    y_b = _chunk_gmlp(u, vb, p)
    mixed = jnp.concatenate([y_a * jax.nn.silu(z_a), y_b * jax.nn.silu(z_b)], axis=-1)
    out = mixed @ p['w_out']
    return x + _rmsnorm(out, p['ln_post']) * gate, s_f, s_b


def setup_inputs(seed: int = 0) -> dict:
    key = jax.random.key(seed)
    ks = jax.random.split(key, 32)
    f32 = jnp.float32
    nrm = lambda k, shape, s: s * jax.random.normal(k, shape, f32)
    st_shape = (DEC_BATCH, DEPTH, H_A, HEAD_A, HEAD_A)
    return {
        'x_prompt': nrm(ks[0], (BATCH, SEQ, D_MODEL), 1.0),
        'x_sample': nrm(ks[1], (DEC_BATCH, DEC_SEQ, D_MODEL), 1.0),
        'c': nrm(ks[2], (DEC_BATCH, D_MODEL), 1.0),
        'state_fwd': nrm(ks[3], st_shape, 1.0),
        'state_bwd': nrm(ks[4], st_shape, 1.0),
        'c_ctx': nrm(ks[5], (D_MODEL,), 1.0),
        'ln_pre': 1.0 + nrm(ks[6], (DEPTH, D_MODEL), 0.05),
        'ln_post': 1.0 + nrm(ks[7], (DEPTH, D_MODEL), 0.05),
        'w_mod': nrm(ks[8], (DEPTH, D_MODEL, 3 * D_MODEL), 0.5 * D_MODEL ** -0.5),
        'b_mod': nrm(ks[9], (DEPTH, 3 * D_MODEL), 0.02),
        'w_in': nrm(ks[10], (DEPTH, D_MODEL, D_IN), D_MODEL ** -0.5),
        'ts_mu': 0.25 + nrm(ks[11], (DEPTH, 2, 3 * D_A), 0.1),
        'w0': nrm(ks[12], (DEPTH, 2, D_A), 0.5),
        'w_up': nrm(ks[13], (DEPTH, 2, LORA_W, D_A), 0.5 * LORA_W ** -0.5),
        'a0': nrm(ks[14], (DEPTH, 2, D_A), 0.2),
        'a_up': nrm(ks[15], (DEPTH, 2, LORA_A, D_A), 0.5 * LORA_A ** -0.5),
        'k_k': 0.85 + nrm(ks[16], (DEPTH, 2, D_A), 0.05),
        'k_a': 1.0 + nrm(ks[17], (DEPTH, 2, D_A), 0.05),
        'r_k': nrm(ks[18], (DEPTH, 2, H_A, HEAD_A), 0.1),
        'gn_w': 1.0 + nrm(ks[19], (DEPTH, D_A), 0.05),
        'gn_b': nrm(ks[20], (DEPTH, D_A), 0.02),
        'sgu_ln_g': 1.0 + nrm(ks[21], (DEPTH, D_B), 0.05),
        'sgu_ln_b': nrm(ks[22], (DEPTH, D_B), 0.02),
        'w_s': nrm(ks[23], (DEPTH, H_B, CHUNK, CHUNK), CHUNK ** -0.5),
        'b_s': 1.0 + nrm(ks[24], (DEPTH, H_B, CHUNK), 0.1),
        'w_out': nrm(ks[25], (DEPTH, D_MIX, D_MODEL), D_MIX ** -0.5),
    }


def reference(x_prompt, x_sample, c, state_fwd, state_bwd, c_ctx,
              ln_pre, ln_post, w_mod, b_mod, w_in, ts_mu, w0, w_up, a0, a_up,
              k_k, k_a, r_k, gn_w, gn_b, sgu_ln_g, sgu_ln_b, w_s, b_s, w_out):
    f32 = jnp.float32
    y_prompt, y_sample = x_prompt, x_sample
    zero_state = jnp.zeros((x_prompt.shape[0], H_A, HEAD_A, HEAD_A), f32)
    new_f, new_b = [], []
    for l in range(DEPTH):
        p = {'ln_pre': ln_pre[l], 'ln_post': ln_post[l], 'w_in': w_in[l], 'ts_mu': ts_mu[l],
             'w0': w0[l], 'w_up': w_up[l], 'a0': a0[l], 'a_up': a_up[l], 'k_k': k_k[l],
             'k_a': k_a[l], 'r_k': r_k[l], 'gn_w': gn_w[l], 'gn_b': gn_b[l],
             'sgu_ln_g': sgu_ln_g[l], 'sgu_ln_b': sgu_ln_b[l], 'w_s': w_s[l], 'b_s': b_s[l],
             'w_out': w_out[l]}
        mod_ctx = tuple(m[None, None, :] for m in _modulation(c_ctx, w_mod[l], b_mod[l]))
        mod_lat = tuple(m[:, None, :] for m in _modulation(c, w_mod[l], b_mod[l]))
        y_prompt, s_f, s_b = _mixer_layer(y_prompt, mod_ctx, (zero_state, zero_state), p)
        new_f.append(s_f)
        new_b.append(s_b)
        s0_lat = (state_fwd[:, l].astype(f32), state_bwd[:, l].astype(f32))
        y_sample, _, _ = _mixer_layer(y_sample, mod_lat, s0_lat, p)
    new_state_fwd = jnp.stack(new_f, axis=1).astype(x_prompt.dtype)
    new_state_bwd = jnp.stack(new_b, axis=1).astype(x_prompt.dtype)
    return (y_prompt, y_sample, new_state_fwd, new_state_bwd)
```

```python
import contextlib
import numpy as np
import concourse.bass as bass
import concourse.mybir as mybir
from concourse.bass_utils import run_bass_kernel_spmd

F32 = mybir.dt.float32
BF16 = mybir.dt.bfloat16
AF = mybir.ActivationFunctionType
ALU = mybir.AluOpType
AX = mybir.AxisListType

D = 1024
DIN = 3840
NORM_EPS = 1e-6
GN_EPS = 6.4e-4
L2_EPS = 1e-12
EXPM05 = float(np.exp(-0.5))
C2H = 0.5 * EXPM05
VOFF = 15.0


class Sched:
    def __init__(self, nc, es):
        self.nc = nc
        self.es = es
        self.engs = dict(pe=nc.tensor, dve=nc.vector, act=nc.scalar, pool=nc.gpsimd, sp=nc.sync)
        self.sems = {e: es.enter_context(nc.semaphore("sem_" + e)) for e in self.engs}
        self.cnt = {e: 0 for e in self.engs}
        self.lastw = {}
        self.readers = {}
        self.seen = {e: {} for e in self.engs}
        self.chans = {}
        self.semobj = {}
        self.clock = {e: 0.0 for e in self.engs}
        self.ttime = {}
        self.lastfin = 0.0
        self.cost = dict(pe=0.09, dve=0.55, act=0.45, pool=1.1, sp=0.1)

    def _time(self, eng, needs, tok, cost):
        ready = 0.0
        for sname, val in needs.items():
            t = self.ttime.get((sname, val), 0.0)
            if t > ready:
                ready = t
        start = max(self.clock[eng], ready)
        fin = start + cost
        self.clock[eng] = fin if tok[0].startswith("sem_") else start + 0.1
        self.ttime[tok] = fin
        if fin > self.lastfin:
            self.lastfin = fin

    def _need(self, eng, needs):
        for sname, val in needs.items():
            if self.seen[eng].get(sname, 0) >= val:
                continue
            self.engs[eng].wait_ge(self.semobj[sname], val)
            self.seen[eng][sname] = val

    def _collect(self, eng, reads, writes):
        needs = {}

        def add(tok):
            if tok is None:
                return
            s, v = tok
            if s == "sem_pe" and eng == "pe":
                return
            if needs.get(s, 0) < v:
                needs[s] = v

        for b in reads:
            add(self.lastw.get(b))
        for b in writes:
            add(self.lastw.get(b))
            for s, v in self.readers.get(b, {}).items():
                add((s, v))
        return needs

    def _record(self, tok, reads, writes):
        s, v = tok
        for b in reads:
            self.readers.setdefault(b, {})[s] = v
        for b in writes:
            self.lastw[b] = tok
            self.readers[b] = {}

    def op(self, eng, reads, writes, fn, c=None):
        needs = self._collect(eng, reads, writes)
        self._need(eng, needs)
        inst = fn(self.engs[eng])
        self.cnt[eng] += 1
        sname = "sem_" + eng
        self.semobj[sname] = self.sems[eng]
        inst.then_inc(self.sems[eng], 1)
        self._time(eng, needs, (sname, self.cnt[eng]), self.cost[eng] if c is None else c)
        self._record((sname, self.cnt[eng]), reads, writes)

    def ops(self, eng, reads, writes, fns):
        needs = self._collect(eng, reads, writes)
        self._need(eng, needs)
        inst = None
        for fn in fns:
            inst = fn(self.engs[eng])
        self.cnt[eng] += 1
        sname = "sem_" + eng
        self.semobj[sname] = self.sems[eng]
        inst.then_inc(self.sems[eng], 1)
        self._time(eng, needs, (sname, self.cnt[eng]), self.cost[eng] * len(fns))
        self._record((sname, self.cnt[eng]), reads, writes)

    def dma(self, eng, chan_key, reads, writes, out, in_):
        if chan_key not in self.chans:
            sem = self.es.enter_context(self.nc.semaphore("dch_%d" % len(self.chans)))
            self.chans[chan_key] = [sem, 0, "dch_%d" % len(self.chans)]
            self.semobj[self.chans[chan_key][2]] = sem
        ch = self.chans[chan_key]
        needs = self._collect(eng, reads, writes)
        self._need(eng, needs)
        self.engs[eng].dma_start(out=out, in_=in_).then_inc(ch[0], 16)
        ch[1] += 16
        self._time(eng, needs, (ch[2], ch[1]), 2.5)
        self._record((ch[2], ch[1]), reads, writes)

    def run_streams(self, queue, make_gen, W, before=None, compat=None, offset=0.0):
        active, free, qi = [], list(range(W)), 0
        while qi < len(queue) or active:
            while free and qi < len(queue):
                if compat is not None and not compat(queue[qi], [a[3] for a in active]):
                    break
                if before is not None:
                    before(queue[qi])
                slot = free.pop(0)
                vt0 = min([a[2] for a in active]) if active else min(self.clock.values())
                if qi < W:
                    vt0 += qi * offset
                active.append([slot, make_gen(slot, queue[qi]), vt0, queue[qi]])
                qi += 1
            item = min(active, key=lambda a: a[2])
            self.lastfin = 0.0
            try:
                next(item[1])
                if self.lastfin > 0.0:
                    item[2] = self.lastfin
                else:
                    item[2] += 0.5
            except StopIteration:
                active.remove(item)
                free.append(item[0])

    def barrier(self, engines=("pe", "dve", "act", "pool", "sp")):
        needs = {}
        for e in self.engs:
            if self.cnt[e] > 0:
                needs["sem_" + e] = self.cnt[e]
        for ch in self.chans.values():
            if ch[1] > 0:
                needs[ch[2]] = ch[1]
        for e in engines:
            n2 = {s: v for s, v in needs.items() if not (e == "pe" and s == "sem_pe")}
            self._need(e, n2)


def build(NT_S, NP, debug=False):
    nc = bass.Bass("TRN2", target_bir_lowering=False)
    LS = NT_S * 128
    NTILES = NT_S + 2 * NP
    okind = "ExternalOutput" if debug else "Internal"

    def din(name, shape, dt=F32):
        return nc.dram_tensor(name, list(shape), dt, kind="ExternalInput").ap()

    def dout(name, shape, dt=F32):
        return nc.dram_tensor(name, list(shape), dt, kind="ExternalOutput").ap()

    def dscr(name, shape, dt):
        if debug:
            return nc.dram_tensor(name, list(shape), dt, kind="ExternalOutput").ap()
        return nc.dram_tensor(name, list(shape), dt).ap()

    xs = din("xs", [LS, D])
    xp = din("xp", [NP, 256, D])
    cvT = din("cvT", [128, 8, 2])
    w_in = din("w_in", [D, DIN])
    w_out = din("w_out", [D, D])
    w_mod = din("w_mod", [D, 3 * D])
    pf = din("pf", [128, 80])
    rows = din("rows", [128, 3072])
    bmg = din("bmg", [2, 1024])
    lora_w = din("lora_w", [128, 4, 512])
    rk_in = din("rk_in", [128, 16])
    wsT_in = din("wsT", [128, 8, 128])
    bs_in = din("bs", [128, 8])
    consts = din("consts", [128, 6, 128])
    sel_in = din("sel", [2, 2, 128])
    s0T = din("s0T", [2, 128, 4, 64])

    ys = dout("ys", [LS, D])
    yp = dout("yp", [NP, 256, D])
    nst = dout("nst", [NP, 2, 128, 4, 64])

    st_rkv = dscr("st_rkv", [NTILES, 128, 12, 128], BF16)
    st_low = dscr("st_low", [NTILES, 128, 2, 128], BF16)
    st_vt = dscr("st_vt", [NTILES, 128, 512], BF16)
    st_sza = dscr("st_sza", [NTILES, 128, 512], BF16)
    st_mixb = dscr("st_mixb", [NTILES, 128, 512], BF16)
    st_y = dscr("st_y", [2, NTILES, 128, 512], F32)
    st_bc = dscr("st_bc", [2, NTILES, 128, 8], F32)
    st_gt = dscr("st_gt", [128, 2, 1024], F32)

    seqs = [(xs, NT_S, 0, 0, ys)]
    for p in range(NP):
        seqs.append((xp[p], 2, 1, NT_S + 2 * p, yp[p]))

    with contextlib.ExitStack() as es:
        K = Sched(nc, es)
        K.debug = debug

        def dump(name, key, ap, shape, dt=F32):
            if not debug:
                return
            dd = nc.dram_tensor("dbg_" + name, list(shape), dt, kind="ExternalOutput").ap()
            K.dma("sp", "dbgch", [key], [], dd, ap)
        K.dump = dump

        def sb(name, shape, dt, stack=es):
            return stack.enter_context(nc.sbuf_tensor(name, list(shape), dt))

        def ps(name, shape, dt, stack=es):
            return stack.enter_context(nc.psum_tensor(name, list(shape), dt))

        PBIG = ps("pbig", [128, 7 * 512], F32)
        PB = [PBIG[:, i * 512:(i + 1) * 512] for i in range(7)]
        PT = ps("ptb", [128, 1024], BF16)

        pf_t = sb("pf_t", [128, 80], F32)
        cst = sb("cst", [128, 6, 128], BF16)
        sel_t = sb("sel_t", [2, 2, 128], F32)
        bs_t = sb("bs_t", [128, 8], F32)
        wsT = sb("wsT_sb", [128, 8, 128], BF16)
        lora = sb("lora", [128, 4, 512], BF16)
        rkm = sb("rkm", [128, 16], BF16)
        gfm = sb("gfm", [128, 8, 2], F32)
        shfm = sb("shfm", [128, 8, 2], F32)
        cs0 = sb("cs0", [128, 12], F32)
        omka = sb("omka", [128, 8], F32)
        mA = sb("mA", [128, 2, 4, 128], BF16)
        mT = sb("mT", [128, 2, 2, 128], BF16)

        K.dma("sp", "pf_t", [], ["pf_t"], pf_t[:], pf)
        K.dma("sp", "sel_t", [], ["sel_t"], sel_t[:], sel_in)
        K.dma("sp", "bs_t", [], ["bs_t"], bs_t[:], bs_in)
        K.op("dve", ["pf_t"], ["cs0"], lambda e: e.tensor_tensor(out=cs0[:], in0=pf_t[:, 24:36], in1=pf_t[:, 36:48], op=ALU.add))
        K.op("dve", ["cs0"], ["cs0"], lambda e: e.tensor_scalar(out=cs0[:], in0=cs0[:], scalar1=-1.0, scalar2=1.0, op0=ALU.mult, op1=ALU.add))
        K.op("dve", ["pf_t"], ["omka"], lambda e: e.tensor_scalar(out=omka[:], in0=pf_t[:, 72:80], scalar1=-1.0, scalar2=1.0, op0=ALU.mult, op1=ALU.add))

        with contextlib.ExitStack() as esA:
            win = sb("win", [128, 8, DIN], BF16, esA)
            rows_t = sb("rows_t", [128, 3072], F32, esA)
            with contextlib.ExitStack() as es0:
                cst_f = sb("cst_f", [128, 6, 128], F32, es0)
                Gt = sb("Gt", [128, 2, 1024], F32, es0)
                K.dma("sp", "rows_t", [], ["rows_t"], rows_t[:], rows)
                K.dma("sp", "cst_f", [], ["cst_f"], cst_f[:], consts)
                K.op("dve", ["cst_f"], ["cst"], lambda e: e.tensor_copy(out=cst[:], in_=cst_f[:]))
                for d, (s_i, i_i, t_i) in enumerate([(1, 2, 3), (3, 4, 1)]):
                    for j in range(4):
                        src = s_i if j % 2 == 0 else i_i
                        K.op("pool", ["cst_f"], ["mA"], lambda e, d=d, j=j, src=src: e.tensor_copy(out=mA[:, d, j, :], in_=cst_f[:, src, :]))
                    for j in range(2):
                        K.op("pool", ["cst_f"], ["mT"], lambda e, d=d, j=j, t_i=t_i: e.tensor_copy(out=mT[:, d, j, :], in_=cst_f[:, t_i, :]))
                wst = [sb("wst%d" % i, [128, DIN], F32, es0) for i in range(2)]
                tmpf = sb("tmpf", [128, 4, 512], F32, es0)
                cv_t = sb("cv_t", [128, 8, 2], F32, es0)
                scv = sb("scv", [128, 8, 2], F32, es0)
                bmg_t = sb("bmg_t", [2, 1024], F32, es0)
                grow = sb("grow", [2, 1024], F32, es0)
                modfm = sb("modfm", [128, 16, 2], F32, es0)
                K.dma("sp", "cv_t", [], ["cv_t"], cv_t[:], cvT)
                K.dma("sp", "bmg_t", [], ["bmg_t"], bmg_t[:], bmg)
                K.op("act", ["cv_t"], ["scv"], lambda e: e.activation(out=scv[:], in_=cv_t[:], func=AF.Silu))
                K.dma("sp", "tmpf", [], ["tmpf"], tmpf[:], lora_w)
                K.op("dve", ["tmpf"], ["lora"], lambda e: e.tensor_copy(out=lora[:], in_=tmpf[:]))
                K.dma("sp", "tmpf", ["tmpf"], ["tmpf"], tmpf[:, 0, 0:16], rk_in)
                K.op("dve", ["tmpf"], ["rkm"], lambda e: e.tensor_copy(out=rkm[:], in_=tmpf[:, 0, 0:16]))
                K.dma("sp", "tmpf", [], ["tmpf"], tmpf[:, 0:2, :].rearrange("p a b -> p (a b)"), wsT_in.rearrange("p g q -> p (g q)"))
                K.op("dve", ["tmpf"], ["wsT"], lambda e: e.tensor_copy(out=wsT[:].rearrange("p g q -> p (g q)"), in_=tmpf[:, 0:2, :].rearrange("p a b -> p (a b)")))
                wsi = [sb("wsi%d" % i, [128, DIN], F32, es0) for i in range(2)]

                def wdma(kc):
                    K.dma("act", "wsi%d" % (kc % 2), [], ["wsi%d" % (kc % 2)], wsi[kc % 2][:, :], w_in[kc * 128:(kc + 1) * 128, :])
                wdma(0)
                wdma(1)
                for kc in range(8):
                    K.op("act", ["wsi%d" % (kc % 2)], ["win"], lambda e, kc=kc: e.copy(out=win[:, kc, :], in_=wsi[kc % 2][:, :]))
                    if kc + 2 < 8:
                        wdma(kc + 2)
                for kc in range(8):
                    wm = wst[kc % 2]
                    K.dma("sp", "wst%d" % (kc % 2), [], ["wst%d" % (kc % 2)], wm[:, 0:3072], w_mod[kc * 128:(kc + 1) * 128, :])
                    fns = []
                    for blk in range(16):
                        fns.append(lambda e, blk=blk, kc=kc, wm=wm: e.matmul(PB[0][:, blk * 2:blk * 2 + 2], lhsT=wm[:, blk * 128:(blk + 1) * 128], rhs=scv[:, kc, :], start=(kc == 0 and blk == 0), stop=(kc == 7), skip_group_check=True))
                    for n in range(2):
                        fns.append(lambda e, n=n, kc=kc, wm=wm: e.matmul(PB[1 + n][0:2, :], lhsT=scv[:, kc, :], rhs=wm[:, 2048 + n * 512:2048 + (n + 1) * 512], start=(kc == 0), stop=(kc == 7)))
                    K.ops("pe", ["wst%d" % (kc % 2), "scv"], ["pb0", "pb1", "pb2"], fns)
                K.op("dve", ["pb0", "pf_t"], ["modfm"], lambda e: e.tensor_tensor(out=modfm[:], in0=PB[0][:, 0:32].rearrange("p (b c) -> p b c", c=2), in1=pf_t[:, 0:16].unsqueeze(2).to_broadcast([128, 16, 2]), op=ALU.add))
                K.op("dve", ["modfm"], ["shfm"], lambda e: e.tensor_copy(out=shfm[:], in_=modfm[:, 0:8, :]))
                K.op("dve", ["modfm"], ["gfm"], lambda e: e.tensor_scalar(out=gfm[:], in0=modfm[:, 8:16, :], scalar1=1.0, scalar2=None, op0=ALU.add))
                K.op("dve", ["gfm", "pf_t"], ["gfm"], lambda e: e.tensor_tensor(out=gfm[:], in0=gfm[:], in1=pf_t[:, 16:24].unsqueeze(2).to_broadcast([128, 8, 2]), op=ALU.mult))
                for n in range(2):
                    K.op("dve", ["pb%d" % (1 + n), "bmg_t"], ["grow"], lambda e, n=n: e.tensor_tensor(out=grow[:, n * 512:(n + 1) * 512], in0=PB[1 + n][0:2, :], in1=bmg_t[:, n * 512:(n + 1) * 512], op=ALU.add))
                for cv in range(2):
                    for n in range(2):
                        K.ops("pe", ["grow", "sel_t"], ["pb%d" % (3 + n)], [lambda e, cv=cv, n=n: e.matmul(PB[3 + n][:, :], lhsT=sel_t[:, cv, :], rhs=grow[:, n * 512:(n + 1) * 512], start=True, stop=True)])
                        K.op("dve", ["pb%d" % (3 + n), "rows_t"], ["Gt"], lambda e, cv=cv, n=n: e.tensor_tensor(out=Gt[:, cv, n * 512:(n + 1) * 512], in0=PB[3 + n][:, :], in1=rows_t[:, n * 512:(n + 1) * 512], op=ALU.mult))
                K.dma("sp", "Gt", ["Gt"], [], st_gt, Gt[:])
                dump("gfm", "gfm", gfm[:], [128, 8, 2])
                dump("shfm", "shfm", shfm[:], [128, 8, 2])
                dump("Gt", "Gt", Gt[:], [128, 2, 1024])
                dump("scv", "scv", scv[:], [128, 8, 2])
                K.barrier()

            phase_a(nc, K, esA, sb, PB, PT, seqs, win, dict(
                pf_t=pf_t, rows_t=rows_t, cst=cst, bs_t=bs_t, wsT=wsT, gfm=gfm, shfm=shfm, cs0=cs0,
                st_rkv=st_rkv, st_low=st_low, st_vt=st_vt, st_sza=st_sza, st_mixb=st_mixb))
            K.barrier()

        with contextlib.ExitStack() as esB:
            phase_b(nc, K, esB, sb, PB, PT, seqs, dict(
                pf_t=pf_t, cst=cst, lora=lora, rkm=rkm, omka=omka, mA=mA, mT=mT,
                st_rkv=st_rkv, st_low=st_low, st_vt=st_vt, st_y=st_y, st_bc=st_bc, s0T=s0T, nst=nst, PBIG=PBIG), NT_S, NP)
            K.barrier()
        with contextlib.ExitStack() as esC:
            phase_c(nc, K, esC, sb, PB, PT, seqs, dict(rows=rows, st_gt=st_gt, cst=cst, w_out=w_out,
                    st_y=st_y, st_bc=st_bc, st_vt=st_vt, st_sza=st_sza, st_mixb=st_mixb))
        K.barrier(engines=("sp",))
    return nc


def phase_a(nc, K, esA, sb, PB, PT, seqs, win, R):
    pf_t, rows_t, cst, bs_t, wsT = R["pf_t"], R["rows_t"], R["cst"], R["bs_t"], R["wsT"]
    gfm, shfm, cs0 = R["gfm"], R["shfm"], R["cs0"]
    ident = cst[:, 0, :]
    NXB = 2
    xt = [sb("xt%d" % i, [128, D], F32, esA) for i in range(NXB)]
    xn = [sb("xn%d" % i, [128, D], BF16, esA) for i in range(4)]
    sq = sb("sqj", [128, D], BF16, esA)
    stat = sb("statA", [128, 16], F32, esA)
    hT = [sb("hT%d" % i, [128, 8, 512], BF16, esA) for i in range(2)]
    raw = [sb("raw%d" % i, [128, 12, 514], BF16, esA) for i in range(2)]
    rkvp = sb("rkvp", [128, 12, 512], BF16, esA)
    tmps = [sb("shtmp%d" % i, [128, 512], F32, esA) for i in range(2)]
    low = [sb("low%d" % i, [128, 2, 512], BF16, esA) for i in range(2)]
    sza = [sb("sza%d" % i, [128, 512], BF16, esA) for i in range(4)]
    u_t = [sb("u_t%d" % i, [128, 512], BF16, esA) for i in range(4)]
    szb = [sb("szb%d" % i, [128, 512], BF16, esA) for i in range(4)]
    vb = [sb("vb%d" % i, [128, 512], F32, esA) for i in range(4)]
    vn = [sb("vn%d" % i, [128, 512], BF16, esA) for i in range(4)]
    mixb = [sb("mixb%d" % i, [128, 512], BF16, esA) for i in range(4)]
    vts = [sb("vts%d" % i, [128, 512], BF16, esA) for i in range(2)]
    lnst = [sb("lnst%d" % i, [128, 8], F32, esA) for i in range(4)]

    cnt = {"x": 0, "tile": 0, "grp": 0}

    def run_lanes(lanes):
        lanes = [[list(g_), W_, []] for g_, W_ in lanes]
        while any(l[0] or l[2] for l in lanes):
            for l in lanes:
                while l[0] and len(l[2]) < l[1]:
                    l[2].append(l[0].pop(0))
                for g_ in list(l[2]):
                    try:
                        next(g_)
                    except StopIteration:
                        l[2].remove(g_)

    def shift_gen(g, GS, nt_g, gt0):
        rw = raw[g % 2]
        rk = "raw%d" % (g % 2)
        for j in range(12):
            tk = "shtmp%d" % (j % 2)
            tm = tmps[j % 2]
            K.op("dve", [rk, "cs0"], [tk], lambda e, j=j, tm=tm: e.tensor_scalar(out=tm[:, 0:GS], in0=rw[:, j, 1:GS + 1], scalar1=cs0[:, j:j + 1], scalar2=None, op0=ALU.mult))
            K.op("dve", [rk, tk, "pf_t"], [tk], lambda e, j=j, tm=tm: e.scalar_tensor_tensor(out=tm[:, 0:GS], in0=rw[:, j, 0:GS], scalar=pf_t[:, 24 + j:25 + j], in1=tm[:, 0:GS], op0=ALU.mult, op1=ALU.add))
            K.op("dve", [rk, tk, "pf_t"], ["rkvp"], lambda e, j=j, tm=tm: e.scalar_tensor_tensor(out=rkvp[:, j, 0:GS], in0=rw[:, j, 2:GS + 2], scalar=pf_t[:, 36 + j:37 + j], in1=tm[:, 0:GS], op0=ALU.mult, op1=ALU.add))
            yield
        for a in range(nt_g):
            gt = gt0 + a
            K.dma("sp", "rkvp", ["rkvp"], [], R["st_rkv"][gt], rkvp[:, :, a * 128:(a + 1) * 128])
            vs = vts[gt % 2]
            vk = "vts%d" % (gt % 2)
            K.ops("pe", ["rkvp", "cst"], ["ptb"], [lambda e, a=a, q=q: e.transpose(out=PT[:, q * 128:(q + 1) * 128], in_=rkvp[:, 8 + q, a * 128:(a + 1) * 128], identity=ident) for q in range(4)])
            K.op("act", ["ptb"], [vk], lambda e, vs=vs: e.copy(out=vs[:], in_=PT[:, 0:512]))
            K.dma("act", vk, [vk], [], R["st_vt"][gt], vs[:])
            yield

    def front_compute(xap, g, a):
        t0 = (g * (GS_cur[0] // 128) + a) * 128
        xi = cnt["x"] % NXB
        cnt["x"] += 1
        xk = "xt%d" % xi
        K.dma("sp", xk, [], [xk], xt[xi][:], xap[t0:t0 + 128, :])
        sc = cnt["tile"] % 8
        cnt["tile"] += 1
        nk = "xn%d" % a
        K.op("act", [xk], [nk, "statA%d" % sc], lambda e: e.activation(out=xn[a][:], in_=xt[xi][:], func=AF.Square, accum_out=stat[:, sc:sc + 1]))
        K.op("dve", ["statA%d" % sc], ["statA%d" % sc], lambda e: e.tensor_scalar(out=stat[:, sc:sc + 1], in0=stat[:, sc:sc + 1], scalar1=1.0 / D, scalar2=NORM_EPS, op0=ALU.mult, op1=ALU.add))
        K.op("act", ["statA%d" % sc], ["statA%d" % sc], lambda e: e.sqrt(out=stat[:, sc:sc + 1], in_=stat[:, sc:sc + 1]))
        K.op("dve", ["statA%d" % sc], ["statA%d" % sc], lambda e: e.reciprocal(out=stat[:, sc:sc + 1], in_=stat[:, sc:sc + 1]))
        K.op("dve", [xk, "statA%d" % sc], [nk], lambda e: e.tensor_scalar(out=xn[a][:], in0=xt[xi][:], scalar1=stat[:, sc:sc + 1], scalar2=None, op0=ALU.mult))

    def front_transpose(a, cv, hTg, hk):
        nk = "xn%d" % a
        K.ops("pe", [nk, "cst"], ["ptb"], [lambda e, kc=kc: e.transpose(out=PT[:, kc * 128:(kc + 1) * 128], in_=xn[a][:, kc * 128:(kc + 1) * 128], identity=ident) for kc in range(8)])
        for kc in range(8):
            K.op("dve", ["ptb", "gfm", "shfm"], [hk], lambda e, kc=kc: e.tensor_scalar(out=hTg[:, kc, a * 128:(a + 1) * 128], in0=PT[:, kc * 128:(kc + 1) * 128], scalar1=gfm[:, kc, cv:cv + 1], scalar2=shfm[:, kc, cv:cv + 1], op0=ALU.mult, op1=ALU.add))

    GS_cur = [512]

    def tm_gen(s2, hTg, hk, a, gt):
        ls = lnst[s2]
        lsk = "lnst%d" % s2

        bank = lambda ci: 3 + (ci + a) % 4

        def proj(ci, c0):
            pbi = bank(ci)
            K.ops("pe", [hk, "win"], ["pb%d" % pbi], [lambda e, kc=kc: e.matmul(PB[pbi][:, :], lhsT=hTg[:, kc, a * 128:(a + 1) * 128], rhs=win[:, kc, c0:c0 + 512], start=(kc == 0), stop=(kc == 7)) for kc in range(8)])
        proj(0, 1536)
        K.op("act", ["pb%d" % bank(0)], ["sza%d" % s2], lambda e: e.activation(out=sza[s2][:], in_=PB[bank(0)][:, :], func=AF.Silu))
        yield
        K.dma("act", "sza%d" % s2, ["sza%d" % s2], [], R["st_sza"][gt], sza[s2][:])
        proj(1, 2304)
        K.op("act", ["pb%d" % bank(1)], ["u_t%d" % s2], lambda e: e.copy(out=u_t[s2][:], in_=PB[bank(1)][:, :]))
        yield
        proj(2, 2816)
        K.op("act", ["pb%d" % bank(2), lsk], ["vb%d" % s2, lsk], lambda e: e.activation(out=vb[s2][:], in_=PB[bank(2)][:, :], func=AF.Copy, accum_out=ls[:, 0:1]))
        yield
        proj(3, 3328)
        K.op("act", ["pb%d" % bank(3)], ["szb%d" % s2], lambda e: e.activation(out=szb[s2][:], in_=PB[bank(3)][:, :], func=AF.Silu))
        yield
        K.op("act", ["vb%d" % s2, lsk], ["sqj", lsk], lambda e: e.activation(out=sq[:, 0:512], in_=vb[s2][:], func=AF.Square, accum_out=ls[:, 1:2]))
        yield
        K.op("dve", [lsk], [lsk], lambda e: e.tensor_scalar(out=ls[:, 2:3], in0=ls[:, 0:1], scalar1=1.0 / 512, scalar2=None, op0=ALU.mult))
        yield
        K.op("dve", [lsk], [lsk], lambda e: e.tensor_tensor(out=ls[:, 3:4], in0=ls[:, 2:3], in1=ls[:, 2:3], op=ALU.mult))
        yield
        K.op("dve", [lsk], [lsk], lambda e: e.scalar_tensor_tensor(out=ls[:, 4:5], in0=ls[:, 1:2], scalar=1.0 / 512, in1=ls[:, 3:4], op0=ALU.mult, op1=ALU.subtract))
        yield
        K.op("dve", [lsk], [lsk], lambda e: e.tensor_scalar(out=ls[:, 5:6], in0=ls[:, 4:5], scalar1=NORM_EPS, scalar2=None, op0=ALU.add))
        yield
        K.op("act", [lsk], [lsk], lambda e: e.sqrt(out=ls[:, 5:6], in_=ls[:, 5:6]))
        yield
        K.op("dve", [lsk], [lsk], lambda e: e.reciprocal(out=ls[:, 5:6], in_=ls[:, 5:6]))
        yield
        K.op("dve", ["vb%d" % s2, lsk], ["vb%d" % s2], lambda e: e.tensor_scalar(out=vb[s2][:], in0=vb[s2][:], scalar1=ls[:, 2:3], scalar2=ls[:, 5:6], op0=ALU.subtract, op1=ALU.mult))
        yield
        K.op("pool", ["vb%d" % s2, "rows_t"], ["vb%d" % s2], lambda e: e.tensor_tensor(out=vb[s2][:], in0=vb[s2][:], in1=rows_t[:, 2048:2560], op=ALU.mult))
        yield
        K.op("pool", ["vb%d" % s2, "rows_t"], ["vn%d" % s2], lambda e: e.tensor_tensor(out=vn[s2][:], in0=vb[s2][:], in1=rows_t[:, 2560:3072], op=ALU.add))
        yield
        K.ops("pe", ["vn%d" % s2, "wsT"], ["pb0"], [lambda e, gg=gg: e.matmul(PB[0][:, gg * 64:(gg + 1) * 64], lhsT=wsT[:, gg, :], rhs=vn[s2][:, gg * 64:(gg + 1) * 64], start=True, stop=True) for gg in range(8)])
        K.op("dve", ["pb0", "bs_t"], ["vb%d" % s2], lambda e: e.tensor_tensor(out=vb[s2][:].rearrange("p (g c) -> p g c", c=64), in0=PB[0][:, :].rearrange("p (g c) -> p g c", c=64), in1=bs_t[:, :].unsqueeze(2).to_broadcast([128, 8, 64]), op=ALU.add))
        yield
        K.op("pool", ["vb%d" % s2, "u_t%d" % s2], ["vb%d" % s2], lambda e: e.tensor_tensor(out=vb[s2][:], in0=vb[s2][:], in1=u_t[s2][:], op=ALU.mult))
        yield
        K.op("pool", ["vb%d" % s2, "szb%d" % s2], ["mixb%d" % s2], lambda e: e.tensor_tensor(out=mixb[s2][:], in0=vb[s2][:], in1=szb[s2][:], op=ALU.mult))
        yield
        K.dma("sp", "mixb%d" % s2, ["mixb%d" % s2], [], R["st_mixb"][gt], mixb[s2][:])

    for (xap, NT, cv, gt_base, _y) in seqs:
        GS = min(512, NT * 128)
        GS_cur[0] = GS
        nt_g = GS // 128
        NG = NT // nt_g
        hbuf = {}
        hi = cnt["grp"] % 2
        cnt["grp"] += 1
        hbuf[0] = (hT[hi], "hT%d" % hi)
        for a in range(nt_g):
            front_compute(xap, 0, a)
        for a in range(nt_g):
            front_transpose(a, cv, *hbuf[0])
        for g in range(NG):
            hTg, hk = hbuf[g]
            rw = raw[g % 2]
            rk = "raw%d" % (g % 2)
            if g == 0:
                K.op("pool", [], [rk], lambda e, rw=rw: e.memset(rw[:, :, 0:1], 0.0))
            lw = low[g % 2]
            lk = "low%d" % (g % 2)
            for bi, blk in enumerate(list(range(12)) + [16, 17]):
                pbi = bi % 3
                pk = "pb%d" % pbi
                K.ops("pe", [hk, "win"], [pk], [lambda e, kc=kc, blk=blk, pbi=pbi: e.matmul(PB[pbi][:, 0:GS], lhsT=win[:, kc, blk * 128:(blk + 1) * 128], rhs=hTg[:, kc, 0:GS], start=(kc == 0), stop=(kc == 7)) for kc in range(8)])
                if blk < 12:
                    K.op("act", [pk], [rk], lambda e, blk=blk, pbi=pbi, rw=rw: e.copy(out=rw[:, blk, 1:GS + 1], in_=PB[pbi][:, 0:GS]))
                elif blk == 16:
                    K.op("act", [pk], [lk], lambda e, pbi=pbi, lw=lw: e.activation(out=lw[:, 0, 0:GS], in_=PB[pbi][:, 0:GS], func=AF.Tanh))
                else:
                    K.op("act", [pk], [lk], lambda e, pbi=pbi, lw=lw: e.copy(out=lw[:, 1, 0:GS], in_=PB[pbi][:, 0:GS]))
                if g + 1 < NG and bi in (2, 5, 8, 11):
                    front_compute(xap, g + 1, (bi - 2) // 3)
            for a in range(nt_g):
                gt = gt_base + g * nt_g + a
                K.dma("act", lk, [lk], [], R["st_low"][gt], lw[:, :, a * 128:(a + 1) * 128])
            gens = []
            if g > 0:
                rwp = raw[(g - 1) % 2]
                rkp = "raw%d" % ((g - 1) % 2)
                K.op("pool", [rk], [rkp], lambda e, rw=rw, rwp=rwp: e.tensor_copy(out=rwp[:, :, GS + 1:GS + 2], in_=rw[:, :, 1:2]))
                K.op("pool", [rkp], [rk], lambda e, rw=rw, rwp=rwp: e.tensor_copy(out=rw[:, :, 0:1], in_=rwp[:, :, GS:GS + 1]))
                gens.append(shift_gen(g - 1, GS, nt_g, gt_base + (g - 1) * nt_g))
            tms = [tm_gen(a % 4, hTg, hk, a, gt_base + g * nt_g + a) for a in range(nt_g)]
            run_lanes([(gens, 1), (tms, 4)])
            if g + 1 < NG:
                hi = cnt["grp"] % 2
                cnt["grp"] += 1
                hbuf[g + 1] = (hT[hi], "hT%d" % hi)
                for a in range(nt_g):
                    front_transpose(a, cv, *hbuf[g + 1])
        g = NG - 1
        rw = raw[g % 2]
        rk = "raw%d" % (g % 2)
        K.op("pool", [], [rk], lambda e, rw=rw: e.memset(rw[:, :, GS + 1:GS + 2], 0.0))
        run_lanes([([shift_gen(g, GS, nt_g, gt_base + g * nt_g)], 1)])


def phase_b(nc, K, esB, sb, PB, PT, seqs, R, NT_S, NP):
    PBIG = R["PBIG"]
    pf_t, cst, lora, rkm, omka, mA, mT = R["pf_t"], R["cst"], R["lora"], R["rkm"], R["omka"], R["mA"], R["mT"]
    ident = cst[:, 0, :]

    WB = 3

    def T(name, shape, dt):
        return [sb("%s_%d" % (name, p), shape, dt, esB) for p in range(WB)]
    rk = T("Brk", [128, 8, 128], BF16)
    lowT = T("Blow", [128, 2, 128], BF16)
    Vt = T("Bvt", [128, 512], BF16)
    arm = T("Barm", [128, 4, 2, 2, 128], BF16)
    bT = T("BbT", [128, 4, 128], BF16)
    kT = T("BkT", [128, 4, 128], BF16)
    rkd = T("Brkd", [128, 4, 128], BF16)
    kk2 = T("Bkk2", [128, 4, 128], BF16)
    MM1 = T("BMM1", [128, 8, 2, 128], BF16)
    MM2 = T("BMM2", [128, 8, 2, 128], BF16)
    Q0T = T("BQ0T", [128, 8, 128], BF16)
    QR = [T("BQRa", [128, 8, 2, 128], BF16), T("BQRb", [128, 8, 2, 128], BF16)]
    QT = [T("BQTa", [128, 8, 128], BF16), T("BQTb", [128, 8, 128], BF16)]
    TT = T("BTT", [128, 8, 128], BF16)
    Bt = T("BBt", [128, 512], BF16)
    Kt = T("BKt", [128, 512], BF16)
    Zs = T("BZs", [128, 512], BF16)
    Us = T("BUs", [128, 512], BF16)
    ysb = T("Bysb", [128, 512], F32)
    bcs = T("Bbcs", [128, 8], F32)
    f32n = ["sg", "aa", "E1", "E2", "gam", "gamx", "gami", "kk"]
    F = {n: T("Bf_" + n, [128, 4, 128], F32) for n in f32n}
    F["tq"], F["rn"], F["kd"], F["kkn"] = F["sg"], F["E1"], F["E2"], F["kk"]
    tot = T("Btot", [128, 16], F32)
    gC = T("BgC", [128, 4], F32)
    ones = sb("Bones", [128, 128], F32, esB)
    S = [[[sb("BS_%d_%d_%d" % (sp, d, i), [128, 4, 64], F32, esB) for i in range(2)] for d in range(2)] for sp in range(2)]
    Sb = [[[sb("BSb_%d_%d_%d" % (sp, d, i), [128, 4, 64], BF16, esB) for i in range(2)] for d in range(2)] for sp in range(2)]
    t1s = [sb("Bt1_%d" % i, [128, 4, 64], F32, esB) for i in range(WB)]

    K.op("pool", [], ["Bones"], lambda e: e.memset(ones[:], 1.0))
    hpar = sb("Bhpar", [128, 34], F32, esB)
    K.op("pool", ["pf_t"], ["hpar"], lambda e: e.tensor_scalar(out=hpar[:, 0:16], in0=pf_t[:, 48:64], scalar1=0.5, scalar2=None, op0=ALU.mult))
    K.op("pool", ["hpar"], ["hpar"], lambda e: e.memset(hpar[:, 16:17], C2H))
    K.op("pool", ["hpar"], ["hpar"], lambda e: e.memset(hpar[:, 17:18], L2_EPS))
    K.op("pool", ["pf_t", "hpar"], ["hpar"], lambda e: e.tensor_scalar(out=hpar[:, 18:26], in0=pf_t[:, 72:80], scalar1=0.5, scalar2=None, op0=ALU.mult))
    K.op("pool", ["pf_t", "hpar"], ["hpar"], lambda e: e.tensor_scalar(out=hpar[:, 26:34], in0=pf_t[:, 72:80], scalar1=-0.5, scalar2=1.0, op0=ALU.mult, op1=ALU.add))
    for p in range(WB):
        K.op("pool", [], ["Barm_%d" % p], lambda e, p=p: e.memset(arm[p][:].rearrange("p a b c d -> p (a b c d)"), 0.0))

    chain_done = {}

    def visit(p, gt, d, sp, sidx, last_out_ap, ckey, cidx):
        t1 = t1s[p]
        b0, b1 = 2 * p, 2 * p + 1
        b2 = b0
        kt1 = "Bt1_%d" % p
        k = lambda n: "%s_%d" % (n, p)
        _al = dict(tq="sg", rn="E1", kd="E2", kkn="kk")
        fk = lambda n: "Bf_%s_%d" % (_al.get(n, n), p)
        Sin, Sbin = S[sp][d][sidx], Sb[sp][d][sidx]
        Sout, Sbout = S[sp][d][1 - sidx], Sb[sp][d][1 - sidx]
        kSin, kSbin = "BS_%d_%d_%d" % (sp, d, sidx), "BSb_%d_%d_%d" % (sp, d, sidx)
        kSout, kSbout = "BS_%d_%d_%d" % (sp, d, 1 - sidx), "BSb_%d_%d_%d" % (sp, d, 1 - sidx)
        c_w0, c_a0, c_kk, c_ka = 48 + 4 * d, 56 + 4 * d, 64 + 4 * d, 72 + 4 * d
        bc3 = lambda ap: ap.unsqueeze(2).to_broadcast([128, 4, 128])
        rT_ = rk[p][:, 0:4, :]
        kT_ = rk[p][:, 4:8, :]
        K.dma("sp", k("Brk"), [], [k("Brk")], rk[p][:], R["st_rkv"][gt][:, 0:8, :])
        yield
        K.dma("sp", k("Blow"), [], [k("Blow")], lowT[p][:], R["st_low"][gt])
        yield
        K.dma("sp", k("Bvt"), [], [k("Bvt")], Vt[p][:], R["st_vt"][gt])
        yield
        K.ops("pe", [k("Blow"), "lora"], ["pb%d" % b0], [lambda e, q=q: e.matmul(PB[b0][:, q * 128:(q + 1) * 128], lhsT=lora[:, d, q * 128:(q + 1) * 128], rhs=lowT[p][:, 0, :], start=True, stop=True) for q in range(4)])
        K.ops("pe", [k("Blow"), "lora"], ["pb%d" % b1], [lambda e, q=q: e.matmul(PB[b1][:, q * 128:(q + 1) * 128], lhsT=lora[:, 2 + d, q * 128:(q + 1) * 128], rhs=lowT[p][:, 1, :], start=True, stop=True) for q in range(4)])
        yield
        TH, THA, C2, X2 = F["sg"][p], F["aa"][p], F["E1"][p], F["E2"][p]
        for q in range(4):
            K.op("act", ["pb%d" % b0, "hpar"], [fk("sg")], lambda e, q=q: e.activation(out=TH[:, q, :], in_=PB[b0][:, q * 128:(q + 1) * 128], func=AF.Tanh, scale=0.5, bias=hpar[:, 4 * d + q:4 * d + q + 1]))
        yield
        for q in range(4):
            K.op("act", ["pb%d" % b1, "hpar"], [fk("aa")], lambda e, q=q: e.activation(out=THA[:, q, :], in_=PB[b1][:, q * 128:(q + 1) * 128], func=AF.Tanh, scale=0.5, bias=hpar[:, 8 + 4 * d + q:8 + 4 * d + q + 1]))
        yield
        for q in range(4):
            if d == 0:
                K.op("dve", [fk("sg"), "Bones"], [fk("E1")], lambda e, q=q: e.tensor_tensor_scan(out=C2[:, q, :], data0=ones[:, :], data1=TH[:, q, :], initial=0.0, op0=ALU.add, op1=ALU.add))
            else:
                K.op("dve", [fk("sg"), "Bones"], [fk("E1")], lambda e, q=q: e.tensor_tensor_scan(out=C2[:, q, ::-1], data0=ones[:, :], data1=TH[:, q, ::-1], initial=0.0, op0=ALU.add, op1=ALU.add))
        yield
        K.op("pool", [fk("E1"), fk("sg")], [fk("E2")], lambda e: e.tensor_tensor(out=X2[:], in0=C2[:], in1=TH[:], op=ALU.subtract))
        yield
        tcol = 127 if d == 0 else 0
        K.op("pool", [fk("E1")], [k("Btot")], lambda e: e.tensor_scalar(out=tot[p][:, 0:4], in0=C2[:, :, tcol], scalar1=-C2H, scalar2=None, op0=ALU.mult))
        yield
        K.op("act", [k("Btot")], [k("BgC")], lambda e: e.activation(out=gC[p][:], in_=tot[p][:, 0:4], func=AF.Exp))
        yield
        K.op("act", [fk("E1")], [fk("gam")], lambda e: e.activation(out=F["gam"][p][:], in_=C2[:], func=AF.Exp, scale=-C2H))
        yield
        K.op("act", [fk("E2"), "hpar"], [fk("gamx")], lambda e: e.activation(out=F["gamx"][p][:], in_=X2[:], func=AF.Exp, scale=-C2H, bias=hpar[:, 16:17]))
        yield
        K.op("act", [fk("E1")], [fk("gami")], lambda e: e.activation(out=F["gami"][p][:], in_=C2[:], func=AF.Exp, scale=C2H))
        yield
        K.op("dve", [k("Brk"), "pf_t"], [fk("kk")], lambda e: e.tensor_tensor(out=F["kk"][p][:], in0=kT_, in1=bc3(pf_t[:, c_kk:c_kk + 4]), op=ALU.mult))
        yield
        K.op("pool", [fk("kk")], [k("Bkk2")], lambda e: e.tensor_tensor(out=kk2[p][:], in0=F["kk"][p][:], in1=F["kk"][p][:], op=ALU.mult))
        yield
        K.ops("pe", [k("Bkk2"), "cst"], ["pb%d" % b2], [lambda e, q=q: e.matmul(PB[b2][:, q * 128:(q + 1) * 128], lhsT=cst[:, 5, :], rhs=kk2[p][:, q, :], start=True, stop=True) for q in range(4)])
        yield
        K.op("act", ["pb%d" % b2, "hpar"], [fk("rn")], lambda e: e.activation(out=F["rn"][p][:].rearrange("p a b -> p (a b)"), in_=PB[b2][:, :], func=AF.Ln, bias=hpar[:, 17:18]))
        yield
        K.op("act", [fk("rn")], [fk("rn")], lambda e: e.activation(out=F["rn"][p][:], in_=F["rn"][p][:], func=AF.Exp, scale=-0.5))
        yield
        K.op("pool", [fk("kk"), fk("rn")], [fk("kkn")], lambda e: e.tensor_tensor(out=F["kkn"][p][:], in0=F["kk"][p][:], in1=F["rn"][p][:], op=ALU.mult))
        yield
        for q in range(4):
            K.op("pool", [fk("aa"), "hpar"], [fk("tq")], lambda e, q=q: e.tensor_scalar(out=F["tq"][p][:, q, :], in0=THA[:, q, :], scalar1=hpar[:, 18 + 4 * d + q:19 + 4 * d + q], scalar2=hpar[:, 26 + 4 * d + q:27 + 4 * d + q], op0=ALU.mult, op1=ALU.add))
        yield
        K.op("pool", [fk("tq"), k("Brk")], [fk("kd")], lambda e: e.tensor_tensor(out=F["kd"][p][:], in0=F["tq"][p][:], in1=kT_, op=ALU.mult))
        yield
        for hp in range(2):
            rs = slice(hp * 64, (hp + 1) * 64)
            K.op("dve", [fk("kkn"), fk("gamx")], [k("Barm")], lambda e, rs=rs, hp=hp: e.scalar_tensor_tensor(out=arm[p][rs, :, hp, 0, :], in0=F["kkn"][p][rs, :, :], scalar=-1.0, in1=F["gamx"][p][rs, :, :], op0=ALU.mult, op1=ALU.mult))
            yield
            K.op("pool", [k("Brk"), fk("gam")], [k("Barm")], lambda e, rs=rs, hp=hp: e.tensor_tensor(out=arm[p][rs, :, hp, 1, :], in0=rk[p][rs, 0:4, :], in1=F["gam"][p][rs, :, :], op=ALU.mult))
            yield
        K.op("dve", [fk("kkn"), fk("aa")], [fk("tq")], lambda e: e.scalar_tensor_tensor(out=F["tq"][p][:], in0=THA[:], scalar=1.0, in1=F["kkn"][p][:], op0=ALU.add, op1=ALU.mult))
        yield
        K.op("dve", [fk("tq"), fk("gami")], [k("BbT")], lambda e: e.scalar_tensor_tensor(out=bT[p][:], in0=F["tq"][p][:], scalar=0.5, in1=F["gami"][p][:], op0=ALU.mult, op1=ALU.mult))
        yield
        K.op("pool", [fk("kd"), fk("gami")], [k("BkT")], lambda e: e.tensor_tensor(out=kT[p][:], in0=F["kd"][p][:], in1=F["gami"][p][:], op=ALU.mult))
        yield
        K.op("pool", [fk("kd"), k("Brk")], [k("Brkd")], lambda e: e.tensor_tensor(out=rkd[p][:], in0=F["kd"][p][:], in1=rT_, op=ALU.mult))
        yield
        K.ops("pe", [k("Brkd"), "rkm"], ["pb6"], [lambda e, q=q: e.matmul(PB[6][:, q * 2:q * 2 + 2], lhsT=rkd[p][:, q, :], rhs=rkm[:, d * 8 + q * 2:d * 8 + q * 2 + 2], start=True, stop=True) for q in range(4)])
        K.op("act", ["pb6"], [k("Bbcs")], lambda e: e.copy(out=bcs[p][:], in_=PB[6][:, 0:8]))
        yield
        K.dma("act", k("Bbcs"), [k("Bbcs")], [], R["st_bc"][d, gt], bcs[p][:])
        yield
        K.ops("pe", [k("BbT"), k("BkT"), "cst"], ["ptb"],
              [lambda e, q=q: e.transpose(out=PT[:, q * 128:(q + 1) * 128], in_=bT[p][:, q, :], identity=ident) for q in range(4)] +
              [lambda e, q=q: e.transpose(out=PT[:, 512 + q * 128:512 + (q + 1) * 128], in_=kT[p][:, q, :], identity=ident) for q in range(4)])
        K.op("act", ["ptb"], [k("BBt")], lambda e: e.copy(out=Bt[p][:], in_=PT[:, 0:512]))
        K.op("act", ["ptb"], [k("BKt")], lambda e: e.copy(out=Kt[p][:], in_=PT[:, 512:1024]))
        yield
        for q in range(4):
            fa, fb, fc = [], [], []
            for hp in range(2):
                rhsA = arm[p][:, q, hp, :, :].rearrange("p a t -> p (a t)")
                fa.append(lambda e, hp=hp, rhsA=rhsA: e.matmul(PB[b0][:, hp * 256:(hp + 1) * 256], lhsT=bT[p][:, q, :], rhs=rhsA, start=True, stop=True))
                fb.append(lambda e, hp=hp, rhsA=rhsA: e.matmul(PB[b1][:, hp * 256:(hp + 1) * 256], lhsT=kT[p][:, q, :], rhs=rhsA, start=True, stop=True))
                fc.append(lambda e, hp=hp: e.matmul(PB[b0][:, hp * 128:(hp + 1) * 128], lhsT=arm[p][:, q, hp, 0, :], rhs=bT[p][:, q, :], start=True, stop=True))
            K.ops("pe", [k("BbT"), k("Barm")], ["pb%d" % b0], fa)
            K.ops("pe", [k("BkT"), k("Barm")], ["pb%d" % b1], fb)
            yield
            K.op("dve", ["pb%d" % b0, "mA"], [k("BMM1") + "q%d" % q], lambda e, q=q: e.tensor_tensor(out=MM1[p][:, 2 * q:2 * q + 2, :, :].rearrange("p h a t -> p (h a t)"), in0=PB[b0][:, :], in1=mA[:, d, :, :].rearrange("p a t -> p (a t)"), op=ALU.mult))
            yield
            K.ops("pe", [k("BbT"), k("Barm")], ["pb%d" % b0], fc)
            K.op("dve", ["pb%d" % b1, "mA"], [k("BMM2") + "q%d" % q], lambda e, q=q: e.tensor_tensor(out=MM2[p][:, 2 * q:2 * q + 2, :, :].rearrange("p h a t -> p (h a t)"), in0=PB[b1][:, :], in1=mA[:, d, :, :].rearrange("p a t -> p (a t)"), op=ALU.mult))
            yield
            K.op("dve", ["pb%d" % b0, "mT"], [k("BQ0T") + "q%d" % q], lambda e, q=q: e.tensor_tensor(out=Q0T[p][:, 2 * q:2 * q + 2, :].rearrange("p h t -> p (h t)"), in0=PB[b0][:, 0:256], in1=mT[:, d, :, :].rearrange("p a t -> p (a t)"), op=ALU.mult))
            yield
        kQR = lambda a, q: "BQR%s_%dq%d" % ("ab"[a], p, q)
        kQT = lambda a, q: "BQT%s_%dq%d" % ("ab"[a], p, q)
        XA = PB[b0]
        XB = PB[b1]
        for q in range(4):
            hs = [2 * q, 2 * q + 1]
            fa = []
            for hh, h in enumerate(hs):
                fa.append(lambda e, hh=hh, h=h: e.matmul(XA[:, hh * 256:hh * 256 + 128], lhsT=Q0T[p][:, h, :], rhs=MM1[p][:, h, 0, :], start=True, stop=True))
                fa.append(lambda e, hh=hh, h=h: e.matmul(XA[:, hh * 256 + 128:hh * 256 + 256], lhsT=ident, rhs=MM1[p][:, h, 0, :], start=False, stop=False, skip_group_check=True))
                fa.append(lambda e, hh=hh, h=h: e.matmul(XA[:, hh * 256 + 128:hh * 256 + 256], lhsT=ident, rhs=ident, start=False, stop=True, skip_group_check=True))
            K.ops("pe", [k("BMM1") + "q%d" % q, k("BQ0T") + "q%d" % q, "cst"], ["pb%d" % b0], fa)
            K.ops("pe", [k("BMM1") + "q%d" % q, k("BQ0T") + "q%d" % q], ["pb%d" % b1], [lambda e, hh=hh, h=h: e.matmul(XB[:, hh * 128:(hh + 1) * 128], lhsT=MM1[p][:, h, 0, :], rhs=Q0T[p][:, h, :], start=True, stop=True) for hh, h in enumerate(hs)])
            yield
            K.op("act", ["pb%d" % b0], [kQR(1, q)], lambda e, q=q: e.copy(out=QR[1][p][:, 2 * q:2 * q + 2, :, :].rearrange("p h a t -> p (h a t)"), in_=XA[:, :]), c=0.5)
            yield
            K.op("dve", ["pb%d" % b1], [kQT(1, q)], lambda e, q=q: e.tensor_copy(out=QT[1][p][:, 2 * q:2 * q + 2, :].rearrange("p h t -> p (h t)"), in_=XB[:, 0:256]), c=0.42)
            yield
        for lev in range(1, 6):
            cur, nxt = lev % 2, (lev + 1) % 2
            for q in range(4):
                hs = [2 * q, 2 * q + 1]
                fa = []
                for hh, h in enumerate(hs):
                    fa.append(lambda e, hh=hh, h=h: e.matmul(XA[:, hh * 256:(hh + 1) * 256], lhsT=QT[cur][p][:, h, :], rhs=QR[cur][p][:, h, :, :].rearrange("p a t -> p (a t)"), start=True, stop=False))
                    fa.append(lambda e, hh=hh, h=h: e.matmul(XA[:, hh * 256 + 128:(hh + 1) * 256], lhsT=ident, rhs=QR[cur][p][:, h, 1, :], start=False, stop=True))
                K.ops("pe", [kQR(cur, q), kQT(cur, q), "cst"], ["pb%d" % b0], fa)
                K.ops("pe", [kQR(cur, q), kQT(cur, q)], ["pb%d" % b1], [lambda e, hh=hh, h=h: e.matmul(XB[:, hh * 128:(hh + 1) * 128], lhsT=QR[cur][p][:, h, 0, :], rhs=QT[cur][p][:, h, :], start=True, stop=True) for hh, h in enumerate(hs)])
                yield
                K.op("act", ["pb%d" % b0], [kQR(nxt, q)], lambda e, q=q: e.copy(out=QR[nxt][p][:, 2 * q:2 * q + 2, :, :].rearrange("p h a t -> p (h a t)"), in_=XA[:, :]), c=0.5)
                yield
                K.op("dve", ["pb%d" % b1], [kQT(nxt, q)], lambda e, q=q: e.tensor_copy(out=QT[nxt][p][:, 2 * q:2 * q + 2, :].rearrange("p h t -> p (h t)"), in_=XB[:, 0:256]), c=0.42)
                yield
        for q in range(4):
            hs = [2 * q, 2 * q + 1]
            ff = []
            bq = b0 if q % 2 == 0 else b1
            for hh, h in enumerate(hs):
                ff.append(lambda e, hh=hh, h=h, bq=bq: e.matmul(PB[bq][:, hh * 128:(hh + 1) * 128], lhsT=QT[0][p][:, h, :], rhs=QR[0][p][:, h, 1, :], start=True, stop=False))
                ff.append(lambda e, hh=hh, h=h, bq=bq: e.matmul(PB[bq][:, hh * 128:(hh + 1) * 128], lhsT=ident, rhs=QR[0][p][:, h, 1, :], start=False, stop=True))
            K.ops("pe", [kQR(0, q), kQT(0, q), "cst"], ["pb%d" % bq], ff)
            yield
            if q % 2 == 0:
                K.op("act", ["pb%d" % bq], [k("BTT") + "q%d" % q], lambda e, q=q, bq=bq: e.copy(out=TT[p][:, 2 * q:2 * q + 2, :].rearrange("p h t -> p (h t)"), in_=PB[bq][:, 0:256]), c=0.4)
            else:
                K.op("dve", ["pb%d" % bq], [k("BTT") + "q%d" % q], lambda e, q=q, bq=bq: e.tensor_copy(out=TT[p][:, 2 * q:2 * q + 2, :].rearrange("p h t -> p (h t)"), in_=PB[bq][:, 0:256]), c=0.42)
            yield
        while chain_done.get(ckey, 0) < cidx:
            yield
        fz = []
        for h in range(8):
            q, hp = h // 2, h % 2
            fz.append(lambda e, h=h, q=q, hp=hp: e.matmul(PB[b0][:, h * 64:(h + 1) * 64], lhsT=arm[p][:, q, hp, 0, :], rhs=Sbin[:, q, :], start=True, stop=False))
            fz.append(lambda e, h=h: e.matmul(PB[b0][:, h * 64:(h + 1) * 64], lhsT=MM2[p][:, h, 0, :], rhs=Vt[p][:, h * 64:(h + 1) * 64], start=False, stop=True))
        K.ops("pe", [k("Barm"), kSbin, k("Bvt")] + [k("BMM2") + "q%d" % q for q in range(4)], ["pb%d" % b0], fz)
        yield
        K.op("act", ["pb%d" % b0], [k("BZs")], lambda e: e.copy(out=Zs[p][:], in_=PB[b0][:, :]))
        yield
        K.ops("pe", [k("BTT") + "q%d" % q for q in range(4)] + [k("BZs")], ["pb%d" % b1], [lambda e, h=h: e.matmul(PB[b1][:, h * 64:(h + 1) * 64], lhsT=TT[p][:, h, :], rhs=Zs[p][:, h * 64:(h + 1) * 64], start=True, stop=True) for h in range(8)])
        yield
        K.op("dve", ["pb%d" % b1], [k("BUs")], lambda e: e.tensor_copy(out=Us[p][:], in_=PB[b1][:, :]))
        yield
        fd = []
        for h in range(8):
            q = h // 2
            fd.append(lambda e, h=h, q=q: e.matmul(PB[b2][:, h * 64:(h + 1) * 64], lhsT=Bt[p][:, q * 128:(q + 1) * 128], rhs=Us[p][:, h * 64:(h + 1) * 64], start=True, stop=False))
            fd.append(lambda e, h=h, q=q: e.matmul(PB[b2][:, h * 64:(h + 1) * 64], lhsT=Kt[p][:, q * 128:(q + 1) * 128], rhs=Vt[p][:, h * 64:(h + 1) * 64], start=False, stop=True))
        K.ops("pe", [k("BBt"), k("BKt"), k("BUs"), k("Bvt")], ["pb%d" % b2], fd)
        yield
        Dv = PB[b2][:, :].rearrange("p (q hp j) -> p q hp j", hp=2, j=64)
        for hp in range(2):
            rs = slice(hp * 64, (hp + 1) * 64)
            K.op("dve", ["pb%d" % b2, kSin], [kt1], lambda e, rs=rs, hp=hp: e.tensor_tensor(out=t1[rs, :, :], in0=Dv[rs, :, hp, :], in1=Sin[rs, :, :], op=ALU.add))
            yield
        gb = gC[p][:, :].unsqueeze(2).to_broadcast([128, 4, 64])
        K.op("dve", [kt1, k("BgC")], [kSout], lambda e: e.tensor_tensor(out=Sout[:], in0=t1[:], in1=gb, op=ALU.mult))
        yield
        K.op("pool", [kt1, k("BgC")], [kSbout], lambda e: e.tensor_tensor(out=Sbout[:], in0=t1[:], in1=gb, op=ALU.mult))
        yield
        fy = []
        for h in range(8):
            q, hp = h // 2, h % 2
            fy.append(lambda e, h=h, q=q, hp=hp: e.matmul(PB[b0][:, h * 64:(h + 1) * 64], lhsT=arm[p][:, q, hp, 1, :], rhs=Sbin[:, q, :], start=True, stop=False))
            fy.append(lambda e, h=h: e.matmul(PB[b0][:, h * 64:(h + 1) * 64], lhsT=MM1[p][:, h, 1, :], rhs=Us[p][:, h * 64:(h + 1) * 64], start=False, stop=False))
            fy.append(lambda e, h=h: e.matmul(PB[b0][:, h * 64:(h + 1) * 64], lhsT=MM2[p][:, h, 1, :], rhs=Vt[p][:, h * 64:(h + 1) * 64], start=False, stop=True))
        K.ops("pe", [k("Barm"), kSbin, k("BUs"), k("Bvt")] + [k("BMM1") + "q%d" % q for q in range(4)] + [k("BMM2") + "q%d" % q for q in range(4)], ["pb%d" % b0], fy)
        yield
        chain_done[ckey] = cidx + 1
        K.op("act", ["pb%d" % b0], [k("Bysb")], lambda e: e.copy(out=ysb[p][:], in_=PB[b0][:, :]))
        yield
        K.dma("act", k("Bysb"), [k("Bysb")], [], R["st_y"][d, gt], ysb[p][:])
        yield
        if last_out_ap is not None:
            K.dma("sp", kSout, [kSout], [], last_out_ap, Sout[:])
            yield

    queue = []
    for si, (xap, NT, cv, gt_base, _y) in enumerate(seqs):
        sp = si % 2
        for step in range(NT):
            sidx = step % 2
            last = (step == NT - 1) and si > 0
            queue.append((si, step, gt_base + step, 0, sp, sidx, R["nst"][si - 1, 0] if last else None, (si, 0), step))
            queue.append((si, step, gt_base + NT - 1 - step, 1, sp, sidx, R["nst"][si - 1, 1] if last else None, (si, 1), step))

    def init_state(si):
        sp = si % 2
        for d in range(2):
            if si == 0:
                K.dma("sp", "BS_%d_%d_0" % (sp, d), [], ["BS_%d_%d_0" % (sp, d)], S[sp][d][0][:], R["s0T"][d])
                K.op("pool", ["BS_%d_%d_0" % (sp, d)], ["BSb_%d_%d_0" % (sp, d)], lambda e, d=d: e.tensor_copy(out=Sb[sp][d][0][:], in_=S[sp][d][0][:]))
            else:
                K.op("pool", [], ["BS_%d_%d_0" % (sp, d)], lambda e, d=d: e.memset(S[sp][d][0][:].rearrange("p a b -> p (a b)"), 0.0))
                K.op("pool", [], ["BSb_%d_%d_0" % (sp, d)], lambda e, d=d: e.memset(Sb[sp][d][0][:].rearrange("p a b -> p (a b)"), 0.0))

    def before(item):
        si, step, gt, d, sp, sidx, lo = item[:7]
        if step == 0 and d == 0:
            init_state(si)

    K.run_streams(queue, lambda slot, it: visit(slot, it[2], it[3], it[4], it[5], it[6], it[7], it[8]), WB, before, offset=VOFF)


def phase_c(nc, K, esC, sb, PB, PT, seqs, R):
    cst, w_out = R["cst"], R["w_out"]
    ident = cst[:, 0, :]
    rows_t = sb("Crows", [128, 3072], F32, esC)
    Gt = sb("CGt", [128, 2, 1024], F32, esC)
    K.dma("sp", "rows_t", [], ["rows_t"], rows_t[:], R["rows"])
    K.dma("sp", "Gt", [], ["Gt"], Gt[:], R["st_gt"])
    wout = sb("Cwout", [128, 8, D], BF16, esC)
    wst = [sb("Cwst%d" % i, [128, D], F32, esC) for i in range(2)]
    def wdma(kc):
        K.dma("act", "Cwst%d" % (kc % 2), [], ["Cwst%d" % (kc % 2)], wst[kc % 2][:], w_out[kc * 128:(kc + 1) * 128, :])
    wdma(0)
    wdma(1)
    for kc in range(8):
        K.op("dve", ["Cwst%d" % (kc % 2)], ["Cwout"], lambda e, kc=kc: e.tensor_copy(out=wout[:, kc, :], in_=wst[kc % 2][:]))
        if kc + 2 < 8:
            wdma(kc + 2)

    WC = 5
    ctr = [0]

    def T(name, shape, dt):
        return [sb("%s_%d" % (name, p), shape, dt, esC) for p in range(WC)]
    yf = T("Cyf", [128, 8, 64], F32)
    yb = T("Cyb", [128, 8, 64], F32)
    bcf = T("Cbcf", [128, 8], F32)
    bcb = T("Cbcb", [128, 8], F32)
    Vt = T("Cvt", [128, 8, 64], BF16)
    sza = T("Csza", [128, 512], BF16)
    mixed = T("Cmixed", [128, D], BF16)
    xt = T("Cxt", [128, D], F32)
    junk = sb("Cjunk", [128, 512], F32, esC)
    ysq = T("Cysq", [128, 8, 64], F32)
    st = T("Cst", [128, 40], F32)
    bon = T("Cbon", [128, 8, 64], F32)
    mixT = T("CmixT", [128, 8, 128], BF16)
    ot = T("Cot", [128, D], F32)

    def tile(p, xap, a, cv, gt, yap):
        if True:
            n0 = 2 * (ctr[0] % 3)
            ctr[0] += 1
            k = lambda n: "%s_%d" % (n, p)
            K.dma("sp", k("Cyf"), [], [k("Cyf")], yf[p][:].rearrange("p h j -> p (h j)"), R["st_y"][0, gt])
            yield
            K.dma("sp", k("Cyb"), [], [k("Cyb")], yb[p][:].rearrange("p h j -> p (h j)"), R["st_y"][1, gt])
            yield
            K.dma("sp", k("Cbcf"), [], [k("Cbcf")], bcf[p][:], R["st_bc"][0, gt])
            yield
            K.dma("sp", k("Cbcb"), [], [k("Cbcb")], bcb[p][:], R["st_bc"][1, gt])
            yield
            K.dma("sp", k("Cvt"), [], [k("Cvt")], Vt[p][:].rearrange("p h j -> p (h j)"), R["st_vt"][gt])
            yield
            K.dma("sp", k("Csza"), [], [k("Csza")], sza[p][:], R["st_sza"][gt])
            yield
            K.dma("sp", k("Cmixed") + "b", [], [k("Cmixed") + "b"], mixed[p][:, 512:1024], R["st_mixb"][gt])
            yield
            K.dma("sp", k("Cxt"), [], [k("Cxt")], xt[p][:], xap[a * 128:(a + 1) * 128, :])
            yield
            K.op("pool", [k("Cyf"), k("Cyb")], [k("Cyf")], lambda e: e.tensor_tensor(out=yf[p][:], in0=yf[p][:], in1=yb[p][:], op=ALU.add))
            yield
            s_ = st[p]
            K.op("dve", [k("Cyf")], [k("Cst") + "a"], lambda e: e.tensor_reduce(out=s_[:, 0:8], in_=yf[p][:], axis=AX.X, op=ALU.add))
            yield
            K.op("pool", [k("Cyf")], [k("Cysq")], lambda e: e.tensor_tensor(out=ysq[p][:], in0=yf[p][:], in1=yf[p][:], op=ALU.mult))
            yield
            K.op("dve", [k("Cysq")], [k("Cst") + "b"], lambda e: e.tensor_reduce(out=s_[:, 8:16], in_=ysq[p][:], axis=AX.X, op=ALU.add))
            yield
            sk = [k("Cst") + "a", k("Cst") + "b"]
            kc_ = k("Cst") + "c"
            K.op("dve", sk, [kc_], lambda e: e.tensor_scalar(out=s_[:, 16:24], in0=s_[:, 0:8], scalar1=1.0 / 64, scalar2=None, op0=ALU.mult))
            yield
            K.op("dve", [kc_], [kc_], lambda e: e.tensor_tensor(out=s_[:, 24:32], in0=s_[:, 16:24], in1=s_[:, 16:24], op=ALU.mult))
            yield
            K.op("dve", sk + [kc_], [kc_], lambda e: e.scalar_tensor_tensor(out=s_[:, 32:40], in0=s_[:, 8:16], scalar=1.0 / 64, in1=s_[:, 24:32], op0=ALU.mult, op1=ALU.subtract))
            yield
            K.op("dve", [kc_], [kc_], lambda e: e.tensor_scalar(out=s_[:, 32:40], in0=s_[:, 32:40], scalar1=GN_EPS, scalar2=None, op0=ALU.add))
            yield
            K.op("act", [kc_], [kc_], lambda e: e.sqrt(out=s_[:, 32:40], in_=s_[:, 32:40]))
            yield
            K.op("dve", [kc_], [kc_], lambda e: e.reciprocal(out=s_[:, 32:40], in_=s_[:, 32:40]))
            yield
            b8 = lambda ap: ap.unsqueeze(2).to_broadcast([128, 8, 64])
            K.op("dve", [k("Cyf"), kc_], [k("Cyf")], lambda e: e.tensor_tensor(out=yf[p][:], in0=yf[p][:], in1=b8(s_[:, 16:24]), op=ALU.subtract))
            yield
            K.op("dve", [k("Cyf"), kc_], [k("Cyf")], lambda e: e.tensor_tensor(out=yf[p][:], in0=yf[p][:], in1=b8(s_[:, 32:40]), op=ALU.mult))
            yield
            yfl = yf[p][:].rearrange("p h j -> p (h j)")
            K.op("pool", [k("Cyf"), "rows_t"], [k("Cyf")], lambda e: e.tensor_tensor(out=yfl, in0=yfl, in1=rows_t[:, 1024:1536], op=ALU.mult))
            yield
            K.op("pool", [k("Cyf"), "rows_t"], [k("Cyf")], lambda e: e.tensor_tensor(out=yfl, in0=yfl, in1=rows_t[:, 1536:2048], op=ALU.add))
            yield
            K.op("dve", [k("Cbcf"), k("Cbcb")], [k("Cbcf")], lambda e: e.tensor_tensor(out=bcf[p][:], in0=bcf[p][:], in1=bcb[p][:], op=ALU.add))
            yield
            K.op("dve", [k("Cvt"), k("Cbcf")], [k("Cbon")], lambda e: e.tensor_tensor(out=bon[p][:], in0=Vt[p][:], in1=b8(bcf[p][:, :]), op=ALU.mult))
            yield
            K.op("dve", [k("Cyf"), k("Cbon")], [k("Cyf")], lambda e: e.tensor_tensor(out=yf[p][:], in0=yf[p][:], in1=bon[p][:], op=ALU.add))
            yield
            K.op("dve", [k("Cyf"), k("Csza")], [k("Cmixed") + "a"], lambda e: e.tensor_tensor(out=mixed[p][:, 0:512], in0=yfl, in1=sza[p][:], op=ALU.mult))
            yield
            K.ops("pe", [k("Cmixed") + "a", k("Cmixed") + "b", "cst"], ["ptb"], [lambda e, kc=kc: e.transpose(out=PT[:, kc * 128:(kc + 1) * 128], in_=mixed[p][:, kc * 128:(kc + 1) * 128], identity=ident) for kc in range(8)])
            K.op("act", ["ptb"], [k("CmixT")], lambda e: e.copy(out=mixT[p][:].rearrange("p a b -> p (a b)"), in_=PT[:, :]))
            yield
            for n in range(2):
                K.ops("pe", [k("CmixT"), "Cwout"], ["pb%d" % (n0 + n)], [lambda e, kc=kc, n=n: e.matmul(PB[n0 + n][:, :], lhsT=mixT[p][:, kc, :], rhs=wout[:, kc, n * 512:(n + 1) * 512], start=(kc == 0), stop=(kc == 7)) for kc in range(8)])
            kr = k("Cst") + "r"
            for n in range(2):
                K.op("act", ["pb%d" % (n0 + n)], ["Cjunk", kr + "%d" % n], lambda e, n=n: e.activation(out=junk[:, :], in_=PB[n0 + n][:, :], func=AF.Square, accum_out=s_[:, 2 + n:3 + n] if False else s_[:, 0 + n:1 + n]))
            K.op("dve", [kr + "0", kr + "1"] + sk + [kc_], [kr], lambda e: e.tensor_tensor(out=s_[:, 2:3], in0=s_[:, 0:1], in1=s_[:, 1:2], op=ALU.add))
            K.op("dve", [kr], [kr], lambda e: e.tensor_scalar(out=s_[:, 2:3], in0=s_[:, 2:3], scalar1=1.0 / D, scalar2=NORM_EPS, op0=ALU.mult, op1=ALU.add))
            K.op("act", [kr], [kr], lambda e: e.sqrt(out=s_[:, 2:3], in_=s_[:, 2:3]))
            K.op("dve", [kr], [kr], lambda e: e.reciprocal(out=s_[:, 2:3], in_=s_[:, 2:3]))
            for n in range(2):
                K.op("dve", ["pb%d" % (n0 + n), kr, "Gt"], [k("Cot")], lambda e, n=n: e.scalar_tensor_tensor(out=ot[p][:, n * 512:(n + 1) * 512], in0=PB[n0 + n][:, :], scalar=s_[:, 2:3], in1=Gt[:, cv, n * 512:(n + 1) * 512], op0=ALU.mult, op1=ALU.mult))
            K.op("pool", [k("Cot"), k("Cxt")], [k("Cot")], lambda e: e.tensor_tensor(out=ot[p][:], in0=ot[p][:], in1=xt[p][:], op=ALU.add))
            yield
            K.dma("sp", k("Cot"), [k("Cot")], [], yap[a * 128:(a + 1) * 128, :], ot[p][:])
            yield


    queue = []
    for si, (xap, NT, cv, gt_base, yap) in enumerate(seqs):
        for a in range(NT):
            queue.append((xap, a, cv, gt_base + a, yap))
    K.run_streams(queue, lambda slot, it: tile(slot, *it), WC)


def _prep_shared(inp):
    f = np.float32
    g = lambda k: np.asarray(inp[k], dtype=f)
    fm = lambda v, nb: np.ascontiguousarray(v.reshape(nb, 128).T)
    pf = np.zeros((128, 80), f)
    b_mod = g("b_mod")[0]
    pf[:, 0:16] = fm(b_mod[0:2048], 16)
    pf[:, 16:24] = fm(g("ln_pre")[0], 8)
    ts = g("ts_mu")[0]
    pf[:, 24:36] = fm(ts[0], 12)
    pf[:, 36:48] = fm(ts[1], 12)
    for d in range(2):
        pf[:, 48 + 4 * d:52 + 4 * d] = fm(g("w0")[0, d], 4)
        pf[:, 56 + 4 * d:60 + 4 * d] = fm(g("a0")[0, d], 4)
        pf[:, 64 + 4 * d:68 + 4 * d] = fm(g("k_k")[0, d], 4)
        pf[:, 72 + 4 * d:76 + 4 * d] = fm(g("k_a")[0, d], 4)
    rows = np.zeros((128, 3072), f)
    rows[:, 0:1024] = g("ln_post")[0][None, :]
    rows[:, 1024:1536] = g("gn_w")[0][None, :]
    rows[:, 1536:2048] = g("gn_b")[0][None, :]
    rows[:, 2048:2560] = g("sgu_ln_g")[0][None, :]
    rows[:, 2560:3072] = g("sgu_ln_b")[0][None, :]
    bmg = np.ascontiguousarray(np.broadcast_to(b_mod[2048:3072][None, :], (2, 1024))).astype(f)
    lora_w = np.zeros((128, 4, 512), f)
    for d in range(2):
        lora_w[d * 64:(d + 1) * 64, d, :] = g("w_up")[0, d]
        lora_w[d * 64:(d + 1) * 64, 2 + d, :] = g("a_up")[0, d]
    rk = g("r_k")[0]
    rk_in = np.zeros((128, 2, 4, 2), f)
    for d in range(2):
        for q in range(4):
            for hp in range(2):
                rk_in[hp * 64:(hp + 1) * 64, d, q, hp] = rk[d, 2 * q + hp]
    rk_in = rk_in.reshape(128, 16)
    wsT = np.ascontiguousarray(np.transpose(g("w_s")[0], (2, 0, 1)))
    bs = np.ascontiguousarray(g("b_s")[0].T)
    s_idx = np.arange(128)[:, None]
    t_idx = np.arange(128)[None, :]
    consts = np.zeros((128, 6, 128), f)
    consts[:, 0] = (s_idx == t_idx)
    consts[:, 1] = (s_idx < t_idx)
    consts[:, 2] = (s_idx <= t_idx)
    consts[:, 3] = (s_idx > t_idx)
    consts[:, 4] = (s_idx >= t_idx)
    consts[:, 5] = ((s_idx // 64) == (t_idx // 64))
    sel = np.zeros((2, 2, 128), f)
    sel[0, 0, :] = 1.0
    sel[1, 1, :] = 1.0
    return dict(w_in=np.ascontiguousarray(g("w_in")[0]), w_out=np.ascontiguousarray(g("w_out")[0]),
                w_mod=np.ascontiguousarray(g("w_mod")[0]), pf=pf, rows=rows, bmg=bmg, lora_w=lora_w,
                rk_in=rk_in, wsT=wsT, bs=bs, consts=consts, sel=sel)


def _prep_core(inp, shared, i, NT_S, NP):
    f = np.float32
    m = dict(shared)
    m["xs"] = np.ascontiguousarray(np.asarray(inp["x_sample"][i], f)[:NT_S * 128])
    m["xp"] = np.ascontiguousarray(np.asarray(inp["x_prompt"][NP * i:NP * (i + 1)], f))
    c = np.asarray(inp["c"][i], f)
    cc = np.asarray(inp["c_ctx"], f)
    cvT = np.zeros((128, 8, 2), f)
    cvT[:, :, 0] = c.reshape(8, 128).T
    cvT[:, :, 1] = cc.reshape(8, 128).T
    m["cvT"] = cvT
    s0T = np.zeros((2, 128, 4, 64), f)
    for d, key in enumerate(["state_fwd", "state_bwd"]):
        S = np.asarray(inp[key][i, 0], f)
        S4 = S.reshape(4, 2, 64, 64)
        s0T[d] = np.transpose(S4, (1, 3, 0, 2)).reshape(128, 4, 64)
    m["s0T"] = s0T
    return m


_CACHE = {}


def kernel(**inputs):
    NT_S, NP, NCORE = 32, 4, 8
    if "nc" not in _CACHE:
        _CACHE["nc"] = build(NT_S, NP)
    nc = _CACHE["nc"]
    shared = _prep_shared(inputs)
    in_maps = [_prep_core(inputs, shared, i, NT_S, NP) for i in range(NCORE)]
    res = run_bass_kernel_spmd(nc, in_maps, core_ids=list(range(NCORE)))
    y_sample = np.stack([np.asarray(r["ys"]) for r in res.results], 0).astype(np.float32)
    y_prompt = np.concatenate([np.asarray(r["yp"]) for r in res.results], 0).astype(np.float32)
    nf, nb = [], []
    for r in res.results:
        nst = np.asarray(r["nst"])
        for p in range(NP):
            for d, lst in ((0, nf), (1, nb)):
                S = nst[p, d].reshape(2, 64, 4, 64)
                lst.append(np.transpose(S, (2, 0, 3, 1)).reshape(8, 64, 64)[None])
    new_f = np.stack(nf, 0).astype(np.float32)
    new_b = np.stack(nb, 0).astype(np.float32)
    return (y_prompt, y_sample, new_f, new_b)
```

```python
import contextlib
import numpy as np
import concourse.bass as bass
import concourse.mybir as mybir
from concourse.bass_utils import run_bass_kernel_spmd

F32 = mybir.dt.float32
BF16 = mybir.dt.bfloat16
AF = mybir.ActivationFunctionType
ALU = mybir.AluOpType
AX = mybir.AxisListType

D = 1024
DIN = 3840
NORM_EPS = 1e-6
GN_EPS = 6.4e-4
L2_EPS = 1e-12
EXPM05 = float(np.exp(-0.5))
C2H = 0.5 * EXPM05
VOFF = 15.0


class Sched:
    def __init__(self, nc, es):
        self.nc = nc
        self.es = es
        self.engs = dict(pe=nc.tensor, dve=nc.vector, act=nc.scalar, pool=nc.gpsimd, sp=nc.sync)
        self.sems = {e: es.enter_context(nc.semaphore("sem_" + e)) for e in self.engs}
        self.cnt = {e: 0 for e in self.engs}
        self.lastw = {}
        self.readers = {}
        self.seen = {e: {} for e in self.engs}
        self.chans = {}
        self.semobj = {}
        self.clock = {e: 0.0 for e in self.engs}
        self.ttime = {}
        self.lastfin = 0.0
        self.cost = dict(pe=0.09, dve=0.55, act=0.45, pool=1.1, sp=0.1)

    def _time(self, eng, needs, tok, cost):
        ready = 0.0
        for sname, val in needs.items():
            t = self.ttime.get((sname, val), 0.0)
            if t > ready:
                ready = t
        start = max(self.clock[eng], ready)
        fin = start + cost
        self.clock[eng] = fin if tok[0].startswith("sem_") else start + 0.1
        self.ttime[tok] = fin
        if fin > self.lastfin:
            self.lastfin = fin

    def _need(self, eng, needs):
        for sname, val in needs.items():
            if self.seen[eng].get(sname, 0) >= val:
                continue
            self.engs[eng].wait_ge(self.semobj[sname], val)
            self.seen[eng][sname] = val

    def _collect(self, eng, reads, writes):
        needs = {}

        def add(tok):
            if tok is None:
                return
            s, v = tok
            if s == "sem_pe" and eng == "pe":
                return
            if needs.get(s, 0) < v:
                needs[s] = v

        for b in reads:
            add(self.lastw.get(b))
        for b in writes:
            add(self.lastw.get(b))
            for s, v in self.readers.get(b, {}).items():
                add((s, v))
        return needs

    def _record(self, tok, reads, writes):
        s, v = tok
        for b in reads:
            self.readers.setdefault(b, {})[s] = v
        for b in writes:
            self.lastw[b] = tok
            self.readers[b] = {}

    def op(self, eng, reads, writes, fn, c=None):
        needs = self._collect(eng, reads, writes)
        self._need(eng, needs)
        inst = fn(self.engs[eng])
        self.cnt[eng] += 1
        sname = "sem_" + eng
        self.semobj[sname] = self.sems[eng]
        inst.then_inc(self.sems[eng], 1)
        self._time(eng, needs, (sname, self.cnt[eng]), self.cost[eng] if c is None else c)
        self._record((sname, self.cnt[eng]), reads, writes)

    def ops(self, eng, reads, writes, fns):
        needs = self._collect(eng, reads, writes)
        self._need(eng, needs)
        inst = None
        for fn in fns:
            inst = fn(self.engs[eng])
        self.cnt[eng] += 1
        sname = "sem_" + eng
        self.semobj[sname] = self.sems[eng]
        inst.then_inc(self.sems[eng], 1)
        self._time(eng, needs, (sname, self.cnt[eng]), self.cost[eng] * len(fns))
        self._record((sname, self.cnt[eng]), reads, writes)

    def dma(self, eng, chan_key, reads, writes, out, in_):
        if chan_key not in self.chans:
            sem = self.es.enter_context(self.nc.semaphore("dch_%d" % len(self.chans)))
            self.chans[chan_key] = [sem, 0, "dch_%d" % len(self.chans)]
            self.semobj[self.chans[chan_key][2]] = sem
        ch = self.chans[chan_key]
        needs = self._collect(eng, reads, writes)
        self._need(eng, needs)
        self.engs[eng].dma_start(out=out, in_=in_).then_inc(ch[0], 16)
        ch[1] += 16
        self._time(eng, needs, (ch[2], ch[1]), 2.5)
        self._record((ch[2], ch[1]), reads, writes)

    def run_streams(self, queue, make_gen, W, before=None, compat=None, offset=0.0):
        active, free, qi = [], list(range(W)), 0
        while qi < len(queue) or active:
            while free and qi < len(queue):
                if compat is not None and not compat(queue[qi], [a[3] for a in active]):
                    break
                if before is not None:
                    before(queue[qi])
                slot = free.pop(0)
                vt0 = min([a[2] for a in active]) if active else min(self.clock.values())
                if qi < W:
                    vt0 += qi * offset
                active.append([slot, make_gen(slot, queue[qi]), vt0, queue[qi]])
                qi += 1
            item = min(active, key=lambda a: a[2])
            self.lastfin = 0.0
            try:
                next(item[1])
                if self.lastfin > 0.0:
                    item[2] = self.lastfin
                else:
                    item[2] += 0.5
            except StopIteration:
                active.remove(item)
                free.append(item[0])

    def barrier(self, engines=("pe", "dve", "act", "pool", "sp")):
        needs = {}
        for e in self.engs:
            if self.cnt[e] > 0:
                needs["sem_" + e] = self.cnt[e]
        for ch in self.chans.values():
            if ch[1] > 0:
                needs[ch[2]] = ch[1]
        for e in engines:
            n2 = {s: v for s, v in needs.items() if not (e == "pe" and s == "sem_pe")}
            self._need(e, n2)


def build(NT_S, NP, debug=False):
    nc = bass.Bass("TRN2", target_bir_lowering=False)
    LS = NT_S * 128
    NTILES = NT_S + 2 * NP
    okind = "ExternalOutput" if debug else "Internal"

    def din(name, shape, dt=F32):
        return nc.dram_tensor(name, list(shape), dt, kind="ExternalInput").ap()

    def dout(name, shape, dt=F32):
        return nc.dram_tensor(name, list(shape), dt, kind="ExternalOutput").ap()

    def dscr(name, shape, dt):
        if debug:
            return nc.dram_tensor(name, list(shape), dt, kind="ExternalOutput").ap()
        return nc.dram_tensor(name, list(shape), dt).ap()

    xs = din("xs", [LS, D])
    xp = din("xp", [NP, 256, D])
    cvT = din("cvT", [128, 8, 2])
    w_in = din("w_in", [D, DIN])
    w_out = din("w_out", [D, D])
    w_mod = din("w_mod", [D, 3 * D])
    pf = din("pf", [128, 80])
    rows = din("rows", [128, 3072])
    bmg = din("bmg", [2, 1024])
    lora_w = din("lora_w", [128, 4, 512])
    rk_in = din("rk_in", [128, 16])
    wsT_in = din("wsT", [128, 8, 128])
    bs_in = din("bs", [128, 8])
    consts = din("consts", [128, 6, 128])
    sel_in = din("sel", [2, 2, 128])
    s0T = din("s0T", [2, 128, 4, 64])

    ys = dout("ys", [LS, D])
    yp = dout("yp", [NP, 256, D])
    nst = dout("nst", [NP, 2, 128, 4, 64])

    st_rkv = dscr("st_rkv", [NTILES, 128, 12, 128], BF16)
    st_low = dscr("st_low", [NTILES, 128, 2, 128], BF16)
    st_vt = dscr("st_vt", [NTILES, 128, 512], BF16)
    st_sza = dscr("st_sza", [NTILES, 128, 512], BF16)
    st_mixb = dscr("st_mixb", [NTILES, 128, 512], BF16)
    st_y = dscr("st_y", [2, NTILES, 128, 512], F32)
    st_bc = dscr("st_bc", [2, NTILES, 128, 8], F32)
    st_gt = dscr("st_gt", [128, 2, 1024], F32)

    seqs = [(xs, NT_S, 0, 0, ys)]
    for p in range(NP):
        seqs.append((xp[p], 2, 1, NT_S + 2 * p, yp[p]))

    with contextlib.ExitStack() as es:
        K = Sched(nc, es)
        K.debug = debug

        def dump(name, key, ap, shape, dt=F32):
            if not debug:
                return
            dd = nc.dram_tensor("dbg_" + name, list(shape), dt, kind="ExternalOutput").ap()
            K.dma("sp", "dbgch", [key], [], dd, ap)
        K.dump = dump

        def sb(name, shape, dt, stack=es):
            return stack.enter_context(nc.sbuf_tensor(name, list(shape), dt))

        def ps(name, shape, dt, stack=es):
            return stack.enter_context(nc.psum_tensor(name, list(shape), dt))

        PBIG = ps("pbig", [128, 7 * 512], F32)
        PB = [PBIG[:, i * 512:(i + 1) * 512] for i in range(7)]
        PT = ps("ptb", [128, 1024], BF16)

        pf_t = sb("pf_t", [128, 80], F32)
        cst = sb("cst", [128, 6, 128], BF16)
        sel_t = sb("sel_t", [2, 2, 128], F32)
        bs_t = sb("bs_t", [128, 8], F32)
        wsT = sb("wsT_sb", [128, 8, 128], BF16)
        lora = sb("lora", [128, 4, 512], BF16)
        rkm = sb("rkm", [128, 16], BF16)
        gfm = sb("gfm", [128, 8, 2], F32)
        shfm = sb("shfm", [128, 8, 2], F32)
        cs0 = sb("cs0", [128, 12], F32)
        omka = sb("omka", [128, 8], F32)
        mA = sb("mA", [128, 2, 4, 128], BF16)
        mT = sb("mT", [128, 2, 2, 128], BF16)

        K.dma("sp", "pf_t", [], ["pf_t"], pf_t[:], pf)
        K.dma("sp", "sel_t", [], ["sel_t"], sel_t[:], sel_in)
        K.dma("sp", "bs_t", [], ["bs_t"], bs_t[:], bs_in)
        K.op("dve", ["pf_t"], ["cs0"], lambda e: e.tensor_tensor(out=cs0[:], in0=pf_t[:, 24:36], in1=pf_t[:, 36:48], op=ALU.add))
        K.op("dve", ["cs0"], ["cs0"], lambda e: e.tensor_scalar(out=cs0[:], in0=cs0[:], scalar1=-1.0, scalar2=1.0, op0=ALU.mult, op1=ALU.add))
        K.op("dve", ["pf_t"], ["omka"], lambda e: e.tensor_scalar(out=omka[:], in0=pf_t[:, 72:80], scalar1=-1.0, scalar2=1.0, op0=ALU.mult, op1=ALU.add))

        with contextlib.ExitStack() as esA:
            win = sb("win", [128, 8, DIN], BF16, esA)
            rows_t = sb("rows_t", [128, 3072], F32, esA)
            with contextlib.ExitStack() as es0:
                cst_f = sb("cst_f", [128, 6, 128], F32, es0)
                Gt = sb("Gt", [128, 2, 1024], F32, es0)
                K.dma("sp", "rows_t", [], ["rows_t"], rows_t[:], rows)
                K.dma("sp", "cst_f", [], ["cst_f"], cst_f[:], consts)
                K.op("dve", ["cst_f"], ["cst"], lambda e: e.tensor_copy(out=cst[:], in_=cst_f[:]))
                for d, (s_i, i_i, t_i) in enumerate([(1, 2, 3), (3, 4, 1)]):
                    for j in range(4):
                        src = s_i if j % 2 == 0 else i_i
                        K.op("pool", ["cst_f"], ["mA"], lambda e, d=d, j=j, src=src: e.tensor_copy(out=mA[:, d, j, :], in_=cst_f[:, src, :]))
                    for j in range(2):
                        K.op("pool", ["cst_f"], ["mT"], lambda e, d=d, j=j, t_i=t_i: e.tensor_copy(out=mT[:, d, j, :], in_=cst_f[:, t_i, :]))
                wst = [sb("wst%d" % i, [128, DIN], F32, es0) for i in range(2)]
                tmpf = sb("tmpf", [128, 4, 512], F32, es0)
                cv_t = sb("cv_t", [128, 8, 2], F32, es0)
                scv = sb("scv", [128, 8, 2], F32, es0)
                bmg_t = sb("bmg_t", [2, 1024], F32, es0)
                grow = sb("grow", [2, 1024], F32, es0)
                modfm = sb("modfm", [128, 16, 2], F32, es0)
                K.dma("sp", "cv_t", [], ["cv_t"], cv_t[:], cvT)
                K.dma("sp", "bmg_t", [], ["bmg_t"], bmg_t[:], bmg)
                K.op("act", ["cv_t"], ["scv"], lambda e: e.activation(out=scv[:], in_=cv_t[:], func=AF.Silu))
                K.dma("sp", "tmpf", [], ["tmpf"], tmpf[:], lora_w)
                K.op("dve", ["tmpf"], ["lora"], lambda e: e.tensor_copy(out=lora[:], in_=tmpf[:]))
                K.dma("sp", "tmpf", ["tmpf"], ["tmpf"], tmpf[:, 0, 0:16], rk_in)
                K.op("dve", ["tmpf"], ["rkm"], lambda e: e.tensor_copy(out=rkm[:], in_=tmpf[:, 0, 0:16]))
                K.dma("sp", "tmpf", [], ["tmpf"], tmpf[:, 0:2, :].rearrange("p a b -> p (a b)"), wsT_in.rearrange("p g q -> p (g q)"))
                K.op("dve", ["tmpf"], ["wsT"], lambda e: e.tensor_copy(out=wsT[:].rearrange("p g q -> p (g q)"), in_=tmpf[:, 0:2, :].rearrange("p a b -> p (a b)")))
                wsi = [sb("wsi%d" % i, [128, DIN], F32, es0) for i in range(2)]

                def wdma(kc):
                    K.dma("act", "wsi%d" % (kc % 2), [], ["wsi%d" % (kc % 2)], wsi[kc % 2][:, :], w_in[kc * 128:(kc + 1) * 128, :])
                wdma(0)
                wdma(1)
                for kc in range(8):
                    K.op("act", ["wsi%d" % (kc % 2)], ["win"], lambda e, kc=kc: e.copy(out=win[:, kc, :], in_=wsi[kc % 2][:, :]))
                    if kc + 2 < 8:
                        wdma(kc + 2)
                for kc in range(8):
                    wm = wst[kc % 2]
                    K.dma("sp", "wst%d" % (kc % 2), [], ["wst%d" % (kc % 2)], wm[:, 0:3072], w_mod[kc * 128:(kc + 1) * 128, :])
                    fns = []
                    for blk in range(16):
                        fns.append(lambda e, blk=blk, kc=kc, wm=wm: e.matmul(PB[0][:, blk * 2:blk * 2 + 2], lhsT=wm[:, blk * 128:(blk + 1) * 128], rhs=scv[:, kc, :], start=(kc == 0 and blk == 0), stop=(kc == 7), skip_group_check=True))
                    for n in range(2):
                        fns.append(lambda e, n=n, kc=kc, wm=wm: e.matmul(PB[1 + n][0:2, :], lhsT=scv[:, kc, :], rhs=wm[:, 2048 + n * 512:2048 + (n + 1) * 512], start=(kc == 0), stop=(kc == 7)))
                    K.ops("pe", ["wst%d" % (kc % 2), "scv"], ["pb0", "pb1", "pb2"], fns)
                K.op("dve", ["pb0", "pf_t"], ["modfm"], lambda e: e.tensor_tensor(out=modfm[:], in0=PB[0][:, 0:32].rearrange("p (b c) -> p b c", c=2), in1=pf_t[:, 0:16].unsqueeze(2).to_broadcast([128, 16, 2]), op=ALU.add))
                K.op("dve", ["modfm"], ["shfm"], lambda e: e.tensor_copy(out=shfm[:], in_=modfm[:, 0:8, :]))
                K.op("dve", ["modfm"], ["gfm"], lambda e: e.tensor_scalar(out=gfm[:], in0=modfm[:, 8:16, :], scalar1=1.0, scalar2=None, op0=ALU.add))
                K.op("dve", ["gfm", "pf_t"], ["gfm"], lambda e: e.tensor_tensor(out=gfm[:], in0=gfm[:], in1=pf_t[:, 16:24].unsqueeze(2).to_broadcast([128, 8, 2]), op=ALU.mult))
                for n in range(2):
                    K.op("dve", ["pb%d" % (1 + n), "bmg_t"], ["grow"], lambda e, n=n: e.tensor_tensor(out=grow[:, n * 512:(n + 1) * 512], in0=PB[1 + n][0:2, :], in1=bmg_t[:, n * 512:(n + 1) * 512], op=ALU.add))
                for cv in range(2):
                    for n in range(2):
                        K.ops("pe", ["grow", "sel_t"], ["pb%d" % (3 + n)], [lambda e, cv=cv, n=n: e.matmul(PB[3 + n][:, :], lhsT=sel_t[:, cv, :], rhs=grow[:, n * 512:(n + 1) * 512], start=True, stop=True)])
                        K.op("dve", ["pb%d" % (3 + n), "rows_t"], ["Gt"], lambda e, cv=cv, n=n: e.tensor_tensor(out=Gt[:, cv, n * 512:(n + 1) * 512], in0=PB[3 + n][:, :], in1=rows_t[:, n * 512:(n + 1) * 512], op=ALU.mult))
                K.dma("sp", "Gt", ["Gt"], [], st_gt, Gt[:])
                dump("gfm", "gfm", gfm[:], [128, 8, 2])
                dump("shfm", "shfm", shfm[:], [128, 8, 2])
                dump("Gt", "Gt", Gt[:], [128, 2, 1024])
                dump("scv", "scv", scv[:], [128, 8, 2])
                K.barrier()

            phase_a(nc, K, esA, sb, PB, PT, seqs, win, dict(
                pf_t=pf_t, rows_t=rows_t, cst=cst, bs_t=bs_t, wsT=wsT, gfm=gfm, shfm=shfm, cs0=cs0,
                st_rkv=st_rkv, st_low=st_low, st_vt=st_vt, st_sza=st_sza, st_mixb=st_mixb))
            K.barrier()

        with contextlib.ExitStack() as esB:
            phase_b(nc, K, esB, sb, PB, PT, seqs, dict(
                pf_t=pf_t, cst=cst, lora=lora, rkm=rkm, omka=omka, mA=mA, mT=mT,
                st_rkv=st_rkv, st_low=st_low, st_vt=st_vt, st_y=st_y, st_bc=st_bc, s0T=s0T, nst=nst, PBIG=PBIG), NT_S, NP)
            K.barrier()
        with contextlib.ExitStack() as esC:
            phase_c(nc, K, esC, sb, PB, PT, seqs, dict(rows=rows, st_gt=st_gt, cst=cst, w_out=w_out,
                    st_y=st_y, st_bc=st_bc, st_vt=st_vt, st_sza=st_sza, st_mixb=st_mixb))
        K.barrier(engines=("sp",))
    return nc


def phase_a(nc, K, esA, sb, PB, PT, seqs, win, R):
    pf_t, rows_t, cst, bs_t, wsT = R["pf_t"], R["rows_t"], R["cst"], R["bs_t"], R["wsT"]
    gfm, shfm, cs0 = R["gfm"], R["shfm"], R["cs0"]
    ident = cst[:, 0, :]
    NXB = 2
    xt = [sb("xt%d" % i, [128, D], F32, esA) for i in range(NXB)]
    xn = [sb("xn%d" % i, [128, D], BF16, esA) for i in range(4)]
    sq = sb("sqj", [128, D], BF16, esA)
    stat = sb("statA", [128, 16], F32, esA)
    hT = [sb("hT%d" % i, [128, 8, 512], BF16, esA) for i in range(2)]
    raw = [sb("raw%d" % i, [128, 12, 514], BF16, esA) for i in range(2)]
    rkvp = sb("rkvp", [128, 12, 512], BF16, esA)
    tmps = [sb("shtmp%d" % i, [128, 512], F32, esA) for i in range(2)]
    low = [sb("low%d" % i, [128, 2, 512], BF16, esA) for i in range(2)]
    sza = [sb("sza%d" % i, [128, 512], BF16, esA) for i in range(4)]
    u_t = [sb("u_t%d" % i, [128, 512], BF16, esA) for i in range(4)]
    szb = [sb("szb%d" % i, [128, 512], BF16, esA) for i in range(4)]
    vb = [sb("vb%d" % i, [128, 512], F32, esA) for i in range(4)]
    vn = [sb("vn%d" % i, [128, 512], BF16, esA) for i in range(4)]
    mixb = [sb("mixb%d" % i, [128, 512], BF16, esA) for i in range(4)]
    vts = [sb("vts%d" % i, [128, 512], BF16, esA) for i in range(2)]
    lnst = [sb("lnst%d" % i, [128, 8], F32, esA) for i in range(4)]

    cnt = {"x": 0, "tile": 0, "grp": 0}

    def run_lanes(lanes):
        lanes = [[list(g_), W_, []] for g_, W_ in lanes]
        while any(l[0] or l[2] for l in lanes):
            for l in lanes:
                while l[0] and len(l[2]) < l[1]:
                    l[2].append(l[0].pop(0))
                for g_ in list(l[2]):
                    try:
                        next(g_)
                    except StopIteration:
                        l[2].remove(g_)

    def shift_gen(g, GS, nt_g, gt0):
        rw = raw[g % 2]
        rk = "raw%d" % (g % 2)
        for j in range(12):
            tk = "shtmp%d" % (j % 2)
            tm = tmps[j % 2]
            K.op("dve", [rk, "cs0"], [tk], lambda e, j=j, tm=tm: e.tensor_scalar(out=tm[:, 0:GS], in0=rw[:, j, 1:GS + 1], scalar1=cs0[:, j:j + 1], scalar2=None, op0=ALU.mult))
            K.op("dve", [rk, tk, "pf_t"], [tk], lambda e, j=j, tm=tm: e.scalar_tensor_tensor(out=tm[:, 0:GS], in0=rw[:, j, 0:GS], scalar=pf_t[:, 24 + j:25 + j], in1=tm[:, 0:GS], op0=ALU.mult, op1=ALU.add))
            K.op("dve", [rk, tk, "pf_t"], ["rkvp"], lambda e, j=j, tm=tm: e.scalar_tensor_tensor(out=rkvp[:, j, 0:GS], in0=rw[:, j, 2:GS + 2], scalar=pf_t[:, 36 + j:37 + j], in1=tm[:, 0:GS], op0=ALU.mult, op1=ALU.add))
            yield
        for a in range(nt_g):
            gt = gt0 + a
            K.dma("sp", "rkvp", ["rkvp"], [], R["st_rkv"][gt], rkvp[:, :, a * 128:(a + 1) * 128])
            vs = vts[gt % 2]
            vk = "vts%d" % (gt % 2)
            K.ops("pe", ["rkvp", "cst"], ["ptb"], [lambda e, a=a, q=q: e.transpose(out=PT[:, q * 128:(q + 1) * 128], in_=rkvp[:, 8 + q, a * 128:(a + 1) * 128], identity=ident) for q in range(4)])
            K.op("act", ["ptb"], [vk], lambda e, vs=vs: e.copy(out=vs[:], in_=PT[:, 0:512]))
            K.dma("act", vk, [vk], [], R["st_vt"][gt], vs[:])
            yield

    def front_compute(xap, g, a):
        t0 = (g * (GS_cur[0] // 128) + a) * 128
        xi = cnt["x"] % NXB
        cnt["x"] += 1
        xk = "xt%d" % xi
        K.dma("sp", xk, [], [xk], xt[xi][:], xap[t0:t0 + 128, :])
        sc = cnt["tile"] % 8
        cnt["tile"] += 1
        nk = "xn%d" % a
        K.op("act", [xk], [nk, "statA%d" % sc], lambda e: e.activation(out=xn[a][:], in_=xt[xi][:], func=AF.Square, accum_out=stat[:, sc:sc + 1]))
        K.op("dve", ["statA%d" % sc], ["statA%d" % sc], lambda e: e.tensor_scalar(out=stat[:, sc:sc + 1], in0=stat[:, sc:sc + 1], scalar1=1.0 / D, scalar2=NORM_EPS, op0=ALU.mult, op1=ALU.add))
        K.op("act", ["statA%d" % sc], ["statA%d" % sc], lambda e: e.sqrt(out=stat[:, sc:sc + 1], in_=stat[:, sc:sc + 1]))
        K.op("dve", ["statA%d" % sc], ["statA%d" % sc], lambda e: e.reciprocal(out=stat[:, sc:sc + 1], in_=stat[:, sc:sc + 1]))
        K.op("act", [xk, "statA%d" % sc], [nk], lambda e: e.activation(out=xn[a][:], in_=xt[xi][:], func=AF.Copy, scale=stat[:, sc:sc + 1]))

    def front_transpose(a, cv, hTg, hk):
        nk = "xn%d" % a
        K.ops("pe", [nk, "cst"], ["ptb"], [lambda e, kc=kc: e.transpose(out=PT[:, kc * 128:(kc + 1) * 128], in_=xn[a][:, kc * 128:(kc + 1) * 128], identity=ident) for kc in range(8)])
        for kc in range(8):
            K.op("dve", ["ptb", "gfm", "shfm"], [hk], lambda e, kc=kc: e.tensor_scalar(out=hTg[:, kc, a * 128:(a + 1) * 128], in0=PT[:, kc * 128:(kc + 1) * 128], scalar1=gfm[:, kc, cv:cv + 1], scalar2=shfm[:, kc, cv:cv + 1], op0=ALU.mult, op1=ALU.add))

    GS_cur = [512]

    def tm_gen(s2, hTg, hk, a, gt):
        ls = lnst[s2]
        lsk = "lnst%d" % s2

        bank = lambda ci: 3 + (ci + a) % 4

        def proj(ci, c0):
            pbi = bank(ci)
            K.ops("pe", [hk, "win"], ["pb%d" % pbi], [lambda e, kc=kc: e.matmul(PB[pbi][:, :], lhsT=hTg[:, kc, a * 128:(a + 1) * 128], rhs=win[:, kc, c0:c0 + 512], start=(kc == 0), stop=(kc == 7)) for kc in range(8)])
        proj(0, 1536)
        K.op("act", ["pb%d" % bank(0)], ["sza%d" % s2], lambda e: e.activation(out=sza[s2][:], in_=PB[bank(0)][:, :], func=AF.Silu))
        yield
        K.dma("act", "sza%d" % s2, ["sza%d" % s2], [], R["st_sza"][gt], sza[s2][:])
        proj(1, 2304)
        K.op("act", ["pb%d" % bank(1)], ["u_t%d" % s2], lambda e: e.copy(out=u_t[s2][:], in_=PB[bank(1)][:, :]))
        yield
        proj(2, 2816)
        K.op("act", ["pb%d" % bank(2), lsk], ["vb%d" % s2, lsk], lambda e: e.activation(out=vb[s2][:], in_=PB[bank(2)][:, :], func=AF.Copy, accum_out=ls[:, 0:1]))
        yield
        proj(3, 3328)
        K.op("act", ["pb%d" % bank(3)], ["szb%d" % s2], lambda e: e.activation(out=szb[s2][:], in_=PB[bank(3)][:, :], func=AF.Silu))
        yield
        K.op("act", ["vb%d" % s2, lsk], ["sqj", lsk], lambda e: e.activation(out=sq[:, 0:512], in_=vb[s2][:], func=AF.Square, accum_out=ls[:, 1:2]))
        yield
        K.op("dve", [lsk], [lsk], lambda e: e.tensor_scalar(out=ls[:, 2:3], in0=ls[:, 0:1], scalar1=1.0 / 512, scalar2=None, op0=ALU.mult))
        yield
        K.op("dve", [lsk], [lsk], lambda e: e.tensor_tensor(out=ls[:, 3:4], in0=ls[:, 2:3], in1=ls[:, 2:3], op=ALU.mult))
        yield
        K.op("dve", [lsk], [lsk], lambda e: e.scalar_tensor_tensor(out=ls[:, 4:5], in0=ls[:, 1:2], scalar=1.0 / 512, in1=ls[:, 3:4], op0=ALU.mult, op1=ALU.subtract))
        yield
        K.op("dve", [lsk], [lsk], lambda e: e.tensor_scalar(out=ls[:, 5:6], in0=ls[:, 4:5], scalar1=NORM_EPS, scalar2=None, op0=ALU.add))
        yield
        K.op("act", [lsk], [lsk], lambda e: e.sqrt(out=ls[:, 5:6], in_=ls[:, 5:6]))
        yield
        K.op("dve", [lsk], [lsk], lambda e: e.reciprocal(out=ls[:, 5:6], in_=ls[:, 5:6]))
        yield
        K.op("dve", ["vb%d" % s2, lsk], ["vb%d" % s2], lambda e: e.tensor_scalar(out=vb[s2][:], in0=vb[s2][:], scalar1=ls[:, 2:3], scalar2=ls[:, 5:6], op0=ALU.subtract, op1=ALU.mult))
        yield
        K.op("pool", ["vb%d" % s2, "rows_t"], ["vb%d" % s2], lambda e: e.tensor_tensor(out=vb[s2][:], in0=vb[s2][:], in1=rows_t[:, 2048:2560], op=ALU.mult))
        yield
        K.op("pool", ["vb%d" % s2, "rows_t"], ["vn%d" % s2], lambda e: e.tensor_tensor(out=vn[s2][:], in0=vb[s2][:], in1=rows_t[:, 2560:3072], op=ALU.add))
        yield
        K.ops("pe", ["vn%d" % s2, "wsT"], ["pb0"], [lambda e, gg=gg: e.matmul(PB[0][:, gg * 64:(gg + 1) * 64], lhsT=wsT[:, gg, :], rhs=vn[s2][:, gg * 64:(gg + 1) * 64], start=True, stop=True) for gg in range(8)])
        K.op("dve", ["pb0", "bs_t"], ["vb%d" % s2], lambda e: e.tensor_tensor(out=vb[s2][:].rearrange("p (g c) -> p g c", c=64), in0=PB[0][:, :].rearrange("p (g c) -> p g c", c=64), in1=bs_t[:, :].unsqueeze(2).to_broadcast([128, 8, 64]), op=ALU.add))
        yield
        K.op("pool", ["vb%d" % s2, "u_t%d" % s2], ["vb%d" % s2], lambda e: e.tensor_tensor(out=vb[s2][:], in0=vb[s2][:], in1=u_t[s2][:], op=ALU.mult))
        yield
        K.op("pool", ["vb%d" % s2, "szb%d" % s2], ["mixb%d" % s2], lambda e: e.tensor_tensor(out=mixb[s2][:], in0=vb[s2][:], in1=szb[s2][:], op=ALU.mult))
        yield
        K.dma("sp", "mixb%d" % s2, ["mixb%d" % s2], [], R["st_mixb"][gt], mixb[s2][:])

    pre = None
    for si, (xap, NT, cv, gt_base, _y) in enumerate(seqs):
        GS = min(512, NT * 128)
        GS_cur[0] = GS
        nt_g = GS // 128
        NG = NT // nt_g
        nxt = seqs[si + 1] if si + 1 < len(seqs) else None
        nt_n = min(512, nxt[1] * 128) // 128 if nxt is not None else 0
        hbuf = {}
        if pre is None:
            hi = cnt["grp"] % 2
            cnt["grp"] += 1
            hbuf[0] = (hT[hi], "hT%d" % hi)
            for a in range(nt_g):
                front_compute(xap, 0, a)
            for a in range(nt_g):
                front_transpose(a, cv, *hbuf[0])
        else:
            hbuf[0] = pre
            pre = None
        for g in range(NG):
            hTg, hk = hbuf[g]
            rw = raw[g % 2]
            rk = "raw%d" % (g % 2)
            if g == 0:
                K.op("pool", [], [rk], lambda e, rw=rw: e.memset(rw[:, :, 0:1], 0.0))
            lw = low[g % 2]
            lk = "low%d" % (g % 2)
            for bi, blk in enumerate(list(range(12)) + [16, 17]):
                pbi = bi % 3
                pk = "pb%d" % pbi
                K.ops("pe", [hk, "win"], [pk], [lambda e, kc=kc, blk=blk, pbi=pbi: e.matmul(PB[pbi][:, 0:GS], lhsT=win[:, kc, blk * 128:(blk + 1) * 128], rhs=hTg[:, kc, 0:GS], start=(kc == 0), stop=(kc == 7)) for kc in range(8)])
                if blk < 12:
                    K.op("act", [pk], [rk], lambda e, blk=blk, pbi=pbi, rw=rw: e.copy(out=rw[:, blk, 1:GS + 1], in_=PB[pbi][:, 0:GS]))
                elif blk == 16:
                    K.op("act", [pk], [lk], lambda e, pbi=pbi, lw=lw: e.activation(out=lw[:, 0, 0:GS], in_=PB[pbi][:, 0:GS], func=AF.Tanh))
                else:
                    K.op("act", [pk], [lk], lambda e, pbi=pbi, lw=lw: e.copy(out=lw[:, 1, 0:GS], in_=PB[pbi][:, 0:GS]))
                if g + 1 < NG and bi in (2, 5, 8, 11):
                    front_compute(xap, g + 1, (bi - 2) // 3)
                elif g + 1 == NG and nxt is not None and bi in (2, 5, 8, 11) and (bi - 2) // 3 < nt_n:
                    front_compute(nxt[0], 0, (bi - 2) // 3)
            for a in range(nt_g):
                gt = gt_base + g * nt_g + a
                K.dma("act", lk, [lk], [], R["st_low"][gt], lw[:, :, a * 128:(a + 1) * 128])
            gens = []
            if g > 0:
                rwp = raw[(g - 1) % 2]
                rkp = "raw%d" % ((g - 1) % 2)
                K.op("pool", [rk], [rkp], lambda e, rw=rw, rwp=rwp: e.tensor_copy(out=rwp[:, :, GS + 1:GS + 2], in_=rw[:, :, 1:2]))
                K.op("pool", [rkp], [rk], lambda e, rw=rw, rwp=rwp: e.tensor_copy(out=rw[:, :, 0:1], in_=rwp[:, :, GS:GS + 1]))
                gens.append(shift_gen(g - 1, GS, nt_g, gt_base + (g - 1) * nt_g))
            tms = [tm_gen(a % 4, hTg, hk, a, gt_base + g * nt_g + a) for a in range(nt_g)]
            run_lanes([(gens, 1), (tms, 4)])
            if g + 1 < NG:
                hi = cnt["grp"] % 2
                cnt["grp"] += 1
                hbuf[g + 1] = (hT[hi], "hT%d" % hi)
                for a in range(nt_g):
                    front_transpose(a, cv, *hbuf[g + 1])
            elif nxt is not None:
                hi = cnt["grp"] % 2
                cnt["grp"] += 1
                pre = (hT[hi], "hT%d" % hi)
                for a in range(nt_n):
                    front_transpose(a, nxt[2], *pre)
        g = NG - 1
        rw = raw[g % 2]
        rk = "raw%d" % (g % 2)
        K.op("pool", [], [rk], lambda e, rw=rw: e.memset(rw[:, :, GS + 1:GS + 2], 0.0))
        run_lanes([([shift_gen(g, GS, nt_g, gt_base + g * nt_g)], 1)])


def phase_b(nc, K, esB, sb, PB, PT, seqs, R, NT_S, NP):
    PBIG = R["PBIG"]
    pf_t, cst, lora, rkm, omka, mA, mT = R["pf_t"], R["cst"], R["lora"], R["rkm"], R["omka"], R["mA"], R["mT"]
    ident = cst[:, 0, :]

    WB = 3

    def T(name, shape, dt):
        return [sb("%s_%d" % (name, p), shape, dt, esB) for p in range(WB)]
    rk = T("Brk", [128, 8, 128], BF16)
    lowT = T("Blow", [128, 2, 128], BF16)
    Vt = T("Bvt", [128, 512], BF16)
    arm = T("Barm", [128, 4, 2, 2, 128], BF16)
    bT = T("BbT", [128, 4, 128], BF16)
    kT = T("BkT", [128, 4, 128], BF16)
    rkd = T("Brkd", [128, 4, 128], BF16)
    kk2 = T("Bkk2", [128, 4, 128], BF16)
    MM1 = T("BMM1", [128, 8, 2, 128], BF16)
    MM2 = T("BMM2", [128, 8, 2, 128], BF16)
    Q0T = T("BQ0T", [128, 8, 128], BF16)
    QR = [T("BQRa", [128, 8, 2, 128], BF16), T("BQRb", [128, 8, 2, 128], BF16)]
    QT = [T("BQTa", [128, 8, 128], BF16), T("BQTb", [128, 8, 128], BF16)]
    TT = T("BTT", [128, 8, 128], BF16)
    Bt = T("BBt", [128, 512], BF16)
    Kt = T("BKt", [128, 512], BF16)
    Zs = T("BZs", [128, 512], BF16)
    Us = T("BUs", [128, 512], BF16)
    ysb = T("Bysb", [128, 512], F32)
    bcs = T("Bbcs", [128, 8], F32)
    f32n = ["sg", "aa", "E1", "E2", "gam", "gamx", "gami", "kk"]
    F = {n: T("Bf_" + n, [128, 4, 128], F32) for n in f32n}
    F["tq"], F["rn"], F["kd"], F["kkn"] = F["sg"], F["E1"], F["E2"], F["kk"]
    tot = T("Btot", [128, 16], F32)
    gC = T("BgC", [128, 4], F32)
    ones = sb("Bones", [128, 128], F32, esB)
    S = [[[sb("BS_%d_%d_%d" % (sp, d, i), [128, 4, 64], F32, esB) for i in range(2)] for d in range(2)] for sp in range(2)]
    Sb = [[[sb("BSb_%d_%d_%d" % (sp, d, i), [128, 4, 64], BF16, esB) for i in range(2)] for d in range(2)] for sp in range(2)]
    t1s = [sb("Bt1_%d" % i, [128, 4, 64], F32, esB) for i in range(WB)]

    K.op("pool", [], ["Bones"], lambda e: e.memset(ones[:], 1.0))
    hpar = sb("Bhpar", [128, 34], F32, esB)
    K.op("pool", ["pf_t"], ["hpar"], lambda e: e.tensor_scalar(out=hpar[:, 0:16], in0=pf_t[:, 48:64], scalar1=0.5, scalar2=None, op0=ALU.mult))
    K.op("pool", ["hpar"], ["hpar"], lambda e: e.memset(hpar[:, 16:17], C2H))
    K.op("pool", ["hpar"], ["hpar"], lambda e: e.memset(hpar[:, 17:18], L2_EPS))
    K.op("pool", ["pf_t", "hpar"], ["hpar"], lambda e: e.tensor_scalar(out=hpar[:, 18:26], in0=pf_t[:, 72:80], scalar1=0.5, scalar2=None, op0=ALU.mult))
    K.op("pool", ["pf_t", "hpar"], ["hpar"], lambda e: e.tensor_scalar(out=hpar[:, 26:34], in0=pf_t[:, 72:80], scalar1=-0.5, scalar2=1.0, op0=ALU.mult, op1=ALU.add))
    for p in range(WB):
        K.op("pool", [], ["Barm_%d" % p], lambda e, p=p: e.memset(arm[p][:].rearrange("p a b c d -> p (a b c d)"), 0.0))

    chain_done = {}

    def visit(p, gt, d, sp, sidx, last_out_ap, ckey, cidx):
        t1 = t1s[p]
        b0, b1 = 2 * p, 2 * p + 1
        b2 = b0
        kt1 = "Bt1_%d" % p
        k = lambda n: "%s_%d" % (n, p)
        _al = dict(tq="sg", rn="E1", kd="E2", kkn="kk")
        fk = lambda n: "Bf_%s_%d" % (_al.get(n, n), p)
        Sin, Sbin = S[sp][d][sidx], Sb[sp][d][sidx]
        Sout, Sbout = S[sp][d][1 - sidx], Sb[sp][d][1 - sidx]
        kSin, kSbin = "BS_%d_%d_%d" % (sp, d, sidx), "BSb_%d_%d_%d" % (sp, d, sidx)
        kSout, kSbout = "BS_%d_%d_%d" % (sp, d, 1 - sidx), "BSb_%d_%d_%d" % (sp, d, 1 - sidx)
        c_w0, c_a0, c_kk, c_ka = 48 + 4 * d, 56 + 4 * d, 64 + 4 * d, 72 + 4 * d
        bc3 = lambda ap: ap.unsqueeze(2).to_broadcast([128, 4, 128])
        rT_ = rk[p][:, 0:4, :]
        kT_ = rk[p][:, 4:8, :]
        K.dma("sp", k("Brk"), [], [k("Brk")], rk[p][:], R["st_rkv"][gt][:, 0:8, :])
        yield
        K.dma("sp", k("Blow"), [], [k("Blow")], lowT[p][:], R["st_low"][gt])
        yield
        K.dma("sp", k("Bvt"), [], [k("Bvt")], Vt[p][:], R["st_vt"][gt])
        yield
        K.ops("pe", [k("Blow"), "lora"], ["pb%d" % b0], [lambda e, q=q: e.matmul(PB[b0][:, q * 128:(q + 1) * 128], lhsT=lora[:, d, q * 128:(q + 1) * 128], rhs=lowT[p][:, 0, :], start=True, stop=True) for q in range(4)])
        K.ops("pe", [k("Blow"), "lora"], ["pb%d" % b1], [lambda e, q=q: e.matmul(PB[b1][:, q * 128:(q + 1) * 128], lhsT=lora[:, 2 + d, q * 128:(q + 1) * 128], rhs=lowT[p][:, 1, :], start=True, stop=True) for q in range(4)])
        yield
        TH, THA, C2, X2 = F["sg"][p], F["aa"][p], F["E1"][p], F["E2"][p]
        for q in range(4):
            K.op("act", ["pb%d" % b0, "hpar"], [fk("sg")], lambda e, q=q: e.activation(out=TH[:, q, :], in_=PB[b0][:, q * 128:(q + 1) * 128], func=AF.Tanh, scale=0.5, bias=hpar[:, 4 * d + q:4 * d + q + 1]))
        yield
        for q in range(4):
            K.op("act", ["pb%d" % b1, "hpar"], [fk("aa")], lambda e, q=q: e.activation(out=THA[:, q, :], in_=PB[b1][:, q * 128:(q + 1) * 128], func=AF.Tanh, scale=0.5, bias=hpar[:, 8 + 4 * d + q:8 + 4 * d + q + 1]))
        yield
        for q in range(4):
            if d == 0:
                K.op("dve", [fk("sg"), "Bones"], [fk("E1")], lambda e, q=q: e.tensor_tensor_scan(out=C2[:, q, :], data0=ones[:, :], data1=TH[:, q, :], initial=0.0, op0=ALU.add, op1=ALU.add))
            else:
                K.op("dve", [fk("sg"), "Bones"], [fk("E1")], lambda e, q=q: e.tensor_tensor_scan(out=C2[:, q, ::-1], data0=ones[:, :], data1=TH[:, q, ::-1], initial=0.0, op0=ALU.add, op1=ALU.add))
        yield
        K.op("pool", [fk("E1"), fk("sg")], [fk("E2")], lambda e: e.tensor_tensor(out=X2[:], in0=C2[:], in1=TH[:], op=ALU.subtract))
        yield
        tcol = 127 if d == 0 else 0
        K.op("pool", [fk("E1")], [k("Btot")], lambda e: e.tensor_scalar(out=tot[p][:, 0:4], in0=C2[:, :, tcol], scalar1=-C2H, scalar2=None, op0=ALU.mult))
        yield
        K.op("act", [k("Btot")], [k("BgC")], lambda e: e.activation(out=gC[p][:], in_=tot[p][:, 0:4], func=AF.Exp))
        yield
        K.op("act", [fk("E1")], [fk("gam")], lambda e: e.activation(out=F["gam"][p][:], in_=C2[:], func=AF.Exp, scale=-C2H))
        yield
        K.op("act", [fk("E2"), "hpar"], [fk("gamx")], lambda e: e.activation(out=F["gamx"][p][:], in_=X2[:], func=AF.Exp, scale=-C2H, bias=hpar[:, 16:17]))
        yield
        K.op("act", [fk("E1")], [fk("gami")], lambda e: e.activation(out=F["gami"][p][:], in_=C2[:], func=AF.Exp, scale=C2H))
        yield
        K.op("dve", [k("Brk"), "pf_t"], [fk("kk")], lambda e: e.tensor_tensor(out=F["kk"][p][:], in0=kT_, in1=bc3(pf_t[:, c_kk:c_kk + 4]), op=ALU.mult))
        yield
        K.op("pool", [fk("kk")], [k("Bkk2")], lambda e: e.tensor_tensor(out=kk2[p][:], in0=F["kk"][p][:], in1=F["kk"][p][:], op=ALU.mult))
        yield
        K.ops("pe", [k("Bkk2"), "cst"], ["pb%d" % b2], [lambda e, q=q: e.matmul(PB[b2][:, q * 128:(q + 1) * 128], lhsT=cst[:, 5, :], rhs=kk2[p][:, q, :], start=True, stop=True) for q in range(4)])
        yield
        K.op("act", ["pb%d" % b2, "hpar"], [fk("rn")], lambda e: e.activation(out=F["rn"][p][:].rearrange("p a b -> p (a b)"), in_=PB[b2][:, :], func=AF.Ln, bias=hpar[:, 17:18]))
        yield
        K.op("act", [fk("rn")], [fk("rn")], lambda e: e.activation(out=F["rn"][p][:], in_=F["rn"][p][:], func=AF.Exp, scale=-0.5))
        yield
        K.op("pool", [fk("kk"), fk("rn")], [fk("kkn")], lambda e: e.tensor_tensor(out=F["kkn"][p][:], in0=F["kk"][p][:], in1=F["rn"][p][:], op=ALU.mult))
        yield
        for q in range(4):
            K.op("pool", [fk("aa"), "hpar"], [fk("tq")], lambda e, q=q: e.tensor_scalar(out=F["tq"][p][:, q, :], in0=THA[:, q, :], scalar1=hpar[:, 18 + 4 * d + q:19 + 4 * d + q], scalar2=hpar[:, 26 + 4 * d + q:27 + 4 * d + q], op0=ALU.mult, op1=ALU.add))
        yield
        K.op("pool", [fk("tq"), k("Brk")], [fk("kd")], lambda e: e.tensor_tensor(out=F["kd"][p][:], in0=F["tq"][p][:], in1=kT_, op=ALU.mult))
        yield
        for hp in range(2):
            rs = slice(hp * 64, (hp + 1) * 64)
            K.op("dve", [fk("kkn"), fk("gamx")], [k("Barm")], lambda e, rs=rs, hp=hp: e.scalar_tensor_tensor(out=arm[p][rs, :, hp, 0, :], in0=F["kkn"][p][rs, :, :], scalar=-1.0, in1=F["gamx"][p][rs, :, :], op0=ALU.mult, op1=ALU.mult))
            yield
            K.op("pool", [k("Brk"), fk("gam")], [k("Barm")], lambda e, rs=rs, hp=hp: e.tensor_tensor(out=arm[p][rs, :, hp, 1, :], in0=rk[p][rs, 0:4, :], in1=F["gam"][p][rs, :, :], op=ALU.mult))
            yield
        K.op("dve", [fk("kkn"), fk("aa")], [fk("tq")], lambda e: e.scalar_tensor_tensor(out=F["tq"][p][:], in0=THA[:], scalar=1.0, in1=F["kkn"][p][:], op0=ALU.add, op1=ALU.mult))
        yield
        K.op("dve", [fk("tq"), fk("gami")], [k("BbT")], lambda e: e.scalar_tensor_tensor(out=bT[p][:], in0=F["tq"][p][:], scalar=0.5, in1=F["gami"][p][:], op0=ALU.mult, op1=ALU.mult))
        yield
        K.op("pool", [fk("kd"), fk("gami")], [k("BkT")], lambda e: e.tensor_tensor(out=kT[p][:], in0=F["kd"][p][:], in1=F["gami"][p][:], op=ALU.mult))
        yield
        K.op("pool", [fk("kd"), k("Brk")], [k("Brkd")], lambda e: e.tensor_tensor(out=rkd[p][:], in0=F["kd"][p][:], in1=rT_, op=ALU.mult))
        yield
        K.ops("pe", [k("Brkd"), "rkm"], ["pb6"], [lambda e, q=q: e.matmul(PB[6][:, q * 2:q * 2 + 2], lhsT=rkd[p][:, q, :], rhs=rkm[:, d * 8 + q * 2:d * 8 + q * 2 + 2], start=True, stop=True) for q in range(4)])
        K.op("act", ["pb6"], [k("Bbcs")], lambda e: e.copy(out=bcs[p][:], in_=PB[6][:, 0:8]))
        yield
        K.dma("act", k("Bbcs"), [k("Bbcs")], [], R["st_bc"][d, gt], bcs[p][:])
        yield
        K.ops("pe", [k("BbT"), k("BkT"), "cst"], ["ptb"],
              [lambda e, q=q: e.transpose(out=PT[:, q * 128:(q + 1) * 128], in_=bT[p][:, q, :], identity=ident) for q in range(4)] +
              [lambda e, q=q: e.transpose(out=PT[:, 512 + q * 128:512 + (q + 1) * 128], in_=kT[p][:, q, :], identity=ident) for q in range(4)])
        K.op("act", ["ptb"], [k("BBt")], lambda e: e.copy(out=Bt[p][:], in_=PT[:, 0:512]))
        K.op("act", ["ptb"], [k("BKt")], lambda e: e.copy(out=Kt[p][:], in_=PT[:, 512:1024]))
        yield
        for q in range(4):
            fa, fb, fc = [], [], []
            for hp in range(2):
                rhsA = arm[p][:, q, hp, :, :].rearrange("p a t -> p (a t)")
                fa.append(lambda e, hp=hp, rhsA=rhsA: e.matmul(PB[b0][:, hp * 256:(hp + 1) * 256], lhsT=bT[p][:, q, :], rhs=rhsA, start=True, stop=True))
                fb.append(lambda e, hp=hp, rhsA=rhsA: e.matmul(PB[b1][:, hp * 256:(hp + 1) * 256], lhsT=kT[p][:, q, :], rhs=rhsA, start=True, stop=True))
                fc.append(lambda e, hp=hp: e.matmul(PB[b0][:, hp * 128:(hp + 1) * 128], lhsT=arm[p][:, q, hp, 0, :], rhs=bT[p][:, q, :], start=True, stop=True))
            K.ops("pe", [k("BbT"), k("Barm")], ["pb%d" % b0], fa)
            K.ops("pe", [k("BkT"), k("Barm")], ["pb%d" % b1], fb)
            yield
            K.op("dve", ["pb%d" % b0, "mA"], [k("BMM1") + "q%d" % q], lambda e, q=q: e.tensor_tensor(out=MM1[p][:, 2 * q:2 * q + 2, :, :].rearrange("p h a t -> p (h a t)"), in0=PB[b0][:, :], in1=mA[:, d, :, :].rearrange("p a t -> p (a t)"), op=ALU.mult))
            yield
            K.ops("pe", [k("BbT"), k("Barm")], ["pb%d" % b0], fc)
            K.op("dve", ["pb%d" % b1, "mA"], [k("BMM2") + "q%d" % q], lambda e, q=q: e.tensor_tensor(out=MM2[p][:, 2 * q:2 * q + 2, :, :].rearrange("p h a t -> p (h a t)"), in0=PB[b1][:, :], in1=mA[:, d, :, :].rearrange("p a t -> p (a t)"), op=ALU.mult))
            yield
            K.op("dve", ["pb%d" % b0, "mT"], [k("BQ0T") + "q%d" % q], lambda e, q=q: e.tensor_tensor(out=Q0T[p][:, 2 * q:2 * q + 2, :].rearrange("p h t -> p (h t)"), in0=PB[b0][:, 0:256], in1=mT[:, d, :, :].rearrange("p a t -> p (a t)"), op=ALU.mult))
            yield
        kQR = lambda a, q: "BQR%s_%dq%d" % ("ab"[a], p, q)
        kQT = lambda a, q: "BQT%s_%dq%d" % ("ab"[a], p, q)
        XA = PB[b0]
        XB = PB[b1]
        for q in range(4):
            hs = [2 * q, 2 * q + 1]
            fa = []
            for hh, h in enumerate(hs):
                fa.append(lambda e, hh=hh, h=h: e.matmul(XA[:, hh * 256:hh * 256 + 128], lhsT=Q0T[p][:, h, :], rhs=MM1[p][:, h, 0, :], start=True, stop=True))
                fa.append(lambda e, hh=hh, h=h: e.matmul(XA[:, hh * 256 + 128:hh * 256 + 256], lhsT=ident, rhs=MM1[p][:, h, 0, :], start=False, stop=False, skip_group_check=True))
                fa.append(lambda e, hh=hh, h=h: e.matmul(XA[:, hh * 256 + 128:hh * 256 + 256], lhsT=ident, rhs=ident, start=False, stop=True, skip_group_check=True))
            K.ops("pe", [k("BMM1") + "q%d" % q, k("BQ0T") + "q%d" % q, "cst"], ["pb%d" % b0], fa)
            K.ops("pe", [k("BMM1") + "q%d" % q, k("BQ0T") + "q%d" % q], ["pb%d" % b1], [lambda e, hh=hh, h=h: e.matmul(XB[:, hh * 128:(hh + 1) * 128], lhsT=MM1[p][:, h, 0, :], rhs=Q0T[p][:, h, :], start=True, stop=True) for hh, h in enumerate(hs)])
            yield
            K.op("act", ["pb%d" % b0], [kQR(1, q)], lambda e, q=q: e.copy(out=QR[1][p][:, 2 * q:2 * q + 2, :, :].rearrange("p h a t -> p (h a t)"), in_=XA[:, :]), c=0.5)
            yield
            K.op("dve", ["pb%d" % b1], [kQT(1, q)], lambda e, q=q: e.tensor_copy(out=QT[1][p][:, 2 * q:2 * q + 2, :].rearrange("p h t -> p (h t)"), in_=XB[:, 0:256]), c=0.42)
            yield
        for lev in range(1, 6):
            cur, nxt = lev % 2, (lev + 1) % 2
            for q in range(4):
                hs = [2 * q, 2 * q + 1]
                fa = []
                for hh, h in enumerate(hs):
                    fa.append(lambda e, hh=hh, h=h: e.matmul(XA[:, hh * 256:(hh + 1) * 256], lhsT=QT[cur][p][:, h, :], rhs=QR[cur][p][:, h, :, :].rearrange("p a t -> p (a t)"), start=True, stop=False))
                    fa.append(lambda e, hh=hh, h=h: e.matmul(XA[:, hh * 256 + 128:(hh + 1) * 256], lhsT=ident, rhs=QR[cur][p][:, h, 1, :], start=False, stop=True))
                K.ops("pe", [kQR(cur, q), kQT(cur, q), "cst"], ["pb%d" % b0], fa)
                K.ops("pe", [kQR(cur, q), kQT(cur, q)], ["pb%d" % b1], [lambda e, hh=hh, h=h: e.matmul(XB[:, hh * 128:(hh + 1) * 128], lhsT=QR[cur][p][:, h, 0, :], rhs=QT[cur][p][:, h, :], start=True, stop=True) for hh, h in enumerate(hs)])
                yield
                K.op("act", ["pb%d" % b0], [kQR(nxt, q)], lambda e, q=q: e.copy(out=QR[nxt][p][:, 2 * q:2 * q + 2, :, :].rearrange("p h a t -> p (h a t)"), in_=XA[:, :]), c=0.5)
                yield
                K.op("dve", ["pb%d" % b1], [kQT(nxt, q)], lambda e, q=q: e.tensor_copy(out=QT[nxt][p][:, 2 * q:2 * q + 2, :].rearrange("p h t -> p (h t)"), in_=XB[:, 0:256]), c=0.42)
                yield
        for q in range(4):
            hs = [2 * q, 2 * q + 1]
            ff = []
            bq = b0 if q % 2 == 0 else b1
            for hh, h in enumerate(hs):
                ff.append(lambda e, hh=hh, h=h, bq=bq: e.matmul(PB[bq][:, hh * 128:(hh + 1) * 128], lhsT=QT[0][p][:, h, :], rhs=QR[0][p][:, h, 1, :], start=True, stop=False))
                ff.append(lambda e, hh=hh, h=h, bq=bq: e.matmul(PB[bq][:, hh * 128:(hh + 1) * 128], lhsT=ident, rhs=QR[0][p][:, h, 1, :], start=False, stop=True))
            K.ops("pe", [kQR(0, q), kQT(0, q), "cst"], ["pb%d" % bq], ff)
            yield
            if q % 2 == 0:
                K.op("act", ["pb%d" % bq], [k("BTT") + "q%d" % q], lambda e, q=q, bq=bq: e.copy(out=TT[p][:, 2 * q:2 * q + 2, :].rearrange("p h t -> p (h t)"), in_=PB[bq][:, 0:256]), c=0.4)
            else:
                K.op("dve", ["pb%d" % bq], [k("BTT") + "q%d" % q], lambda e, q=q, bq=bq: e.tensor_copy(out=TT[p][:, 2 * q:2 * q + 2, :].rearrange("p h t -> p (h t)"), in_=PB[bq][:, 0:256]), c=0.42)
            yield
        while chain_done.get(ckey, 0) < cidx:
            yield
        fz = []
        for h in range(8):
            q, hp = h // 2, h % 2
            fz.append(lambda e, h=h, q=q, hp=hp: e.matmul(PB[b0][:, h * 64:(h + 1) * 64], lhsT=arm[p][:, q, hp, 0, :], rhs=Sbin[:, q, :], start=True, stop=False))
            fz.append(lambda e, h=h: e.matmul(PB[b0][:, h * 64:(h + 1) * 64], lhsT=MM2[p][:, h, 0, :], rhs=Vt[p][:, h * 64:(h + 1) * 64], start=False, stop=True))
        K.ops("pe", [k("Barm"), kSbin, k("Bvt")] + [k("BMM2") + "q%d" % q for q in range(4)], ["pb%d" % b0], fz)
        yield
        K.op("act", ["pb%d" % b0], [k("BZs")], lambda e: e.copy(out=Zs[p][:], in_=PB[b0][:, :]))
        yield
        K.ops("pe", [k("BTT") + "q%d" % q for q in range(4)] + [k("BZs")], ["pb%d" % b1], [lambda e, h=h: e.matmul(PB[b1][:, h * 64:(h + 1) * 64], lhsT=TT[p][:, h, :], rhs=Zs[p][:, h * 64:(h + 1) * 64], start=True, stop=True) for h in range(8)])
        yield
        K.op("dve", ["pb%d" % b1], [k("BUs")], lambda e: e.tensor_copy(out=Us[p][:], in_=PB[b1][:, :]))
        yield
        fd = []
        for h in range(8):
            q = h // 2
            fd.append(lambda e, h=h, q=q: e.matmul(PB[b2][:, h * 64:(h + 1) * 64], lhsT=Bt[p][:, q * 128:(q + 1) * 128], rhs=Us[p][:, h * 64:(h + 1) * 64], start=True, stop=False))
            fd.append(lambda e, h=h, q=q: e.matmul(PB[b2][:, h * 64:(h + 1) * 64], lhsT=Kt[p][:, q * 128:(q + 1) * 128], rhs=Vt[p][:, h * 64:(h + 1) * 64], start=False, stop=True))
        K.ops("pe", [k("BBt"), k("BKt"), k("BUs"), k("Bvt")], ["pb%d" % b2], fd)
        yield
        Dv = PB[b2][:, :].rearrange("p (q hp j) -> p q hp j", hp=2, j=64)
        for hp in range(2):
            rs = slice(hp * 64, (hp + 1) * 64)
            K.op("dve", ["pb%d" % b2, kSin], [kt1], lambda e, rs=rs, hp=hp: e.tensor_tensor(out=t1[rs, :, :], in0=Dv[rs, :, hp, :], in1=Sin[rs, :, :], op=ALU.add))
            yield
        gb = gC[p][:, :].unsqueeze(2).to_broadcast([128, 4, 64])
        K.op("dve", [kt1, k("BgC")], [kSout], lambda e: e.tensor_tensor(out=Sout[:], in0=t1[:], in1=gb, op=ALU.mult))
        yield
        K.op("pool", [kt1, k("BgC")], [kSbout], lambda e: e.tensor_tensor(out=Sbout[:], in0=t1[:], in1=gb, op=ALU.mult))
        yield
        fy = []
        for h in range(8):
            q, hp = h // 2, h % 2
            fy.append(lambda e, h=h, q=q, hp=hp: e.matmul(PB[b0][:, h * 64:(h + 1) * 64], lhsT=arm[p][:, q, hp, 1, :], rhs=Sbin[:, q, :], start=True, stop=False))
            fy.append(lambda e, h=h: e.matmul(PB[b0][:, h * 64:(h + 1) * 64], lhsT=MM1[p][:, h, 1, :], rhs=Us[p][:, h * 64:(h + 1) * 64], start=False, stop=False))
            fy.append(lambda e, h=h: e.matmul(PB[b0][:, h * 64:(h + 1) * 64], lhsT=MM2[p][:, h, 1, :], rhs=Vt[p][:, h * 64:(h + 1) * 64], start=False, stop=True))
        K.ops("pe", [k("Barm"), kSbin, k("BUs"), k("Bvt")] + [k("BMM1") + "q%d" % q for q in range(4)] + [k("BMM2") + "q%d" % q for q in range(4)], ["pb%d" % b0], fy)
        yield
        chain_done[ckey] = cidx + 1
        K.op("act", ["pb%d" % b0], [k("Bysb")], lambda e: e.copy(out=ysb[p][:], in_=PB[b0][:, :]))
        yield
        K.dma("act", k("Bysb"), [k("Bysb")], [], R["st_y"][d, gt], ysb[p][:])
        yield
        if last_out_ap is not None:
            K.dma("sp", kSout, [kSout], [], last_out_ap, Sout[:])
            yield

    queue = []
    for si, (xap, NT, cv, gt_base, _y) in enumerate(seqs):
        sp = si % 2
        for step in range(NT):
            sidx = step % 2
            last = (step == NT - 1) and si > 0
            queue.append((si, step, gt_base + step, 0, sp, sidx, R["nst"][si - 1, 0] if last else None, (si, 0), step))
            queue.append((si, step, gt_base + NT - 1 - step, 1, sp, sidx, R["nst"][si - 1, 1] if last else None, (si, 1), step))

    def init_state(si):
        sp = si % 2
        for d in range(2):
            if si == 0:
                K.dma("sp", "BS_%d_%d_0" % (sp, d), [], ["BS_%d_%d_0" % (sp, d)], S[sp][d][0][:], R["s0T"][d])
                K.op("pool", ["BS_%d_%d_0" % (sp, d)], ["BSb_%d_%d_0" % (sp, d)], lambda e, d=d: e.tensor_copy(out=Sb[sp][d][0][:], in_=S[sp][d][0][:]))
            else:
                K.op("pool", [], ["BS_%d_%d_0" % (sp, d)], lambda e, d=d: e.memset(S[sp][d][0][:].rearrange("p a b -> p (a b)"), 0.0))
                K.op("pool", [], ["BSb_%d_%d_0" % (sp, d)], lambda e, d=d: e.memset(Sb[sp][d][0][:].rearrange("p a b -> p (a b)"), 0.0))

    def before(item):
        si, step, gt, d, sp, sidx, lo = item[:7]
        if step == 0 and d == 0:
            init_state(si)

    K.run_streams(queue, lambda slot, it: visit(slot, it[2], it[3], it[4], it[5], it[6], it[7], it[8]), WB, before, offset=VOFF)


def phase_c(nc, K, esC, sb, PB, PT, seqs, R):
    cst, w_out = R["cst"], R["w_out"]
    ident = cst[:, 0, :]
    rows_t = sb("Crows", [128, 3072], F32, esC)
    Gt = sb("CGt", [128, 2, 1024], F32, esC)
    K.dma("sp", "rows_t", [], ["rows_t"], rows_t[:], R["rows"])
    K.dma("sp", "Gt", [], ["Gt"], Gt[:], R["st_gt"])
    wout = sb("Cwout", [128, 8, D], BF16, esC)
    wst = [sb("Cwst%d" % i, [128, D], F32, esC) for i in range(2)]
    def wdma(kc):
        K.dma("act", "Cwst%d" % (kc % 2), [], ["Cwst%d" % (kc % 2)], wst[kc % 2][:], w_out[kc * 128:(kc + 1) * 128, :])
    wdma(0)
    wdma(1)
    for kc in range(8):
        K.op("dve", ["Cwst%d" % (kc % 2)], ["Cwout"], lambda e, kc=kc: e.tensor_copy(out=wout[:, kc, :], in_=wst[kc % 2][:]))
        if kc + 2 < 8:
            wdma(kc + 2)

    WC = 5
    ctr = [0]

    def T(name, shape, dt):
        return [sb("%s_%d" % (name, p), shape, dt, esC) for p in range(WC)]
    yf = T("Cyf", [128, 8, 64], F32)
    yb = T("Cyb", [128, 8, 64], F32)
    bcf = T("Cbcf", [128, 8], F32)
    bcb = T("Cbcb", [128, 8], F32)
    Vt = T("Cvt", [128, 8, 64], BF16)
    sza = T("Csza", [128, 512], BF16)
    mixed = T("Cmixed", [128, D], BF16)
    xt = T("Cxt", [128, D], F32)
    junk = sb("Cjunk", [128, 512], F32, esC)
    ysq = T("Cysq", [128, 8, 64], F32)
    st = T("Cst", [128, 40], F32)
    bon = T("Cbon", [128, 8, 64], F32)
    mixT = T("CmixT", [128, 8, 128], BF16)
    ot = T("Cot", [128, D], F32)

    def tile(p, xap, a, cv, gt, yap):
        if True:
            n0 = 2 * (ctr[0] % 3)
            ctr[0] += 1
            k = lambda n: "%s_%d" % (n, p)
            K.dma("sp", k("Cyf"), [], [k("Cyf")], yf[p][:].rearrange("p h j -> p (h j)"), R["st_y"][0, gt])
            yield
            K.dma("sp", k("Cyb"), [], [k("Cyb")], yb[p][:].rearrange("p h j -> p (h j)"), R["st_y"][1, gt])
            yield
            K.dma("sp", k("Cbcf"), [], [k("Cbcf")], bcf[p][:], R["st_bc"][0, gt])
            yield
            K.dma("sp", k("Cbcb"), [], [k("Cbcb")], bcb[p][:], R["st_bc"][1, gt])
            yield
            K.dma("sp", k("Cvt"), [], [k("Cvt")], Vt[p][:].rearrange("p h j -> p (h j)"), R["st_vt"][gt])
            yield
            K.dma("sp", k("Csza"), [], [k("Csza")], sza[p][:], R["st_sza"][gt])
            yield
            K.dma("sp", k("Cmixed") + "b", [], [k("Cmixed") + "b"], mixed[p][:, 512:1024], R["st_mixb"][gt])
            yield
            K.dma("sp", k("Cxt"), [], [k("Cxt")], xt[p][:], xap[a * 128:(a + 1) * 128, :])
            yield
            K.op("pool", [k("Cyf"), k("Cyb")], [k("Cyf")], lambda e: e.tensor_tensor(out=yf[p][:], in0=yf[p][:], in1=yb[p][:], op=ALU.add))
            yield
            s_ = st[p]
            K.op("dve", [k("Cyf")], [k("Cst") + "a"], lambda e: e.tensor_reduce(out=s_[:, 0:8], in_=yf[p][:], axis=AX.X, op=ALU.add))
            yield
            K.op("pool", [k("Cyf")], [k("Cysq")], lambda e: e.tensor_tensor(out=ysq[p][:], in0=yf[p][:], in1=yf[p][:], op=ALU.mult))
            yield
            K.op("dve", [k("Cysq")], [k("Cst") + "b"], lambda e: e.tensor_reduce(out=s_[:, 8:16], in_=ysq[p][:], axis=AX.X, op=ALU.add))
            yield
            sk = [k("Cst") + "a", k("Cst") + "b"]
            kc_ = k("Cst") + "c"
            K.op("dve", sk, [kc_], lambda e: e.tensor_scalar(out=s_[:, 16:24], in0=s_[:, 0:8], scalar1=1.0 / 64, scalar2=None, op0=ALU.mult))
            yield
            K.op("dve", [kc_], [kc_], lambda e: e.tensor_tensor(out=s_[:, 24:32], in0=s_[:, 16:24], in1=s_[:, 16:24], op=ALU.mult))
            yield
            K.op("dve", sk + [kc_], [kc_], lambda e: e.scalar_tensor_tensor(out=s_[:, 32:40], in0=s_[:, 8:16], scalar=1.0 / 64, in1=s_[:, 24:32], op0=ALU.mult, op1=ALU.subtract))
            yield
            K.op("dve", [kc_], [kc_], lambda e: e.tensor_scalar(out=s_[:, 32:40], in0=s_[:, 32:40], scalar1=GN_EPS, scalar2=None, op0=ALU.add))
            yield
            K.op("act", [kc_], [kc_], lambda e: e.sqrt(out=s_[:, 32:40], in_=s_[:, 32:40]))
            yield
            K.op("dve", [kc_], [kc_], lambda e: e.reciprocal(out=s_[:, 32:40], in_=s_[:, 32:40]))
            yield
            b8 = lambda ap: ap.unsqueeze(2).to_broadcast([128, 8, 64])
            K.op("dve", [k("Cyf"), kc_], [k("Cyf")], lambda e: e.tensor_tensor(out=yf[p][:], in0=yf[p][:], in1=b8(s_[:, 16:24]), op=ALU.subtract))
            yield
            K.op("dve", [k("Cyf"), kc_], [k("Cyf")], lambda e: e.tensor_tensor(out=yf[p][:], in0=yf[p][:], in1=b8(s_[:, 32:40]), op=ALU.mult))
            yield
            yfl = yf[p][:].rearrange("p h j -> p (h j)")
            K.op("pool", [k("Cyf"), "rows_t"], [k("Cyf")], lambda e: e.tensor_tensor(out=yfl, in0=yfl, in1=rows_t[:, 1024:1536], op=ALU.mult))
            yield
            K.op("pool", [k("Cyf"), "rows_t"], [k("Cyf")], lambda e: e.tensor_tensor(out=yfl, in0=yfl, in1=rows_t[:, 1536:2048], op=ALU.add))
            yield
            K.op("dve", [k("Cbcf"), k("Cbcb")], [k("Cbcf")], lambda e: e.tensor_tensor(out=bcf[p][:], in0=bcf[p][:], in1=bcb[p][:], op=ALU.add))
            yield
            K.op("dve", [k("Cvt"), k("Cbcf")], [k("Cbon")], lambda e: e.tensor_tensor(out=bon[p][:], in0=Vt[p][:], in1=b8(bcf[p][:, :]), op=ALU.mult))
            yield
            K.op("dve", [k("Cyf"), k("Cbon")], [k("Cyf")], lambda e: e.tensor_tensor(out=yf[p][:], in0=yf[p][:], in1=bon[p][:], op=ALU.add))
            yield
            K.op("dve", [k("Cyf"), k("Csza")], [k("Cmixed") + "a"], lambda e: e.tensor_tensor(out=mixed[p][:, 0:512], in0=yfl, in1=sza[p][:], op=ALU.mult))
            yield
            K.ops("pe", [k("Cmixed") + "a", k("Cmixed") + "b", "cst"], ["ptb"], [lambda e, kc=kc: e.transpose(out=PT[:, kc * 128:(kc + 1) * 128], in_=mixed[p][:, kc * 128:(kc + 1) * 128], identity=ident) for kc in range(8)])
            K.op("act", ["ptb"], [k("CmixT")], lambda e: e.copy(out=mixT[p][:].rearrange("p a b -> p (a b)"), in_=PT[:, :]))
            yield
            for n in range(2):
                K.ops("pe", [k("CmixT"), "Cwout"], ["pb%d" % (n0 + n)], [lambda e, kc=kc, n=n: e.matmul(PB[n0 + n][:, :], lhsT=mixT[p][:, kc, :], rhs=wout[:, kc, n * 512:(n + 1) * 512], start=(kc == 0), stop=(kc == 7)) for kc in range(8)])
            kr = k("Cst") + "r"
            for n in range(2):
                K.op("act", ["pb%d" % (n0 + n)], ["Cjunk", kr + "%d" % n], lambda e, n=n: e.activation(out=junk[:, :], in_=PB[n0 + n][:, :], func=AF.Square, accum_out=s_[:, 2 + n:3 + n] if False else s_[:, 0 + n:1 + n]))
            K.op("dve", [kr + "0", kr + "1"] + sk + [kc_], [kr], lambda e: e.tensor_tensor(out=s_[:, 2:3], in0=s_[:, 0:1], in1=s_[:, 1:2], op=ALU.add))
            K.op("dve", [kr], [kr], lambda e: e.tensor_scalar(out=s_[:, 2:3], in0=s_[:, 2:3], scalar1=1.0 / D, scalar2=NORM_EPS, op0=ALU.mult, op1=ALU.add))
            K.op("act", [kr], [kr], lambda e: e.sqrt(out=s_[:, 2:3], in_=s_[:, 2:3]))
            K.op("dve", [kr], [kr], lambda e: e.reciprocal(out=s_[:, 2:3], in_=s_[:, 2:3]))
            for n in range(2):
                K.op("dve", ["pb%d" % (n0 + n), kr, "Gt"], [k("Cot")], lambda e, n=n: e.scalar_tensor_tensor(out=ot[p][:, n * 512:(n + 1) * 512], in0=PB[n0 + n][:, :], scalar=s_[:, 2:3], in1=Gt[:, cv, n * 512:(n + 1) * 512], op0=ALU.mult, op1=ALU.mult))
            K.op("pool", [k("Cot"), k("Cxt")], [k("Cot")], lambda e: e.tensor_tensor(out=ot[p][:], in0=ot[p][:], in1=xt[p][:], op=ALU.add))
            yield
            K.dma("sp", k("Cot"), [k("Cot")], [], yap[a * 128:(a + 1) * 128, :], ot[p][:])
            yield


    queue = []
    for si, (xap, NT, cv, gt_base, yap) in enumerate(seqs):
        for a in range(NT):
            queue.append((xap, a, cv, gt_base + a, yap))
    K.run_streams(queue, lambda slot, it: tile(slot, *it), WC)


def _prep_shared(inp):
    f = np.float32
    g = lambda k: np.asarray(inp[k], dtype=f)
    fm = lambda v, nb: np.ascontiguousarray(v.reshape(nb, 128).T)
    pf = np.zeros((128, 80), f)
    b_mod = g("b_mod")[0]
    pf[:, 0:16] = fm(b_mod[0:2048], 16)
    pf[:, 16:24] = fm(g("ln_pre")[0], 8)
    ts = g("ts_mu")[0]
    pf[:, 24:36] = fm(ts[0], 12)
    pf[:, 36:48] = fm(ts[1], 12)
    for d in range(2):
        pf[:, 48 + 4 * d:52 + 4 * d] = fm(g("w0")[0, d], 4)
        pf[:, 56 + 4 * d:60 + 4 * d] = fm(g("a0")[0, d], 4)
        pf[:, 64 + 4 * d:68 + 4 * d] = fm(g("k_k")[0, d], 4)
        pf[:, 72 + 4 * d:76 + 4 * d] = fm(g("k_a")[0, d], 4)
    rows = np.zeros((128, 3072), f)
    rows[:, 0:1024] = g("ln_post")[0][None, :]
    rows[:, 1024:1536] = g("gn_w")[0][None, :]
    rows[:, 1536:2048] = g("gn_b")[0][None, :]
    rows[:, 2048:2560] = g("sgu_ln_g")[0][None, :]
    rows[:, 2560:3072] = g("sgu_ln_b")[0][None, :]
    bmg = np.ascontiguousarray(np.broadcast_to(b_mod[2048:3072][None, :], (2, 1024))).astype(f)
    lora_w = np.zeros((128, 4, 512), f)
    for d in range(2):
        lora_w[d * 64:(d + 1) * 64, d, :] = g("w_up")[0, d]
        lora_w[d * 64:(d + 1) * 64, 2 + d, :] = g("a_up")[0, d]
    rk = g("r_k")[0]
    rk_in = np.zeros((128, 2, 4, 2), f)
    for d in range(2):
        for q in range(4):
            for hp in range(2):
                rk_in[hp * 64:(hp + 1) * 64, d, q, hp] = rk[d, 2 * q + hp]
    rk_in = rk_in.reshape(128, 16)
    wsT = np.ascontiguousarray(np.transpose(g("w_s")[0], (2, 0, 1)))
    bs = np.ascontiguousarray(g("b_s")[0].T)
    s_idx = np.arange(128)[:, None]
    t_idx = np.arange(128)[None, :]
    consts = np.zeros((128, 6, 128), f)
    consts[:, 0] = (s_idx == t_idx)
    consts[:, 1] = (s_idx < t_idx)
    consts[:, 2] = (s_idx <= t_idx)
    consts[:, 3] = (s_idx > t_idx)
    consts[:, 4] = (s_idx >= t_idx)
    consts[:, 5] = ((s_idx // 64) == (t_idx // 64))
    sel = np.zeros((2, 2, 128), f)
    sel[0, 0, :] = 1.0
    sel[1, 1, :] = 1.0
    return dict(w_in=np.ascontiguousarray(g("w_in")[0]), w_out=np.ascontiguousarray(g("w_out")[0]),
                w_mod=np.ascontiguousarray(g("w_mod")[0]), pf=pf, rows=rows, bmg=bmg, lora_w=lora_w,
                rk_in=rk_in, wsT=wsT, bs=bs, consts=consts, sel=sel)


def _prep_core(inp, shared, i, NT_S, NP):
    f = np.float32
    m = dict(shared)
    m["xs"] = np.ascontiguousarray(np.asarray(inp["x_sample"][i], f)[:NT_S * 128])
    m["xp"] = np.ascontiguousarray(np.asarray(inp["x_prompt"][NP * i:NP * (i + 1)], f))
    c = np.asarray(inp["c"][i], f)
    cc = np.asarray(inp["c_ctx"], f)
    cvT = np.zeros((128, 8, 2), f)
    cvT[:, :, 0] = c.reshape(8, 128).T
    cvT[:, :, 1] = cc.reshape(8, 128).T
    m["cvT"] = cvT
    s0T = np.zeros((2, 128, 4, 64), f)
    for d, key in enumerate(["state_fwd", "state_bwd"]):
        S = np.asarray(inp[key][i, 0], f)
        S4 = S.reshape(4, 2, 64, 64)
        s0T[d] = np.transpose(S4, (1, 3, 0, 2)).reshape(128, 4, 64)
    m["s0T"] = s0T
    return m


_CACHE = {}


def kernel(**inputs):
    NT_S, NP, NCORE = 32, 4, 8
    if "nc" not in _CACHE:
        _CACHE["nc"] = build(NT_S, NP)
    nc = _CACHE["nc"]
    shared = _prep_shared(inputs)
    in_maps = [_prep_core(inputs, shared, i, NT_S, NP) for i in range(NCORE)]
    res = run_bass_kernel_spmd(nc, in_maps, core_ids=list(range(NCORE)))
    y_sample = np.stack([np.asarray(r["ys"]) for r in res.results], 0).astype(np.float32)
    y_prompt = np.concatenate([np.asarray(r["yp"]) for r in res.results], 0).astype(np.float32)
    nf, nb = [], []
    for r in res.results:
        nst = np.asarray(r["nst"])
        for p in range(NP):
            for d, lst in ((0, nf), (1, nb)):
                S = nst[p, d].reshape(2, 64, 4, 64)
                lst.append(np.transpose(S, (2, 0, 3, 1)).reshape(8, 64, 64)[None])
    new_f = np.stack(nf, 0).astype(np.float32)
    new_b = np.stack(nb, 0).astype(np.float32)
    return (y_prompt, y_sample, new_f, new_b)
```

```python
import contextlib
import numpy as np
import concourse.bass as bass
import concourse.mybir as mybir
from concourse.bass_utils import run_bass_kernel_spmd

F32 = mybir.dt.float32
BF16 = mybir.dt.bfloat16
AF = mybir.ActivationFunctionType
ALU = mybir.AluOpType
AX = mybir.AxisListType

D = 1024
DIN = 3840
NORM_EPS = 1e-6
GN_EPS = 6.4e-4
L2_EPS = 1e-12
EXPM05 = float(np.exp(-0.5))
C2H = 0.5 * EXPM05
VOFF = 15.0


class Sched:
    def __init__(self, nc, es):
        self.nc = nc
        self.es = es
        self.engs = dict(pe=nc.tensor, dve=nc.vector, act=nc.scalar, pool=nc.gpsimd, sp=nc.sync)
        self.sems = {e: es.enter_context(nc.semaphore("sem_" + e)) for e in self.engs}
        self.cnt = {e: 0 for e in self.engs}
        self.lastw = {}
        self.readers = {}
        self.seen = {e: {} for e in self.engs}
        self.chans = {}
        self.semobj = {}
        self.clock = {e: 0.0 for e in self.engs}
        self.ttime = {}
        self.lastfin = 0.0
        self.cost = dict(pe=0.09, dve=0.55, act=0.45, pool=1.1, sp=0.1)

    def _time(self, eng, needs, tok, cost):
        ready = 0.0
        for sname, val in needs.items():
            t = self.ttime.get((sname, val), 0.0)
            if t > ready:
                ready = t
        start = max(self.clock[eng], ready)
        fin = start + cost
        self.clock[eng] = fin if tok[0].startswith("sem_") else start + 0.1
        self.ttime[tok] = fin
        if fin > self.lastfin:
            self.lastfin = fin

    def _need(self, eng, needs):
        for sname, val in needs.items():
            if self.seen[eng].get(sname, 0) >= val:
                continue
            self.engs[eng].wait_ge(self.semobj[sname], val)
            self.seen[eng][sname] = val

    def _collect(self, eng, reads, writes):
        needs = {}

        def add(tok):
            if tok is None:
                return
            s, v = tok
            if s == "sem_pe" and eng == "pe":
                return
            if needs.get(s, 0) < v:
                needs[s] = v

        for b in reads:
            add(self.lastw.get(b))
        for b in writes:
            add(self.lastw.get(b))
            for s, v in self.readers.get(b, {}).items():
                add((s, v))
        return needs

    def _record(self, tok, reads, writes):
        s, v = tok
        for b in reads:
            self.readers.setdefault(b, {})[s] = v
        for b in writes:
            self.lastw[b] = tok
            self.readers[b] = {}

    def op(self, eng, reads, writes, fn, c=None):
        needs = self._collect(eng, reads, writes)
        self._need(eng, needs)
        inst = fn(self.engs[eng])
        self.cnt[eng] += 1
        sname = "sem_" + eng
        self.semobj[sname] = self.sems[eng]
        inst.then_inc(self.sems[eng], 1)
        self._time(eng, needs, (sname, self.cnt[eng]), self.cost[eng] if c is None else c)
        self._record((sname, self.cnt[eng]), reads, writes)

    def ops(self, eng, reads, writes, fns):
        needs = self._collect(eng, reads, writes)
        self._need(eng, needs)
        inst = None
        for fn in fns:
            inst = fn(self.engs[eng])
        self.cnt[eng] += 1
        sname = "sem_" + eng
        self.semobj[sname] = self.sems[eng]
        inst.then_inc(self.sems[eng], 1)
        self._time(eng, needs, (sname, self.cnt[eng]), self.cost[eng] * len(fns))
        self._record((sname, self.cnt[eng]), reads, writes)

    def dma(self, eng, chan_key, reads, writes, out, in_):
        if chan_key not in self.chans:
            sem = self.es.enter_context(self.nc.semaphore("dch_%d" % len(self.chans)))
            self.chans[chan_key] = [sem, 0, "dch_%d" % len(self.chans)]
            self.semobj[self.chans[chan_key][2]] = sem
        ch = self.chans[chan_key]
        needs = self._collect(eng, reads, writes)
        self._need(eng, needs)
        self.engs[eng].dma_start(out=out, in_=in_).then_inc(ch[0], 16)
        ch[1] += 16
        self._time(eng, needs, (ch[2], ch[1]), 2.5)
        self._record((ch[2], ch[1]), reads, writes)

    def run_streams(self, queue, make_gen, W, before=None, compat=None, offset=0.0):
        active, free, qi = [], list(range(W)), 0
        while qi < len(queue) or active:
            while free and qi < len(queue):
                if compat is not None and not compat(queue[qi], [a[3] for a in active]):
                    break
                if before is not None:
                    before(queue[qi])
                slot = free.pop(0)
                vt0 = min([a[2] for a in active]) if active else min(self.clock.values())
                if qi < W:
                    vt0 += qi * offset
                active.append([slot, make_gen(slot, queue[qi]), vt0, queue[qi]])
                qi += 1
            item = min(active, key=lambda a: a[2])
            self.lastfin = 0.0
            try:
                next(item[1])
                if self.lastfin > 0.0:
                    item[2] = self.lastfin
                else:
                    item[2] += 0.5
            except StopIteration:
                active.remove(item)
                free.append(item[0])

    def barrier(self, engines=("pe", "dve", "act", "pool", "sp")):
        needs = {}
        for e in self.engs:
            if self.cnt[e] > 0:
                needs["sem_" + e] = self.cnt[e]
        for ch in self.chans.values():
            if ch[1] > 0:
                needs[ch[2]] = ch[1]
        for e in engines:
            n2 = {s: v for s, v in needs.items() if not (e == "pe" and s == "sem_pe")}
            self._need(e, n2)


def build(NT_S, NP, debug=False):
    nc = bass.Bass("TRN2", target_bir_lowering=False)
    LS = NT_S * 128
    NTILES = NT_S + 2 * NP
    okind = "ExternalOutput" if debug else "Internal"

    def din(name, shape, dt=F32):
        return nc.dram_tensor(name, list(shape), dt, kind="ExternalInput").ap()

    def dout(name, shape, dt=F32):
        return nc.dram_tensor(name, list(shape), dt, kind="ExternalOutput").ap()

    def dscr(name, shape, dt):
        if debug:
            return nc.dram_tensor(name, list(shape), dt, kind="ExternalOutput").ap()
        return nc.dram_tensor(name, list(shape), dt).ap()

    xs = din("xs", [LS, D])
    xp = din("xp", [NP, 256, D])
    cvT = din("cvT", [128, 8, 2])
    w_in = din("w_in", [D, DIN])
    w_out = din("w_out", [D, D])
    w_mod = din("w_mod", [D, 3 * D])
    pf = din("pf", [128, 80])
    rows = din("rows", [128, 3072])
    bmg = din("bmg", [2, 1024])
    lora_w = din("lora_w", [128, 4, 512])
    rk_in = din("rk_in", [128, 16])
    wsT_in = din("wsT", [128, 8, 128])
    bs_in = din("bs", [128, 8])
    consts = din("consts", [128, 6, 128])
    sel_in = din("sel", [2, 2, 128])
    s0T = din("s0T", [2, 128, 4, 64])

    ys = dout("ys", [LS, D])
    yp = dout("yp", [NP, 256, D])
    nst = dout("nst", [NP, 2, 128, 4, 64])

    st_rkv = dscr("st_rkv", [NTILES, 128, 12, 128], BF16)
    st_low = dscr("st_low", [NTILES, 128, 2, 128], BF16)
    st_vt = dscr("st_vt", [NTILES, 128, 512], BF16)
    st_sza = dscr("st_sza", [NTILES, 128, 512], BF16)
    st_mixb = dscr("st_mixb", [NTILES, 128, 512], BF16)
    st_y = dscr("st_y", [2, NTILES, 128, 512], F32)
    st_bc = dscr("st_bc", [2, NTILES, 128, 8], F32)
    st_gt = dscr("st_gt", [128, 2, 1024], F32)

    seqs = [(xs, NT_S, 0, 0, ys)]
    for p in range(NP):
        seqs.append((xp[p], 2, 1, NT_S + 2 * p, yp[p]))

    with contextlib.ExitStack() as es:
        K = Sched(nc, es)
        K.debug = debug

        def dump(name, key, ap, shape, dt=F32):
            if not debug:
                return
            dd = nc.dram_tensor("dbg_" + name, list(shape), dt, kind="ExternalOutput").ap()
            K.dma("sp", "dbgch", [key], [], dd, ap)
        K.dump = dump

        def sb(name, shape, dt, stack=es):
            return stack.enter_context(nc.sbuf_tensor(name, list(shape), dt))

        def ps(name, shape, dt, stack=es):
            return stack.enter_context(nc.psum_tensor(name, list(shape), dt))

        PBIG = ps("pbig", [128, 7 * 512], F32)
        PB = [PBIG[:, i * 512:(i + 1) * 512] for i in range(7)]
        PT = ps("ptb", [128, 1024], BF16)

        pf_t = sb("pf_t", [128, 80], F32)
        cst = sb("cst", [128, 6, 128], BF16)
        sel_t = sb("sel_t", [2, 2, 128], F32)
        bs_t = sb("bs_t", [128, 8], F32)
        wsT = sb("wsT_sb", [128, 8, 128], BF16)
        lora = sb("lora", [128, 4, 512], BF16)
        rkm = sb("rkm", [128, 16], BF16)
        gfm = sb("gfm", [128, 8, 2], F32)
        shfm = sb("shfm", [128, 8, 2], F32)
        cs0 = sb("cs0", [128, 12], F32)
        omka = sb("omka", [128, 8], F32)
        mA = sb("mA", [128, 2, 4, 128], BF16)
        mT = sb("mT", [128, 2, 2, 128], BF16)

        K.dma("sp", "pf_t", [], ["pf_t"], pf_t[:], pf)
        K.dma("sp", "sel_t", [], ["sel_t"], sel_t[:], sel_in)
        K.dma("sp", "bs_t", [], ["bs_t"], bs_t[:], bs_in)
        K.op("dve", ["pf_t"], ["cs0"], lambda e: e.tensor_tensor(out=cs0[:], in0=pf_t[:, 24:36], in1=pf_t[:, 36:48], op=ALU.add))
        K.op("dve", ["cs0"], ["cs0"], lambda e: e.tensor_scalar(out=cs0[:], in0=cs0[:], scalar1=-1.0, scalar2=1.0, op0=ALU.mult, op1=ALU.add))
        K.op("dve", ["pf_t"], ["omka"], lambda e: e.tensor_scalar(out=omka[:], in0=pf_t[:, 72:80], scalar1=-1.0, scalar2=1.0, op0=ALU.mult, op1=ALU.add))

        with contextlib.ExitStack() as esA:
            win = sb("win", [128, 8, DIN], BF16, esA)
            rows_t = sb("rows_t", [128, 3072], F32, esA)
            with contextlib.ExitStack() as es0:
                cst_f = sb("cst_f", [128, 6, 128], F32, es0)
                Gt = sb("Gt", [128, 2, 1024], F32, es0)
                K.dma("sp", "rows_t", [], ["rows_t"], rows_t[:], rows)
                K.dma("sp", "cst_f", [], ["cst_f"], cst_f[:], consts)
                K.op("dve", ["cst_f"], ["cst"], lambda e: e.tensor_copy(out=cst[:], in_=cst_f[:]))
                for d, (s_i, i_i, t_i) in enumerate([(1, 2, 3), (3, 4, 1)]):
                    for j in range(4):
                        src = s_i if j % 2 == 0 else i_i
                        K.op("pool", ["cst_f"], ["mA"], lambda e, d=d, j=j, src=src: e.tensor_copy(out=mA[:, d, j, :], in_=cst_f[:, src, :]))
                    for j in range(2):
                        K.op("pool", ["cst_f"], ["mT"], lambda e, d=d, j=j, t_i=t_i: e.tensor_copy(out=mT[:, d, j, :], in_=cst_f[:, t_i, :]))
                wst = [sb("wst%d" % i, [128, DIN], F32, es0) for i in range(2)]
                tmpf = sb("tmpf", [128, 4, 512], F32, es0)
                cv_t = sb("cv_t", [128, 8, 2], F32, es0)
                scv = sb("scv", [128, 8, 2], F32, es0)
                bmg_t = sb("bmg_t", [2, 1024], F32, es0)
                grow = sb("grow", [2, 1024], F32, es0)
                modfm = sb("modfm", [128, 16, 2], F32, es0)
                K.dma("sp", "cv_t", [], ["cv_t"], cv_t[:], cvT)
                K.dma("sp", "bmg_t", [], ["bmg_t"], bmg_t[:], bmg)
                K.op("act", ["cv_t"], ["scv"], lambda e: e.activation(out=scv[:], in_=cv_t[:], func=AF.Silu))
                K.dma("sp", "tmpf", [], ["tmpf"], tmpf[:], lora_w)
                K.op("dve", ["tmpf"], ["lora"], lambda e: e.tensor_copy(out=lora[:], in_=tmpf[:]))
                K.dma("sp", "tmpf", ["tmpf"], ["tmpf"], tmpf[:, 0, 0:16], rk_in)
                K.op("dve", ["tmpf"], ["rkm"], lambda e: e.tensor_copy(out=rkm[:], in_=tmpf[:, 0, 0:16]))
                K.dma("sp", "tmpf", [], ["tmpf"], tmpf[:, 0:2, :].rearrange("p a b -> p (a b)"), wsT_in.rearrange("p g q -> p (g q)"))
                K.op("dve", ["tmpf"], ["wsT"], lambda e: e.tensor_copy(out=wsT[:].rearrange("p g q -> p (g q)"), in_=tmpf[:, 0:2, :].rearrange("p a b -> p (a b)")))
                wsi = [sb("wsi%d" % i, [128, DIN], F32, es0) for i in range(2)]

                def wdma(kc):
                    K.dma("act", "wsi%d" % (kc % 2), [], ["wsi%d" % (kc % 2)], wsi[kc % 2][:, :], w_in[kc * 128:(kc + 1) * 128, :])
                wdma(0)
                wdma(1)
                for kc in range(8):
                    K.op("act", ["wsi%d" % (kc % 2)], ["win"], lambda e, kc=kc: e.copy(out=win[:, kc, :], in_=wsi[kc % 2][:, :]))
                    if kc + 2 < 8:
                        wdma(kc + 2)
                for kc in range(8):
                    wm = wst[kc % 2]
                    K.dma("sp", "wst%d" % (kc % 2), [], ["wst%d" % (kc % 2)], wm[:, 0:3072], w_mod[kc * 128:(kc + 1) * 128, :])
                    fns = []
                    for blk in range(16):
                        fns.append(lambda e, blk=blk, kc=kc, wm=wm: e.matmul(PB[0][:, blk * 2:blk * 2 + 2], lhsT=wm[:, blk * 128:(blk + 1) * 128], rhs=scv[:, kc, :], start=(kc == 0 and blk == 0), stop=(kc == 7), skip_group_check=True))
                    for n in range(2):
                        fns.append(lambda e, n=n, kc=kc, wm=wm: e.matmul(PB[1 + n][0:2, :], lhsT=scv[:, kc, :], rhs=wm[:, 2048 + n * 512:2048 + (n + 1) * 512], start=(kc == 0), stop=(kc == 7)))
                    K.ops("pe", ["wst%d" % (kc % 2), "scv"], ["pb0", "pb1", "pb2"], fns)
                K.op("dve", ["pb0", "pf_t"], ["modfm"], lambda e: e.tensor_tensor(out=modfm[:], in0=PB[0][:, 0:32].rearrange("p (b c) -> p b c", c=2), in1=pf_t[:, 0:16].unsqueeze(2).to_broadcast([128, 16, 2]), op=ALU.add))
                K.op("dve", ["modfm"], ["shfm"], lambda e: e.tensor_copy(out=shfm[:], in_=modfm[:, 0:8, :]))
                K.op("dve", ["modfm"], ["gfm"], lambda e: e.tensor_scalar(out=gfm[:], in0=modfm[:, 8:16, :], scalar1=1.0, scalar2=None, op0=ALU.add))
                K.op("dve", ["gfm", "pf_t"], ["gfm"], lambda e: e.tensor_tensor(out=gfm[:], in0=gfm[:], in1=pf_t[:, 16:24].unsqueeze(2).to_broadcast([128, 8, 2]), op=ALU.mult))
                for n in range(2):
                    K.op("dve", ["pb%d" % (1 + n), "bmg_t"], ["grow"], lambda e, n=n: e.tensor_tensor(out=grow[:, n * 512:(n + 1) * 512], in0=PB[1 + n][0:2, :], in1=bmg_t[:, n * 512:(n + 1) * 512], op=ALU.add))
                for cv in range(2):
                    for n in range(2):
                        K.ops("pe", ["grow", "sel_t"], ["pb%d" % (3 + n)], [lambda e, cv=cv, n=n: e.matmul(PB[3 + n][:, :], lhsT=sel_t[:, cv, :], rhs=grow[:, n * 512:(n + 1) * 512], start=True, stop=True)])
                        K.op("dve", ["pb%d" % (3 + n), "rows_t"], ["Gt"], lambda e, cv=cv, n=n: e.tensor_tensor(out=Gt[:, cv, n * 512:(n + 1) * 512], in0=PB[3 + n][:, :], in1=rows_t[:, n * 512:(n + 1) * 512], op=ALU.mult))
                K.dma("sp", "Gt", ["Gt"], [], st_gt, Gt[:])
                dump("gfm", "gfm", gfm[:], [128, 8, 2])
                dump("shfm", "shfm", shfm[:], [128, 8, 2])
                dump("Gt", "Gt", Gt[:], [128, 2, 1024])
                dump("scv", "scv", scv[:], [128, 8, 2])
                K.barrier()

            phase_a(nc, K, esA, sb, PB, PT, seqs, win, dict(
                pf_t=pf_t, rows_t=rows_t, cst=cst, bs_t=bs_t, wsT=wsT, gfm=gfm, shfm=shfm, cs0=cs0,
                st_rkv=st_rkv, st_low=st_low, st_vt=st_vt, st_sza=st_sza, st_mixb=st_mixb))
            K.barrier()

        with contextlib.ExitStack() as esB:
            phase_b(nc, K, esB, sb, PB, PT, seqs, dict(
                pf_t=pf_t, cst=cst, lora=lora, rkm=rkm, omka=omka, mA=mA, mT=mT,
                st_rkv=st_rkv, st_low=st_low, st_vt=st_vt, st_y=st_y, st_bc=st_bc, s0T=s0T, nst=nst, PBIG=PBIG), NT_S, NP)
            K.barrier()
        with contextlib.ExitStack() as esC:
            phase_c(nc, K, esC, sb, PB, PT, seqs, dict(rows=rows, st_gt=st_gt, cst=cst, w_out=w_out,
                    st_y=st_y, st_bc=st_bc, st_vt=st_vt, st_sza=st_sza, st_mixb=st_mixb))
        K.barrier(engines=("sp",))
    return nc


def phase_a(nc, K, esA, sb, PB, PT, seqs, win, R):
    pf_t, rows_t, cst, bs_t, wsT = R["pf_t"], R["rows_t"], R["cst"], R["bs_t"], R["wsT"]
    gfm, shfm, cs0 = R["gfm"], R["shfm"], R["cs0"]
    ident = cst[:, 0, :]
    NXB = 2
    xt = [sb("xt%d" % i, [128, D], F32, esA) for i in range(NXB)]
    xn = [sb("xn%d" % i, [128, D], BF16, esA) for i in range(4)]
    sq = sb("sqj", [128, D], BF16, esA)
    stat = sb("statA", [128, 16], F32, esA)
    hT = [sb("hT%d" % i, [128, 8, 512], BF16, esA) for i in range(2)]
    raw = [sb("raw%d" % i, [128, 12, 514], BF16, esA) for i in range(2)]
    rkvp = sb("rkvp", [128, 12, 512], BF16, esA)
    tmps = [sb("shtmp%d" % i, [128, 512], F32, esA) for i in range(2)]
    low = [sb("low%d" % i, [128, 2, 512], BF16, esA) for i in range(2)]
    sza = [sb("sza%d" % i, [128, 512], BF16, esA) for i in range(4)]
    u_t = [sb("u_t%d" % i, [128, 512], BF16, esA) for i in range(4)]
    szb = [sb("szb%d" % i, [128, 512], BF16, esA) for i in range(4)]
    vb = [sb("vb%d" % i, [128, 512], F32, esA) for i in range(4)]
    vn = [sb("vn%d" % i, [128, 512], BF16, esA) for i in range(4)]
    mixb = [sb("mixb%d" % i, [128, 512], BF16, esA) for i in range(4)]
    vts = [sb("vts%d" % i, [128, 512], BF16, esA) for i in range(2)]
    lnst = [sb("lnst%d" % i, [128, 8], F32, esA) for i in range(4)]

    cnt = {"x": 0, "tile": 0, "grp": 0}

    def run_lanes(lanes):
        lanes = [[list(g_), W_, []] for g_, W_ in lanes]
        while any(l[0] or l[2] for l in lanes):
            for l in lanes:
                while l[0] and len(l[2]) < l[1]:
                    l[2].append(l[0].pop(0))
                for g_ in list(l[2]):
                    try:
                        next(g_)
                    except StopIteration:
                        l[2].remove(g_)

    def shift_gen(ri, GS, nt_g, gt0):
        rw = raw[ri]
        rk = "raw%d" % ri
        for j in range(12):
            tk = "shtmp%d" % (j % 2)
            tm = tmps[j % 2]
            K.op("dve", [rk, "cs0"], [tk], lambda e, j=j, tm=tm: e.tensor_scalar(out=tm[:, 0:GS], in0=rw[:, j, 1:GS + 1], scalar1=cs0[:, j:j + 1], scalar2=None, op0=ALU.mult))
            K.op("dve", [rk, tk, "pf_t"], [tk], lambda e, j=j, tm=tm: e.scalar_tensor_tensor(out=tm[:, 0:GS], in0=rw[:, j, 0:GS], scalar=pf_t[:, 24 + j:25 + j], in1=tm[:, 0:GS], op0=ALU.mult, op1=ALU.add))
            K.op("dve", [rk, tk, "pf_t"], ["rkvp"], lambda e, j=j, tm=tm: e.scalar_tensor_tensor(out=rkvp[:, j, 0:GS], in0=rw[:, j, 2:GS + 2], scalar=pf_t[:, 36 + j:37 + j], in1=tm[:, 0:GS], op0=ALU.mult, op1=ALU.add))
            yield
        for a in range(nt_g):
            gt = gt0 + a
            K.dma("sp", "rkvp", ["rkvp"], [], R["st_rkv"][gt], rkvp[:, :, a * 128:(a + 1) * 128])
            vs = vts[gt % 2]
            vk = "vts%d" % (gt % 2)
            K.ops("pe", ["rkvp", "cst"], ["ptb"], [lambda e, a=a, q=q: e.transpose(out=PT[:, q * 128:(q + 1) * 128], in_=rkvp[:, 8 + q, a * 128:(a + 1) * 128], identity=ident) for q in range(4)])
            K.op("act", ["ptb"], [vk], lambda e, vs=vs: e.copy(out=vs[:], in_=PT[:, 0:512]))
            K.dma("act", vk, [vk], [], R["st_vt"][gt], vs[:])
            yield

    def front_compute(xap, g, a):
        t0 = (g * (GS_cur[0] // 128) + a) * 128
        xi = cnt["x"] % NXB
        cnt["x"] += 1
        xk = "xt%d" % xi
        K.dma("sp", xk, [], [xk], xt[xi][:], xap[t0:t0 + 128, :])
        sc = cnt["tile"] % 8
        cnt["tile"] += 1
        nk = "xn%d" % a
        K.op("act", [xk], [nk, "statA%d" % sc], lambda e: e.activation(out=xn[a][:], in_=xt[xi][:], func=AF.Square, accum_out=stat[:, sc:sc + 1]))
        K.op("dve", ["statA%d" % sc], ["statA%d" % sc], lambda e: e.tensor_scalar(out=stat[:, sc:sc + 1], in0=stat[:, sc:sc + 1], scalar1=1.0 / D, scalar2=NORM_EPS, op0=ALU.mult, op1=ALU.add))
        K.op("act", ["statA%d" % sc], ["statA%d" % sc], lambda e: e.sqrt(out=stat[:, sc:sc + 1], in_=stat[:, sc:sc + 1]))
        K.op("dve", ["statA%d" % sc], ["statA%d" % sc], lambda e: e.reciprocal(out=stat[:, sc:sc + 1], in_=stat[:, sc:sc + 1]))
        K.op("act", [xk, "statA%d" % sc], [nk], lambda e: e.activation(out=xn[a][:], in_=xt[xi][:], func=AF.Copy, scale=stat[:, sc:sc + 1]))

    def front_transpose(a, cv, hTg, hk):
        nk = "xn%d" % a
        K.ops("pe", [nk, "cst"], ["ptb"], [lambda e, kc=kc: e.transpose(out=PT[:, kc * 128:(kc + 1) * 128], in_=xn[a][:, kc * 128:(kc + 1) * 128], identity=ident) for kc in range(8)])
        for kc in range(8):
            K.op("dve", ["ptb", "gfm", "shfm"], [hk], lambda e, kc=kc: e.tensor_scalar(out=hTg[:, kc, a * 128:(a + 1) * 128], in0=PT[:, kc * 128:(kc + 1) * 128], scalar1=gfm[:, kc, cv:cv + 1], scalar2=shfm[:, kc, cv:cv + 1], op0=ALU.mult, op1=ALU.add))

    GS_cur = [512]

    def tm_gen(s2, hTg, hk, a, gt):
        ls = lnst[s2]
        lsk = "lnst%d" % s2

        bank = lambda ci: 3 + (ci + a) % 4

        def proj(ci, c0):
            pbi = bank(ci)
            K.ops("pe", [hk, "win"], ["pb%d" % pbi], [lambda e, kc=kc: e.matmul(PB[pbi][:, :], lhsT=hTg[:, kc, a * 128:(a + 1) * 128], rhs=win[:, kc, c0:c0 + 512], start=(kc == 0), stop=(kc == 7)) for kc in range(8)])
        proj(0, 1536)
        K.op("act", ["pb%d" % bank(0)], ["sza%d" % s2], lambda e: e.activation(out=sza[s2][:], in_=PB[bank(0)][:, :], func=AF.Silu))
        yield
        K.dma("act", "sza%d" % s2, ["sza%d" % s2], [], R["st_sza"][gt], sza[s2][:])
        proj(1, 2304)
        K.op("act", ["pb%d" % bank(1)], ["u_t%d" % s2], lambda e: e.copy(out=u_t[s2][:], in_=PB[bank(1)][:, :]))
        yield
        proj(2, 2816)
        K.op("act", ["pb%d" % bank(2), lsk], ["vb%d" % s2, lsk], lambda e: e.activation(out=vb[s2][:], in_=PB[bank(2)][:, :], func=AF.Copy, accum_out=ls[:, 0:1]))
        yield
        proj(3, 3328)
        K.op("act", ["pb%d" % bank(3)], ["szb%d" % s2], lambda e: e.activation(out=szb[s2][:], in_=PB[bank(3)][:, :], func=AF.Silu))
        yield
        K.op("act", ["vb%d" % s2, lsk], ["sqj", lsk], lambda e: e.activation(out=sq[:, 0:512], in_=vb[s2][:], func=AF.Square, accum_out=ls[:, 1:2]))
        yield
        K.op("dve", [lsk], [lsk], lambda e: e.tensor_scalar(out=ls[:, 2:3], in0=ls[:, 0:1], scalar1=1.0 / 512, scalar2=None, op0=ALU.mult))
        yield
        K.op("dve", [lsk], [lsk], lambda e: e.tensor_tensor(out=ls[:, 3:4], in0=ls[:, 2:3], in1=ls[:, 2:3], op=ALU.mult))
        yield
        K.op("dve", [lsk], [lsk], lambda e: e.scalar_tensor_tensor(out=ls[:, 4:5], in0=ls[:, 1:2], scalar=1.0 / 512, in1=ls[:, 3:4], op0=ALU.mult, op1=ALU.subtract))
        yield
        K.op("dve", [lsk], [lsk], lambda e: e.tensor_scalar(out=ls[:, 5:6], in0=ls[:, 4:5], scalar1=NORM_EPS, scalar2=None, op0=ALU.add))
        yield
        K.op("act", [lsk], [lsk], lambda e: e.sqrt(out=ls[:, 5:6], in_=ls[:, 5:6]))
        yield
        K.op("dve", [lsk], [lsk], lambda e: e.reciprocal(out=ls[:, 5:6], in_=ls[:, 5:6]))
        yield
        K.op("dve", ["vb%d" % s2, lsk], ["vb%d" % s2], lambda e: e.tensor_scalar(out=vb[s2][:], in0=vb[s2][:], scalar1=ls[:, 2:3], scalar2=ls[:, 5:6], op0=ALU.subtract, op1=ALU.mult))
        yield
        K.op("pool", ["vb%d" % s2, "rows_t"], ["vb%d" % s2], lambda e: e.tensor_tensor(out=vb[s2][:], in0=vb[s2][:], in1=rows_t[:, 2048:2560], op=ALU.mult))
        yield
        K.op("pool", ["vb%d" % s2, "rows_t"], ["vn%d" % s2], lambda e: e.tensor_tensor(out=vn[s2][:], in0=vb[s2][:], in1=rows_t[:, 2560:3072], op=ALU.add))
        yield
        K.ops("pe", ["vn%d" % s2, "wsT"], ["pb0"], [lambda e, gg=gg: e.matmul(PB[0][:, gg * 64:(gg + 1) * 64], lhsT=wsT[:, gg, :], rhs=vn[s2][:, gg * 64:(gg + 1) * 64], start=True, stop=True) for gg in range(8)])
        K.op("dve", ["pb0", "bs_t"], ["vb%d" % s2], lambda e: e.tensor_tensor(out=vb[s2][:].rearrange("p (g c) -> p g c", c=64), in0=PB[0][:, :].rearrange("p (g c) -> p g c", c=64), in1=bs_t[:, :].unsqueeze(2).to_broadcast([128, 8, 64]), op=ALU.add))
        yield
        K.op("pool", ["vb%d" % s2, "u_t%d" % s2], ["vb%d" % s2], lambda e: e.tensor_tensor(out=vb[s2][:], in0=vb[s2][:], in1=u_t[s2][:], op=ALU.mult))
        yield
        K.op("pool", ["vb%d" % s2, "szb%d" % s2], ["mixb%d" % s2], lambda e: e.tensor_tensor(out=mixb[s2][:], in0=vb[s2][:], in1=szb[s2][:], op=ALU.mult))
        yield
        K.dma("sp", "mixb%d" % s2, ["mixb%d" % s2], [], R["st_mixb"][gt], mixb[s2][:])

    pre = None
    pending = [None]
    cnt["rawp"] = 0
    for si, (xap, NT, cv, gt_base, _y) in enumerate(seqs):
        GS = min(512, NT * 128)
        GS_cur[0] = GS
        nt_g = GS // 128
        NG = NT // nt_g
        nxt = seqs[si + 1] if si + 1 < len(seqs) else None
        nt_n = min(512, nxt[1] * 128) // 128 if nxt is not None else 0
        hbuf = {}
        if pre is None:
            hi = cnt["grp"] % 2
            cnt["grp"] += 1
            hbuf[0] = (hT[hi], "hT%d" % hi)
            for a in range(nt_g):
                front_compute(xap, 0, a)
            for a in range(nt_g):
                front_transpose(a, cv, *hbuf[0])
        else:
            hbuf[0] = pre
            pre = None
        for g in range(NG):
            hTg, hk = hbuf[g]
            rp = cnt["rawp"] % 2
            cnt["rawp"] += 1
            rw = raw[rp]
            rk = "raw%d" % rp
            if g == 0:
                K.op("pool", [], [rk], lambda e, rw=rw: e.memset(rw[:, :, 0:1], 0.0))
            lw = low[rp]
            lk = "low%d" % rp
            for bi, blk in enumerate(list(range(12)) + [16, 17]):
                pbi = bi % 3
                pk = "pb%d" % pbi
                K.ops("pe", [hk, "win"], [pk], [lambda e, kc=kc, blk=blk, pbi=pbi: e.matmul(PB[pbi][:, 0:GS], lhsT=win[:, kc, blk * 128:(blk + 1) * 128], rhs=hTg[:, kc, 0:GS], start=(kc == 0), stop=(kc == 7)) for kc in range(8)])
                if blk < 12:
                    K.op("act", [pk], [rk], lambda e, blk=blk, pbi=pbi, rw=rw: e.copy(out=rw[:, blk, 1:GS + 1], in_=PB[pbi][:, 0:GS]))
                elif blk == 16:
                    K.op("act", [pk], [lk], lambda e, pbi=pbi, lw=lw: e.activation(out=lw[:, 0, 0:GS], in_=PB[pbi][:, 0:GS], func=AF.Tanh))
                else:
                    K.op("act", [pk], [lk], lambda e, pbi=pbi, lw=lw: e.copy(out=lw[:, 1, 0:GS], in_=PB[pbi][:, 0:GS]))
                if g + 1 < NG and bi in (2, 5, 8, 11):
                    front_compute(xap, g + 1, (bi - 2) // 3)
                elif g + 1 == NG and nxt is not None and bi in (2, 5, 8, 11) and (bi - 2) // 3 < nt_n:
                    front_compute(nxt[0], 0, (bi - 2) // 3)
            for a in range(nt_g):
                gt = gt_base + g * nt_g + a
                K.dma("act", lk, [lk], [], R["st_low"][gt], lw[:, :, a * 128:(a + 1) * 128])
            gens = []
            if g > 0:
                rwp = raw[1 - rp]
                rkp = "raw%d" % (1 - rp)
                K.op("pool", [rk], [rkp], lambda e, rw=rw, rwp=rwp: e.tensor_copy(out=rwp[:, :, GS + 1:GS + 2], in_=rw[:, :, 1:2]))
                K.op("pool", [rkp], [rk], lambda e, rw=rw, rwp=rwp: e.tensor_copy(out=rw[:, :, 0:1], in_=rwp[:, :, GS:GS + 1]))
                gens.append(shift_gen(1 - rp, GS, nt_g, gt_base + (g - 1) * nt_g))
            elif pending[0] is not None:
                gens.append(pending[0])
                pending[0] = None
            tms = [tm_gen(a % 4, hTg, hk, a, gt_base + g * nt_g + a) for a in range(nt_g)]
            run_lanes([(gens, 1), (tms, 4)])
            if g + 1 < NG:
                hi = cnt["grp"] % 2
                cnt["grp"] += 1
                hbuf[g + 1] = (hT[hi], "hT%d" % hi)
                for a in range(nt_g):
                    front_transpose(a, cv, *hbuf[g + 1])
            elif nxt is not None:
                hi = cnt["grp"] % 2
                cnt["grp"] += 1
                pre = (hT[hi], "hT%d" % hi)
                for a in range(nt_n):
                    front_transpose(a, nxt[2], *pre)
        g = NG - 1
        K.op("pool", [], [rk], lambda e, rw=rw: e.memset(rw[:, :, GS + 1:GS + 2], 0.0))
        pending[0] = shift_gen(rp, GS, nt_g, gt_base + g * nt_g)
        if nxt is None:
            run_lanes([([pending[0]], 1)])
            pending[0] = None


def phase_b(nc, K, esB, sb, PB, PT, seqs, R, NT_S, NP):
    PBIG = R["PBIG"]
    pf_t, cst, lora, rkm, omka, mA, mT = R["pf_t"], R["cst"], R["lora"], R["rkm"], R["omka"], R["mA"], R["mT"]
    ident = cst[:, 0, :]

    WB = 3

    def T(name, shape, dt):
        return [sb("%s_%d" % (name, p), shape, dt, esB) for p in range(WB)]
    rk = T("Brk", [128, 8, 128], BF16)
    lowT = T("Blow", [128, 2, 128], BF16)
    Vt = T("Bvt", [128, 512], BF16)
    arm = T("Barm", [128, 4, 2, 2, 128], BF16)
    bT = T("BbT", [128, 4, 128], BF16)
    kT = T("BkT", [128, 4, 128], BF16)
    rkd = T("Brkd", [128, 4, 128], BF16)
    kk2 = T("Bkk2", [128, 4, 128], BF16)
    MM1 = T("BMM1", [128, 8, 2, 128], BF16)
    MM2 = T("BMM2", [128, 8, 2, 128], BF16)
    Q0T = T("BQ0T", [128, 8, 128], BF16)
    QR = [T("BQRa", [128, 8, 2, 128], BF16), T("BQRb", [128, 8, 2, 128], BF16)]
    QT = [T("BQTa", [128, 8, 128], BF16), T("BQTb", [128, 8, 128], BF16)]
    TT = T("BTT", [128, 8, 128], BF16)
    Bt = T("BBt", [128, 512], BF16)
    Kt = T("BKt", [128, 512], BF16)
    Zs = T("BZs", [128, 512], BF16)
    Us = T("BUs", [128, 512], BF16)
    ysb = T("Bysb", [128, 512], F32)
    bcs = T("Bbcs", [128, 8], F32)
    f32n = ["sg", "aa", "E1", "E2", "gam", "gamx", "gami", "kk"]
    F = {n: T("Bf_" + n, [128, 4, 128], F32) for n in f32n}
    F["tq"], F["rn"], F["kd"], F["kkn"] = F["sg"], F["E1"], F["E2"], F["kk"]
    tot = T("Btot", [128, 16], F32)
    gC = T("BgC", [128, 4], F32)
    ones = sb("Bones", [128, 128], F32, esB)
    S = [[[sb("BS_%d_%d_%d" % (sp, d, i), [128, 4, 64], F32, esB) for i in range(2)] for d in range(2)] for sp in range(2)]
    Sb = [[[sb("BSb_%d_%d_%d" % (sp, d, i), [128, 4, 64], BF16, esB) for i in range(2)] for d in range(2)] for sp in range(2)]
    t1s = [sb("Bt1_%d" % i, [128, 4, 64], F32, esB) for i in range(WB)]

    K.op("pool", [], ["Bones"], lambda e: e.memset(ones[:], 1.0))
    hpar = sb("Bhpar", [128, 34], F32, esB)
    K.op("pool", ["pf_t"], ["hpar"], lambda e: e.tensor_scalar(out=hpar[:, 0:16], in0=pf_t[:, 48:64], scalar1=0.5, scalar2=None, op0=ALU.mult))
    K.op("pool", ["hpar"], ["hpar"], lambda e: e.memset(hpar[:, 16:17], C2H))
    K.op("pool", ["hpar"], ["hpar"], lambda e: e.memset(hpar[:, 17:18], L2_EPS))
    K.op("pool", ["pf_t", "hpar"], ["hpar"], lambda e: e.tensor_scalar(out=hpar[:, 18:26], in0=pf_t[:, 72:80], scalar1=0.5, scalar2=None, op0=ALU.mult))
    K.op("pool", ["pf_t", "hpar"], ["hpar"], lambda e: e.tensor_scalar(out=hpar[:, 26:34], in0=pf_t[:, 72:80], scalar1=-0.5, scalar2=1.0, op0=ALU.mult, op1=ALU.add))
    for p in range(WB):
        K.op("pool", [], ["Barm_%d" % p], lambda e, p=p: e.memset(arm[p][:].rearrange("p a b c d -> p (a b c d)"), 0.0))

    chain_done = {}

    def visit(p, gt, d, sp, sidx, last_out_ap, ckey, cidx):
        t1 = t1s[p]
        b0, b1 = 2 * p, 2 * p + 1
        b2 = b0
        kt1 = "Bt1_%d" % p
        k = lambda n: "%s_%d" % (n, p)
        _al = dict(tq="sg", rn="E1", kd="E2", kkn="kk")
        fk = lambda n: "Bf_%s_%d" % (_al.get(n, n), p)
        Sin, Sbin = S[sp][d][sidx], Sb[sp][d][sidx]
        Sout, Sbout = S[sp][d][1 - sidx], Sb[sp][d][1 - sidx]
        kSin, kSbin = "BS_%d_%d_%d" % (sp, d, sidx), "BSb_%d_%d_%d" % (sp, d, sidx)
        kSout, kSbout = "BS_%d_%d_%d" % (sp, d, 1 - sidx), "BSb_%d_%d_%d" % (sp, d, 1 - sidx)
        c_w0, c_a0, c_kk, c_ka = 48 + 4 * d, 56 + 4 * d, 64 + 4 * d, 72 + 4 * d
        bc3 = lambda ap: ap.unsqueeze(2).to_broadcast([128, 4, 128])
        rT_ = rk[p][:, 0:4, :]
        kT_ = rk[p][:, 4:8, :]
        K.dma("sp", k("Brk"), [], [k("Brk")], rk[p][:], R["st_rkv"][gt][:, 0:8, :])
        yield
        K.dma("sp", k("Blow"), [], [k("Blow")], lowT[p][:], R["st_low"][gt])
        yield
        K.dma("sp", k("Bvt"), [], [k("Bvt")], Vt[p][:], R["st_vt"][gt])
        yield
        K.ops("pe", [k("Blow"), "lora"], ["pb%d" % b0], [lambda e, q=q: e.matmul(PB[b0][:, q * 128:(q + 1) * 128], lhsT=lora[:, d, q * 128:(q + 1) * 128], rhs=lowT[p][:, 0, :], start=True, stop=True) for q in range(4)])
        K.ops("pe", [k("Blow"), "lora"], ["pb%d" % b1], [lambda e, q=q: e.matmul(PB[b1][:, q * 128:(q + 1) * 128], lhsT=lora[:, 2 + d, q * 128:(q + 1) * 128], rhs=lowT[p][:, 1, :], start=True, stop=True) for q in range(4)])
        yield
        TH, THA, C2, X2 = F["sg"][p], F["aa"][p], F["E1"][p], F["E2"][p]
        for q in range(4):
            K.op("act", ["pb%d" % b0, "hpar"], [fk("sg")], lambda e, q=q: e.activation(out=TH[:, q, :], in_=PB[b0][:, q * 128:(q + 1) * 128], func=AF.Tanh, scale=0.5, bias=hpar[:, 4 * d + q:4 * d + q + 1]))
        yield
        for q in range(4):
            K.op("act", ["pb%d" % b1, "hpar"], [fk("aa")], lambda e, q=q: e.activation(out=THA[:, q, :], in_=PB[b1][:, q * 128:(q + 1) * 128], func=AF.Tanh, scale=0.5, bias=hpar[:, 8 + 4 * d + q:8 + 4 * d + q + 1]))
        yield
        for q in range(4):
            if d == 0:
                K.op("dve", [fk("sg"), "Bones"], [fk("E1")], lambda e, q=q: e.tensor_tensor_scan(out=C2[:, q, :], data0=ones[:, :], data1=TH[:, q, :], initial=0.0, op0=ALU.add, op1=ALU.add))
            else:
                K.op("dve", [fk("sg"), "Bones"], [fk("E1")], lambda e, q=q: e.tensor_tensor_scan(out=C2[:, q, ::-1], data0=ones[:, :], data1=TH[:, q, ::-1], initial=0.0, op0=ALU.add, op1=ALU.add))
        yield
        K.op("pool", [fk("E1"), fk("sg")], [fk("E2")], lambda e: e.tensor_tensor(out=X2[:], in0=C2[:], in1=TH[:], op=ALU.subtract))
        yield
        tcol = 127 if d == 0 else 0
        K.op("pool", [fk("E1")], [k("Btot")], lambda e: e.tensor_scalar(out=tot[p][:, 0:4], in0=C2[:, :, tcol], scalar1=-C2H, scalar2=None, op0=ALU.mult))
        yield
        K.op("act", [k("Btot")], [k("BgC")], lambda e: e.activation(out=gC[p][:], in_=tot[p][:, 0:4], func=AF.Exp))
        yield
        K.op("act", [fk("E1")], [fk("gam")], lambda e: e.activation(out=F["gam"][p][:], in_=C2[:], func=AF.Exp, scale=-C2H))
        yield
        K.op("act", [fk("E2"), "hpar"], [fk("gamx")], lambda e: e.activation(out=F["gamx"][p][:], in_=X2[:], func=AF.Exp, scale=-C2H, bias=hpar[:, 16:17]))
        yield
        K.op("act", [fk("E1")], [fk("gami")], lambda e: e.activation(out=F["gami"][p][:], in_=C2[:], func=AF.Exp, scale=C2H))
        yield
        K.op("dve", [k("Brk"), "pf_t"], [fk("kk")], lambda e: e.tensor_tensor(out=F["kk"][p][:], in0=kT_, in1=bc3(pf_t[:, c_kk:c_kk + 4]), op=ALU.mult))
        yield
        K.op("pool", [fk("kk")], [k("Bkk2")], lambda e: e.tensor_tensor(out=kk2[p][:], in0=F["kk"][p][:], in1=F["kk"][p][:], op=ALU.mult))
        yield
        K.ops("pe", [k("Bkk2"), "cst"], ["pb%d" % b2], [lambda e, q=q: e.matmul(PB[b2][:, q * 128:(q + 1) * 128], lhsT=cst[:, 5, :], rhs=kk2[p][:, q, :], start=True, stop=True) for q in range(4)])
        yield
        K.op("act", ["pb%d" % b2, "hpar"], [fk("rn")], lambda e: e.activation(out=F["rn"][p][:].rearrange("p a b -> p (a b)"), in_=PB[b2][:, :], func=AF.Ln, bias=hpar[:, 17:18]))
        yield
        K.op("act", [fk("rn")], [fk("rn")], lambda e: e.activation(out=F["rn"][p][:], in_=F["rn"][p][:], func=AF.Exp, scale=-0.5))
        yield
        K.op("pool", [fk("kk"), fk("rn")], [fk("kkn")], lambda e: e.tensor_tensor(out=F["kkn"][p][:], in0=F["kk"][p][:], in1=F["rn"][p][:], op=ALU.mult))
        yield
        for q in range(4):
            K.op("pool", [fk("aa"), "hpar"], [fk("tq")], lambda e, q=q: e.tensor_scalar(out=F["tq"][p][:, q, :], in0=THA[:, q, :], scalar1=hpar[:, 18 + 4 * d + q:19 + 4 * d + q], scalar2=hpar[:, 26 + 4 * d + q:27 + 4 * d + q], op0=ALU.mult, op1=ALU.add))
        yield
        K.op("pool", [fk("tq"), k("Brk")], [fk("kd")], lambda e: e.tensor_tensor(out=F["kd"][p][:], in0=F["tq"][p][:], in1=kT_, op=ALU.mult))
        yield
        for hp in range(2):
            rs = slice(hp * 64, (hp + 1) * 64)
            K.op("dve", [fk("kkn"), fk("gamx")], [k("Barm")], lambda e, rs=rs, hp=hp: e.scalar_tensor_tensor(out=arm[p][rs, :, hp, 0, :], in0=F["kkn"][p][rs, :, :], scalar=-1.0, in1=F["gamx"][p][rs, :, :], op0=ALU.mult, op1=ALU.mult))
            yield
            K.op("pool", [k("Brk"), fk("gam")], [k("Barm")], lambda e, rs=rs, hp=hp: e.tensor_tensor(out=arm[p][rs, :, hp, 1, :], in0=rk[p][rs, 0:4, :], in1=F["gam"][p][rs, :, :], op=ALU.mult))
            yield
        K.op("dve", [fk("kkn"), fk("aa")], [fk("tq")], lambda e: e.scalar_tensor_tensor(out=F["tq"][p][:], in0=THA[:], scalar=1.0, in1=F["kkn"][p][:], op0=ALU.add, op1=ALU.mult))
        yield
        K.op("dve", [fk("tq"), fk("gami")], [k("BbT")], lambda e: e.scalar_tensor_tensor(out=bT[p][:], in0=F["tq"][p][:], scalar=0.5, in1=F["gami"][p][:], op0=ALU.mult, op1=ALU.mult))
        yield
        K.op("pool", [fk("kd"), fk("gami")], [k("BkT")], lambda e: e.tensor_tensor(out=kT[p][:], in0=F["kd"][p][:], in1=F["gami"][p][:], op=ALU.mult))
        yield
        K.op("pool", [fk("kd"), k("Brk")], [k("Brkd")], lambda e: e.tensor_tensor(out=rkd[p][:], in0=F["kd"][p][:], in1=rT_, op=ALU.mult))
        yield
        K.ops("pe", [k("Brkd"), "rkm"], ["pb6"], [lambda e, q=q: e.matmul(PB[6][:, q * 2:q * 2 + 2], lhsT=rkd[p][:, q, :], rhs=rkm[:, d * 8 + q * 2:d * 8 + q * 2 + 2], start=True, stop=True) for q in range(4)])
        K.op("act", ["pb6"], [k("Bbcs")], lambda e: e.copy(out=bcs[p][:], in_=PB[6][:, 0:8]))
        yield
        K.dma("act", k("Bbcs"), [k("Bbcs")], [], R["st_bc"][d, gt], bcs[p][:])
        yield
        K.ops("pe", [k("BbT"), k("BkT"), "cst"], ["ptb"],
              [lambda e, q=q: e.transpose(out=PT[:, q * 128:(q + 1) * 128], in_=bT[p][:, q, :], identity=ident) for q in range(4)] +
              [lambda e, q=q: e.transpose(out=PT[:, 512 + q * 128:512 + (q + 1) * 128], in_=kT[p][:, q, :], identity=ident) for q in range(4)])
        K.op("act", ["ptb"], [k("BBt")], lambda e: e.copy(out=Bt[p][:], in_=PT[:, 0:512]))
        K.op("act", ["ptb"], [k("BKt")], lambda e: e.copy(out=Kt[p][:], in_=PT[:, 512:1024]))
        yield
        for q in range(4):
            fa, fb, fc = [], [], []
            for hp in range(2):
                rhsA = arm[p][:, q, hp, :, :].rearrange("p a t -> p (a t)")
                fa.append(lambda e, hp=hp, rhsA=rhsA: e.matmul(PB[b0][:, hp * 256:(hp + 1) * 256], lhsT=bT[p][:, q, :], rhs=rhsA, start=True, stop=True))
                fb.append(lambda e, hp=hp, rhsA=rhsA: e.matmul(PB[b1][:, hp * 256:(hp + 1) * 256], lhsT=kT[p][:, q, :], rhs=rhsA, start=True, stop=True))
                fc.append(lambda e, hp=hp: e.matmul(PB[b0][:, hp * 128:(hp + 1) * 128], lhsT=arm[p][:, q, hp, 0, :], rhs=bT[p][:, q, :], start=True, stop=True))
            K.ops("pe", [k("BbT"), k("Barm")], ["pb%d" % b0], fa)
            K.ops("pe", [k("BkT"), k("Barm")], ["pb%d" % b1], fb)
            yield
            K.op("dve", ["pb%d" % b0, "mA"], [k("BMM1") + "q%d" % q], lambda e, q=q: e.tensor_tensor(out=MM1[p][:, 2 * q:2 * q + 2, :, :].rearrange("p h a t -> p (h a t)"), in0=PB[b0][:, :], in1=mA[:, d, :, :].rearrange("p a t -> p (a t)"), op=ALU.mult))
            yield
            K.ops("pe", [k("BbT"), k("Barm")], ["pb%d" % b0], fc)
            K.op("dve", ["pb%d" % b1, "mA"], [k("BMM2") + "q%d" % q], lambda e, q=q: e.tensor_tensor(out=MM2[p][:, 2 * q:2 * q + 2, :, :].rearrange("p h a t -> p (h a t)"), in0=PB[b1][:, :], in1=mA[:, d, :, :].rearrange("p a t -> p (a t)"), op=ALU.mult))
            yield
            K.op("dve", ["pb%d" % b0, "mT"], [k("BQ0T") + "q%d" % q], lambda e, q=q: e.tensor_tensor(out=Q0T[p][:, 2 * q:2 * q + 2, :].rearrange("p h t -> p (h t)"), in0=PB[b0][:, 0:256], in1=mT[:, d, :, :].rearrange("p a t -> p (a t)"), op=ALU.mult))
            yield
        kQR = lambda a, q: "BQR%s_%dq%d" % ("ab"[a], p, q)
        kQT = lambda a, q: "BQT%s_%dq%d" % ("ab"[a], p, q)
        XA = PB[b0]
        XB = PB[b1]
        for q in range(4):
            hs = [2 * q, 2 * q + 1]
            fa = []
            for hh, h in enumerate(hs):
                fa.append(lambda e, hh=hh, h=h: e.matmul(XA[:, hh * 256:hh * 256 + 128], lhsT=Q0T[p][:, h, :], rhs=MM1[p][:, h, 0, :], start=True, stop=True))
                fa.append(lambda e, hh=hh, h=h: e.matmul(XA[:, hh * 256 + 128:hh * 256 + 256], lhsT=ident, rhs=MM1[p][:, h, 0, :], start=False, stop=False, skip_group_check=True))
                fa.append(lambda e, hh=hh, h=h: e.matmul(XA[:, hh * 256 + 128:hh * 256 + 256], lhsT=ident, rhs=ident, start=False, stop=True, skip_group_check=True))
            K.ops("pe", [k("BMM1") + "q%d" % q, k("BQ0T") + "q%d" % q, "cst"], ["pb%d" % b0], fa)
            K.ops("pe", [k("BMM1") + "q%d" % q, k("BQ0T") + "q%d" % q], ["pb%d" % b1], [lambda e, hh=hh, h=h: e.matmul(XB[:, hh * 128:(hh + 1) * 128], lhsT=MM1[p][:, h, 0, :], rhs=Q0T[p][:, h, :], start=True, stop=True) for hh, h in enumerate(hs)])
            yield
            K.op("act", ["pb%d" % b0], [kQR(1, q)], lambda e, q=q: e.copy(out=QR[1][p][:, 2 * q:2 * q + 2, :, :].rearrange("p h a t -> p (h a t)"), in_=XA[:, :]), c=0.5)
            yield
            K.op("dve", ["pb%d" % b1], [kQT(1, q)], lambda e, q=q: e.tensor_copy(out=QT[1][p][:, 2 * q:2 * q + 2, :].rearrange("p h t -> p (h t)"), in_=XB[:, 0:256]), c=0.42)
            yield
        for lev in range(1, 6):
            cur, nxt = lev % 2, (lev + 1) % 2
            for q in range(4):
                hs = [2 * q, 2 * q + 1]
                fa = []
                for hh, h in enumerate(hs):
                    fa.append(lambda e, hh=hh, h=h: e.matmul(XA[:, hh * 256:(hh + 1) * 256], lhsT=QT[cur][p][:, h, :], rhs=QR[cur][p][:, h, :, :].rearrange("p a t -> p (a t)"), start=True, stop=False))
                    fa.append(lambda e, hh=hh, h=h: e.matmul(XA[:, hh * 256 + 128:(hh + 1) * 256], lhsT=ident, rhs=QR[cur][p][:, h, 1, :], start=False, stop=True))
                K.ops("pe", [kQR(cur, q), kQT(cur, q), "cst"], ["pb%d" % b0], fa)
                K.ops("pe", [kQR(cur, q), kQT(cur, q)], ["pb%d" % b1], [lambda e, hh=hh, h=h: e.matmul(XB[:, hh * 128:(hh + 1) * 128], lhsT=QR[cur][p][:, h, 0, :], rhs=QT[cur][p][:, h, :], start=True, stop=True) for hh, h in enumerate(hs)])
                yield
                K.op("act", ["pb%d" % b0], [kQR(nxt, q)], lambda e, q=q: e.copy(out=QR[nxt][p][:, 2 * q:2 * q + 2, :, :].rearrange("p h a t -> p (h a t)"), in_=XA[:, :]), c=0.5)
                yield
                K.op("dve", ["pb%d" % b1], [kQT(nxt, q)], lambda e, q=q: e.tensor_copy(out=QT[nxt][p][:, 2 * q:2 * q + 2, :].rearrange("p h t -> p (h t)"), in_=XB[:, 0:256]), c=0.42)
                yield
        for q in range(4):
            hs = [2 * q, 2 * q + 1]
            ff = []
            bq = b0 if q % 2 == 0 else b1
            for hh, h in enumerate(hs):
                ff.append(lambda e, hh=hh, h=h, bq=bq: e.matmul(PB[bq][:, hh * 128:(hh + 1) * 128], lhsT=QT[0][p][:, h, :], rhs=QR[0][p][:, h, 1, :], start=True, stop=False))
                ff.append(lambda e, hh=hh, h=h, bq=bq: e.matmul(PB[bq][:, hh * 128:(hh + 1) * 128], lhsT=ident, rhs=QR[0][p][:, h, 1, :], start=False, stop=True))
            K.ops("pe", [kQR(0, q), kQT(0, q), "cst"], ["pb%d" % bq], ff)
            yield
            if q % 2 == 0:
                K.op("act", ["pb%d" % bq], [k("BTT") + "q%d" % q], lambda e, q=q, bq=bq: e.copy(out=TT[p][:, 2 * q:2 * q + 2, :].rearrange("p h t -> p (h t)"), in_=PB[bq][:, 0:256]), c=0.4)
            else:
                K.op("dve", ["pb%d" % bq], [k("BTT") + "q%d" % q], lambda e, q=q, bq=bq: e.tensor_copy(out=TT[p][:, 2 * q:2 * q + 2, :].rearrange("p h t -> p (h t)"), in_=PB[bq][:, 0:256]), c=0.42)
            yield
        while chain_done.get(ckey, 0) < cidx:
            yield
        fz = []
        for h in range(8):
            q, hp = h // 2, h % 2
            fz.append(lambda e, h=h, q=q, hp=hp: e.matmul(PB[b0][:, h * 64:(h + 1) * 64], lhsT=arm[p][:, q, hp, 0, :], rhs=Sbin[:, q, :], start=True, stop=False))
            fz.append(lambda e, h=h: e.matmul(PB[b0][:, h * 64:(h + 1) * 64], lhsT=MM2[p][:, h, 0, :], rhs=Vt[p][:, h * 64:(h + 1) * 64], start=False, stop=True))
        K.ops("pe", [k("Barm"), kSbin, k("Bvt")] + [k("BMM2") + "q%d" % q for q in range(4)], ["pb%d" % b0], fz)
        yield
        K.op("act", ["pb%d" % b0], [k("BZs")], lambda e: e.copy(out=Zs[p][:], in_=PB[b0][:, :]))
        yield
        K.ops("pe", [k("BTT") + "q%d" % q for q in range(4)] + [k("BZs")], ["pb%d" % b1], [lambda e, h=h: e.matmul(PB[b1][:, h * 64:(h + 1) * 64], lhsT=TT[p][:, h, :], rhs=Zs[p][:, h * 64:(h + 1) * 64], start=True, stop=True) for h in range(8)])
        yield
        K.op("dve", ["pb%d" % b1], [k("BUs")], lambda e: e.tensor_copy(out=Us[p][:], in_=PB[b1][:, :]))
        yield
        fd = []
        for h in range(8):
            q = h // 2
            fd.append(lambda e, h=h, q=q: e.matmul(PB[b2][:, h * 64:(h + 1) * 64], lhsT=Bt[p][:, q * 128:(q + 1) * 128], rhs=Us[p][:, h * 64:(h + 1) * 64], start=True, stop=False))
            fd.append(lambda e, h=h, q=q: e.matmul(PB[b2][:, h * 64:(h + 1) * 64], lhsT=Kt[p][:, q * 128:(q + 1) * 128], rhs=Vt[p][:, h * 64:(h + 1) * 64], start=False, stop=True))
        K.ops("pe", [k("BBt"), k("BKt"), k("BUs"), k("Bvt")], ["pb%d" % b2], fd)
        yield
        Dv = PB[b2][:, :].rearrange("p (q hp j) -> p q hp j", hp=2, j=64)
        for hp in range(2):
            rs = slice(hp * 64, (hp + 1) * 64)
            K.op("dve", ["pb%d" % b2, kSin], [kt1], lambda e, rs=rs, hp=hp: e.tensor_tensor(out=t1[rs, :, :], in0=Dv[rs, :, hp, :], in1=Sin[rs, :, :], op=ALU.add))
            yield
        gb = gC[p][:, :].unsqueeze(2).to_broadcast([128, 4, 64])
        K.op("dve", [kt1, k("BgC")], [kSout], lambda e: e.tensor_tensor(out=Sout[:], in0=t1[:], in1=gb, op=ALU.mult))
        yield
        K.op("pool", [kt1, k("BgC")], [kSbout], lambda e: e.tensor_tensor(out=Sbout[:], in0=t1[:], in1=gb, op=ALU.mult))
        yield
        fy = []
        for h in range(8):
            q, hp = h // 2, h % 2
            fy.append(lambda e, h=h, q=q, hp=hp: e.matmul(PB[b0][:, h * 64:(h + 1) * 64], lhsT=arm[p][:, q, hp, 1, :], rhs=Sbin[:, q, :], start=True, stop=False))
            fy.append(lambda e, h=h: e.matmul(PB[b0][:, h * 64:(h + 1) * 64], lhsT=MM1[p][:, h, 1, :], rhs=Us[p][:, h * 64:(h + 1) * 64], start=False, stop=False))
            fy.append(lambda e, h=h: e.matmul(PB[b0][:, h * 64:(h + 1) * 64], lhsT=MM2[p][:, h, 1, :], rhs=Vt[p][:, h * 64:(h + 1) * 64], start=False, stop=True))
        K.ops("pe", [k("Barm"), kSbin, k("BUs"), k("Bvt")] + [k("BMM1") + "q%d" % q for q in range(4)] + [k("BMM2") + "q%d" % q for q in range(4)], ["pb%d" % b0], fy)
        yield
        chain_done[ckey] = cidx + 1
        K.op("act", ["pb%d" % b0], [k("Bysb")], lambda e: e.copy(out=ysb[p][:], in_=PB[b0][:, :]))
        yield
        K.dma("act", k("Bysb"), [k("Bysb")], [], R["st_y"][d, gt], ysb[p][:])
        yield
        if last_out_ap is not None:
            K.dma("sp", kSout, [kSout], [], last_out_ap, Sout[:])
            yield

    queue = []
    for si, (xap, NT, cv, gt_base, _y) in enumerate(seqs):
        sp = si % 2
        for step in range(NT):
            sidx = step % 2
            last = (step == NT - 1) and si > 0
            queue.append((si, step, gt_base + step, 0, sp, sidx, R["nst"][si - 1, 0] if last else None, (si, 0), step))
            queue.append((si, step, gt_base + NT - 1 - step, 1, sp, sidx, R["nst"][si - 1, 1] if last else None, (si, 1), step))

    def init_state(si):
        sp = si % 2
        for d in range(2):
            if si == 0:
                K.dma("sp", "BS_%d_%d_0" % (sp, d), [], ["BS_%d_%d_0" % (sp, d)], S[sp][d][0][:], R["s0T"][d])
                K.op("pool", ["BS_%d_%d_0" % (sp, d)], ["BSb_%d_%d_0" % (sp, d)], lambda e, d=d: e.tensor_copy(out=Sb[sp][d][0][:], in_=S[sp][d][0][:]))
            else:
                K.op("pool", [], ["BS_%d_%d_0" % (sp, d)], lambda e, d=d: e.memset(S[sp][d][0][:].rearrange("p a b -> p (a b)"), 0.0))
                K.op("pool", [], ["BSb_%d_%d_0" % (sp, d)], lambda e, d=d: e.memset(Sb[sp][d][0][:].rearrange("p a b -> p (a b)"), 0.0))

    def before(item):
        si, step, gt, d, sp, sidx, lo = item[:7]
        if step == 0 and d == 0:
            init_state(si)

    K.run_streams(queue, lambda slot, it: visit(slot, it[2], it[3], it[4], it[5], it[6], it[7], it[8]), WB, before, offset=VOFF)


def phase_c(nc, K, esC, sb, PB, PT, seqs, R):
    cst, w_out = R["cst"], R["w_out"]
    ident = cst[:, 0, :]
    rows_t = sb("Crows", [128, 3072], F32, esC)
    Gt = sb("CGt", [128, 2, 1024], F32, esC)
    K.dma("sp", "rows_t", [], ["rows_t"], rows_t[:], R["rows"])
    K.dma("sp", "Gt", [], ["Gt"], Gt[:], R["st_gt"])
    wout = sb("Cwout", [128, 8, D], BF16, esC)
    wst = [sb("Cwst%d" % i, [128, D], F32, esC) for i in range(2)]
    def wdma(kc):
        K.dma("act", "Cwst%d" % (kc % 2), [], ["Cwst%d" % (kc % 2)], wst[kc % 2][:], w_out[kc * 128:(kc + 1) * 128, :])
    wdma(0)
    wdma(1)
    for kc in range(8):
        K.op("dve", ["Cwst%d" % (kc % 2)], ["Cwout"], lambda e, kc=kc: e.tensor_copy(out=wout[:, kc, :], in_=wst[kc % 2][:]))
        if kc + 2 < 8:
            wdma(kc + 2)

    WC = 5
    ctr = [0]

    def T(name, shape, dt):
        return [sb("%s_%d" % (name, p), shape, dt, esC) for p in range(WC)]
    yf = T("Cyf", [128, 8, 64], F32)
    yb = T("Cyb", [128, 8, 64], F32)
    bcf = T("Cbcf", [128, 8], F32)
    bcb = T("Cbcb", [128, 8], F32)
    Vt = T("Cvt", [128, 8, 64], BF16)
    sza = T("Csza", [128, 512], BF16)
    mixed = T("Cmixed", [128, D], BF16)
    xt = T("Cxt", [128, D], F32)
    junk = sb("Cjunk", [128, 512], F32, esC)
    ysq = T("Cysq", [128, 8, 64], F32)
    st = T("Cst", [128, 40], F32)
    bon = T("Cbon", [128, 8, 64], F32)
    mixT = T("CmixT", [128, 8, 128], BF16)
    ot = T("Cot", [128, D], F32)

    def tile(p, xap, a, cv, gt, yap):
        if True:
            n0 = 2 * (ctr[0] % 3)
            ctr[0] += 1
            k = lambda n: "%s_%d" % (n, p)
            K.dma("sp", k("Cyf"), [], [k("Cyf")], yf[p][:].rearrange("p h j -> p (h j)"), R["st_y"][0, gt])
            yield
            K.dma("sp", k("Cyb"), [], [k("Cyb")], yb[p][:].rearrange("p h j -> p (h j)"), R["st_y"][1, gt])
            yield
            K.dma("sp", k("Cbcf"), [], [k("Cbcf")], bcf[p][:], R["st_bc"][0, gt])
            yield
            K.dma("sp", k("Cbcb"), [], [k("Cbcb")], bcb[p][:], R["st_bc"][1, gt])
            yield
            K.dma("sp", k("Cvt"), [], [k("Cvt")], Vt[p][:].rearrange("p h j -> p (h j)"), R["st_vt"][gt])
            yield
            K.dma("sp", k("Csza"), [], [k("Csza")], sza[p][:], R["st_sza"][gt])
            yield
            K.dma("sp", k("Cmixed") + "b", [], [k("Cmixed") + "b"], mixed[p][:, 512:1024], R["st_mixb"][gt])
            yield
            K.dma("sp", k("Cxt"), [], [k("Cxt")], xt[p][:], xap[a * 128:(a + 1) * 128, :])
            yield
            K.op("pool", [k("Cyf"), k("Cyb")], [k("Cyf")], lambda e: e.tensor_tensor(out=yf[p][:], in0=yf[p][:], in1=yb[p][:], op=ALU.add))
            yield
            s_ = st[p]
            K.op("dve", [k("Cyf")], [k("Cst") + "a"], lambda e: e.tensor_reduce(out=s_[:, 0:8], in_=yf[p][:], axis=AX.X, op=ALU.add))
            yield
            K.op("pool", [k("Cyf")], [k("Cysq")], lambda e: e.tensor_tensor(out=ysq[p][:], in0=yf[p][:], in1=yf[p][:], op=ALU.mult))
            yield
            K.op("dve", [k("Cysq")], [k("Cst") + "b"], lambda e: e.tensor_reduce(out=s_[:, 8:16], in_=ysq[p][:], axis=AX.X, op=ALU.add))
            yield
            sk = [k("Cst") + "a", k("Cst") + "b"]
            kc_ = k("Cst") + "c"
            K.op("dve", sk, [kc_], lambda e: e.tensor_scalar(out=s_[:, 16:24], in0=s_[:, 0:8], scalar1=1.0 / 64, scalar2=None, op0=ALU.mult))
            yield
            K.op("dve", [kc_], [kc_], lambda e: e.tensor_tensor(out=s_[:, 24:32], in0=s_[:, 16:24], in1=s_[:, 16:24], op=ALU.mult))
            yield
            K.op("dve", sk + [kc_], [kc_], lambda e: e.scalar_tensor_tensor(out=s_[:, 32:40], in0=s_[:, 8:16], scalar=1.0 / 64, in1=s_[:, 24:32], op0=ALU.mult, op1=ALU.subtract))
            yield
            K.op("dve", [kc_], [kc_], lambda e: e.tensor_scalar(out=s_[:, 32:40], in0=s_[:, 32:40], scalar1=GN_EPS, scalar2=None, op0=ALU.add))
            yield
            K.op("act", [kc_], [kc_], lambda e: e.sqrt(out=s_[:, 32:40], in_=s_[:, 32:40]))
            yield
            K.op("dve", [kc_], [kc_], lambda e: e.reciprocal(out=s_[:, 32:40], in_=s_[:, 32:40]))
            yield
            b8 = lambda ap: ap.unsqueeze(2).to_broadcast([128, 8, 64])
            K.op("dve", [k("Cyf"), kc_], [k("Cyf")], lambda e: e.tensor_tensor(out=yf[p][:], in0=yf[p][:], in1=b8(s_[:, 16:24]), op=ALU.subtract))
            yield
            K.op("dve", [k("Cyf"), kc_], [k("Cyf")], lambda e: e.tensor_tensor(out=yf[p][:], in0=yf[p][:], in1=b8(s_[:, 32:40]), op=ALU.mult))
            yield
            yfl = yf[p][:].rearrange("p h j -> p (h j)")
            K.op("pool", [k("Cyf"), "rows_t"], [k("Cyf")], lambda e: e.tensor_tensor(out=yfl, in0=yfl, in1=rows_t[:, 1024:1536], op=ALU.mult))
            yield
            K.op("pool", [k("Cyf"), "rows_t"], [k("Cyf")], lambda e: e.tensor_tensor(out=yfl, in0=yfl, in1=rows_t[:, 1536:2048], op=ALU.add))
            yield
            K.op("dve", [k("Cbcf"), k("Cbcb")], [k("Cbcf")], lambda e: e.tensor_tensor(out=bcf[p][:], in0=bcf[p][:], in1=bcb[p][:], op=ALU.add))
            yield
            K.op("dve", [k("Cvt"), k("Cbcf")], [k("Cbon")], lambda e: e.tensor_tensor(out=bon[p][:], in0=Vt[p][:], in1=b8(bcf[p][:, :]), op=ALU.mult))
            yield
            K.op("dve", [k("Cyf"), k("Cbon")], [k("Cyf")], lambda e: e.tensor_tensor(out=yf[p][:], in0=yf[p][:], in1=bon[p][:], op=ALU.add))
            yield
            K.op("dve", [k("Cyf"), k("Csza")], [k("Cmixed") + "a"], lambda e: e.tensor_tensor(out=mixed[p][:, 0:512], in0=yfl, in1=sza[p][:], op=ALU.mult))
            yield
            K.ops("pe", [k("Cmixed") + "a", k("Cmixed") + "b", "cst"], ["ptb"], [lambda e, kc=kc: e.transpose(out=PT[:, kc * 128:(kc + 1) * 128], in_=mixed[p][:, kc * 128:(kc + 1) * 128], identity=ident) for kc in range(8)])
            K.op("act", ["ptb"], [k("CmixT")], lambda e: e.copy(out=mixT[p][:].rearrange("p a b -> p (a b)"), in_=PT[:, :]))
            yield
            for n in range(2):
                K.ops("pe", [k("CmixT"), "Cwout"], ["pb%d" % (n0 + n)], [lambda e, kc=kc, n=n: e.matmul(PB[n0 + n][:, :], lhsT=mixT[p][:, kc, :], rhs=wout[:, kc, n * 512:(n + 1) * 512], start=(kc == 0), stop=(kc == 7)) for kc in range(8)])
            kr = k("Cst") + "r"
            for n in range(2):
                K.op("act", ["pb%d" % (n0 + n)], ["Cjunk", kr + "%d" % n], lambda e, n=n: e.activation(out=junk[:, :], in_=PB[n0 + n][:, :], func=AF.Square, accum_out=s_[:, 2 + n:3 + n] if False else s_[:, 0 + n:1 + n]))
            K.op("dve", [kr + "0", kr + "1"] + sk + [kc_], [kr], lambda e: e.tensor_tensor(out=s_[:, 2:3], in0=s_[:, 0:1], in1=s_[:, 1:2], op=ALU.add))
            K.op("dve", [kr], [kr], lambda e: e.tensor_scalar(out=s_[:, 2:3], in0=s_[:, 2:3], scalar1=1.0 / D, scalar2=NORM_EPS, op0=ALU.mult, op1=ALU.add))
            K.op("act", [kr], [kr], lambda e: e.sqrt(out=s_[:, 2:3], in_=s_[:, 2:3]))
            K.op("dve", [kr], [kr], lambda e: e.reciprocal(out=s_[:, 2:3], in_=s_[:, 2:3]))
            for n in range(2):
                K.op("dve", ["pb%d" % (n0 + n), kr, "Gt"], [k("Cot")], lambda e, n=n: e.scalar_tensor_tensor(out=ot[p][:, n * 512:(n + 1) * 512], in0=PB[n0 + n][:, :], scalar=s_[:, 2:3], in1=Gt[:, cv, n * 512:(n + 1) * 512], op0=ALU.mult, op1=ALU.mult))
            K.op("pool", [k("Cot"), k("Cxt")], [k("Cot")], lambda e: e.tensor_tensor(out=ot[p][:], in0=ot[p][:], in1=xt[p][:], op=ALU.add))
            yield
            K.dma("sp", k("Cot"), [k("Cot")], [], yap[a * 128:(a + 1) * 128, :], ot[p][:])
            yield


    queue = []
    for si, (xap, NT, cv, gt_base, yap) in enumerate(seqs):
        for a in range(NT):
            queue.append((xap, a, cv, gt_base + a, yap))
    K.run_streams(queue, lambda slot, it: tile(slot, *it), WC)


def _prep_shared(inp):
    f = np.float32
    g = lambda k: np.asarray(inp[k], dtype=f)
    fm = lambda v, nb: np.ascontiguousarray(v.reshape(nb, 128).T)
    pf = np.zeros((128, 80), f)
    b_mod = g("b_mod")[0]
    pf[:, 0:16] = fm(b_mod[0:2048], 16)
    pf[:, 16:24] = fm(g("ln_pre")[0], 8)
    ts = g("ts_mu")[0]
    pf[:, 24:36] = fm(ts[0], 12)
    pf[:, 36:48] = fm(ts[1], 12)
    for d in range(2):
        pf[:, 48 + 4 * d:52 + 4 * d] = fm(g("w0")[0, d], 4)
        pf[:, 56 + 4 * d:60 + 4 * d] = fm(g("a0")[0, d], 4)
        pf[:, 64 + 4 * d:68 + 4 * d] = fm(g("k_k")[0, d], 4)
        pf[:, 72 + 4 * d:76 + 4 * d] = fm(g("k_a")[0, d], 4)
    rows = np.zeros((128, 3072), f)
    rows[:, 0:1024] = g("ln_post")[0][None, :]
    rows[:, 1024:1536] = g("gn_w")[0][None, :]
    rows[:, 1536:2048] = g("gn_b")[0][None, :]
    rows[:, 2048:2560] = g("sgu_ln_g")[0][None, :]
    rows[:, 2560:3072] = g("sgu_ln_b")[0][None, :]
    bmg = np.ascontiguousarray(np.broadcast_to(b_mod[2048:3072][None, :], (2, 1024))).astype(f)
    lora_w = np.zeros((128, 4, 512), f)
    for d in range(2):
        lora_w[d * 64:(d + 1) * 64, d, :] = g("w_up")[0, d]
        lora_w[d * 64:(d + 1) * 64, 2 + d, :] = g("a_up")[0, d]
    rk = g("r_k")[0]
    rk_in = np.zeros((128, 2, 4, 2), f)
    for d in range(2):
        for q in range(4):
            for hp in range(2):
                rk_in[hp * 64:(hp + 1) * 64, d, q, hp] = rk[d, 2 * q + hp]
    rk_in = rk_in.reshape(128, 16)
    wsT = np.ascontiguousarray(np.transpose(g("w_s")[0], (2, 0, 1)))
    bs = np.ascontiguousarray(g("b_s")[0].T)
    s_idx = np.arange(128)[:, None]
    t_idx = np.arange(128)[None, :]
    consts = np.zeros((128, 6, 128), f)
    consts[:, 0] = (s_idx == t_idx)
    consts[:, 1] = (s_idx < t_idx)
    consts[:, 2] = (s_idx <= t_idx)
    consts[:, 3] = (s_idx > t_idx)
    consts[:, 4] = (s_idx >= t_idx)
    consts[:, 5] = ((s_idx // 64) == (t_idx // 64))
    sel = np.zeros((2, 2, 128), f)
    sel[0, 0, :] = 1.0
    sel[1, 1, :] = 1.0
    return dict(w_in=np.ascontiguousarray(g("w_in")[0]), w_out=np.ascontiguousarray(g("w_out")[0]),
                w_mod=np.ascontiguousarray(g("w_mod")[0]), pf=pf, rows=rows, bmg=bmg, lora_w=lora_w,
                rk_in=rk_in, wsT=wsT, bs=bs, consts=consts, sel=sel)


def _prep_core(inp, shared, i, NT_S, NP):
    f = np.float32
    m = dict(shared)
    m["xs"] = np.ascontiguousarray(np.asarray(inp["x_sample"][i], f)[:NT_S * 128])
    m["xp"] = np.ascontiguousarray(np.asarray(inp["x_prompt"][NP * i:NP * (i + 1)], f))
    c = np.asarray(inp["c"][i], f)
    cc = np.asarray(inp["c_ctx"], f)
    cvT = np.zeros((128, 8, 2), f)
    cvT[:, :, 0] = c.reshape(8, 128).T
    cvT[:, :, 1] = cc.reshape(8, 128).T
    m["cvT"] = cvT
    s0T = np.zeros((2, 128, 4, 64), f)
    for d, key in enumerate(["state_fwd", "state_bwd"]):
        S = np.asarray(inp[key][i, 0], f)
        S4 = S.reshape(4, 2, 64, 64)
        s0T[d] = np.transpose(S4, (1, 3, 0, 2)).reshape(128, 4, 64)
    m["s0T"] = s0T
    return m


_CACHE = {}


def kernel(**inputs):
    NT_S, NP, NCORE = 32, 4, 8
    if "nc" not in _CACHE:
        _CACHE["nc"] = build(NT_S, NP)
    nc = _CACHE["nc"]
    shared = _prep_shared(inputs)
    in_maps = [_prep_core(inputs, shared, i, NT_S, NP) for i in range(NCORE)]
    res = run_bass_kernel_spmd(nc, in_maps, core_ids=list(range(NCORE)))
    y_sample = np.stack([np.asarray(r["ys"]) for r in res.results], 0).astype(np.float32)
    y_prompt = np.concatenate([np.asarray(r["yp"]) for r in res.results], 0).astype(np.float32)
    nf, nb = [], []
    for r in res.results:
        nst = np.asarray(r["nst"])
        for p in range(NP):
            for d, lst in ((0, nf), (1, nb)):
                S = nst[p, d].reshape(2, 64, 4, 64)
                lst.append(np.transpose(S, (2, 0, 3, 1)).reshape(8, 64, 64)[None])
    new_f = np.stack(nf, 0).astype(np.float32)
    new_b = np.stack(nb, 0).astype(np.float32)
    return (y_prompt, y_sample, new_f, new_b)
```

```python
import contextlib
import numpy as np
import concourse.bass as bass
import concourse.mybir as mybir
from concourse.bass_utils import run_bass_kernel_spmd

F32 = mybir.dt.float32
BF16 = mybir.dt.bfloat16
AF = mybir.ActivationFunctionType
ALU = mybir.AluOpType
AX = mybir.AxisListType

D = 1024
DIN = 3840
NORM_EPS = 1e-6
GN_EPS = 6.4e-4
L2_EPS = 1e-12
EXPM05 = float(np.exp(-0.5))
C2H = 0.5 * EXPM05
VOFF = 15.0


class Sched:
    def __init__(self, nc, es):
        self.nc = nc
        self.es = es
        self.engs = dict(pe=nc.tensor, dve=nc.vector, act=nc.scalar, pool=nc.gpsimd, sp=nc.sync)
        self.sems = {e: es.enter_context(nc.semaphore("sem_" + e)) for e in self.engs}
        self.cnt = {e: 0 for e in self.engs}
        self.lastw = {}
        self.readers = {}
        self.seen = {e: {} for e in self.engs}
        self.chans = {}
        self.semobj = {}
        self.clock = {e: 0.0 for e in self.engs}
        self.ttime = {}
        self.lastfin = 0.0
        self.cost = dict(pe=0.09, dve=0.55, act=0.45, pool=1.1, sp=0.1)

    def _time(self, eng, needs, tok, cost):
        ready = 0.0
        for sname, val in needs.items():
            t = self.ttime.get((sname, val), 0.0)
            if t > ready:
                ready = t
        start = max(self.clock[eng], ready)
        fin = start + cost
        self.clock[eng] = fin if tok[0].startswith("sem_") else start + 0.1
        self.ttime[tok] = fin
        if fin > self.lastfin:
            self.lastfin = fin

    def _need(self, eng, needs):
        for sname, val in needs.items():
            if self.seen[eng].get(sname, 0) >= val:
                continue
            self.engs[eng].wait_ge(self.semobj[sname], val)
            self.seen[eng][sname] = val

    def _collect(self, eng, reads, writes):
        needs = {}

        def add(tok):
            if tok is None:
                return
            s, v = tok
            if s == "sem_pe" and eng == "pe":
                return
            if needs.get(s, 0) < v:
                needs[s] = v

        for b in reads:
            add(self.lastw.get(b))
        for b in writes:
            add(self.lastw.get(b))
            for s, v in self.readers.get(b, {}).items():
                add((s, v))
        return needs

    def _record(self, tok, reads, writes):
        s, v = tok
        for b in reads:
            self.readers.setdefault(b, {})[s] = v
        for b in writes:
            self.lastw[b] = tok
            self.readers[b] = {}

    def op(self, eng, reads, writes, fn, c=None):
        needs = self._collect(eng, reads, writes)
        self._need(eng, needs)
        inst = fn(self.engs[eng])
        self.cnt[eng] += 1
        sname = "sem_" + eng
        self.semobj[sname] = self.sems[eng]
        inst.then_inc(self.sems[eng], 1)
        self._time(eng, needs, (sname, self.cnt[eng]), self.cost[eng] if c is None else c)
        self._record((sname, self.cnt[eng]), reads, writes)

    def ops(self, eng, reads, writes, fns):
        needs = self._collect(eng, reads, writes)
        self._need(eng, needs)
        inst = None
        for fn in fns:
            inst = fn(self.engs[eng])
        self.cnt[eng] += 1
        sname = "sem_" + eng
        self.semobj[sname] = self.sems[eng]
        inst.then_inc(self.sems[eng], 1)
        self._time(eng, needs, (sname, self.cnt[eng]), self.cost[eng] * len(fns))
        self._record((sname, self.cnt[eng]), reads, writes)

    def dma(self, eng, chan_key, reads, writes, out, in_):
        if chan_key not in self.chans:
            sem = self.es.enter_context(self.nc.semaphore("dch_%d" % len(self.chans)))
            self.chans[chan_key] = [sem, 0, "dch_%d" % len(self.chans)]
            self.semobj[self.chans[chan_key][2]] = sem
        ch = self.chans[chan_key]
        needs = self._collect(eng, reads, writes)
        self._need(eng, needs)
        self.engs[eng].dma_start(out=out, in_=in_).then_inc(ch[0], 16)
        ch[1] += 16
        self._time(eng, needs, (ch[2], ch[1]), 2.5)
        self._record((ch[2], ch[1]), reads, writes)

    def run_streams(self, queue, make_gen, W, before=None, compat=None, offset=0.0):
        active, free, qi = [], list(range(W)), 0
        while qi < len(queue) or active:
            while free and qi < len(queue):
                if compat is not None and not compat(queue[qi], [a[3] for a in active]):
                    break
                if before is not None:
                    before(queue[qi])
                slot = free.pop(0)
                vt0 = min([a[2] for a in active]) if active else min(self.clock.values())
                if qi < W:
                    vt0 += qi * offset
                active.append([slot, make_gen(slot, queue[qi]), vt0, queue[qi]])
                qi += 1
            item = min(active, key=lambda a: a[2])
            self.lastfin = 0.0
            try:
                next(item[1])
                if self.lastfin > 0.0:
                    item[2] = self.lastfin
                else:
                    item[2] += 0.5
            except StopIteration:
                active.remove(item)
                free.append(item[0])

    def barrier(self, engines=("pe", "dve", "act", "pool", "sp")):
        needs = {}
        for e in self.engs:
            if self.cnt[e] > 0:
                needs["sem_" + e] = self.cnt[e]
        for ch in self.chans.values():
            if ch[1] > 0:
                needs[ch[2]] = ch[1]
        for e in engines:
            n2 = {s: v for s, v in needs.items() if not (e == "pe" and s == "sem_pe")}
            self._need(e, n2)


def build(NT_S, NP, debug=False):
    nc = bass.Bass("TRN2", target_bir_lowering=False)
    LS = NT_S * 128
    NTILES = NT_S + 2 * NP
    okind = "ExternalOutput" if debug else "Internal"

    def din(name, shape, dt=F32):
        return nc.dram_tensor(name, list(shape), dt, kind="ExternalInput").ap()

    def dout(name, shape, dt=F32):
        return nc.dram_tensor(name, list(shape), dt, kind="ExternalOutput").ap()

    def dscr(name, shape, dt):
        if debug:
            return nc.dram_tensor(name, list(shape), dt, kind="ExternalOutput").ap()
        return nc.dram_tensor(name, list(shape), dt).ap()

    xs = din("xs", [LS, D])
    xp = din("xp", [NP, 256, D])
    cvT = din("cvT", [128, 8, 2])
    w_in = din("w_in", [D, DIN])
    w_out = din("w_out", [D, D])
    w_mod = din("w_mod", [D, 3 * D])
    pf = din("pf", [128, 80])
    rows = din("rows", [128, 3072])
    bmg = din("bmg", [2, 1024])
    lora_w = din("lora_w", [128, 4, 512])
    rk_in = din("rk_in", [128, 16])
    wsT_in = din("wsT", [128, 8, 128])
    bs_in = din("bs", [128, 8])
    consts = din("consts", [128, 6, 128])
    sel_in = din("sel", [2, 2, 128])
    s0T = din("s0T", [2, 128, 4, 64])

    ys = dout("ys", [LS, D])
    yp = dout("yp", [NP, 256, D])
    nst = dout("nst", [NP, 2, 128, 4, 64])

    st_rkv = dscr("st_rkv", [NTILES, 128, 12, 128], BF16)
    st_low = dscr("st_low", [NTILES, 128, 2, 128], BF16)
    st_vt = dscr("st_vt", [NTILES, 128, 512], BF16)
    st_sza = dscr("st_sza", [NTILES, 128, 512], BF16)
    st_mixb = dscr("st_mixb", [NTILES, 128, 512], BF16)
    st_y = dscr("st_y", [2, NTILES, 128, 512], F32)
    st_bc = dscr("st_bc", [2, NTILES, 128, 8], F32)
    st_gt = dscr("st_gt", [128, 2, 1024], F32)

    seqs = [(xs, NT_S, 0, 0, ys)]
    for p in range(NP):
        seqs.append((xp[p], 2, 1, NT_S + 2 * p, yp[p]))

    with contextlib.ExitStack() as es:
        K = Sched(nc, es)
        K.debug = debug

        def dump(name, key, ap, shape, dt=F32):
            if not debug:
                return
            dd = nc.dram_tensor("dbg_" + name, list(shape), dt, kind="ExternalOutput").ap()
            K.dma("sp", "dbgch", [key], [], dd, ap)
        K.dump = dump

        def sb(name, shape, dt, stack=es):
            return stack.enter_context(nc.sbuf_tensor(name, list(shape), dt))

        def ps(name, shape, dt, stack=es):
            return stack.enter_context(nc.psum_tensor(name, list(shape), dt))

        PBIG = ps("pbig", [128, 7 * 512], F32)
        PB = [PBIG[:, i * 512:(i + 1) * 512] for i in range(7)]
        PT = ps("ptb", [128, 1024], BF16)

        pf_t = sb("pf_t", [128, 80], F32)
        cst = sb("cst", [128, 6, 128], BF16)
        sel_t = sb("sel_t", [2, 2, 128], F32)
        bs_t = sb("bs_t", [128, 8], F32)
        wsT = sb("wsT_sb", [128, 8, 128], BF16)
        lora = sb("lora", [128, 4, 512], BF16)
        rkm = sb("rkm", [128, 16], BF16)
        gfm = sb("gfm", [128, 8, 2], F32)
        shfm = sb("shfm", [128, 8, 2], F32)
        cs0 = sb("cs0", [128, 12], F32)
        omka = sb("omka", [128, 8], F32)
        mA = sb("mA", [128, 2, 4, 128], BF16)
        mT = sb("mT", [128, 2, 2, 128], BF16)

        K.dma("sp", "pf_t", [], ["pf_t"], pf_t[:], pf)
        K.dma("sp", "sel_t", [], ["sel_t"], sel_t[:], sel_in)
        K.dma("sp", "bs_t", [], ["bs_t"], bs_t[:], bs_in)
        K.op("dve", ["pf_t"], ["cs0"], lambda e: e.tensor_tensor(out=cs0[:], in0=pf_t[:, 24:36], in1=pf_t[:, 36:48], op=ALU.add))
        K.op("dve", ["cs0"], ["cs0"], lambda e: e.tensor_scalar(out=cs0[:], in0=cs0[:], scalar1=-1.0, scalar2=1.0, op0=ALU.mult, op1=ALU.add))
        K.op("dve", ["pf_t"], ["omka"], lambda e: e.tensor_scalar(out=omka[:], in0=pf_t[:, 72:80], scalar1=-1.0, scalar2=1.0, op0=ALU.mult, op1=ALU.add))

        with contextlib.ExitStack() as esA:
            win = sb("win", [128, 8, DIN], BF16, esA)
            rows_t = sb("rows_t", [128, 3072], F32, esA)
            with contextlib.ExitStack() as es0:
                cst_f = sb("cst_f", [128, 6, 128], F32, es0)
                Gt = sb("Gt", [128, 2, 1024], F32, es0)
                K.dma("sp", "rows_t", [], ["rows_t"], rows_t[:], rows)
                K.dma("sp", "cst_f", [], ["cst_f"], cst_f[:], consts)
                K.op("dve", ["cst_f"], ["cst"], lambda e: e.tensor_copy(out=cst[:], in_=cst_f[:]))
                for d, (s_i, i_i, t_i) in enumerate([(1, 2, 3), (3, 4, 1)]):
                    for j in range(4):
                        src = s_i if j % 2 == 0 else i_i
                        K.op("pool", ["cst_f"], ["mA"], lambda e, d=d, j=j, src=src: e.tensor_copy(out=mA[:, d, j, :], in_=cst_f[:, src, :]))
                    for j in range(2):
                        K.op("pool", ["cst_f"], ["mT"], lambda e, d=d, j=j, t_i=t_i: e.tensor_copy(out=mT[:, d, j, :], in_=cst_f[:, t_i, :]))
                wst = [sb("wst%d" % i, [128, DIN], F32, es0) for i in range(2)]
                tmpf = sb("tmpf", [128, 4, 512], F32, es0)
                cv_t = sb("cv_t", [128, 8, 2], F32, es0)
                scv = sb("scv", [128, 8, 2], F32, es0)
                bmg_t = sb("bmg_t", [2, 1024], F32, es0)
                grow = sb("grow", [2, 1024], F32, es0)
                modfm = sb("modfm", [128, 16, 2], F32, es0)
                K.dma("sp", "cv_t", [], ["cv_t"], cv_t[:], cvT)
                K.dma("sp", "bmg_t", [], ["bmg_t"], bmg_t[:], bmg)
                K.op("act", ["cv_t"], ["scv"], lambda e: e.activation(out=scv[:], in_=cv_t[:], func=AF.Silu))
                K.dma("sp", "tmpf", [], ["tmpf"], tmpf[:], lora_w)
                K.op("dve", ["tmpf"], ["lora"], lambda e: e.tensor_copy(out=lora[:], in_=tmpf[:]))
                K.dma("sp", "tmpf", ["tmpf"], ["tmpf"], tmpf[:, 0, 0:16], rk_in)
                K.op("dve", ["tmpf"], ["rkm"], lambda e: e.tensor_copy(out=rkm[:], in_=tmpf[:, 0, 0:16]))
                K.dma("sp", "tmpf", [], ["tmpf"], tmpf[:, 0:2, :].rearrange("p a b -> p (a b)"), wsT_in.rearrange("p g q -> p (g q)"))
                K.op("dve", ["tmpf"], ["wsT"], lambda e: e.tensor_copy(out=wsT[:].rearrange("p g q -> p (g q)"), in_=tmpf[:, 0:2, :].rearrange("p a b -> p (a b)")))
                wsi = [sb("wsi%d" % i, [128, DIN], F32, es0) for i in range(2)]

                def wdma(kc):
                    K.dma("act", "wsi%d" % (kc % 2), [], ["wsi%d" % (kc % 2)], wsi[kc % 2][:, :], w_in[kc * 128:(kc + 1) * 128, :])
                wdma(0)
                wdma(1)
                for kc in range(8):
                    K.op("act", ["wsi%d" % (kc % 2)], ["win"], lambda e, kc=kc: e.copy(out=win[:, kc, :], in_=wsi[kc % 2][:, :]))
                    if kc + 2 < 8:
                        wdma(kc + 2)
                for kc in range(8):
                    wm = wst[kc % 2]
                    K.dma("sp", "wst%d" % (kc % 2), [], ["wst%d" % (kc % 2)], wm[:, 0:3072], w_mod[kc * 128:(kc + 1) * 128, :])
                    fns = []
                    for blk in range(16):
                        fns.append(lambda e, blk=blk, kc=kc, wm=wm: e.matmul(PB[0][:, blk * 2:blk * 2 + 2], lhsT=wm[:, blk * 128:(blk + 1) * 128], rhs=scv[:, kc, :], start=(kc == 0 and blk == 0), stop=(kc == 7), skip_group_check=True))
                    for n in range(2):
                        fns.append(lambda e, n=n, kc=kc, wm=wm: e.matmul(PB[1 + n][0:2, :], lhsT=scv[:, kc, :], rhs=wm[:, 2048 + n * 512:2048 + (n + 1) * 512], start=(kc == 0), stop=(kc == 7)))
                    K.ops("pe", ["wst%d" % (kc % 2), "scv"], ["pb0", "pb1", "pb2"], fns)
                K.op("dve", ["pb0", "pf_t"], ["modfm"], lambda e: e.tensor_tensor(out=modfm[:], in0=PB[0][:, 0:32].rearrange("p (b c) -> p b c", c=2), in1=pf_t[:, 0:16].unsqueeze(2).to_broadcast([128, 16, 2]), op=ALU.add))
                K.op("dve", ["modfm"], ["shfm"], lambda e: e.tensor_copy(out=shfm[:], in_=modfm[:, 0:8, :]))
                K.op("dve", ["modfm"], ["gfm"], lambda e: e.tensor_scalar(out=gfm[:], in0=modfm[:, 8:16, :], scalar1=1.0, scalar2=None, op0=ALU.add))
                K.op("dve", ["gfm", "pf_t"], ["gfm"], lambda e: e.tensor_tensor(out=gfm[:], in0=gfm[:], in1=pf_t[:, 16:24].unsqueeze(2).to_broadcast([128, 8, 2]), op=ALU.mult))
                for n in range(2):
                    K.op("dve", ["pb%d" % (1 + n), "bmg_t"], ["grow"], lambda e, n=n: e.tensor_tensor(out=grow[:, n * 512:(n + 1) * 512], in0=PB[1 + n][0:2, :], in1=bmg_t[:, n * 512:(n + 1) * 512], op=ALU.add))
                for cv in range(2):
                    for n in range(2):
                        K.ops("pe", ["grow", "sel_t"], ["pb%d" % (3 + n)], [lambda e, cv=cv, n=n: e.matmul(PB[3 + n][:, :], lhsT=sel_t[:, cv, :], rhs=grow[:, n * 512:(n + 1) * 512], start=True, stop=True)])
                        K.op("dve", ["pb%d" % (3 + n), "rows_t"], ["Gt"], lambda e, cv=cv, n=n: e.tensor_tensor(out=Gt[:, cv, n * 512:(n + 1) * 512], in0=PB[3 + n][:, :], in1=rows_t[:, n * 512:(n + 1) * 512], op=ALU.mult))
                K.dma("sp", "Gt", ["Gt"], [], st_gt, Gt[:])
                dump("gfm", "gfm", gfm[:], [128, 8, 2])
                dump("shfm", "shfm", shfm[:], [128, 8, 2])
                dump("Gt", "Gt", Gt[:], [128, 2, 1024])
                dump("scv", "scv", scv[:], [128, 8, 2])
                K.barrier()

            phase_a(nc, K, esA, sb, PB, PT, seqs, win, dict(
                pf_t=pf_t, rows_t=rows_t, cst=cst, bs_t=bs_t, wsT=wsT, gfm=gfm, shfm=shfm, cs0=cs0,
                st_rkv=st_rkv, st_low=st_low, st_vt=st_vt, st_sza=st_sza, st_mixb=st_mixb))
            K.barrier()

        with contextlib.ExitStack() as esB:
            phase_b(nc, K, esB, sb, PB, PT, seqs, dict(
                pf_t=pf_t, cst=cst, lora=lora, rkm=rkm, omka=omka, mA=mA, mT=mT,
                st_rkv=st_rkv, st_low=st_low, st_vt=st_vt, st_y=st_y, st_bc=st_bc, s0T=s0T, nst=nst, PBIG=PBIG), NT_S, NP)
            K.barrier()
        with contextlib.ExitStack() as esC:
            phase_c(nc, K, esC, sb, PB, PT, seqs, dict(rows=rows, st_gt=st_gt, cst=cst, w_out=w_out,
                    st_y=st_y, st_bc=st_bc, st_vt=st_vt, st_sza=st_sza, st_mixb=st_mixb))
        K.barrier(engines=("sp",))
    return nc


def phase_a(nc, K, esA, sb, PB, PT, seqs, win, R):
    pf_t, rows_t, cst, bs_t, wsT = R["pf_t"], R["rows_t"], R["cst"], R["bs_t"], R["wsT"]
    gfm, shfm, cs0 = R["gfm"], R["shfm"], R["cs0"]
    ident = cst[:, 0, :]
    NXB = 2
    xt = [sb("xt%d" % i, [128, D], F32, esA) for i in range(NXB)]
    xn = [sb("xn%d" % i, [128, D], BF16, esA) for i in range(4)]
    sq = sb("sqj", [128, D], BF16, esA)
    stat = sb("statA", [128, 16], F32, esA)
    hT = [sb("hT%d" % i, [128, 8, 512], BF16, esA) for i in range(2)]
    raw = [sb("raw%d" % i, [128, 12, 514], BF16, esA) for i in range(2)]
    rkvp = sb("rkvp", [128, 12, 512], BF16, esA)
    tmps = [sb("shtmp%d" % i, [128, 512], F32, esA) for i in range(2)]
    low = [sb("low%d" % i, [128, 2, 512], BF16, esA) for i in range(2)]
    sza = [sb("sza%d" % i, [128, 512], BF16, esA) for i in range(4)]
    u_t = [sb("u_t%d" % i, [128, 512], BF16, esA) for i in range(4)]
    szb = [sb("szb%d" % i, [128, 512], BF16, esA) for i in range(4)]
    vb = [sb("vb%d" % i, [128, 512], F32, esA) for i in range(4)]
    vn = [sb("vn%d" % i, [128, 512], BF16, esA) for i in range(4)]
    mixb = [sb("mixb%d" % i, [128, 512], BF16, esA) for i in range(4)]
    vts = [sb("vts%d" % i, [128, 512], BF16, esA) for i in range(2)]
    lnst = [sb("lnst%d" % i, [128, 8], F32, esA) for i in range(4)]

    cnt = {"x": 0, "tile": 0, "grp": 0}

    def run_lanes(lanes):
        lanes = [[list(g_), W_, []] for g_, W_ in lanes]
        while any(l[0] or l[2] for l in lanes):
            for l in lanes:
                while l[0] and len(l[2]) < l[1]:
                    l[2].append(l[0].pop(0))
                for g_ in list(l[2]):
                    try:
                        next(g_)
                    except StopIteration:
                        l[2].remove(g_)

    def shift_gen(ri, GS, nt_g, gt0):
        rw = raw[ri]
        rk = "raw%d" % ri
        for j in range(12):
            tk = "shtmp%d" % (j % 2)
            tm = tmps[j % 2]
            K.op("dve", [rk, "cs0"], [tk], lambda e, j=j, tm=tm: e.tensor_scalar(out=tm[:, 0:GS], in0=rw[:, j, 1:GS + 1], scalar1=cs0[:, j:j + 1], scalar2=None, op0=ALU.mult))
            K.op("dve", [rk, tk, "pf_t"], [tk], lambda e, j=j, tm=tm: e.scalar_tensor_tensor(out=tm[:, 0:GS], in0=rw[:, j, 0:GS], scalar=pf_t[:, 24 + j:25 + j], in1=tm[:, 0:GS], op0=ALU.mult, op1=ALU.add))
            K.op("dve", [rk, tk, "pf_t"], ["rkvp"], lambda e, j=j, tm=tm: e.scalar_tensor_tensor(out=rkvp[:, j, 0:GS], in0=rw[:, j, 2:GS + 2], scalar=pf_t[:, 36 + j:37 + j], in1=tm[:, 0:GS], op0=ALU.mult, op1=ALU.add))
            yield
        for a in range(nt_g):
            gt = gt0 + a
            K.dma("sp", "rkvp", ["rkvp"], [], R["st_rkv"][gt], rkvp[:, :, a * 128:(a + 1) * 128])
            vs = vts[gt % 2]
            vk = "vts%d" % (gt % 2)
            K.ops("pe", ["rkvp", "cst"], ["ptb"], [lambda e, a=a, q=q: e.transpose(out=PT[:, q * 128:(q + 1) * 128], in_=rkvp[:, 8 + q, a * 128:(a + 1) * 128], identity=ident) for q in range(4)])
            K.op("act", ["ptb"], [vk], lambda e, vs=vs: e.copy(out=vs[:], in_=PT[:, 0:512]))
            K.dma("act", vk, [vk], [], R["st_vt"][gt], vs[:])
            yield

    def front_compute(xap, g, a):
        t0 = (g * (GS_cur[0] // 128) + a) * 128
        xi = cnt["x"] % NXB
        cnt["x"] += 1
        xk = "xt%d" % xi
        K.dma("sp", xk, [], [xk], xt[xi][:], xap[t0:t0 + 128, :])
        sc = cnt["tile"] % 8
        cnt["tile"] += 1
        nk = "xn%d" % a
        K.op("act", [xk], [nk, "statA%d" % sc], lambda e: e.activation(out=xn[a][:], in_=xt[xi][:], func=AF.Square, accum_out=stat[:, sc:sc + 1]))
        K.op("dve", ["statA%d" % sc], ["statA%d" % sc], lambda e: e.tensor_scalar(out=stat[:, sc:sc + 1], in0=stat[:, sc:sc + 1], scalar1=1.0 / D, scalar2=NORM_EPS, op0=ALU.mult, op1=ALU.add))
        K.op("act", ["statA%d" % sc], ["statA%d" % sc], lambda e: e.sqrt(out=stat[:, sc:sc + 1], in_=stat[:, sc:sc + 1]))
        K.op("dve", ["statA%d" % sc], ["statA%d" % sc], lambda e: e.reciprocal(out=stat[:, sc:sc + 1], in_=stat[:, sc:sc + 1]))
        K.op("act", [xk, "statA%d" % sc], [nk], lambda e: e.activation(out=xn[a][:], in_=xt[xi][:], func=AF.Copy, scale=stat[:, sc:sc + 1]))

    def front_transpose(a, cv, hTg, hk):
        nk = "xn%d" % a
        K.ops("pe", [nk, "cst"], ["ptb"], [lambda e, kc=kc: e.transpose(out=PT[:, kc * 128:(kc + 1) * 128], in_=xn[a][:, kc * 128:(kc + 1) * 128], identity=ident) for kc in range(8)])
        for kc in range(8):
            K.op("dve", ["ptb", "gfm", "shfm"], [hk], lambda e, kc=kc: e.tensor_scalar(out=hTg[:, kc, a * 128:(a + 1) * 128], in0=PT[:, kc * 128:(kc + 1) * 128], scalar1=gfm[:, kc, cv:cv + 1], scalar2=shfm[:, kc, cv:cv + 1], op0=ALU.mult, op1=ALU.add))

    GS_cur = [512]

    def tm_gen(s2, hTg, hk, a, gt):
        ls = lnst[s2]
        lsk = "lnst%d" % s2

        bank = lambda ci: 3 + (ci + a) % 4

        def proj(ci, c0):
            pbi = bank(ci)
            K.ops("pe", [hk, "win"], ["pb%d" % pbi], [lambda e, kc=kc: e.matmul(PB[pbi][:, :], lhsT=hTg[:, kc, a * 128:(a + 1) * 128], rhs=win[:, kc, c0:c0 + 512], start=(kc == 0), stop=(kc == 7)) for kc in range(8)])
        proj(0, 1536)
        K.op("act", ["pb%d" % bank(0)], ["sza%d" % s2], lambda e: e.activation(out=sza[s2][:], in_=PB[bank(0)][:, :], func=AF.Silu))
        yield
        K.dma("act", "sza%d" % s2, ["sza%d" % s2], [], R["st_sza"][gt], sza[s2][:])
        proj(1, 2304)
        K.op("act", ["pb%d" % bank(1)], ["u_t%d" % s2], lambda e: e.copy(out=u_t[s2][:], in_=PB[bank(1)][:, :]))
        yield
        proj(2, 2816)
        K.op("act", ["pb%d" % bank(2), lsk], ["vb%d" % s2, lsk], lambda e: e.activation(out=vb[s2][:], in_=PB[bank(2)][:, :], func=AF.Copy, accum_out=ls[:, 0:1]))
        yield
        proj(3, 3328)
        K.op("act", ["pb%d" % bank(3)], ["szb%d" % s2], lambda e: e.activation(out=szb[s2][:], in_=PB[bank(3)][:, :], func=AF.Silu))
        yield
        K.op("act", ["vb%d" % s2, lsk], ["sqj", lsk], lambda e: e.activation(out=sq[:, 0:512], in_=vb[s2][:], func=AF.Square, accum_out=ls[:, 1:2]))
        yield
        K.op("dve", [lsk], [lsk], lambda e: e.tensor_scalar(out=ls[:, 2:3], in0=ls[:, 0:1], scalar1=1.0 / 512, scalar2=None, op0=ALU.mult))
        yield
        K.op("dve", [lsk], [lsk], lambda e: e.tensor_tensor(out=ls[:, 3:4], in0=ls[:, 2:3], in1=ls[:, 2:3], op=ALU.mult))
        yield
        K.op("dve", [lsk], [lsk], lambda e: e.scalar_tensor_tensor(out=ls[:, 4:5], in0=ls[:, 1:2], scalar=1.0 / 512, in1=ls[:, 3:4], op0=ALU.mult, op1=ALU.subtract))
        yield
        K.op("dve", [lsk], [lsk], lambda e: e.tensor_scalar(out=ls[:, 5:6], in0=ls[:, 4:5], scalar1=NORM_EPS, scalar2=None, op0=ALU.add))
        yield
        K.op("act", [lsk], [lsk], lambda e: e.sqrt(out=ls[:, 5:6], in_=ls[:, 5:6]))
        yield
        K.op("dve", [lsk], [lsk], lambda e: e.reciprocal(out=ls[:, 5:6], in_=ls[:, 5:6]))
        yield
        K.op("dve", ["vb%d" % s2, lsk], ["vb%d" % s2], lambda e: e.tensor_scalar(out=vb[s2][:], in0=vb[s2][:], scalar1=ls[:, 2:3], scalar2=ls[:, 5:6], op0=ALU.subtract, op1=ALU.mult))
        yield
        K.op("pool", ["vb%d" % s2, "rows_t"], ["vb%d" % s2], lambda e: e.tensor_tensor(out=vb[s2][:], in0=vb[s2][:], in1=rows_t[:, 2048:2560], op=ALU.mult))
        yield
        K.op("pool", ["vb%d" % s2, "rows_t"], ["vn%d" % s2], lambda e: e.tensor_tensor(out=vn[s2][:], in0=vb[s2][:], in1=rows_t[:, 2560:3072], op=ALU.add))
        yield
        K.ops("pe", ["vn%d" % s2, "wsT"], ["pb0"], [lambda e, gg=gg: e.matmul(PB[0][:, gg * 64:(gg + 1) * 64], lhsT=wsT[:, gg, :], rhs=vn[s2][:, gg * 64:(gg + 1) * 64], start=True, stop=True) for gg in range(8)])
        K.op("dve", ["pb0", "bs_t"], ["vb%d" % s2], lambda e: e.tensor_tensor(out=vb[s2][:].rearrange("p (g c) -> p g c", c=64), in0=PB[0][:, :].rearrange("p (g c) -> p g c", c=64), in1=bs_t[:, :].unsqueeze(2).to_broadcast([128, 8, 64]), op=ALU.add))
        yield
        K.op("pool", ["vb%d" % s2, "u_t%d" % s2], ["vb%d" % s2], lambda e: e.tensor_tensor(out=vb[s2][:], in0=vb[s2][:], in1=u_t[s2][:], op=ALU.mult))
        yield
        K.op("pool", ["vb%d" % s2, "szb%d" % s2], ["mixb%d" % s2], lambda e: e.tensor_tensor(out=mixb[s2][:], in0=vb[s2][:], in1=szb[s2][:], op=ALU.mult))
        yield
        K.dma("sp", "mixb%d" % s2, ["mixb%d" % s2], [], R["st_mixb"][gt], mixb[s2][:])

    pre = None
    pending = [None]
    cnt["rawp"] = 0
    for si, (xap, NT, cv, gt_base, _y) in enumerate(seqs):
        GS = min(512, NT * 128)
        GS_cur[0] = GS
        nt_g = GS // 128
        NG = NT // nt_g
        nxt = seqs[si + 1] if si + 1 < len(seqs) else None
        nt_n = min(512, nxt[1] * 128) // 128 if nxt is not None else 0
        hbuf = {}
        if pre is None:
            hi = cnt["grp"] % 2
            cnt["grp"] += 1
            hbuf[0] = (hT[hi], "hT%d" % hi)
            for a in range(nt_g):
                front_compute(xap, 0, a)
            for a in range(nt_g):
                front_transpose(a, cv, *hbuf[0])
        else:
            hbuf[0] = pre
            pre = None
        for g in range(NG):
            hTg, hk = hbuf[g]
            rp = cnt["rawp"] % 2
            cnt["rawp"] += 1
            rw = raw[rp]
            rk = "raw%d" % rp
            if g == 0:
                K.op("pool", [], [rk], lambda e, rw=rw: e.memset(rw[:, :, 0:1], 0.0))
            lw = low[rp]
            lk = "low%d" % rp
            for bi, blk in enumerate(list(range(12)) + [16, 17]):
                pbi = bi % 3
                pk = "pb%d" % pbi
                K.ops("pe", [hk, "win"], [pk], [lambda e, kc=kc, blk=blk, pbi=pbi: e.matmul(PB[pbi][:, 0:GS], lhsT=win[:, kc, blk * 128:(blk + 1) * 128], rhs=hTg[:, kc, 0:GS], start=(kc == 0), stop=(kc == 7)) for kc in range(8)])
                if blk < 12:
                    K.op("act", [pk], [rk], lambda e, blk=blk, pbi=pbi, rw=rw: e.copy(out=rw[:, blk, 1:GS + 1], in_=PB[pbi][:, 0:GS]))
                elif blk == 16:
                    K.op("act", [pk], [lk], lambda e, pbi=pbi, lw=lw: e.activation(out=lw[:, 0, 0:GS], in_=PB[pbi][:, 0:GS], func=AF.Tanh))
                else:
                    K.op("act", [pk], [lk], lambda e, pbi=pbi, lw=lw: e.copy(out=lw[:, 1, 0:GS], in_=PB[pbi][:, 0:GS]))
                if g + 1 < NG and bi in (2, 5, 8, 11):
                    front_compute(xap, g + 1, (bi - 2) // 3)
                elif g + 1 == NG and nxt is not None and bi in (2, 5, 8, 11) and (bi - 2) // 3 < nt_n:
                    front_compute(nxt[0], 0, (bi - 2) // 3)
            for a in range(nt_g):
                gt = gt_base + g * nt_g + a
                K.dma("act", lk, [lk], [], R["st_low"][gt], lw[:, :, a * 128:(a + 1) * 128])
            gens = []
            if g > 0:
                rwp = raw[1 - rp]
                rkp = "raw%d" % (1 - rp)
                K.op("pool", [rk], [rkp], lambda e, rw=rw, rwp=rwp: e.tensor_copy(out=rwp[:, :, GS + 1:GS + 2], in_=rw[:, :, 1:2]))
                K.op("pool", [rkp], [rk], lambda e, rw=rw, rwp=rwp: e.tensor_copy(out=rw[:, :, 0:1], in_=rwp[:, :, GS:GS + 1]))
                gens.append(shift_gen(1 - rp, GS, nt_g, gt_base + (g - 1) * nt_g))
            elif pending[0] is not None:
                gens.append(pending[0])
                pending[0] = None
            tms = [tm_gen(a % 4, hTg, hk, a, gt_base + g * nt_g + a) for a in range(nt_g)]
            run_lanes([(gens, 1), (tms, 4)])
            if g + 1 < NG:
                hi = cnt["grp"] % 2
                cnt["grp"] += 1
                hbuf[g + 1] = (hT[hi], "hT%d" % hi)
                for a in range(nt_g):
                    front_transpose(a, cv, *hbuf[g + 1])
            elif nxt is not None:
                hi = cnt["grp"] % 2
                cnt["grp"] += 1
                pre = (hT[hi], "hT%d" % hi)
                for a in range(nt_n):
                    front_transpose(a, nxt[2], *pre)
        g = NG - 1
        K.op("pool", [], [rk], lambda e, rw=rw: e.memset(rw[:, :, GS + 1:GS + 2], 0.0))
        pending[0] = shift_gen(rp, GS, nt_g, gt_base + g * nt_g)
        if nxt is None:
            run_lanes([([pending[0]], 1)])
            pending[0] = None


def phase_b(nc, K, esB, sb, PB, PT, seqs, R, NT_S, NP):
    PBIG = R["PBIG"]
    pf_t, cst, lora, rkm, omka, mA, mT = R["pf_t"], R["cst"], R["lora"], R["rkm"], R["omka"], R["mA"], R["mT"]
    ident = cst[:, 0, :]

    WB = 3

    def T(name, shape, dt):
        return [sb("%s_%d" % (name, p), shape, dt, esB) for p in range(WB)]
    rk = T("Brk", [128, 8, 128], BF16)
    lowT = T("Blow", [128, 2, 128], BF16)
    Vt = T("Bvt", [128, 512], BF16)
    arm = T("Barm", [128, 4, 2, 2, 128], BF16)
    bT = T("BbT", [128, 4, 128], BF16)
    kT = T("BkT", [128, 4, 128], BF16)
    rkd = T("Brkd", [128, 4, 128], BF16)
    kk2 = T("Bkk2", [128, 4, 128], BF16)
    MM1 = T("BMM1", [128, 8, 2, 128], BF16)
    MM2 = T("BMM2", [128, 8, 2, 128], BF16)
    Q0T = T("BQ0T", [128, 8, 128], BF16)
    QR = [T("BQRa", [128, 8, 2, 128], BF16), T("BQRb", [128, 8, 2, 128], BF16)]
    QT = [T("BQTa", [128, 8, 128], BF16), T("BQTb", [128, 8, 128], BF16)]
    TT = T("BTT", [128, 8, 128], BF16)
    Bt = T("BBt", [128, 512], BF16)
    Kt = T("BKt", [128, 512], BF16)
    Zs = T("BZs", [128, 512], BF16)
    Us = T("BUs", [128, 512], BF16)
    ysb = T("Bysb", [128, 512], F32)
    bcs = T("Bbcs", [128, 8], F32)
    f32n = ["sg", "aa", "E1", "E2", "gam", "gamx", "gami", "kk"]
    F = {n: T("Bf_" + n, [128, 4, 128], F32) for n in f32n}
    F["tq"], F["rn"], F["kd"], F["kkn"] = F["sg"], F["E1"], F["E2"], F["kk"]
    tot = T("Btot", [128, 16], F32)
    gC = T("BgC", [128, 4], F32)
    ones = sb("Bones", [128, 128], F32, esB)
    S = [[[sb("BS_%d_%d_%d" % (sp, d, i), [128, 4, 64], F32, esB) for i in range(2)] for d in range(2)] for sp in range(2)]
    Sb = [[[sb("BSb_%d_%d_%d" % (sp, d, i), [128, 4, 64], BF16, esB) for i in range(2)] for d in range(2)] for sp in range(2)]
    t1s = [sb("Bt1_%d" % i, [128, 4, 64], F32, esB) for i in range(WB)]

    K.op("pool", [], ["Bones"], lambda e: e.memset(ones[:], 1.0))
    hpar = sb("Bhpar", [128, 34], F32, esB)
    K.op("pool", ["pf_t"], ["hpar"], lambda e: e.tensor_scalar(out=hpar[:, 0:16], in0=pf_t[:, 48:64], scalar1=0.5, scalar2=None, op0=ALU.mult))
    K.op("pool", ["hpar"], ["hpar"], lambda e: e.memset(hpar[:, 16:17], C2H))
    K.op("pool", ["hpar"], ["hpar"], lambda e: e.memset(hpar[:, 17:18], L2_EPS))
    K.op("pool", ["pf_t", "hpar"], ["hpar"], lambda e: e.tensor_scalar(out=hpar[:, 18:26], in0=pf_t[:, 72:80], scalar1=0.5, scalar2=None, op0=ALU.mult))
    K.op("pool", ["pf_t", "hpar"], ["hpar"], lambda e: e.tensor_scalar(out=hpar[:, 26:34], in0=pf_t[:, 72:80], scalar1=-0.5, scalar2=1.0, op0=ALU.mult, op1=ALU.add))
    for p in range(WB):
        K.op("dve", [], ["Barm_%d" % p], lambda e, p=p: e.memset(arm[p][:].rearrange("p a b c d -> p (a b c d)"), 0.0))

    chain_done = {}

    def visit(p, gt, d, sp, sidx, last_out_ap, ckey, cidx):
        t1 = t1s[p]
        b0, b1 = 2 * p, 2 * p + 1
        b2 = b0
        kt1 = "Bt1_%d" % p
        k = lambda n: "%s_%d" % (n, p)
        _al = dict(tq="sg", rn="E1", kd="E2", kkn="kk")
        fk = lambda n: "Bf_%s_%d" % (_al.get(n, n), p)
        Sin, Sbin = S[sp][d][sidx], Sb[sp][d][sidx]
        Sout, Sbout = S[sp][d][1 - sidx], Sb[sp][d][1 - sidx]
        kSin, kSbin = "BS_%d_%d_%d" % (sp, d, sidx), "BSb_%d_%d_%d" % (sp, d, sidx)
        kSout, kSbout = "BS_%d_%d_%d" % (sp, d, 1 - sidx), "BSb_%d_%d_%d" % (sp, d, 1 - sidx)
        c_w0, c_a0, c_kk, c_ka = 48 + 4 * d, 56 + 4 * d, 64 + 4 * d, 72 + 4 * d
        bc3 = lambda ap: ap.unsqueeze(2).to_broadcast([128, 4, 128])
        rT_ = rk[p][:, 0:4, :]
        kT_ = rk[p][:, 4:8, :]
        K.dma("sp", k("Brk"), [], [k("Brk")], rk[p][:], R["st_rkv"][gt][:, 0:8, :])
        yield
        K.dma("sp", k("Blow"), [], [k("Blow")], lowT[p][:], R["st_low"][gt])
        yield
        K.dma("sp", k("Bvt"), [], [k("Bvt")], Vt[p][:], R["st_vt"][gt])
        yield
        K.ops("pe", [k("Blow"), "lora"], ["pb%d" % b0], [lambda e, q=q: e.matmul(PB[b0][:, q * 128:(q + 1) * 128], lhsT=lora[:, d, q * 128:(q + 1) * 128], rhs=lowT[p][:, 0, :], start=True, stop=True) for q in range(4)])
        K.ops("pe", [k("Blow"), "lora"], ["pb%d" % b1], [lambda e, q=q: e.matmul(PB[b1][:, q * 128:(q + 1) * 128], lhsT=lora[:, 2 + d, q * 128:(q + 1) * 128], rhs=lowT[p][:, 1, :], start=True, stop=True) for q in range(4)])
        yield
        TH, THA, C2, X2 = F["sg"][p], F["aa"][p], F["E1"][p], F["E2"][p]
        for q in range(4):
            K.op("act", ["pb%d" % b0, "hpar"], [fk("sg")], lambda e, q=q: e.activation(out=TH[:, q, :], in_=PB[b0][:, q * 128:(q + 1) * 128], func=AF.Tanh, scale=0.5, bias=hpar[:, 4 * d + q:4 * d + q + 1]))
        yield
        for q in range(4):
            K.op("act", ["pb%d" % b1, "hpar"], [fk("aa")], lambda e, q=q: e.activation(out=THA[:, q, :], in_=PB[b1][:, q * 128:(q + 1) * 128], func=AF.Tanh, scale=0.5, bias=hpar[:, 8 + 4 * d + q:8 + 4 * d + q + 1]))
        yield
        for q in range(4):
            if d == 0:
                K.op("dve", [fk("sg"), "Bones"], [fk("E1")], lambda e, q=q: e.tensor_tensor_scan(out=C2[:, q, :], data0=ones[:, :], data1=TH[:, q, :], initial=0.0, op0=ALU.add, op1=ALU.add))
            else:
                K.op("dve", [fk("sg"), "Bones"], [fk("E1")], lambda e, q=q: e.tensor_tensor_scan(out=C2[:, q, ::-1], data0=ones[:, :], data1=TH[:, q, ::-1], initial=0.0, op0=ALU.add, op1=ALU.add))
        yield
        K.op("pool", [fk("E1"), fk("sg")], [fk("E2")], lambda e: e.tensor_tensor(out=X2[:], in0=C2[:], in1=TH[:], op=ALU.subtract))
        yield
        tcol = 127 if d == 0 else 0
        K.op("pool", [fk("E1")], [k("Btot")], lambda e: e.tensor_scalar(out=tot[p][:, 0:4], in0=C2[:, :, tcol], scalar1=-C2H, scalar2=None, op0=ALU.mult))
        yield
        K.op("act", [k("Btot")], [k("BgC")], lambda e: e.activation(out=gC[p][:], in_=tot[p][:, 0:4], func=AF.Exp))
        yield
        K.op("act", [fk("E1")], [fk("gam")], lambda e: e.activation(out=F["gam"][p][:], in_=C2[:], func=AF.Exp, scale=-C2H))
        yield
        K.op("act", [fk("E2"), "hpar"], [fk("gamx")], lambda e: e.activation(out=F["gamx"][p][:], in_=X2[:], func=AF.Exp, scale=-C2H, bias=hpar[:, 16:17]))
        yield
        K.op("act", [fk("E1")], [fk("gami")], lambda e: e.activation(out=F["gami"][p][:], in_=C2[:], func=AF.Exp, scale=C2H))
        yield
        K.op("dve", [k("Brk"), "pf_t"], [fk("kk")], lambda e: e.tensor_tensor(out=F["kk"][p][:], in0=kT_, in1=bc3(pf_t[:, c_kk:c_kk + 4]), op=ALU.mult))
        yield
        K.op("pool", [fk("kk")], [k("Bkk2")], lambda e: e.tensor_tensor(out=kk2[p][:], in0=F["kk"][p][:], in1=F["kk"][p][:], op=ALU.mult))
        yield
        K.ops("pe", [k("Bkk2"), "cst"], ["pb%d" % b2], [lambda e, q=q: e.matmul(PB[b2][:, q * 128:(q + 1) * 128], lhsT=cst[:, 5, :], rhs=kk2[p][:, q, :], start=True, stop=True) for q in range(4)])
        yield
        K.op("act", ["pb%d" % b2, "hpar"], [fk("rn")], lambda e: e.activation(out=F["rn"][p][:].rearrange("p a b -> p (a b)"), in_=PB[b2][:, :], func=AF.Ln, bias=hpar[:, 17:18]))
        yield
        K.op("act", [fk("rn")], [fk("rn")], lambda e: e.activation(out=F["rn"][p][:], in_=F["rn"][p][:], func=AF.Exp, scale=-0.5))
        yield
        K.op("pool", [fk("kk"), fk("rn")], [fk("kkn")], lambda e: e.tensor_tensor(out=F["kkn"][p][:], in0=F["kk"][p][:], in1=F["rn"][p][:], op=ALU.mult))
        yield
        for q in range(4):
            K.op("pool", [fk("aa"), "hpar"], [fk("tq")], lambda e, q=q: e.tensor_scalar(out=F["tq"][p][:, q, :], in0=THA[:, q, :], scalar1=hpar[:, 18 + 4 * d + q:19 + 4 * d + q], scalar2=hpar[:, 26 + 4 * d + q:27 + 4 * d + q], op0=ALU.mult, op1=ALU.add))
        yield
        K.op("pool", [fk("tq"), k("Brk")], [fk("kd")], lambda e: e.tensor_tensor(out=F["kd"][p][:], in0=F["tq"][p][:], in1=kT_, op=ALU.mult))
        yield
        for hp in range(2):
            rs = slice(hp * 64, (hp + 1) * 64)
            K.op("dve", [fk("kkn"), fk("gamx")], [k("Barm")], lambda e, rs=rs, hp=hp: e.scalar_tensor_tensor(out=arm[p][rs, :, hp, 0, :], in0=F["kkn"][p][rs, :, :], scalar=-1.0, in1=F["gamx"][p][rs, :, :], op0=ALU.mult, op1=ALU.mult))
            yield
            K.op("pool", [k("Brk"), fk("gam")], [k("Barm")], lambda e, rs=rs, hp=hp: e.tensor_tensor(out=arm[p][rs, :, hp, 1, :], in0=rk[p][rs, 0:4, :], in1=F["gam"][p][rs, :, :], op=ALU.mult))
            yield
        K.op("dve", [fk("kkn"), fk("aa")], [fk("tq")], lambda e: e.scalar_tensor_tensor(out=F["tq"][p][:], in0=THA[:], scalar=1.0, in1=F["kkn"][p][:], op0=ALU.add, op1=ALU.mult))
        yield
        K.op("dve", [fk("tq"), fk("gami")], [k("BbT")], lambda e: e.scalar_tensor_tensor(out=bT[p][:], in0=F["tq"][p][:], scalar=0.5, in1=F["gami"][p][:], op0=ALU.mult, op1=ALU.mult))
        yield
        K.op("pool", [fk("kd"), fk("gami")], [k("BkT")], lambda e: e.tensor_tensor(out=kT[p][:], in0=F["kd"][p][:], in1=F["gami"][p][:], op=ALU.mult))
        yield
        K.op("pool", [fk("kd"), k("Brk")], [k("Brkd")], lambda e: e.tensor_tensor(out=rkd[p][:], in0=F["kd"][p][:], in1=rT_, op=ALU.mult))
        yield
        K.ops("pe", [k("Brkd"), "rkm"], ["pb6"], [lambda e, q=q: e.matmul(PB[6][:, q * 2:q * 2 + 2], lhsT=rkd[p][:, q, :], rhs=rkm[:, d * 8 + q * 2:d * 8 + q * 2 + 2], start=True, stop=True) for q in range(4)])
        K.op("act", ["pb6"], [k("Bbcs")], lambda e: e.copy(out=bcs[p][:], in_=PB[6][:, 0:8]))
        yield
        K.dma("act", k("Bbcs"), [k("Bbcs")], [], R["st_bc"][d, gt], bcs[p][:])
        yield
        K.ops("pe", [k("BbT"), k("BkT"), "cst"], ["ptb"],
              [lambda e, q=q: e.transpose(out=PT[:, q * 128:(q + 1) * 128], in_=bT[p][:, q, :], identity=ident) for q in range(4)] +
              [lambda e, q=q: e.transpose(out=PT[:, 512 + q * 128:512 + (q + 1) * 128], in_=kT[p][:, q, :], identity=ident) for q in range(4)])
        K.op("act", ["ptb"], [k("BBt")], lambda e: e.copy(out=Bt[p][:], in_=PT[:, 0:512]))
        K.op("act", ["ptb"], [k("BKt")], lambda e: e.copy(out=Kt[p][:], in_=PT[:, 512:1024]))
        yield
        for q in range(4):
            fa, fb, fc = [], [], []
            for hp in range(2):
                rhsA = arm[p][:, q, hp, :, :].rearrange("p a t -> p (a t)")
                fa.append(lambda e, hp=hp, rhsA=rhsA: e.matmul(PB[b0][:, hp * 256:(hp + 1) * 256], lhsT=bT[p][:, q, :], rhs=rhsA, start=True, stop=True))
                fb.append(lambda e, hp=hp, rhsA=rhsA: e.matmul(PB[b1][:, hp * 256:(hp + 1) * 256], lhsT=kT[p][:, q, :], rhs=rhsA, start=True, stop=True))
                fc.append(lambda e, hp=hp: e.matmul(PB[b0][:, hp * 128:(hp + 1) * 128], lhsT=arm[p][:, q, hp, 0, :], rhs=bT[p][:, q, :], start=True, stop=True))
            K.ops("pe", [k("BbT"), k("Barm")], ["pb%d" % b0], fa)
            K.ops("pe", [k("BkT"), k("Barm")], ["pb%d" % b1], fb)
            yield
            K.op("dve", ["pb%d" % b0, "mA"], [k("BMM1") + "q%d" % q], lambda e, q=q: e.tensor_tensor(out=MM1[p][:, 2 * q:2 * q + 2, :, :].rearrange("p h a t -> p (h a t)"), in0=PB[b0][:, :], in1=mA[:, d, :, :].rearrange("p a t -> p (a t)"), op=ALU.mult))
            yield
            K.ops("pe", [k("BbT"), k("Barm")], ["pb%d" % b0], fc)
            K.op("dve", ["pb%d" % b1, "mA"], [k("BMM2") + "q%d" % q], lambda e, q=q: e.tensor_tensor(out=MM2[p][:, 2 * q:2 * q + 2, :, :].rearrange("p h a t -> p (h a t)"), in0=PB[b1][:, :], in1=mA[:, d, :, :].rearrange("p a t -> p (a t)"), op=ALU.mult))
            yield
            K.op("dve", ["pb%d" % b0, "mT"], [k("BQ0T") + "q%d" % q], lambda e, q=q: e.tensor_tensor(out=Q0T[p][:, 2 * q:2 * q + 2, :].rearrange("p h t -> p (h t)"), in0=PB[b0][:, 0:256], in1=mT[:, d, :, :].rearrange("p a t -> p (a t)"), op=ALU.mult))
            yield
        kQR = lambda a, q: "BQR%s_%dq%d" % ("ab"[a], p, q)
        kQT = lambda a, q: "BQT%s_%dq%d" % ("ab"[a], p, q)
        XA = PB[b0]
        XB = PB[b1]
        for q in range(4):
            hs = [2 * q, 2 * q + 1]
            fa = []
            for hh, h in enumerate(hs):
                fa.append(lambda e, hh=hh, h=h: e.matmul(XA[:, hh * 256:hh * 256 + 128], lhsT=Q0T[p][:, h, :], rhs=MM1[p][:, h, 0, :], start=True, stop=True))
                fa.append(lambda e, hh=hh, h=h: e.matmul(XA[:, hh * 256 + 128:hh * 256 + 256], lhsT=ident, rhs=MM1[p][:, h, 0, :], start=False, stop=False, skip_group_check=True))
                fa.append(lambda e, hh=hh, h=h: e.matmul(XA[:, hh * 256 + 128:hh * 256 + 256], lhsT=ident, rhs=ident, start=False, stop=True, skip_group_check=True))
            K.ops("pe", [k("BMM1") + "q%d" % q, k("BQ0T") + "q%d" % q, "cst"], ["pb%d" % b0], fa)
            K.ops("pe", [k("BMM1") + "q%d" % q, k("BQ0T") + "q%d" % q], ["pb%d" % b1], [lambda e, hh=hh, h=h: e.matmul(XB[:, hh * 128:(hh + 1) * 128], lhsT=MM1[p][:, h, 0, :], rhs=Q0T[p][:, h, :], start=True, stop=True) for hh, h in enumerate(hs)])
            yield
            K.op("act", ["pb%d" % b0], [kQR(1, q)], lambda e, q=q: e.copy(out=QR[1][p][:, 2 * q:2 * q + 2, :, :].rearrange("p h a t -> p (h a t)"), in_=XA[:, :]), c=0.5)
            yield
            K.op("dve", ["pb%d" % b1], [kQT(1, q)], lambda e, q=q: e.tensor_copy(out=QT[1][p][:, 2 * q:2 * q + 2, :].rearrange("p h t -> p (h t)"), in_=XB[:, 0:256]), c=0.42)
            yield
        for lev in range(1, 6):
            cur, nxt = lev % 2, (lev + 1) % 2
            for q in range(4):
                hs = [2 * q, 2 * q + 1]
                fa = []
                for hh, h in enumerate(hs):
                    fa.append(lambda e, hh=hh, h=h: e.matmul(XA[:, hh * 256:(hh + 1) * 256], lhsT=QT[cur][p][:, h, :], rhs=QR[cur][p][:, h, :, :].rearrange("p a t -> p (a t)"), start=True, stop=False))
                    fa.append(lambda e, hh=hh, h=h: e.matmul(XA[:, hh * 256 + 128:(hh + 1) * 256], lhsT=ident, rhs=QR[cur][p][:, h, 1, :], start=False, stop=True))
                K.ops("pe", [kQR(cur, q), kQT(cur, q), "cst"], ["pb%d" % b0], fa)
                K.ops("pe", [kQR(cur, q), kQT(cur, q)], ["pb%d" % b1], [lambda e, hh=hh, h=h: e.matmul(XB[:, hh * 128:(hh + 1) * 128], lhsT=QR[cur][p][:, h, 0, :], rhs=QT[cur][p][:, h, :], start=True, stop=True) for hh, h in enumerate(hs)])
                yield
                K.op("act", ["pb%d" % b0], [kQR(nxt, q)], lambda e, q=q: e.copy(out=QR[nxt][p][:, 2 * q:2 * q + 2, :, :].rearrange("p h a t -> p (h a t)"), in_=XA[:, :]), c=0.5)
                yield
                K.op("dve", ["pb%d" % b1], [kQT(nxt, q)], lambda e, q=q: e.tensor_copy(out=QT[nxt][p][:, 2 * q:2 * q + 2, :].rearrange("p h t -> p (h t)"), in_=XB[:, 0:256]), c=0.42)
                yield
        for q in range(4):
            hs = [2 * q, 2 * q + 1]
            ff = []
            bq = b0 if q % 2 == 0 else b1
            for hh, h in enumerate(hs):
                ff.append(lambda e, hh=hh, h=h, bq=bq: e.matmul(PB[bq][:, hh * 128:(hh + 1) * 128], lhsT=QT[0][p][:, h, :], rhs=QR[0][p][:, h, 1, :], start=True, stop=False))
                ff.append(lambda e, hh=hh, h=h, bq=bq: e.matmul(PB[bq][:, hh * 128:(hh + 1) * 128], lhsT=ident, rhs=QR[0][p][:, h, 1, :], start=False, stop=True))
            K.ops("pe", [kQR(0, q), kQT(0, q), "cst"], ["pb%d" % bq], ff)
            yield
            if q % 2 == 0:
                K.op("act", ["pb%d" % bq], [k("BTT") + "q%d" % q], lambda e, q=q, bq=bq: e.copy(out=TT[p][:, 2 * q:2 * q + 2, :].rearrange("p h t -> p (h t)"), in_=PB[bq][:, 0:256]), c=0.4)
            else:
                K.op("dve", ["pb%d" % bq], [k("BTT") + "q%d" % q], lambda e, q=q, bq=bq: e.tensor_copy(out=TT[p][:, 2 * q:2 * q + 2, :].rearrange("p h t -> p (h t)"), in_=PB[bq][:, 0:256]), c=0.42)
            yield
        while chain_done.get(ckey, 0) < cidx:
            yield
        fz = []
        for h in range(8):
            q, hp = h // 2, h % 2
            fz.append(lambda e, h=h, q=q, hp=hp: e.matmul(PB[b0][:, h * 64:(h + 1) * 64], lhsT=arm[p][:, q, hp, 0, :], rhs=Sbin[:, q, :], start=True, stop=False))
            fz.append(lambda e, h=h: e.matmul(PB[b0][:, h * 64:(h + 1) * 64], lhsT=MM2[p][:, h, 0, :], rhs=Vt[p][:, h * 64:(h + 1) * 64], start=False, stop=True))
        K.ops("pe", [k("Barm"), kSbin, k("Bvt")] + [k("BMM2") + "q%d" % q for q in range(4)], ["pb%d" % b0], fz)
        yield
        K.op("act", ["pb%d" % b0], [k("BZs")], lambda e: e.copy(out=Zs[p][:], in_=PB[b0][:, :]))
        yield
        K.ops("pe", [k("BTT") + "q%d" % q for q in range(4)] + [k("BZs")], ["pb%d" % b1], [lambda e, h=h: e.matmul(PB[b1][:, h * 64:(h + 1) * 64], lhsT=TT[p][:, h, :], rhs=Zs[p][:, h * 64:(h + 1) * 64], start=True, stop=True) for h in range(8)])
        yield
        K.op("dve", ["pb%d" % b1], [k("BUs")], lambda e: e.tensor_copy(out=Us[p][:], in_=PB[b1][:, :]))
        yield
        fd = []
        for h in range(8):
            q = h // 2
            fd.append(lambda e, h=h, q=q: e.matmul(PB[b2][:, h * 64:(h + 1) * 64], lhsT=Bt[p][:, q * 128:(q + 1) * 128], rhs=Us[p][:, h * 64:(h + 1) * 64], start=True, stop=False))
            fd.append(lambda e, h=h, q=q: e.matmul(PB[b2][:, h * 64:(h + 1) * 64], lhsT=Kt[p][:, q * 128:(q + 1) * 128], rhs=Vt[p][:, h * 64:(h + 1) * 64], start=False, stop=True))
        K.ops("pe", [k("BBt"), k("BKt"), k("BUs"), k("Bvt")], ["pb%d" % b2], fd)
        yield
        Dv = PB[b2][:, :].rearrange("p (q hp j) -> p q hp j", hp=2, j=64)
        for hp in range(2):
            rs = slice(hp * 64, (hp + 1) * 64)
            K.op("dve", ["pb%d" % b2, kSin], [kt1], lambda e, rs=rs, hp=hp: e.tensor_tensor(out=t1[rs, :, :], in0=Dv[rs, :, hp, :], in1=Sin[rs, :, :], op=ALU.add))
            yield
        gb = gC[p][:, :].unsqueeze(2).to_broadcast([128, 4, 64])
        K.op("dve", [kt1, k("BgC")], [kSout], lambda e: e.tensor_tensor(out=Sout[:], in0=t1[:], in1=gb, op=ALU.mult))
        yield
        K.op("pool", [kt1, k("BgC")], [kSbout], lambda e: e.tensor_tensor(out=Sbout[:], in0=t1[:], in1=gb, op=ALU.mult))
        yield
        fy = []
        for h in range(8):
            q, hp = h // 2, h % 2
            fy.append(lambda e, h=h, q=q, hp=hp: e.matmul(PB[b0][:, h * 64:(h + 1) * 64], lhsT=arm[p][:, q, hp, 1, :], rhs=Sbin[:, q, :], start=True, stop=False))
            fy.append(lambda e, h=h: e.matmul(PB[b0][:, h * 64:(h + 1) * 64], lhsT=MM1[p][:, h, 1, :], rhs=Us[p][:, h * 64:(h + 1) * 64], start=False, stop=False))
            fy.append(lambda e, h=h: e.matmul(PB[b0][:, h * 64:(h + 1) * 64], lhsT=MM2[p][:, h, 1, :], rhs=Vt[p][:, h * 64:(h + 1) * 64], start=False, stop=True))
        K.ops("pe", [k("Barm"), kSbin, k("BUs"), k("Bvt")] + [k("BMM1") + "q%d" % q for q in range(4)] + [k("BMM2") + "q%d" % q for q in range(4)], ["pb%d" % b0], fy)
        yield
        chain_done[ckey] = cidx + 1
        K.op("act", ["pb%d" % b0], [k("Bysb")], lambda e: e.copy(out=ysb[p][:], in_=PB[b0][:, :]))
        yield
        K.dma("act", k("Bysb"), [k("Bysb")], [], R["st_y"][d, gt], ysb[p][:])
        yield
        if last_out_ap is not None:
            K.dma("sp", kSout, [kSout], [], last_out_ap, Sout[:])
            yield

    queue = []
    for si, (xap, NT, cv, gt_base, _y) in enumerate(seqs):
        sp = si % 2
        for step in range(NT):
            sidx = step % 2
            last = (step == NT - 1) and si > 0
            queue.append((si, step, gt_base + step, 0, sp, sidx, R["nst"][si - 1, 0] if last else None, (si, 0), step))
            queue.append((si, step, gt_base + NT - 1 - step, 1, sp, sidx, R["nst"][si - 1, 1] if last else None, (si, 1), step))

    def init_state(si):
        sp = si % 2
        for d in range(2):
            if si == 0:
                K.dma("sp", "BS_%d_%d_0" % (sp, d), [], ["BS_%d_%d_0" % (sp, d)], S[sp][d][0][:], R["s0T"][d])
                K.op("pool", ["BS_%d_%d_0" % (sp, d)], ["BSb_%d_%d_0" % (sp, d)], lambda e, d=d: e.tensor_copy(out=Sb[sp][d][0][:], in_=S[sp][d][0][:]))
            else:
                K.op("pool", [], ["BS_%d_%d_0" % (sp, d)], lambda e, d=d: e.memset(S[sp][d][0][:].rearrange("p a b -> p (a b)"), 0.0))
                K.op("pool", [], ["BSb_%d_%d_0" % (sp, d)], lambda e, d=d: e.memset(Sb[sp][d][0][:].rearrange("p a b -> p (a b)"), 0.0))

    def before(item):
        si, step, gt, d, sp, sidx, lo = item[:7]
        if step == 0 and d == 0:
            init_state(si)

    K.run_streams(queue, lambda slot, it: visit(slot, it[2], it[3], it[4], it[5], it[6], it[7], it[8]), WB, before, offset=VOFF)


def phase_c(nc, K, esC, sb, PB, PT, seqs, R):
    cst, w_out = R["cst"], R["w_out"]
    ident = cst[:, 0, :]
    rows_t = sb("Crows", [128, 3072], F32, esC)
    Gt = sb("CGt", [128, 2, 1024], F32, esC)
    K.dma("act", "rows_t", [], ["rows_t"], rows_t[:], R["rows"])
    K.dma("act", "Gt", [], ["Gt"], Gt[:], R["st_gt"])
    wout = sb("Cwout", [128, 8, D], BF16, esC)
    wst = [sb("Cwst%d" % i, [128, D], F32, esC) for i in range(2)]
    def wdma(kc):
        K.dma("act", "Cwst%d" % (kc % 2), [], ["Cwst%d" % (kc % 2)], wst[kc % 2][:], w_out[kc * 128:(kc + 1) * 128, :])
    wdma(0)
    wdma(1)
    for kc in range(8):
        K.op("dve", ["Cwst%d" % (kc % 2)], ["Cwout"], lambda e, kc=kc: e.tensor_copy(out=wout[:, kc, :], in_=wst[kc % 2][:]))
        if kc + 2 < 8:
            wdma(kc + 2)

    WC = 5
    ctr = [0]

    def T(name, shape, dt):
        return [sb("%s_%d" % (name, p), shape, dt, esC) for p in range(WC)]
    yf = T("Cyf", [128, 8, 64], F32)
    yb = T("Cyb", [128, 8, 64], F32)
    bcf = T("Cbcf", [128, 8], F32)
    bcb = T("Cbcb", [128, 8], F32)
    Vt = T("Cvt", [128, 8, 64], BF16)
    sza = T("Csza", [128, 512], BF16)
    mixed = T("Cmixed", [128, D], BF16)
    xt = T("Cxt", [128, D], F32)
    junk = sb("Cjunk", [128, 512], F32, esC)
    ysq = T("Cysq", [128, 8, 64], F32)
    st = T("Cst", [128, 40], F32)
    bon = T("Cbon", [128, 8, 64], F32)
    mixT = T("CmixT", [128, 8, 128], BF16)
    ot = T("Cot", [128, D], F32)

    def tile(p, xap, a, cv, gt, yap):
        if True:
            n0 = 2 * (ctr[0] % 3)
            ctr[0] += 1
            k = lambda n: "%s_%d" % (n, p)
            K.dma("sp", k("Cyf"), [], [k("Cyf")], yf[p][:].rearrange("p h j -> p (h j)"), R["st_y"][0, gt])
            yield
            K.dma("sp", k("Cyb"), [], [k("Cyb")], yb[p][:].rearrange("p h j -> p (h j)"), R["st_y"][1, gt])
            yield
            K.dma("sp", k("Cbcf"), [], [k("Cbcf")], bcf[p][:], R["st_bc"][0, gt])
            yield
            K.dma("sp", k("Cbcb"), [], [k("Cbcb")], bcb[p][:], R["st_bc"][1, gt])
            yield
            K.dma("sp", k("Cvt"), [], [k("Cvt")], Vt[p][:].rearrange("p h j -> p (h j)"), R["st_vt"][gt])
            yield
            K.dma("sp", k("Csza"), [], [k("Csza")], sza[p][:], R["st_sza"][gt])
            yield
            K.dma("sp", k("Cmixed") + "b", [], [k("Cmixed") + "b"], mixed[p][:, 512:1024], R["st_mixb"][gt])
            yield
            K.dma("sp", k("Cxt"), [], [k("Cxt")], xt[p][:], xap[a * 128:(a + 1) * 128, :])
            yield
            K.op("pool", [k("Cyf"), k("Cyb")], [k("Cyf")], lambda e: e.tensor_tensor(out=yf[p][:], in0=yf[p][:], in1=yb[p][:], op=ALU.add))
            yield
            s_ = st[p]
            K.op("dve", [k("Cyf")], [k("Cst") + "a"], lambda e: e.tensor_reduce(out=s_[:, 0:8], in_=yf[p][:], axis=AX.X, op=ALU.add))
            yield
            K.op("pool", [k("Cyf")], [k("Cysq")], lambda e: e.tensor_tensor(out=ysq[p][:], in0=yf[p][:], in1=yf[p][:], op=ALU.mult))
            yield
            K.op("dve", [k("Cysq")], [k("Cst") + "b"], lambda e: e.tensor_reduce(out=s_[:, 8:16], in_=ysq[p][:], axis=AX.X, op=ALU.add))
            yield
            sk = [k("Cst") + "a", k("Cst") + "b"]
            kc_ = k("Cst") + "c"
            K.op("dve", sk, [kc_], lambda e: e.tensor_scalar(out=s_[:, 16:24], in0=s_[:, 0:8], scalar1=1.0 / 64, scalar2=None, op0=ALU.mult))
            yield
            K.op("dve", [kc_], [kc_], lambda e: e.tensor_tensor(out=s_[:, 24:32], in0=s_[:, 16:24], in1=s_[:, 16:24], op=ALU.mult))
            yield
            K.op("dve", sk + [kc_], [kc_], lambda e: e.scalar_tensor_tensor(out=s_[:, 32:40], in0=s_[:, 8:16], scalar=1.0 / 64, in1=s_[:, 24:32], op0=ALU.mult, op1=ALU.subtract))
            yield
            K.op("dve", [kc_], [kc_], lambda e: e.tensor_scalar(out=s_[:, 32:40], in0=s_[:, 32:40], scalar1=GN_EPS, scalar2=None, op0=ALU.add))
            yield
            K.op("act", [kc_], [kc_], lambda e: e.sqrt(out=s_[:, 32:40], in_=s_[:, 32:40]))
            yield
            K.op("dve", [kc_], [kc_], lambda e: e.reciprocal(out=s_[:, 32:40], in_=s_[:, 32:40]))
            yield
            b8 = lambda ap: ap.unsqueeze(2).to_broadcast([128, 8, 64])
            K.op("dve", [k("Cyf"), kc_], [k("Cyf")], lambda e: e.tensor_tensor(out=yf[p][:], in0=yf[p][:], in1=b8(s_[:, 16:24]), op=ALU.subtract))
            yield
            K.op("dve", [k("Cyf"), kc_], [k("Cyf")], lambda e: e.tensor_tensor(out=yf[p][:], in0=yf[p][:], in1=b8(s_[:, 32:40]), op=ALU.mult))
            yield
            yfl = yf[p][:].rearrange("p h j -> p (h j)")
            K.op("pool", [k("Cyf"), "rows_t"], [k("Cyf")], lambda e: e.tensor_tensor(out=yfl, in0=yfl, in1=rows_t[:, 1024:1536], op=ALU.mult))
            yield
            K.op("pool", [k("Cyf"), "rows_t"], [k("Cyf")], lambda e: e.tensor_tensor(out=yfl, in0=yfl, in1=rows_t[:, 1536:2048], op=ALU.add))
            yield
            K.op("dve", [k("Cbcf"), k("Cbcb")], [k("Cbcf")], lambda e: e.tensor_tensor(out=bcf[p][:], in0=bcf[p][:], in1=bcb[p][:], op=ALU.add))
            yield
            K.op("dve", [k("Cvt"), k("Cbcf")], [k("Cbon")], lambda e: e.tensor_tensor(out=bon[p][:], in0=Vt[p][:], in1=b8(bcf[p][:, :]), op=ALU.mult))
            yield
            K.op("dve", [k("Cyf"), k("Cbon")], [k("Cyf")], lambda e: e.tensor_tensor(out=yf[p][:], in0=yf[p][:], in1=bon[p][:], op=ALU.add))
            yield
            K.op("dve", [k("Cyf"), k("Csza")], [k("Cmixed") + "a"], lambda e: e.tensor_tensor(out=mixed[p][:, 0:512], in0=yfl, in1=sza[p][:], op=ALU.mult))
            yield
            K.ops("pe", [k("Cmixed") + "a", k("Cmixed") + "b", "cst"], ["ptb"], [lambda e, kc=kc: e.transpose(out=PT[:, kc * 128:(kc + 1) * 128], in_=mixed[p][:, kc * 128:(kc + 1) * 128], identity=ident) for kc in range(8)])
            K.op("act", ["ptb"], [k("CmixT")], lambda e: e.copy(out=mixT[p][:].rearrange("p a b -> p (a b)"), in_=PT[:, :]))
            yield
            for n in range(2):
                K.ops("pe", [k("CmixT"), "Cwout"], ["pb%d" % (n0 + n)], [lambda e, kc=kc, n=n: e.matmul(PB[n0 + n][:, :], lhsT=mixT[p][:, kc, :], rhs=wout[:, kc, n * 512:(n + 1) * 512], start=(kc == 0), stop=(kc == 7)) for kc in range(8)])
            kr = k("Cst") + "r"
            for n in range(2):
                K.op("act", ["pb%d" % (n0 + n)], ["Cjunk", kr + "%d" % n], lambda e, n=n: e.activation(out=junk[:, :], in_=PB[n0 + n][:, :], func=AF.Square, accum_out=s_[:, 2 + n:3 + n] if False else s_[:, 0 + n:1 + n]))
            K.op("dve", [kr + "0", kr + "1"] + sk + [kc_], [kr], lambda e: e.tensor_tensor(out=s_[:, 2:3], in0=s_[:, 0:1], in1=s_[:, 1:2], op=ALU.add))
            K.op("dve", [kr], [kr], lambda e: e.tensor_scalar(out=s_[:, 2:3], in0=s_[:, 2:3], scalar1=1.0 / D, scalar2=NORM_EPS, op0=ALU.mult, op1=ALU.add))
            K.op("act", [kr], [kr], lambda e: e.sqrt(out=s_[:, 2:3], in_=s_[:, 2:3]))
            K.op("dve", [kr], [kr], lambda e: e.reciprocal(out=s_[:, 2:3], in_=s_[:, 2:3]))
            for n in range(2):
                K.op("dve", ["pb%d" % (n0 + n), kr, "Gt"], [k("Cot")], lambda e, n=n: e.scalar_tensor_tensor(out=ot[p][:, n * 512:(n + 1) * 512], in0=PB[n0 + n][:, :], scalar=s_[:, 2:3], in1=Gt[:, cv, n * 512:(n + 1) * 512], op0=ALU.mult, op1=ALU.mult))
            K.op("pool", [k("Cot"), k("Cxt")], [k("Cot")], lambda e: e.tensor_tensor(out=ot[p][:], in0=ot[p][:], in1=xt[p][:], op=ALU.add))
            yield
            K.dma("sp", k("Cot"), [k("Cot")], [], yap[a * 128:(a + 1) * 128, :], ot[p][:])
            yield


    queue = []
    for si, (xap, NT, cv, gt_base, yap) in enumerate(seqs):
        for a in range(NT):
            queue.append((xap, a, cv, gt_base + a, yap))
    K.run_streams(queue, lambda slot, it: tile(slot, *it), WC, offset=4.0)


def _prep_shared(inp):
    f = np.float32
    g = lambda k: np.asarray(inp[k], dtype=f)
    fm = lambda v, nb: np.ascontiguousarray(v.reshape(nb, 128).T)
    pf = np.zeros((128, 80), f)
    b_mod = g("b_mod")[0]
    pf[:, 0:16] = fm(b_mod[0:2048], 16)
    pf[:, 16:24] = fm(g("ln_pre")[0], 8)
    ts = g("ts_mu")[0]
    pf[:, 24:36] = fm(ts[0], 12)
    pf[:, 36:48] = fm(ts[1], 12)
    for d in range(2):
        pf[:, 48 + 4 * d:52 + 4 * d] = fm(g("w0")[0, d], 4)
        pf[:, 56 + 4 * d:60 + 4 * d] = fm(g("a0")[0, d], 4)
        pf[:, 64 + 4 * d:68 + 4 * d] = fm(g("k_k")[0, d], 4)
        pf[:, 72 + 4 * d:76 + 4 * d] = fm(g("k_a")[0, d], 4)
    rows = np.zeros((128, 3072), f)
    rows[:, 0:1024] = g("ln_post")[0][None, :]
    rows[:, 1024:1536] = g("gn_w")[0][None, :]
    rows[:, 1536:2048] = g("gn_b")[0][None, :]
    rows[:, 2048:2560] = g("sgu_ln_g")[0][None, :]
    rows[:, 2560:3072] = g("sgu_ln_b")[0][None, :]
    bmg = np.ascontiguousarray(np.broadcast_to(b_mod[2048:3072][None, :], (2, 1024))).astype(f)
    lora_w = np.zeros((128, 4, 512), f)
    for d in range(2):
        lora_w[d * 64:(d + 1) * 64, d, :] = g("w_up")[0, d]
        lora_w[d * 64:(d + 1) * 64, 2 + d, :] = g("a_up")[0, d]
    rk = g("r_k")[0]
    rk_in = np.zeros((128, 2, 4, 2), f)
    for d in range(2):
        for q in range(4):
            for hp in range(2):
                rk_in[hp * 64:(hp + 1) * 64, d, q, hp] = rk[d, 2 * q + hp]
    rk_in = rk_in.reshape(128, 16)
    wsT = np.ascontiguousarray(np.transpose(g("w_s")[0], (2, 0, 1)))
    bs = np.ascontiguousarray(g("b_s")[0].T)
    s_idx = np.arange(128)[:, None]
    t_idx = np.arange(128)[None, :]
    consts = np.zeros((128, 6, 128), f)
    consts[:, 0] = (s_idx == t_idx)
    consts[:, 1] = (s_idx < t_idx)
    consts[:, 2] = (s_idx <= t_idx)
    consts[:, 3] = (s_idx > t_idx)
    consts[:, 4] = (s_idx >= t_idx)
    consts[:, 5] = ((s_idx // 64) == (t_idx // 64))
    sel = np.zeros((2, 2, 128), f)
    sel[0, 0, :] = 1.0
    sel[1, 1, :] = 1.0
    return dict(w_in=np.ascontiguousarray(g("w_in")[0]), w_out=np.ascontiguousarray(g("w_out")[0]),
                w_mod=np.ascontiguousarray(g("w_mod")[0]), pf=pf, rows=rows, bmg=bmg, lora_w=lora_w,
                rk_in=rk_in, wsT=wsT, bs=bs, consts=consts, sel=sel)


def _prep_core(inp, shared, i, NT_S, NP):
    f = np.float32
    m = dict(shared)
    m["xs"] = np.ascontiguousarray(np.asarray(inp["x_sample"][i], f)[:NT_S * 128])
    m["xp"] = np.ascontiguousarray(np.asarray(inp["x_prompt"][NP * i:NP * (i + 1)], f))
    c = np.asarray(inp["c"][i], f)
    cc = np.asarray(inp["c_ctx"], f)
    cvT = np.zeros((128, 8, 2), f)
    cvT[:, :, 0] = c.reshape(8, 128).T
    cvT[:, :, 1] = cc.reshape(8, 128).T
    m["cvT"] = cvT
    s0T = np.zeros((2, 128, 4, 64), f)
    for d, key in enumerate(["state_fwd", "state_bwd"]):
        S = np.asarray(inp[key][i, 0], f)
        S4 = S.reshape(4, 2, 64, 64)
        s0T[d] = np.transpose(S4, (1, 3, 0, 2)).reshape(128, 4, 64)
    m["s0T"] = s0T
    return m


_CACHE = {}


def kernel(**inputs):
    NT_S, NP, NCORE = 32, 4, 8
    if "nc" not in _CACHE:
        _CACHE["nc"] = build(NT_S, NP)
    nc = _CACHE["nc"]
    shared = _prep_shared(inputs)
    in_maps = [_prep_core(inputs, shared, i, NT_S, NP) for i in range(NCORE)]
    res = run_bass_kernel_spmd(nc, in_maps, core_ids=list(range(NCORE)))
    y_sample = np.stack([np.asarray(r["ys"]) for r in res.results], 0).astype(np.float32)
    y_prompt = np.concatenate([np.asarray(r["yp"]) for r in res.results], 0).astype(np.float32)
    nf, nb = [], []
    for r in res.results:
        nst = np.asarray(r["nst"])
        for p in range(NP):
            for d, lst in ((0, nf), (1, nb)):
                S = nst[p, d].reshape(2, 64, 4, 64)
                lst.append(np.transpose(S, (2, 0, 3, 1)).reshape(8, 64, 64)[None])
    new_f = np.stack(nf, 0).astype(np.float32)
    new_b = np.stack(nb, 0).astype(np.float32)
    return (y_prompt, y_sample, new_f, new_b)
```

```python
import contextlib
import numpy as np
import concourse.bass as bass
import concourse.mybir as mybir
from concourse.bass_utils import run_bass_kernel_spmd

F32 = mybir.dt.float32
BF16 = mybir.dt.bfloat16
AF = mybir.ActivationFunctionType
ALU = mybir.AluOpType
AX = mybir.AxisListType

D = 1024
DIN = 3840
NORM_EPS = 1e-6
GN_EPS = 6.4e-4
L2_EPS = 1e-12
EXPM05 = float(np.exp(-0.5))
C2H = 0.5 * EXPM05
VOFF = 15.0


class Sched:
    def __init__(self, nc, es):
        self.nc = nc
        self.es = es
        self.engs = dict(pe=nc.tensor, dve=nc.vector, act=nc.scalar, pool=nc.gpsimd, sp=nc.sync)
        self.sems = {e: es.enter_context(nc.semaphore("sem_" + e)) for e in self.engs}
        self.cnt = {e: 0 for e in self.engs}
        self.lastw = {}
        self.readers = {}
        self.seen = {e: {} for e in self.engs}
        self.chans = {}
        self.semobj = {}
        self.clock = {e: 0.0 for e in self.engs}
        self.ttime = {}
        self.lastfin = 0.0
        self.cost = dict(pe=0.09, dve=0.55, act=0.45, pool=1.1, sp=0.1)

    def _time(self, eng, needs, tok, cost):
        ready = 0.0
        for sname, val in needs.items():
            t = self.ttime.get((sname, val), 0.0)
            if t > ready:
                ready = t
        start = max(self.clock[eng], ready)
        fin = start + cost
        self.clock[eng] = fin if tok[0].startswith("sem_") else start + 0.1
        self.ttime[tok] = fin
        if fin > self.lastfin:
            self.lastfin = fin

    def _need(self, eng, needs):
        for sname, val in needs.items():
            if self.seen[eng].get(sname, 0) >= val:
                continue
            self.engs[eng].wait_ge(self.semobj[sname], val)
            self.seen[eng][sname] = val

    def _collect(self, eng, reads, writes):
        needs = {}

        def add(tok):
            if tok is None:
                return
            s, v = tok
            if s == "sem_pe" and eng == "pe":
                return
            if needs.get(s, 0) < v:
                needs[s] = v

        for b in reads:
            add(self.lastw.get(b))
        for b in writes:
            add(self.lastw.get(b))
            for s, v in self.readers.get(b, {}).items():
                add((s, v))
        return needs

    def _record(self, tok, reads, writes):
        s, v = tok
        for b in reads:
            self.readers.setdefault(b, {})[s] = v
        for b in writes:
            self.lastw[b] = tok
            self.readers[b] = {}

    def op(self, eng, reads, writes, fn, c=None):
        needs = self._collect(eng, reads, writes)
        self._need(eng, needs)
        inst = fn(self.engs[eng])
        self.cnt[eng] += 1
        sname = "sem_" + eng
        self.semobj[sname] = self.sems[eng]
        inst.then_inc(self.sems[eng], 1)
        self._time(eng, needs, (sname, self.cnt[eng]), self.cost[eng] if c is None else c)
        self._record((sname, self.cnt[eng]), reads, writes)

    def ops(self, eng, reads, writes, fns):
        needs = self._collect(eng, reads, writes)
        self._need(eng, needs)
        inst = None
        for fn in fns:
            inst = fn(self.engs[eng])
        self.cnt[eng] += 1
        sname = "sem_" + eng
        self.semobj[sname] = self.sems[eng]
        inst.then_inc(self.sems[eng], 1)
        self._time(eng, needs, (sname, self.cnt[eng]), self.cost[eng] * len(fns))
        self._record((sname, self.cnt[eng]), reads, writes)

    def dma(self, eng, chan_key, reads, writes, out, in_):
        if chan_key not in self.chans:
            sem = self.es.enter_context(self.nc.semaphore("dch_%d" % len(self.chans)))
            self.chans[chan_key] = [sem, 0, "dch_%d" % len(self.chans)]
            self.semobj[self.chans[chan_key][2]] = sem
        ch = self.chans[chan_key]
        needs = self._collect(eng, reads, writes)
        self._need(eng, needs)
        self.engs[eng].dma_start(out=out, in_=in_).then_inc(ch[0], 16)
        ch[1] += 16
        self._time(eng, needs, (ch[2], ch[1]), 2.5)
        self._record((ch[2], ch[1]), reads, writes)

    def run_streams(self, queue, make_gen, W, before=None, compat=None, offset=0.0):
        active, free, qi = [], list(range(W)), 0
        while qi < len(queue) or active:
            while free and qi < len(queue):
                if compat is not None and not compat(queue[qi], [a[3] for a in active]):
                    break
                if before is not None:
                    before(queue[qi])
                slot = free.pop(0)
                vt0 = min([a[2] for a in active]) if active else min(self.clock.values())
                if qi < W:
                    vt0 += qi * offset
                active.append([slot, make_gen(slot, queue[qi]), vt0, queue[qi]])
                qi += 1
            item = min(active, key=lambda a: a[2])
            self.lastfin = 0.0
            try:
                next(item[1])
                if self.lastfin > 0.0:
                    item[2] = self.lastfin
                else:
                    item[2] += 0.5
            except StopIteration:
                active.remove(item)
                free.append(item[0])

    def barrier(self, engines=("pe", "dve", "act", "pool", "sp")):
        needs = {}
        for e in self.engs:
            if self.cnt[e] > 0:
                needs["sem_" + e] = self.cnt[e]
        for ch in self.chans.values():
            if ch[1] > 0:
                needs[ch[2]] = ch[1]
        for e in engines:
            n2 = {s: v for s, v in needs.items() if not (e == "pe" and s == "sem_pe")}
            self._need(e, n2)


def build(NT_S, NP, debug=False):
    nc = bass.Bass("TRN2", target_bir_lowering=False)
    LS = NT_S * 128
    NTILES = NT_S + 2 * NP
    okind = "ExternalOutput" if debug else "Internal"

    def din(name, shape, dt=F32):
        return nc.dram_tensor(name, list(shape), dt, kind="ExternalInput").ap()

    def dout(name, shape, dt=F32):
        return nc.dram_tensor(name, list(shape), dt, kind="ExternalOutput").ap()

    def dscr(name, shape, dt):
        if debug:
            return nc.dram_tensor(name, list(shape), dt, kind="ExternalOutput").ap()
        return nc.dram_tensor(name, list(shape), dt).ap()

    xs = din("xs", [LS, D])
    xp = din("xp", [NP, 256, D])
    cvT = din("cvT", [128, 8, 2])
    w_in = din("w_in", [D, DIN])
    w_out = din("w_out", [D, D])
    w_mod = din("w_mod", [D, 3 * D])
    pf = din("pf", [128, 80])
    rows = din("rows", [128, 3072])
    bmg = din("bmg", [2, 1024])
    lora_w = din("lora_w", [128, 4, 512])
    rk_in = din("rk_in", [128, 16])
    wsT_in = din("wsT", [128, 8, 128])
    bs_in = din("bs", [128, 8])
    consts = din("consts", [128, 6, 128])
    sel_in = din("sel", [2, 2, 128])
    s0T = din("s0T", [2, 128, 4, 64])

    ys = dout("ys", [LS, D])
    yp = dout("yp", [NP, 256, D])
    nst = dout("nst", [NP, 2, 128, 4, 64])

    st_rkv = dscr("st_rkv", [NTILES, 128, 12, 128], BF16)
    st_low = dscr("st_low", [NTILES, 128, 2, 128], BF16)
    st_vt = dscr("st_vt", [NTILES, 128, 512], BF16)
    st_sza = dscr("st_sza", [NTILES, 128, 512], BF16)
    st_mixb = dscr("st_mixb", [NTILES, 128, 512], BF16)
    st_y = dscr("st_y", [2, NTILES, 128, 512], F32)
    st_bc = dscr("st_bc", [2, NTILES, 128, 8], F32)
    st_gt = dscr("st_gt", [128, 2, 1024], F32)

    seqs = [(xs, NT_S, 0, 0, ys)]
    for p in range(NP):
        seqs.append((xp[p], 2, 1, NT_S + 2 * p, yp[p]))

    with contextlib.ExitStack() as es:
        K = Sched(nc, es)
        K.debug = debug

        def dump(name, key, ap, shape, dt=F32):
            if not debug:
                return
            dd = nc.dram_tensor("dbg_" + name, list(shape), dt, kind="ExternalOutput").ap()
            K.dma("sp", "dbgch", [key], [], dd, ap)
        K.dump = dump

        def sb(name, shape, dt, stack=es):
            return stack.enter_context(nc.sbuf_tensor(name, list(shape), dt))

        def ps(name, shape, dt, stack=es):
            return stack.enter_context(nc.psum_tensor(name, list(shape), dt))

        PBIG = ps("pbig", [128, 7 * 512], F32)
        PB = [PBIG[:, i * 512:(i + 1) * 512] for i in range(7)]
        PT = ps("ptb", [128, 1024], BF16)

        pf_t = sb("pf_t", [128, 80], F32)
        cst = sb("cst", [128, 6, 128], BF16)
        sel_t = sb("sel_t", [2, 2, 128], F32)
        bs_t = sb("bs_t", [128, 8], F32)
        wsT = sb("wsT_sb", [128, 8, 128], BF16)
        lora = sb("lora", [128, 4, 512], BF16)
        rkm = sb("rkm", [128, 16], BF16)
        gfm = sb("gfm", [128, 8, 2], F32)
        shfm = sb("shfm", [128, 8, 2], F32)
        cs0 = sb("cs0", [128, 12], F32)
        omka = sb("omka", [128, 8], F32)
        mA = sb("mA", [128, 2, 4, 128], BF16)
        mT = sb("mT", [128, 2, 2, 128], BF16)

        K.dma("sp", "pf_t", [], ["pf_t"], pf_t[:], pf)
        K.dma("sp", "sel_t", [], ["sel_t"], sel_t[:], sel_in)
        K.dma("sp", "bs_t", [], ["bs_t"], bs_t[:], bs_in)
        K.op("dve", ["pf_t"], ["cs0"], lambda e: e.tensor_tensor(out=cs0[:], in0=pf_t[:, 24:36], in1=pf_t[:, 36:48], op=ALU.add))
        K.op("dve", ["cs0"], ["cs0"], lambda e: e.tensor_scalar(out=cs0[:], in0=cs0[:], scalar1=-1.0, scalar2=1.0, op0=ALU.mult, op1=ALU.add))
        K.op("dve", ["pf_t"], ["omka"], lambda e: e.tensor_scalar(out=omka[:], in0=pf_t[:, 72:80], scalar1=-1.0, scalar2=1.0, op0=ALU.mult, op1=ALU.add))

        with contextlib.ExitStack() as esA:
            win = sb("win", [128, 8, DIN], BF16, esA)
            rows_t = sb("rows_t", [128, 3072], F32, esA)
            with contextlib.ExitStack() as es0:
                cst_f = sb("cst_f", [128, 6, 128], F32, es0)
                Gt = sb("Gt", [128, 2, 1024], F32, es0)
                K.dma("sp", "rows_t", [], ["rows_t"], rows_t[:], rows)
                K.dma("sp", "cst_f", [], ["cst_f"], cst_f[:], consts)
                K.op("dve", ["cst_f"], ["cst"], lambda e: e.tensor_copy(out=cst[:], in_=cst_f[:]))
                for d, (s_i, i_i, t_i) in enumerate([(1, 2, 3), (3, 4, 1)]):
                    for j in range(4):
                        src = s_i if j % 2 == 0 else i_i
                        K.op("pool", ["cst_f"], ["mA"], lambda e, d=d, j=j, src=src: e.tensor_copy(out=mA[:, d, j, :], in_=cst_f[:, src, :]))
                    for j in range(2):
                        K.op("pool", ["cst_f"], ["mT"], lambda e, d=d, j=j, t_i=t_i: e.tensor_copy(out=mT[:, d, j, :], in_=cst_f[:, t_i, :]))
                wst = [sb("wst%d" % i, [128, DIN], F32, es0) for i in range(2)]
                tmpf = sb("tmpf", [128, 4, 512], F32, es0)
                cv_t = sb("cv_t", [128, 8, 2], F32, es0)
                scv = sb("scv", [128, 8, 2], F32, es0)
                bmg_t = sb("bmg_t", [2, 1024], F32, es0)
                grow = sb("grow", [2, 1024], F32, es0)
                modfm = sb("modfm", [128, 16, 2], F32, es0)
                K.dma("sp", "cv_t", [], ["cv_t"], cv_t[:], cvT)
                K.dma("sp", "bmg_t", [], ["bmg_t"], bmg_t[:], bmg)
                K.op("act", ["cv_t"], ["scv"], lambda e: e.activation(out=scv[:], in_=cv_t[:], func=AF.Silu))
                K.dma("sp", "tmpf", [], ["tmpf"], tmpf[:], lora_w)
                K.op("dve", ["tmpf"], ["lora"], lambda e: e.tensor_copy(out=lora[:], in_=tmpf[:]))
                K.dma("sp", "tmpf", ["tmpf"], ["tmpf"], tmpf[:, 0, 0:16], rk_in)
                K.op("dve", ["tmpf"], ["rkm"], lambda e: e.tensor_copy(out=rkm[:], in_=tmpf[:, 0, 0:16]))
                K.dma("sp", "tmpf", [], ["tmpf"], tmpf[:, 0:2, :].rearrange("p a b -> p (a b)"), wsT_in.rearrange("p g q -> p (g q)"))
                K.op("dve", ["tmpf"], ["wsT"], lambda e: e.tensor_copy(out=wsT[:].rearrange("p g q -> p (g q)"), in_=tmpf[:, 0:2, :].rearrange("p a b -> p (a b)")))
                wsi = [sb("wsi%d" % i, [128, DIN], F32, es0) for i in range(2)]

                def wdma(kc):
                    K.dma("act", "wsi%d" % (kc % 2), [], ["wsi%d" % (kc % 2)], wsi[kc % 2][:, :], w_in[kc * 128:(kc + 1) * 128, :])
                wdma(0)
                wdma(1)
                for kc in range(8):
                    K.op("act", ["wsi%d" % (kc % 2)], ["win"], lambda e, kc=kc: e.copy(out=win[:, kc, :], in_=wsi[kc % 2][:, :]))
                    if kc + 2 < 8:
                        wdma(kc + 2)
                for kc in range(8):
                    wm = wst[kc % 2]
                    K.dma("sp", "wst%d" % (kc % 2), [], ["wst%d" % (kc % 2)], wm[:, 0:3072], w_mod[kc * 128:(kc + 1) * 128, :])
                    fns = []
                    for blk in range(16):
                        fns.append(lambda e, blk=blk, kc=kc, wm=wm: e.matmul(PB[0][:, blk * 2:blk * 2 + 2], lhsT=wm[:, blk * 128:(blk + 1) * 128], rhs=scv[:, kc, :], start=(kc == 0 and blk == 0), stop=(kc == 7), skip_group_check=True))
                    for n in range(2):
                        fns.append(lambda e, n=n, kc=kc, wm=wm: e.matmul(PB[1 + n][0:2, :], lhsT=scv[:, kc, :], rhs=wm[:, 2048 + n * 512:2048 + (n + 1) * 512], start=(kc == 0), stop=(kc == 7)))
                    K.ops("pe", ["wst%d" % (kc % 2), "scv"], ["pb0", "pb1", "pb2"], fns)
                K.op("dve", ["pb0", "pf_t"], ["modfm"], lambda e: e.tensor_tensor(out=modfm[:], in0=PB[0][:, 0:32].rearrange("p (b c) -> p b c", c=2), in1=pf_t[:, 0:16].unsqueeze(2).to_broadcast([128, 16, 2]), op=ALU.add))
                K.op("dve", ["modfm"], ["shfm"], lambda e: e.tensor_copy(out=shfm[:], in_=modfm[:, 0:8, :]))
                K.op("dve", ["modfm"], ["gfm"], lambda e: e.tensor_scalar(out=gfm[:], in0=modfm[:, 8:16, :], scalar1=1.0, scalar2=None, op0=ALU.add))
                K.op("dve", ["gfm", "pf_t"], ["gfm"], lambda e: e.tensor_tensor(out=gfm[:], in0=gfm[:], in1=pf_t[:, 16:24].unsqueeze(2).to_broadcast([128, 8, 2]), op=ALU.mult))
                for n in range(2):
                    K.op("dve", ["pb%d" % (1 + n), "bmg_t"], ["grow"], lambda e, n=n: e.tensor_tensor(out=grow[:, n * 512:(n + 1) * 512], in0=PB[1 + n][0:2, :], in1=bmg_t[:, n * 512:(n + 1) * 512], op=ALU.add))
                for cv in range(2):
                    for n in range(2):
                        K.ops("pe", ["grow", "sel_t"], ["pb%d" % (3 + n)], [lambda e, cv=cv, n=n: e.matmul(PB[3 + n][:, :], lhsT=sel_t[:, cv, :], rhs=grow[:, n * 512:(n + 1) * 512], start=True, stop=True)])
                        K.op("dve", ["pb%d" % (3 + n), "rows_t"], ["Gt"], lambda e, cv=cv, n=n: e.tensor_tensor(out=Gt[:, cv, n * 512:(n + 1) * 512], in0=PB[3 + n][:, :], in1=rows_t[:, n * 512:(n + 1) * 512], op=ALU.mult))
                K.dma("sp", "Gt", ["Gt"], [], st_gt, Gt[:])
                dump("gfm", "gfm", gfm[:], [128, 8, 2])
                dump("shfm", "shfm", shfm[:], [128, 8, 2])
                dump("Gt", "Gt", Gt[:], [128, 2, 1024])
                dump("scv", "scv", scv[:], [128, 8, 2])
                K.barrier()

            phase_a(nc, K, esA, sb, PB, PT, seqs, win, dict(
                pf_t=pf_t, rows_t=rows_t, cst=cst, bs_t=bs_t, wsT=wsT, gfm=gfm, shfm=shfm, cs0=cs0,
                st_rkv=st_rkv, st_low=st_low, st_vt=st_vt, st_sza=st_sza, st_mixb=st_mixb))
            K.barrier()

        with contextlib.ExitStack() as esB:
            phase_b(nc, K, esB, sb, PB, PT, seqs, dict(
                pf_t=pf_t, cst=cst, lora=lora, rkm=rkm, omka=omka, mA=mA, mT=mT,
                st_rkv=st_rkv, st_low=st_low, st_vt=st_vt, st_y=st_y, st_bc=st_bc, s0T=s0T, nst=nst, PBIG=PBIG), NT_S, NP)
            K.barrier()
        with contextlib.ExitStack() as esC:
            phase_c(nc, K, esC, sb, PB, PT, seqs, dict(rows=rows, st_gt=st_gt, cst=cst, w_out=w_out,
                    st_y=st_y, st_bc=st_bc, st_vt=st_vt, st_sza=st_sza, st_mixb=st_mixb))
        K.barrier(engines=("sp",))
    return nc


def phase_a(nc, K, esA, sb, PB, PT, seqs, win, R):
    pf_t, rows_t, cst, bs_t, wsT = R["pf_t"], R["rows_t"], R["cst"], R["bs_t"], R["wsT"]
    gfm, shfm, cs0 = R["gfm"], R["shfm"], R["cs0"]
    ident = cst[:, 0, :]
    NXB = 2
    xt = [sb("xt%d" % i, [128, D], F32, esA) for i in range(NXB)]
    xn = [sb("xn%d" % i, [128, D], BF16, esA) for i in range(4)]
    sq = sb("sqj", [128, D], BF16, esA)
    stat = sb("statA", [128, 16], F32, esA)
    hT = [sb("hT%d" % i, [128, 8, 512], BF16, esA) for i in range(2)]
    raw = [sb("raw%d" % i, [128, 12, 514], BF16, esA) for i in range(2)]
    rkvp = sb("rkvp", [128, 12, 512], BF16, esA)
    tmps = [sb("shtmp%d" % i, [128, 512], F32, esA) for i in range(2)]
    low = [sb("low%d" % i, [128, 2, 512], BF16, esA) for i in range(2)]
    sza = [sb("sza%d" % i, [128, 512], BF16, esA) for i in range(4)]
    u_t = [sb("u_t%d" % i, [128, 512], BF16, esA) for i in range(4)]
    szb = [sb("szb%d" % i, [128, 512], BF16, esA) for i in range(4)]
    vb = [sb("vb%d" % i, [128, 512], F32, esA) for i in range(4)]
    vn = [sb("vn%d" % i, [128, 512], BF16, esA) for i in range(4)]
    mixb = [sb("mixb%d" % i, [128, 512], BF16, esA) for i in range(4)]
    vts = [sb("vts%d" % i, [128, 512], BF16, esA) for i in range(2)]
    lnst = [sb("lnst%d" % i, [128, 8], F32, esA) for i in range(4)]

    cnt = {"x": 0, "tile": 0, "grp": 0}

    def run_lanes(lanes):
        lanes = [[list(g_), W_, []] for g_, W_ in lanes]
        while any(l[0] or l[2] for l in lanes):
            for l in lanes:
                while l[0] and len(l[2]) < l[1]:
                    l[2].append(l[0].pop(0))
                for g_ in list(l[2]):
                    try:
                        next(g_)
                    except StopIteration:
                        l[2].remove(g_)

    def shift_gen(ri, GS, nt_g, gt0):
        rw = raw[ri]
        rk = "raw%d" % ri
        for j in range(12):
            tk = "shtmp%d" % (j % 2)
            tm = tmps[j % 2]
            K.op("dve", [rk, "cs0"], [tk], lambda e, j=j, tm=tm: e.tensor_scalar(out=tm[:, 0:GS], in0=rw[:, j, 1:GS + 1], scalar1=cs0[:, j:j + 1], scalar2=None, op0=ALU.mult))
            K.op("dve", [rk, tk, "pf_t"], [tk], lambda e, j=j, tm=tm: e.scalar_tensor_tensor(out=tm[:, 0:GS], in0=rw[:, j, 0:GS], scalar=pf_t[:, 24 + j:25 + j], in1=tm[:, 0:GS], op0=ALU.mult, op1=ALU.add))
            K.op("dve", [rk, tk, "pf_t"], ["rkvp"], lambda e, j=j, tm=tm: e.scalar_tensor_tensor(out=rkvp[:, j, 0:GS], in0=rw[:, j, 2:GS + 2], scalar=pf_t[:, 36 + j:37 + j], in1=tm[:, 0:GS], op0=ALU.mult, op1=ALU.add))
            yield
        for a in range(nt_g):
            gt = gt0 + a
            K.dma("sp", "rkvp", ["rkvp"], [], R["st_rkv"][gt], rkvp[:, :, a * 128:(a + 1) * 128])
            vs = vts[gt % 2]
            vk = "vts%d" % (gt % 2)
            K.ops("pe", ["rkvp", "cst"], ["ptb"], [lambda e, a=a, q=q: e.transpose(out=PT[:, q * 128:(q + 1) * 128], in_=rkvp[:, 8 + q, a * 128:(a + 1) * 128], identity=ident) for q in range(4)])
            K.op("act", ["ptb"], [vk], lambda e, vs=vs: e.copy(out=vs[:], in_=PT[:, 0:512]))
            K.dma("act", vk, [vk], [], R["st_vt"][gt], vs[:])
            yield

    def front_compute(xap, g, a):
        t0 = (g * (GS_cur[0] // 128) + a) * 128
        xi = cnt["x"] % NXB
        cnt["x"] += 1
        xk = "xt%d" % xi
        K.dma("sp", xk, [], [xk], xt[xi][:], xap[t0:t0 + 128, :])
        sc = cnt["tile"] % 8
        cnt["tile"] += 1
        nk = "xn%d" % a
        K.op("act", [xk], [nk, "statA%d" % sc], lambda e: e.activation(out=xn[a][:], in_=xt[xi][:], func=AF.Square, accum_out=stat[:, sc:sc + 1]))
        K.op("dve", ["statA%d" % sc], ["statA%d" % sc], lambda e: e.tensor_scalar(out=stat[:, sc:sc + 1], in0=stat[:, sc:sc + 1], scalar1=1.0 / D, scalar2=NORM_EPS, op0=ALU.mult, op1=ALU.add))
        K.op("act", ["statA%d" % sc], ["statA%d" % sc], lambda e: e.sqrt(out=stat[:, sc:sc + 1], in_=stat[:, sc:sc + 1]))
        K.op("dve", ["statA%d" % sc], ["statA%d" % sc], lambda e: e.reciprocal(out=stat[:, sc:sc + 1], in_=stat[:, sc:sc + 1]))
        K.op("act", [xk, "statA%d" % sc], [nk], lambda e: e.activation(out=xn[a][:], in_=xt[xi][:], func=AF.Copy, scale=stat[:, sc:sc + 1]))

    def front_transpose(a, cv, hTg, hk):
        nk = "xn%d" % a
        K.ops("pe", [nk, "cst"], ["ptb"], [lambda e, kc=kc: e.transpose(out=PT[:, kc * 128:(kc + 1) * 128], in_=xn[a][:, kc * 128:(kc + 1) * 128], identity=ident) for kc in range(8)])
        for kc in range(8):
            K.op("dve", ["ptb", "gfm", "shfm"], [hk], lambda e, kc=kc: e.tensor_scalar(out=hTg[:, kc, a * 128:(a + 1) * 128], in0=PT[:, kc * 128:(kc + 1) * 128], scalar1=gfm[:, kc, cv:cv + 1], scalar2=shfm[:, kc, cv:cv + 1], op0=ALU.mult, op1=ALU.add))

    GS_cur = [512]

    def tm_gen(s2, hTg, hk, a, gt):
        ls = lnst[s2]
        lsk = "lnst%d" % s2

        bank = lambda ci: 3 + (ci + a) % 4

        def proj(ci, c0):
            pbi = bank(ci)
            K.ops("pe", [hk, "win"], ["pb%d" % pbi], [lambda e, kc=kc: e.matmul(PB[pbi][:, :], lhsT=hTg[:, kc, a * 128:(a + 1) * 128], rhs=win[:, kc, c0:c0 + 512], start=(kc == 0), stop=(kc == 7)) for kc in range(8)])
        proj(0, 1536)
        K.op("act", ["pb%d" % bank(0)], ["sza%d" % s2], lambda e: e.activation(out=sza[s2][:], in_=PB[bank(0)][:, :], func=AF.Silu))
        yield
        K.dma("act", "sza%d" % s2, ["sza%d" % s2], [], R["st_sza"][gt], sza[s2][:])
        proj(1, 2304)
        K.op("act", ["pb%d" % bank(1)], ["u_t%d" % s2], lambda e: e.copy(out=u_t[s2][:], in_=PB[bank(1)][:, :]))
        yield
        proj(2, 2816)
        K.op("act", ["pb%d" % bank(2), lsk], ["vb%d" % s2, lsk], lambda e: e.activation(out=vb[s2][:], in_=PB[bank(2)][:, :], func=AF.Copy, accum_out=ls[:, 0:1]))
        yield
        proj(3, 3328)
        K.op("act", ["pb%d" % bank(3)], ["szb%d" % s2], lambda e: e.activation(out=szb[s2][:], in_=PB[bank(3)][:, :], func=AF.Silu))
        yield
        K.op("act", ["vb%d" % s2, lsk], ["sqj", lsk], lambda e: e.activation(out=sq[:, 0:512], in_=vb[s2][:], func=AF.Square, accum_out=ls[:, 1:2]))
        yield
        K.op("dve", [lsk], [lsk], lambda e: e.tensor_scalar(out=ls[:, 2:3], in0=ls[:, 0:1], scalar1=1.0 / 512, scalar2=None, op0=ALU.mult))
        yield
        K.op("dve", [lsk], [lsk], lambda e: e.tensor_tensor(out=ls[:, 3:4], in0=ls[:, 2:3], in1=ls[:, 2:3], op=ALU.mult))
        yield
        K.op("dve", [lsk], [lsk], lambda e: e.scalar_tensor_tensor(out=ls[:, 4:5], in0=ls[:, 1:2], scalar=1.0 / 512, in1=ls[:, 3:4], op0=ALU.mult, op1=ALU.subtract))
        yield
        K.op("dve", [lsk], [lsk], lambda e: e.tensor_scalar(out=ls[:, 5:6], in0=ls[:, 4:5], scalar1=NORM_EPS, scalar2=None, op0=ALU.add))
        yield
        K.op("act", [lsk], [lsk], lambda e: e.sqrt(out=ls[:, 5:6], in_=ls[:, 5:6]))
        yield
        K.op("dve", [lsk], [lsk], lambda e: e.reciprocal(out=ls[:, 5:6], in_=ls[:, 5:6]))
        yield
        K.op("dve", ["vb%d" % s2, lsk], ["vb%d" % s2], lambda e: e.tensor_scalar(out=vb[s2][:], in0=vb[s2][:], scalar1=ls[:, 2:3], scalar2=ls[:, 5:6], op0=ALU.subtract, op1=ALU.mult))
        yield
        K.op("pool", ["vb%d" % s2, "rows_t"], ["vb%d" % s2], lambda e: e.tensor_tensor(out=vb[s2][:], in0=vb[s2][:], in1=rows_t[:, 2048:2560], op=ALU.mult))
        yield
        K.op("pool", ["vb%d" % s2, "rows_t"], ["vn%d" % s2], lambda e: e.tensor_tensor(out=vn[s2][:], in0=vb[s2][:], in1=rows_t[:, 2560:3072], op=ALU.add))
        yield
        K.ops("pe", ["vn%d" % s2, "wsT"], ["pb0"], [lambda e, gg=gg: e.matmul(PB[0][:, gg * 64:(gg + 1) * 64], lhsT=wsT[:, gg, :], rhs=vn[s2][:, gg * 64:(gg + 1) * 64], start=True, stop=True) for gg in range(8)])
        K.op("dve", ["pb0", "bs_t"], ["vb%d" % s2], lambda e: e.tensor_tensor(out=vb[s2][:].rearrange("p (g c) -> p g c", c=64), in0=PB[0][:, :].rearrange("p (g c) -> p g c", c=64), in1=bs_t[:, :].unsqueeze(2).to_broadcast([128, 8, 64]), op=ALU.add))
        yield
        K.op("pool", ["vb%d" % s2, "u_t%d" % s2], ["vb%d" % s2], lambda e: e.tensor_tensor(out=vb[s2][:], in0=vb[s2][:], in1=u_t[s2][:], op=ALU.mult))
        yield
        K.op("pool", ["vb%d" % s2, "szb%d" % s2], ["mixb%d" % s2], lambda e: e.tensor_tensor(out=mixb[s2][:], in0=vb[s2][:], in1=szb[s2][:], op=ALU.mult))
        yield
        K.dma("sp", "mixb%d" % s2, ["mixb%d" % s2], [], R["st_mixb"][gt], mixb[s2][:])

    pre = None
    pending = [None]
    cnt["rawp"] = 0
    for si, (xap, NT, cv, gt_base, _y) in enumerate(seqs):
        GS = min(512, NT * 128)
        GS_cur[0] = GS
        nt_g = GS // 128
        NG = NT // nt_g
        nxt = seqs[si + 1] if si + 1 < len(seqs) else None
        nt_n = min(512, nxt[1] * 128) // 128 if nxt is not None else 0
        hbuf = {}
        if pre is None:
            hi = cnt["grp"] % 2
            cnt["grp"] += 1
            hbuf[0] = (hT[hi], "hT%d" % hi)
            for a in range(nt_g):
                front_compute(xap, 0, a)
            for a in range(nt_g):
                front_transpose(a, cv, *hbuf[0])
        else:
            hbuf[0] = pre
            pre = None
        for g in range(NG):
            hTg, hk = hbuf[g]
            rp = cnt["rawp"] % 2
            cnt["rawp"] += 1
            rw = raw[rp]
            rk = "raw%d" % rp
            if g == 0:
                K.op("pool", [], [rk], lambda e, rw=rw: e.memset(rw[:, :, 0:1], 0.0))
            lw = low[rp]
            lk = "low%d" % rp
            for bi, blk in enumerate(list(range(12)) + [16, 17]):
                pbi = bi % 3
                pk = "pb%d" % pbi
                K.ops("pe", [hk, "win"], [pk], [lambda e, kc=kc, blk=blk, pbi=pbi: e.matmul(PB[pbi][:, 0:GS], lhsT=win[:, kc, blk * 128:(blk + 1) * 128], rhs=hTg[:, kc, 0:GS], start=(kc == 0), stop=(kc == 7)) for kc in range(8)])
                if blk < 12:
                    K.op("act", [pk], [rk], lambda e, blk=blk, pbi=pbi, rw=rw: e.copy(out=rw[:, blk, 1:GS + 1], in_=PB[pbi][:, 0:GS]))
                elif blk == 16:
                    K.op("act", [pk], [lk], lambda e, pbi=pbi, lw=lw: e.activation(out=lw[:, 0, 0:GS], in_=PB[pbi][:, 0:GS], func=AF.Tanh))
                else:
                    K.op("act", [pk], [lk], lambda e, pbi=pbi, lw=lw: e.copy(out=lw[:, 1, 0:GS], in_=PB[pbi][:, 0:GS]))
                if g + 1 < NG and bi in (2, 5, 8, 11):
                    front_compute(xap, g + 1, (bi - 2) // 3)
                elif g + 1 == NG and nxt is not None and bi in (2, 5, 8, 11) and (bi - 2) // 3 < nt_n:
                    front_compute(nxt[0], 0, (bi - 2) // 3)
            for a in range(nt_g):
                gt = gt_base + g * nt_g + a
                K.dma("act", lk, [lk], [], R["st_low"][gt], lw[:, :, a * 128:(a + 1) * 128])
            gens = []
            if g > 0:
                rwp = raw[1 - rp]
                rkp = "raw%d" % (1 - rp)
                K.op("pool", [rk], [rkp], lambda e, rw=rw, rwp=rwp: e.tensor_copy(out=rwp[:, :, GS + 1:GS + 2], in_=rw[:, :, 1:2]))
                K.op("pool", [rkp], [rk], lambda e, rw=rw, rwp=rwp: e.tensor_copy(out=rw[:, :, 0:1], in_=rwp[:, :, GS:GS + 1]))
                gens.append(shift_gen(1 - rp, GS, nt_g, gt_base + (g - 1) * nt_g))
            elif pending[0] is not None:
                gens.append(pending[0])
                pending[0] = None
            if g == NG - 1 and nxt is None:
                K.op("pool", [], [rk], lambda e, rw=rw: e.memset(rw[:, :, GS + 1:GS + 2], 0.0))
                gens.append(shift_gen(rp, GS, nt_g, gt_base + g * nt_g))
            tms = [tm_gen(a % 4, hTg, hk, a, gt_base + g * nt_g + a) for a in range(nt_g)]
            run_lanes([(gens, 1), (tms, 4)])
            if g + 1 < NG:
                hi = cnt["grp"] % 2
                cnt["grp"] += 1
                hbuf[g + 1] = (hT[hi], "hT%d" % hi)
                for a in range(nt_g):
                    front_transpose(a, cv, *hbuf[g + 1])
            elif nxt is not None:
                hi = cnt["grp"] % 2
                cnt["grp"] += 1
                pre = (hT[hi], "hT%d" % hi)
                for a in range(nt_n):
                    front_transpose(a, nxt[2], *pre)
        g = NG - 1
        if nxt is not None:
            K.op("pool", [], [rk], lambda e, rw=rw: e.memset(rw[:, :, GS + 1:GS + 2], 0.0))
            pending[0] = shift_gen(rp, GS, nt_g, gt_base + g * nt_g)


def phase_b(nc, K, esB, sb, PB, PT, seqs, R, NT_S, NP):
    PBIG = R["PBIG"]
    pf_t, cst, lora, rkm, omka, mA, mT = R["pf_t"], R["cst"], R["lora"], R["rkm"], R["omka"], R["mA"], R["mT"]
    ident = cst[:, 0, :]

    WB = 3

    def T(name, shape, dt):
        return [sb("%s_%d" % (name, p), shape, dt, esB) for p in range(WB)]
    rk = T("Brk", [128, 8, 128], BF16)
    lowT = T("Blow", [128, 2, 128], BF16)
    Vt = T("Bvt", [128, 512], BF16)
    arm = T("Barm", [128, 4, 2, 2, 128], BF16)
    bT = T("BbT", [128, 4, 128], BF16)
    kT = T("BkT", [128, 4, 128], BF16)
    rkd = T("Brkd", [128, 4, 128], BF16)
    kk2 = T("Bkk2", [128, 4, 128], BF16)
    MM1 = T("BMM1", [128, 8, 2, 128], BF16)
    MM2 = T("BMM2", [128, 8, 2, 128], BF16)
    Q0T = T("BQ0T", [128, 8, 128], BF16)
    QR = [T("BQRa", [128, 8, 2, 128], BF16), T("BQRb", [128, 8, 2, 128], BF16)]
    QT = [T("BQTa", [128, 8, 128], BF16), T("BQTb", [128, 8, 128], BF16)]
    TT = T("BTT", [128, 8, 128], BF16)
    Bt = T("BBt", [128, 512], BF16)
    Kt = T("BKt", [128, 512], BF16)
    Zs = T("BZs", [128, 512], BF16)
    Us = T("BUs", [128, 512], BF16)
    ysb = T("Bysb", [128, 512], F32)
    bcs = T("Bbcs", [128, 8], F32)
    f32n = ["sg", "aa", "E1", "E2", "gam", "gamx", "gami", "kk"]
    F = {n: T("Bf_" + n, [128, 4, 128], F32) for n in f32n}
    F["tq"], F["rn"], F["kd"], F["kkn"] = F["sg"], F["E1"], F["E2"], F["kk"]
    tot = T("Btot", [128, 16], F32)
    gC = T("BgC", [128, 4], F32)
    ones = sb("Bones", [128, 128], F32, esB)
    S = [[[sb("BS_%d_%d_%d" % (sp, d, i), [128, 4, 64], F32, esB) for i in range(2)] for d in range(2)] for sp in range(2)]
    Sb = [[[sb("BSb_%d_%d_%d" % (sp, d, i), [128, 4, 64], BF16, esB) for i in range(2)] for d in range(2)] for sp in range(2)]
    t1s = [sb("Bt1_%d" % i, [128, 4, 64], F32, esB) for i in range(WB)]

    K.op("pool", [], ["Bones"], lambda e: e.memset(ones[:], 1.0))
    hpar = sb("Bhpar", [128, 34], F32, esB)
    K.op("pool", ["pf_t"], ["hpar"], lambda e: e.tensor_scalar(out=hpar[:, 0:16], in0=pf_t[:, 48:64], scalar1=0.5, scalar2=None, op0=ALU.mult))
    K.op("pool", ["hpar"], ["hpar"], lambda e: e.memset(hpar[:, 16:17], C2H))
    K.op("pool", ["hpar"], ["hpar"], lambda e: e.memset(hpar[:, 17:18], L2_EPS))
    K.op("pool", ["pf_t", "hpar"], ["hpar"], lambda e: e.tensor_scalar(out=hpar[:, 18:26], in0=pf_t[:, 72:80], scalar1=0.5, scalar2=None, op0=ALU.mult))
    K.op("pool", ["pf_t", "hpar"], ["hpar"], lambda e: e.tensor_scalar(out=hpar[:, 26:34], in0=pf_t[:, 72:80], scalar1=-0.5, scalar2=1.0, op0=ALU.mult, op1=ALU.add))
    for p in range(WB):
        K.op("dve", [], ["Barm_%d" % p], lambda e, p=p: e.memset(arm[p][:].rearrange("p a b c d -> p (a b c d)"), 0.0))

    chain_done = {}

    def visit(p, gt, d, sp, sidx, last_out_ap, ckey, cidx):
        t1 = t1s[p]
        b0, b1 = 2 * p, 2 * p + 1
        b2 = b0
        kt1 = "Bt1_%d" % p
        k = lambda n: "%s_%d" % (n, p)
        _al = dict(tq="sg", rn="E1", kd="E2", kkn="kk")
        fk = lambda n: "Bf_%s_%d" % (_al.get(n, n), p)
        Sin, Sbin = S[sp][d][sidx], Sb[sp][d][sidx]
        Sout, Sbout = S[sp][d][1 - sidx], Sb[sp][d][1 - sidx]
        kSin, kSbin = "BS_%d_%d_%d" % (sp, d, sidx), "BSb_%d_%d_%d" % (sp, d, sidx)
        kSout, kSbout = "BS_%d_%d_%d" % (sp, d, 1 - sidx), "BSb_%d_%d_%d" % (sp, d, 1 - sidx)
        c_w0, c_a0, c_kk, c_ka = 48 + 4 * d, 56 + 4 * d, 64 + 4 * d, 72 + 4 * d
        bc3 = lambda ap: ap.unsqueeze(2).to_broadcast([128, 4, 128])
        rT_ = rk[p][:, 0:4, :]
        kT_ = rk[p][:, 4:8, :]
        K.dma("sp", k("Brk"), [], [k("Brk")], rk[p][:], R["st_rkv"][gt][:, 0:8, :])
        yield
        K.dma("sp", k("Blow"), [], [k("Blow")], lowT[p][:], R["st_low"][gt])
        yield
        K.dma("sp", k("Bvt"), [], [k("Bvt")], Vt[p][:], R["st_vt"][gt])
        yield
        K.ops("pe", [k("Blow"), "lora"], ["pb%d" % b0], [lambda e, q=q: e.matmul(PB[b0][:, q * 128:(q + 1) * 128], lhsT=lora[:, d, q * 128:(q + 1) * 128], rhs=lowT[p][:, 0, :], start=True, stop=True) for q in range(4)])
        K.ops("pe", [k("Blow"), "lora"], ["pb%d" % b1], [lambda e, q=q: e.matmul(PB[b1][:, q * 128:(q + 1) * 128], lhsT=lora[:, 2 + d, q * 128:(q + 1) * 128], rhs=lowT[p][:, 1, :], start=True, stop=True) for q in range(4)])
        yield
        TH, THA, C2, X2 = F["sg"][p], F["aa"][p], F["E1"][p], F["E2"][p]
        for q in range(4):
            K.op("act", ["pb%d" % b0, "hpar"], [fk("sg")], lambda e, q=q: e.activation(out=TH[:, q, :], in_=PB[b0][:, q * 128:(q + 1) * 128], func=AF.Tanh, scale=0.5, bias=hpar[:, 4 * d + q:4 * d + q + 1]))
        yield
        for q in range(4):
            K.op("act", ["pb%d" % b1, "hpar"], [fk("aa")], lambda e, q=q: e.activation(out=THA[:, q, :], in_=PB[b1][:, q * 128:(q + 1) * 128], func=AF.Tanh, scale=0.5, bias=hpar[:, 8 + 4 * d + q:8 + 4 * d + q + 1]))
        yield
        for q in range(4):
            if d == 0:
                K.op("dve", [fk("sg"), "Bones"], [fk("E1")], lambda e, q=q: e.tensor_tensor_scan(out=C2[:, q, :], data0=ones[:, :], data1=TH[:, q, :], initial=0.0, op0=ALU.add, op1=ALU.add))
            else:
                K.op("dve", [fk("sg"), "Bones"], [fk("E1")], lambda e, q=q: e.tensor_tensor_scan(out=C2[:, q, ::-1], data0=ones[:, :], data1=TH[:, q, ::-1], initial=0.0, op0=ALU.add, op1=ALU.add))
        yield
        K.op("pool", [fk("E1"), fk("sg")], [fk("E2")], lambda e: e.tensor_tensor(out=X2[:], in0=C2[:], in1=TH[:], op=ALU.subtract))
        yield
        tcol = 127 if d == 0 else 0
        K.op("pool", [fk("E1")], [k("Btot")], lambda e: e.tensor_scalar(out=tot[p][:, 0:4], in0=C2[:, :, tcol], scalar1=-C2H, scalar2=None, op0=ALU.mult))
        yield
        K.op("act", [k("Btot")], [k("BgC")], lambda e: e.activation(out=gC[p][:], in_=tot[p][:, 0:4], func=AF.Exp))
        yield
        K.op("act", [fk("E1")], [fk("gam")], lambda e: e.activation(out=F["gam"][p][:], in_=C2[:], func=AF.Exp, scale=-C2H))
        yield
        K.op("act", [fk("E2"), "hpar"], [fk("gamx")], lambda e: e.activation(out=F["gamx"][p][:], in_=X2[:], func=AF.Exp, scale=-C2H, bias=hpar[:, 16:17]))
        yield
        K.op("act", [fk("E1")], [fk("gami")], lambda e: e.activation(out=F["gami"][p][:], in_=C2[:], func=AF.Exp, scale=C2H))
        yield
        K.op("dve", [k("Brk"), "pf_t"], [fk("kk")], lambda e: e.tensor_tensor(out=F["kk"][p][:], in0=kT_, in1=bc3(pf_t[:, c_kk:c_kk + 4]), op=ALU.mult))
        yield
        K.op("pool", [fk("kk")], [k("Bkk2")], lambda e: e.tensor_tensor(out=kk2[p][:], in0=F["kk"][p][:], in1=F["kk"][p][:], op=ALU.mult))
        yield
        K.ops("pe", [k("Bkk2"), "cst"], ["pb%d" % b2], [lambda e, q=q: e.matmul(PB[b2][:, q * 128:(q + 1) * 128], lhsT=cst[:, 5, :], rhs=kk2[p][:, q, :], start=True, stop=True) for q in range(4)])
        yield
        K.op("act", ["pb%d" % b2, "hpar"], [fk("rn")], lambda e: e.activation(out=F["rn"][p][:].rearrange("p a b -> p (a b)"), in_=PB[b2][:, :], func=AF.Ln, bias=hpar[:, 17:18]))
        yield
        K.op("act", [fk("rn")], [fk("rn")], lambda e: e.activation(out=F["rn"][p][:], in_=F["rn"][p][:], func=AF.Exp, scale=-0.5))
        yield
        K.op("pool", [fk("kk"), fk("rn")], [fk("kkn")], lambda e: e.tensor_tensor(out=F["kkn"][p][:], in0=F["kk"][p][:], in1=F["rn"][p][:], op=ALU.mult))
        yield
        for q in range(4):
            K.op("pool", [fk("aa"), "hpar"], [fk("tq")], lambda e, q=q: e.tensor_scalar(out=F["tq"][p][:, q, :], in0=THA[:, q, :], scalar1=hpar[:, 18 + 4 * d + q:19 + 4 * d + q], scalar2=hpar[:, 26 + 4 * d + q:27 + 4 * d + q], op0=ALU.mult, op1=ALU.add))
        yield
        K.op("pool", [fk("tq"), k("Brk")], [fk("kd")], lambda e: e.tensor_tensor(out=F["kd"][p][:], in0=F["tq"][p][:], in1=kT_, op=ALU.mult))
        yield
        for hp in range(2):
            rs = slice(hp * 64, (hp + 1) * 64)
            K.op("dve", [fk("kkn"), fk("gamx")], [k("Barm")], lambda e, rs=rs, hp=hp: e.scalar_tensor_tensor(out=arm[p][rs, :, hp, 0, :], in0=F["kkn"][p][rs, :, :], scalar=-1.0, in1=F["gamx"][p][rs, :, :], op0=ALU.mult, op1=ALU.mult))
            yield
            K.op("pool", [k("Brk"), fk("gam")], [k("Barm")], lambda e, rs=rs, hp=hp: e.tensor_tensor(out=arm[p][rs, :, hp, 1, :], in0=rk[p][rs, 0:4, :], in1=F["gam"][p][rs, :, :], op=ALU.mult))
            yield
        K.op("dve", [fk("kkn"), fk("aa")], [fk("tq")], lambda e: e.scalar_tensor_tensor(out=F["tq"][p][:], in0=THA[:], scalar=1.0, in1=F["kkn"][p][:], op0=ALU.add, op1=ALU.mult))
        yield
        K.op("dve", [fk("tq"), fk("gami")], [k("BbT")], lambda e: e.scalar_tensor_tensor(out=bT[p][:], in0=F["tq"][p][:], scalar=0.5, in1=F["gami"][p][:], op0=ALU.mult, op1=ALU.mult))
        yield
        K.op("pool", [fk("kd"), fk("gami")], [k("BkT")], lambda e: e.tensor_tensor(out=kT[p][:], in0=F["kd"][p][:], in1=F["gami"][p][:], op=ALU.mult))
        yield
        K.op("pool", [fk("kd"), k("Brk")], [k("Brkd")], lambda e: e.tensor_tensor(out=rkd[p][:], in0=F["kd"][p][:], in1=rT_, op=ALU.mult))
        yield
        K.ops("pe", [k("Brkd"), "rkm"], ["pb6"], [lambda e, q=q: e.matmul(PB[6][:, q * 2:q * 2 + 2], lhsT=rkd[p][:, q, :], rhs=rkm[:, d * 8 + q * 2:d * 8 + q * 2 + 2], start=True, stop=True) for q in range(4)])
        K.op("act", ["pb6"], [k("Bbcs")], lambda e: e.copy(out=bcs[p][:], in_=PB[6][:, 0:8]))
        yield
        K.dma("act", k("Bbcs"), [k("Bbcs")], [], R["st_bc"][d, gt], bcs[p][:])
        yield
        K.ops("pe", [k("BbT"), k("BkT"), "cst"], ["ptb"],
              [lambda e, q=q: e.transpose(out=PT[:, q * 128:(q + 1) * 128], in_=bT[p][:, q, :], identity=ident) for q in range(4)] +
              [lambda e, q=q: e.transpose(out=PT[:, 512 + q * 128:512 + (q + 1) * 128], in_=kT[p][:, q, :], identity=ident) for q in range(4)])
        K.op("act", ["ptb"], [k("BBt")], lambda e: e.copy(out=Bt[p][:], in_=PT[:, 0:512]))
        K.op("act", ["ptb"], [k("BKt")], lambda e: e.copy(out=Kt[p][:], in_=PT[:, 512:1024]))
        yield
        for q in range(4):
            fa, fb, fc = [], [], []
            for hp in range(2):
                rhsA = arm[p][:, q, hp, :, :].rearrange("p a t -> p (a t)")
                fa.append(lambda e, hp=hp, rhsA=rhsA: e.matmul(PB[b0][:, hp * 256:(hp + 1) * 256], lhsT=bT[p][:, q, :], rhs=rhsA, start=True, stop=True))
                fb.append(lambda e, hp=hp, rhsA=rhsA: e.matmul(PB[b1][:, hp * 256:(hp + 1) * 256], lhsT=kT[p][:, q, :], rhs=rhsA, start=True, stop=True))
                fc.append(lambda e, hp=hp: e.matmul(PB[b0][:, hp * 128:(hp + 1) * 128], lhsT=arm[p][:, q, hp, 0, :], rhs=bT[p][:, q, :], start=True, stop=True))
            K.ops("pe", [k("BbT"), k("Barm")], ["pb%d" % b0], fa)
            K.ops("pe", [k("BkT"), k("Barm")], ["pb%d" % b1], fb)
            yield
            K.op("dve", ["pb%d" % b0, "mA"], [k("BMM1") + "q%d" % q], lambda e, q=q: e.tensor_tensor(out=MM1[p][:, 2 * q:2 * q + 2, :, :].rearrange("p h a t -> p (h a t)"), in0=PB[b0][:, :], in1=mA[:, d, :, :].rearrange("p a t -> p (a t)"), op=ALU.mult))
            yield
            K.ops("pe", [k("BbT"), k("Barm")], ["pb%d" % b0], fc)
            K.op("dve", ["pb%d" % b1, "mA"], [k("BMM2") + "q%d" % q], lambda e, q=q: e.tensor_tensor(out=MM2[p][:, 2 * q:2 * q + 2, :, :].rearrange("p h a t -> p (h a t)"), in0=PB[b1][:, :], in1=mA[:, d, :, :].rearrange("p a t -> p (a t)"), op=ALU.mult))
            yield
            K.op("dve", ["pb%d" % b0, "mT"], [k("BQ0T") + "q%d" % q], lambda e, q=q: e.tensor_tensor(out=Q0T[p][:, 2 * q:2 * q + 2, :].rearrange("p h t -> p (h t)"), in0=PB[b0][:, 0:256], in1=mT[:, d, :, :].rearrange("p a t -> p (a t)"), op=ALU.mult))
            yield
        kQR = lambda a, q: "BQR%s_%dq%d" % ("ab"[a], p, q)
        kQT = lambda a, q: "BQT%s_%dq%d" % ("ab"[a], p, q)
        XA = PB[b0]
        XB = PB[b1]
        for q in range(4):
            hs = [2 * q, 2 * q + 1]
            fa = []
            for hh, h in enumerate(hs):
                fa.append(lambda e, hh=hh, h=h: e.matmul(XA[:, hh * 256:hh * 256 + 128], lhsT=Q0T[p][:, h, :], rhs=MM1[p][:, h, 0, :], start=True, stop=True))
                fa.append(lambda e, hh=hh, h=h: e.matmul(XA[:, hh * 256 + 128:hh * 256 + 256], lhsT=ident, rhs=MM1[p][:, h, 0, :], start=False, stop=False, skip_group_check=True))
                fa.append(lambda e, hh=hh, h=h: e.matmul(XA[:, hh * 256 + 128:hh * 256 + 256], lhsT=ident, rhs=ident, start=False, stop=True, skip_group_check=True))
            K.ops("pe", [k("BMM1") + "q%d" % q, k("BQ0T") + "q%d" % q, "cst"], ["pb%d" % b0], fa)
            K.ops("pe", [k("BMM1") + "q%d" % q, k("BQ0T") + "q%d" % q], ["pb%d" % b1], [lambda e, hh=hh, h=h: e.matmul(XB[:, hh * 128:(hh + 1) * 128], lhsT=MM1[p][:, h, 0, :], rhs=Q0T[p][:, h, :], start=True, stop=True) for hh, h in enumerate(hs)])
            yield
            K.op("act", ["pb%d" % b0], [kQR(1, q)], lambda e, q=q: e.copy(out=QR[1][p][:, 2 * q:2 * q + 2, :, :].rearrange("p h a t -> p (h a t)"), in_=XA[:, :]), c=0.5)
            yield
            K.op("dve", ["pb%d" % b1], [kQT(1, q)], lambda e, q=q: e.tensor_copy(out=QT[1][p][:, 2 * q:2 * q + 2, :].rearrange("p h t -> p (h t)"), in_=XB[:, 0:256]), c=0.42)
            yield
        for lev in range(1, 6):
            cur, nxt = lev % 2, (lev + 1) % 2
            for q in range(4):
                hs = [2 * q, 2 * q + 1]
                fa = []
                for hh, h in enumerate(hs):
                    fa.append(lambda e, hh=hh, h=h: e.matmul(XA[:, hh * 256:(hh + 1) * 256], lhsT=QT[cur][p][:, h, :], rhs=QR[cur][p][:, h, :, :].rearrange("p a t -> p (a t)"), start=True, stop=False))
                    fa.append(lambda e, hh=hh, h=h: e.matmul(XA[:, hh * 256 + 128:(hh + 1) * 256], lhsT=ident, rhs=QR[cur][p][:, h, 1, :], start=False, stop=True))
                K.ops("pe", [kQR(cur, q), kQT(cur, q), "cst"], ["pb%d" % b0], fa)
                K.ops("pe", [kQR(cur, q), kQT(cur, q)], ["pb%d" % b1], [lambda e, hh=hh, h=h: e.matmul(XB[:, hh * 128:(hh + 1) * 128], lhsT=QR[cur][p][:, h, 0, :], rhs=QT[cur][p][:, h, :], start=True, stop=True) for hh, h in enumerate(hs)])
                yield
                K.op("act", ["pb%d" % b0], [kQR(nxt, q)], lambda e, q=q: e.copy(out=QR[nxt][p][:, 2 * q:2 * q + 2, :, :].rearrange("p h a t -> p (h a t)"), in_=XA[:, :]), c=0.5)
                yield
                K.op("dve", ["pb%d" % b1], [kQT(nxt, q)], lambda e, q=q: e.tensor_copy(out=QT[nxt][p][:, 2 * q:2 * q + 2, :].rearrange("p h t -> p (h t)"), in_=XB[:, 0:256]), c=0.42)
                yield
        for q in range(4):
            hs = [2 * q, 2 * q + 1]
            ff = []
            bq = b0 if q % 2 == 0 else b1
            for hh, h in enumerate(hs):
                ff.append(lambda e, hh=hh, h=h, bq=bq: e.matmul(PB[bq][:, hh * 128:(hh + 1) * 128], lhsT=QT[0][p][:, h, :], rhs=QR[0][p][:, h, 1, :], start=True, stop=False))
                ff.append(lambda e, hh=hh, h=h, bq=bq: e.matmul(PB[bq][:, hh * 128:(hh + 1) * 128], lhsT=ident, rhs=QR[0][p][:, h, 1, :], start=False, stop=True))
            K.ops("pe", [kQR(0, q), kQT(0, q), "cst"], ["pb%d" % bq], ff)
            yield
            if q % 2 == 0:
                K.op("act", ["pb%d" % bq], [k("BTT") + "q%d" % q], lambda e, q=q, bq=bq: e.copy(out=TT[p][:, 2 * q:2 * q + 2, :].rearrange("p h t -> p (h t)"), in_=PB[bq][:, 0:256]), c=0.4)
            else:
                K.op("dve", ["pb%d" % bq], [k("BTT") + "q%d" % q], lambda e, q=q, bq=bq: e.tensor_copy(out=TT[p][:, 2 * q:2 * q + 2, :].rearrange("p h t -> p (h t)"), in_=PB[bq][:, 0:256]), c=0.42)
            yield
        while chain_done.get(ckey, 0) < cidx:
            yield
        fz = []
        for h in range(8):
            q, hp = h // 2, h % 2
            fz.append(lambda e, h=h, q=q, hp=hp: e.matmul(PB[b0][:, h * 64:(h + 1) * 64], lhsT=arm[p][:, q, hp, 0, :], rhs=Sbin[:, q, :], start=True, stop=False))
            fz.append(lambda e, h=h: e.matmul(PB[b0][:, h * 64:(h + 1) * 64], lhsT=MM2[p][:, h, 0, :], rhs=Vt[p][:, h * 64:(h + 1) * 64], start=False, stop=True))
        K.ops("pe", [k("Barm"), kSbin, k("Bvt")] + [k("BMM2") + "q%d" % q for q in range(4)], ["pb%d" % b0], fz)
        yield
        K.op("act", ["pb%d" % b0], [k("BZs")], lambda e: e.copy(out=Zs[p][:], in_=PB[b0][:, :]))
        yield
        K.ops("pe", [k("BTT") + "q%d" % q for q in range(4)] + [k("BZs")], ["pb%d" % b1], [lambda e, h=h: e.matmul(PB[b1][:, h * 64:(h + 1) * 64], lhsT=TT[p][:, h, :], rhs=Zs[p][:, h * 64:(h + 1) * 64], start=True, stop=True) for h in range(8)])
        yield
        K.op("dve", ["pb%d" % b1], [k("BUs")], lambda e: e.tensor_copy(out=Us[p][:], in_=PB[b1][:, :]))
        yield
        fd = []
        for h in range(8):
            q = h // 2
            fd.append(lambda e, h=h, q=q: e.matmul(PB[b2][:, h * 64:(h + 1) * 64], lhsT=Bt[p][:, q * 128:(q + 1) * 128], rhs=Us[p][:, h * 64:(h + 1) * 64], start=True, stop=False))
            fd.append(lambda e, h=h, q=q: e.matmul(PB[b2][:, h * 64:(h + 1) * 64], lhsT=Kt[p][:, q * 128:(q + 1) * 128], rhs=Vt[p][:, h * 64:(h + 1) * 64], start=False, stop=True))
        K.ops("pe", [k("BBt"), k("BKt"), k("BUs"), k("Bvt")], ["pb%d" % b2], fd)
        yield
        Dv = PB[b2][:, :].rearrange("p (q hp j) -> p q hp j", hp=2, j=64)
        for hp in range(2):
            rs = slice(hp * 64, (hp + 1) * 64)
            K.op("dve", ["pb%d" % b2, kSin], [kt1], lambda e, rs=rs, hp=hp: e.tensor_tensor(out=t1[rs, :, :], in0=Dv[rs, :, hp, :], in1=Sin[rs, :, :], op=ALU.add))
            yield
        gb = gC[p][:, :].unsqueeze(2).to_broadcast([128, 4, 64])
        K.op("dve", [kt1, k("BgC")], [kSout], lambda e: e.tensor_tensor(out=Sout[:], in0=t1[:], in1=gb, op=ALU.mult))
        yield
        K.op("pool", [kt1, k("BgC")], [kSbout], lambda e: e.tensor_tensor(out=Sbout[:], in0=t1[:], in1=gb, op=ALU.mult))
        yield
        fy = []
        for h in range(8):
            q, hp = h // 2, h % 2
            fy.append(lambda e, h=h, q=q, hp=hp: e.matmul(PB[b0][:, h * 64:(h + 1) * 64], lhsT=arm[p][:, q, hp, 1, :], rhs=Sbin[:, q, :], start=True, stop=False))
            fy.append(lambda e, h=h: e.matmul(PB[b0][:, h * 64:(h + 1) * 64], lhsT=MM1[p][:, h, 1, :], rhs=Us[p][:, h * 64:(h + 1) * 64], start=False, stop=False))
            fy.append(lambda e, h=h: e.matmul(PB[b0][:, h * 64:(h + 1) * 64], lhsT=MM2[p][:, h, 1, :], rhs=Vt[p][:, h * 64:(h + 1) * 64], start=False, stop=True))
        K.ops("pe", [k("Barm"), kSbin, k("BUs"), k("Bvt")] + [k("BMM1") + "q%d" % q for q in range(4)] + [k("BMM2") + "q%d" % q for q in range(4)], ["pb%d" % b0], fy)
        yield
        chain_done[ckey] = cidx + 1
        K.op("act", ["pb%d" % b0], [k("Bysb")], lambda e: e.copy(out=ysb[p][:], in_=PB[b0][:, :]))
        yield
        K.dma("act", k("Bysb"), [k("Bysb")], [], R["st_y"][d, gt], ysb[p][:])
        yield
        if last_out_ap is not None:
            K.dma("sp", kSout, [kSout], [], last_out_ap, Sout[:])
            yield

    queue = []
    for si, (xap, NT, cv, gt_base, _y) in enumerate(seqs):
        sp = si % 2
        for step in range(NT):
            sidx = step % 2
            last = (step == NT - 1) and si > 0
            queue.append((si, step, gt_base + step, 0, sp, sidx, R["nst"][si - 1, 0] if last else None, (si, 0), step))
            queue.append((si, step, gt_base + NT - 1 - step, 1, sp, sidx, R["nst"][si - 1, 1] if last else None, (si, 1), step))

    def init_state(si):
        sp = si % 2
        for d in range(2):
            if si == 0:
                K.dma("sp", "BS_%d_%d_0" % (sp, d), [], ["BS_%d_%d_0" % (sp, d)], S[sp][d][0][:], R["s0T"][d])
                K.op("pool", ["BS_%d_%d_0" % (sp, d)], ["BSb_%d_%d_0" % (sp, d)], lambda e, d=d: e.tensor_copy(out=Sb[sp][d][0][:], in_=S[sp][d][0][:]))
            else:
                K.op("pool", [], ["BS_%d_%d_0" % (sp, d)], lambda e, d=d: e.memset(S[sp][d][0][:].rearrange("p a b -> p (a b)"), 0.0))
                K.op("pool", [], ["BSb_%d_%d_0" % (sp, d)], lambda e, d=d: e.memset(Sb[sp][d][0][:].rearrange("p a b -> p (a b)"), 0.0))

    def before(item):
        si, step, gt, d, sp, sidx, lo = item[:7]
        if step == 0 and d == 0:
            init_state(si)

    K.run_streams(queue, lambda slot, it: visit(slot, it[2], it[3], it[4], it[5], it[6], it[7], it[8]), WB, before, offset=VOFF)


def phase_c(nc, K, esC, sb, PB, PT, seqs, R):
    cst, w_out = R["cst"], R["w_out"]
    ident = cst[:, 0, :]
    rows_t = sb("Crows", [128, 3072], F32, esC)
    Gt = sb("CGt", [128, 2, 1024], F32, esC)
    K.dma("act", "rows_t", [], ["rows_t"], rows_t[:], R["rows"])
    K.dma("act", "Gt", [], ["Gt"], Gt[:], R["st_gt"])
    wout = sb("Cwout", [128, 8, D], BF16, esC)
    wst = [sb("Cwst%d" % i, [128, D], F32, esC) for i in range(2)]
    def wdma(kc):
        K.dma("act", "Cwst%d" % (kc % 2), [], ["Cwst%d" % (kc % 2)], wst[kc % 2][:], w_out[kc * 128:(kc + 1) * 128, :])
    wdma(0)
    wdma(1)
    for kc in range(8):
        K.op("dve", ["Cwst%d" % (kc % 2)], ["Cwout"], lambda e, kc=kc: e.tensor_copy(out=wout[:, kc, :], in_=wst[kc % 2][:]))
        if kc + 2 < 8:
            wdma(kc + 2)

    WC = 5
    ctr = [0]

    def T(name, shape, dt):
        return [sb("%s_%d" % (name, p), shape, dt, esC) for p in range(WC)]
    yf = T("Cyf", [128, 8, 64], F32)
    yb = T("Cyb", [128, 8, 64], F32)
    bcf = T("Cbcf", [128, 8], F32)
    bcb = T("Cbcb", [128, 8], F32)
    Vt = T("Cvt", [128, 8, 64], BF16)
    sza = T("Csza", [128, 512], BF16)
    mixed = T("Cmixed", [128, D], BF16)
    xt = T("Cxt", [128, D], F32)
    junk = sb("Cjunk", [128, 512], F32, esC)
    ysq = T("Cysq", [128, 8, 64], F32)
    st = T("Cst", [128, 40], F32)
    bon = T("Cbon", [128, 8, 64], F32)
    mixT = T("CmixT", [128, 8, 128], BF16)
    ot = T("Cot", [128, D], F32)

    def tile(p, xap, a, cv, gt, yap):
        if True:
            n0 = 2 * (ctr[0] % 3)
            ctr[0] += 1
            k = lambda n: "%s_%d" % (n, p)
            K.dma("sp", k("Cyf"), [], [k("Cyf")], yf[p][:].rearrange("p h j -> p (h j)"), R["st_y"][0, gt])
            yield
            K.dma("sp", k("Cyb"), [], [k("Cyb")], yb[p][:].rearrange("p h j -> p (h j)"), R["st_y"][1, gt])
            yield
            K.dma("sp", k("Cbcf"), [], [k("Cbcf")], bcf[p][:], R["st_bc"][0, gt])
            yield
            K.dma("sp", k("Cbcb"), [], [k("Cbcb")], bcb[p][:], R["st_bc"][1, gt])
            yield
            K.dma("sp", k("Cvt"), [], [k("Cvt")], Vt[p][:].rearrange("p h j -> p (h j)"), R["st_vt"][gt])
            yield
            K.dma("sp", k("Csza"), [], [k("Csza")], sza[p][:], R["st_sza"][gt])
            yield
            K.dma("sp", k("Cmixed") + "b", [], [k("Cmixed") + "b"], mixed[p][:, 512:1024], R["st_mixb"][gt])
            yield
            K.dma("sp", k("Cxt"), [], [k("Cxt")], xt[p][:], xap[a * 128:(a + 1) * 128, :])
            yield
            K.op("pool", [k("Cyf"), k("Cyb")], [k("Cyf")], lambda e: e.tensor_tensor(out=yf[p][:], in0=yf[p][:], in1=yb[p][:], op=ALU.add))
            yield
            s_ = st[p]
            K.op("dve", [k("Cyf")], [k("Cst") + "a"], lambda e: e.tensor_reduce(out=s_[:, 0:8], in_=yf[p][:], axis=AX.X, op=ALU.add))
            yield
            K.op("pool", [k("Cyf")], [k("Cysq")], lambda e: e.tensor_tensor(out=ysq[p][:], in0=yf[p][:], in1=yf[p][:], op=ALU.mult))
            yield
            K.op("dve", [k("Cysq")], [k("Cst") + "b"], lambda e: e.tensor_reduce(out=s_[:, 8:16], in_=ysq[p][:], axis=AX.X, op=ALU.add))
            yield
            sk = [k("Cst") + "a", k("Cst") + "b"]
            kc_ = k("Cst") + "c"
            K.op("dve", sk, [kc_], lambda e: e.tensor_scalar(out=s_[:, 16:24], in0=s_[:, 0:8], scalar1=1.0 / 64, scalar2=None, op0=ALU.mult))
            yield
            K.op("dve", [kc_], [kc_], lambda e: e.tensor_tensor(out=s_[:, 24:32], in0=s_[:, 16:24], in1=s_[:, 16:24], op=ALU.mult))
            yield
            K.op("dve", sk + [kc_], [kc_], lambda e: e.scalar_tensor_tensor(out=s_[:, 32:40], in0=s_[:, 8:16], scalar=1.0 / 64, in1=s_[:, 24:32], op0=ALU.mult, op1=ALU.subtract))
            yield
            K.op("dve", [kc_], [kc_], lambda e: e.tensor_scalar(out=s_[:, 32:40], in0=s_[:, 32:40], scalar1=GN_EPS, scalar2=None, op0=ALU.add))
            yield
            K.op("act", [kc_], [kc_], lambda e: e.sqrt(out=s_[:, 32:40], in_=s_[:, 32:40]))
            yield
            K.op("dve", [kc_], [kc_], lambda e: e.reciprocal(out=s_[:, 32:40], in_=s_[:, 32:40]))
            yield
            b8 = lambda ap: ap.unsqueeze(2).to_broadcast([128, 8, 64])
            K.op("dve", [k("Cyf"), kc_], [k("Cyf")], lambda e: e.tensor_tensor(out=yf[p][:], in0=yf[p][:], in1=b8(s_[:, 16:24]), op=ALU.subtract))
            yield
            K.op("dve", [k("Cyf"), kc_], [k("Cyf")], lambda e: e.tensor_tensor(out=yf[p][:], in0=yf[p][:], in1=b8(s_[:, 32:40]), op=ALU.mult))
            yield
            yfl = yf[p][:].rearrange("p h j -> p (h j)")
            K.op("pool", [k("Cyf"), "rows_t"], [k("Cyf")], lambda e: e.tensor_tensor(out=yfl, in0=yfl, in1=rows_t[:, 1024:1536], op=ALU.mult))
            yield
            K.op("pool", [k("Cyf"), "rows_t"], [k("Cyf")], lambda e: e.tensor_tensor(out=yfl, in0=yfl, in1=rows_t[:, 1536:2048], op=ALU.add))
            yield
            K.op("dve", [k("Cbcf"), k("Cbcb")], [k("Cbcf")], lambda e: e.tensor_tensor(out=bcf[p][:], in0=bcf[p][:], in1=bcb[p][:], op=ALU.add))
            yield
            K.op("dve", [k("Cvt"), k("Cbcf")], [k("Cbon")], lambda e: e.tensor_tensor(out=bon[p][:], in0=Vt[p][:], in1=b8(bcf[p][:, :]), op=ALU.mult))
            yield
            K.op("dve", [k("Cyf"), k("Cbon")], [k("Cyf")], lambda e: e.tensor_tensor(out=yf[p][:], in0=yf[p][:], in1=bon[p][:], op=ALU.add))
            yield
            K.op("dve", [k("Cyf"), k("Csza")], [k("Cmixed") + "a"], lambda e: e.tensor_tensor(out=mixed[p][:, 0:512], in0=yfl, in1=sza[p][:], op=ALU.mult))
            yield
            K.ops("pe", [k("Cmixed") + "a", k("Cmixed") + "b", "cst"], ["ptb"], [lambda e, kc=kc: e.transpose(out=PT[:, kc * 128:(kc + 1) * 128], in_=mixed[p][:, kc * 128:(kc + 1) * 128], identity=ident) for kc in range(8)])
            K.op("act", ["ptb"], [k("CmixT")], lambda e: e.copy(out=mixT[p][:].rearrange("p a b -> p (a b)"), in_=PT[:, :]))
            yield
            for n in range(2):
                K.ops("pe", [k("CmixT"), "Cwout"], ["pb%d" % (n0 + n)], [lambda e, kc=kc, n=n: e.matmul(PB[n0 + n][:, :], lhsT=mixT[p][:, kc, :], rhs=wout[:, kc, n * 512:(n + 1) * 512], start=(kc == 0), stop=(kc == 7)) for kc in range(8)])
            kr = k("Cst") + "r"
            for n in range(2):
                K.op("act", ["pb%d" % (n0 + n)], ["Cjunk", kr + "%d" % n], lambda e, n=n: e.activation(out=junk[:, :], in_=PB[n0 + n][:, :], func=AF.Square, accum_out=s_[:, 2 + n:3 + n] if False else s_[:, 0 + n:1 + n]))
            K.op("dve", [kr + "0", kr + "1"] + sk + [kc_], [kr], lambda e: e.tensor_tensor(out=s_[:, 2:3], in0=s_[:, 0:1], in1=s_[:, 1:2], op=ALU.add))
            K.op("dve", [kr], [kr], lambda e: e.tensor_scalar(out=s_[:, 2:3], in0=s_[:, 2:3], scalar1=1.0 / D, scalar2=NORM_EPS, op0=ALU.mult, op1=ALU.add))
            K.op("act", [kr], [kr], lambda e: e.sqrt(out=s_[:, 2:3], in_=s_[:, 2:3]))
            K.op("dve", [kr], [kr], lambda e: e.reciprocal(out=s_[:, 2:3], in_=s_[:, 2:3]))
            for n in range(2):
                K.op("dve", ["pb%d" % (n0 + n), kr, "Gt"], [k("Cot")], lambda e, n=n: e.scalar_tensor_tensor(out=ot[p][:, n * 512:(n + 1) * 512], in0=PB[n0 + n][:, :], scalar=s_[:, 2:3], in1=Gt[:, cv, n * 512:(n + 1) * 512], op0=ALU.mult, op1=ALU.mult))
            K.op("pool", [k("Cot"), k("Cxt")], [k("Cot")], lambda e: e.tensor_tensor(out=ot[p][:], in0=ot[p][:], in1=xt[p][:], op=ALU.add))
            yield
            K.dma("sp", k("Cot"), [k("Cot")], [], yap[a * 128:(a + 1) * 128, :], ot[p][:])
            yield


    queue = []
    for si, (xap, NT, cv, gt_base, yap) in enumerate(seqs):
        for a in range(NT):
            queue.append((xap, a, cv, gt_base + a, yap))
    K.run_streams(queue, lambda slot, it: tile(slot, *it), WC, offset=4.0)


def _prep_shared(inp):
    f = np.float32
    g = lambda k: np.asarray(inp[k], dtype=f)
    fm = lambda v, nb: np.ascontiguousarray(v.reshape(nb, 128).T)
    pf = np.zeros((128, 80), f)
    b_mod = g("b_mod")[0]
    pf[:, 0:16] = fm(b_mod[0:2048], 16)
    pf[:, 16:24] = fm(g("ln_pre")[0], 8)
    ts = g("ts_mu")[0]
    pf[:, 24:36] = fm(ts[0], 12)
    pf[:, 36:48] = fm(ts[1], 12)
    for d in range(2):
        pf[:, 48 + 4 * d:52 + 4 * d] = fm(g("w0")[0, d], 4)
        pf[:, 56 + 4 * d:60 + 4 * d] = fm(g("a0")[0, d], 4)
        pf[:, 64 + 4 * d:68 + 4 * d] = fm(g("k_k")[0, d], 4)
        pf[:, 72 + 4 * d:76 + 4 * d] = fm(g("k_a")[0, d], 4)
    rows = np.zeros((128, 3072), f)
    rows[:, 0:1024] = g("ln_post")[0][None, :]
    rows[:, 1024:1536] = g("gn_w")[0][None, :]
    rows[:, 1536:2048] = g("gn_b")[0][None, :]
    rows[:, 2048:2560] = g("sgu_ln_g")[0][None, :]
    rows[:, 2560:3072] = g("sgu_ln_b")[0][None, :]
    bmg = np.ascontiguousarray(np.broadcast_to(b_mod[2048:3072][None, :], (2, 1024))).astype(f)
    lora_w = np.zeros((128, 4, 512), f)
    for d in range(2):
        lora_w[d * 64:(d + 1) * 64, d, :] = g("w_up")[0, d]
        lora_w[d * 64:(d + 1) * 64, 2 + d, :] = g("a_up")[0, d]
    rk = g("r_k")[0]
    rk_in = np.zeros((128, 2, 4, 2), f)
    for d in range(2):
        for q in range(4):
            for hp in range(2):
                rk_in[hp * 64:(hp + 1) * 64, d, q, hp] = rk[d, 2 * q + hp]
    rk_in = rk_in.reshape(128, 16)
    wsT = np.ascontiguousarray(np.transpose(g("w_s")[0], (2, 0, 1)))
    bs = np.ascontiguousarray(g("b_s")[0].T)
    s_idx = np.arange(128)[:, None]
    t_idx = np.arange(128)[None, :]
    consts = np.zeros((128, 6, 128), f)
    consts[:, 0] = (s_idx == t_idx)
    consts[:, 1] = (s_idx < t_idx)
    consts[:, 2] = (s_idx <= t_idx)
    consts[:, 3] = (s_idx > t_idx)
    consts[:, 4] = (s_idx >= t_idx)
    consts[:, 5] = ((s_idx // 64) == (t_idx // 64))
    sel = np.zeros((2, 2, 128), f)
    sel[0, 0, :] = 1.0
    sel[1, 1, :] = 1.0
    return dict(w_in=np.ascontiguousarray(g("w_in")[0]), w_out=np.ascontiguousarray(g("w_out")[0]),
                w_mod=np.ascontiguousarray(g("w_mod")[0]), pf=pf, rows=rows, bmg=bmg, lora_w=lora_w,
                rk_in=rk_in, wsT=wsT, bs=bs, consts=consts, sel=sel)


def _prep_core(inp, shared, i, NT_S, NP):
    f = np.float32
    m = dict(shared)
    m["xs"] = np.ascontiguousarray(np.asarray(inp["x_sample"][i], f)[:NT_S * 128])
    m["xp"] = np.ascontiguousarray(np.asarray(inp["x_prompt"][NP * i:NP * (i + 1)], f))
    c = np.asarray(inp["c"][i], f)
    cc = np.asarray(inp["c_ctx"], f)
    cvT = np.zeros((128, 8, 2), f)
    cvT[:, :, 0] = c.reshape(8, 128).T
    cvT[:, :, 1] = cc.reshape(8, 128).T
    m["cvT"] = cvT
    s0T = np.zeros((2, 128, 4, 64), f)
    for d, key in enumerate(["state_fwd", "state_bwd"]):
        S = np.asarray(inp[key][i, 0], f)
        S4 = S.reshape(4, 2, 64, 64)
        s0T[d] = np.transpose(S4, (1, 3, 0, 2)).reshape(128, 4, 64)
    m["s0T"] = s0T
    return m


_CACHE = {}


def kernel(**inputs):
    NT_S, NP, NCORE = 32, 4, 8
    if "nc" not in _CACHE:
        _CACHE["nc"] = build(NT_S, NP)
    nc = _CACHE["nc"]
    shared = _prep_shared(inputs)
    in_maps = [_prep_core(inputs, shared, i, NT_S, NP) for i in range(NCORE)]
    res = run_bass_kernel_spmd(nc, in_maps, core_ids=list(range(NCORE)))
    y_sample = np.stack([np.asarray(r["ys"]) for r in res.results], 0).astype(np.float32)
    y_prompt = np.concatenate([np.asarray(r["yp"]) for r in res.results], 0).astype(np.float32)
    nf, nb = [], []
    for r in res.results:
        nst = np.asarray(r["nst"])
        for p in range(NP):
            for d, lst in ((0, nf), (1, nb)):
                S = nst[p, d].reshape(2, 64, 4, 64)
                lst.append(np.transpose(S, (2, 0, 3, 1)).reshape(8, 64, 64)[None])
    new_f = np.stack(nf, 0).astype(np.float32)
    new_b = np.stack(nb, 0).astype(np.float32)
    return (y_prompt, y_sample, new_f, new_b)
```
